# Optimizing a Trainium2 kernel written in Bass

```python
import jax, jax.numpy as jnp
from jax import lax
import numpy as np

D_MODEL = 2048
BATCH = 2
SEQ = 4096
DEPTH = 4
DEC_BATCH = 4
DEC_SEQ = 2048
PAST_LEN = 128

N_HEADS = 8
HEAD_DIM = 128
V_DIM = 2 * HEAD_DIM
QK_WIDTH = N_HEADS * 2 * HEAD_DIM
V_WIDTH = N_HEADS * V_DIM
CONV_CH = 2048
CONV_WIDTH = 31
CONV_PAD = (CONV_WIDTH - 1) // 2
N_BRANCH = 2
D_FF = -(-8 * D_MODEL // (3 * 256)) * 256
IN_WIDTH = 2 * QK_WIDTH + V_WIDTH + 2 * CONV_CH + N_BRANCH * D_MODEL
Q_BLOCK = 128
NORM_EPS = 1e-6
LN_EPS = 1e-5

kernel_name = "hybrid_diffattn_conformer_encoder"


def _rmsnorm(x, g):
    xf = x.astype(jnp.float32)
    y = xf * lax.rsqrt(jnp.mean(xf * xf, axis=-1, keepdims=True) + NORM_EPS)
    return (y * g.astype(jnp.float32)).astype(x.dtype)


def _layernorm(x, g, b):
    xf = x.astype(jnp.float32)
    xc = xf - jnp.mean(xf, axis=-1, keepdims=True)
    y = xc * lax.rsqrt(jnp.mean(xc * xc, axis=-1, keepdims=True) + LN_EPS)
    return (y * g.astype(jnp.float32) + b.astype(jnp.float32)).astype(x.dtype)


def _alibi_slopes():
    return jnp.asarray(2.0 ** (-8.0 * np.arange(1, N_HEADS + 1) / N_HEADS), dtype=jnp.float32)


def _lambda_init(layer):
    return 0.8 - 0.6 * float(np.exp(-0.3 * layer))


def _diff_attention(q, k, v, lam, g_subln, lam_init):
    b, s = q.shape[0], q.shape[1]
    n_blk = s // Q_BLOCK
    slopes = _alibi_slopes()
    scale = HEAD_DIM ** -0.5
    kpos = jnp.arange(s)
    q_blocks = q.reshape(b, n_blk, Q_BLOCK, N_HEADS, 2, HEAD_DIM).swapaxes(0, 1)

    def one_block(args):
        qb, i = args
        qpos = i * Q_BLOCK + jnp.arange(Q_BLOCK)
        dist = jnp.abs(qpos[:, None] - kpos[None, :]).astype(jnp.float32)
        bias = -slopes[:, None, None] * dist[None]
        sc = jnp.einsum("bqhmd,bkhmd->bhmqk", qb, k, preferred_element_type=jnp.float32)
        sc = sc * scale + bias[None, :, None]
        p = jax.nn.softmax(sc, axis=-1)
        w = p[:, :, 0] - lam * p[:, :, 1]
        return jnp.einsum("bhqk,bkhv->bqhv", w.astype(v.dtype), v)

    o = lax.map(one_block, (q_blocks, jnp.arange(n_blk)))
    o = o.swapaxes(0, 1).reshape(b, s, N_HEADS, V_DIM)
    o = _rmsnorm(o, g_subln) * (1.0 - lam_init)
    return o.reshape(b, s, V_WIDTH)


def _conformer_conv(u, w_dw, b_dw, g_ln, b_ln, w_pw, b_pw):
    val, gate = jnp.split(u, 2, axis=-1)
    y = val * jax.nn.sigmoid(gate)
    y = lax.conv_general_dilated(
        y, w_dw, window_strides=(1,), padding=[(CONV_PAD, CONV_PAD)],
        dimension_numbers=("NWC", "WIO", "NWC"), feature_group_count=CONV_CH) + b_dw
    y = jax.nn.silu(_layernorm(y, g_ln, b_ln))
    return y @ w_pw + b_pw


def _trunk(x, g_mix, w_in, b_gate, lambda_q, lambda_k, g_subln, w_attn_proj,
           w_dw, b_dw, g_conv_ln, b_conv_ln, w_conv_proj, b_conv_proj, w_out,
           g_ffn, w_ffn_in, w_ffn_out, g_final):
    b, s, _ = x.shape
    splits = np.cumsum([QK_WIDTH, QK_WIDTH, V_WIDTH, 2 * CONV_CH]).tolist()
    for l in range(DEPTH):
        h = _rmsnorm(x, g_mix[l])
        z = h @ w_in[l]
        zq, zk, zv, zu, zg = jnp.split(z, splits, axis=-1)
        q = zq.reshape(b, s, N_HEADS, 2, HEAD_DIM)
        k = zk.reshape(b, s, N_HEADS, 2, HEAD_DIM)
        v = zv.reshape(b, s, N_HEADS, V_DIM)
        lam_init = _lambda_init(l)
        lq = lambda_q[l].astype(jnp.float32)
        lk = lambda_k[l].astype(jnp.float32)
        lam = jnp.exp(jnp.sum(lq[0] * lk[0])) - jnp.exp(jnp.sum(lq[1] * lk[1])) + lam_init
        a = _diff_attention(q, k, v, lam, g_subln[l], lam_init) @ w_attn_proj[l]
        c = _conformer_conv(zu, w_dw[l], b_dw[l], g_conv_ln[l], b_conv_ln[l],
                            w_conv_proj[l], b_conv_proj[l])
        gates = jax.nn.sigmoid(zg + b_gate[l]).reshape(b, s, N_BRANCH, D_MODEL)
        merged = gates[:, :, 0] * a + gates[:, :, 1] * c
        x = x + merged @ w_out[l]
        h2 = _rmsnorm(x, g_ffn[l])
        f_gate, f_up = jnp.split(h2 @ w_ffn_in[l], 2, axis=-1)
        x = x + (jax.nn.silu(f_gate) * f_up) @ w_ffn_out[l]
    return _rmsnorm(x, g_final)


def setup_inputs(seed: int = 0) -> dict:
    key = jax.random.key(seed)
    ks = jax.random.split(key, 20)

    def nrm(k, shape, scale):
        return jax.random.normal(k, shape, dtype=jnp.float32) * scale

    def gain(k, shape):
        return 1.0 + nrm(k, shape, 0.02)

    res_scale = (2 * DEPTH) ** -0.5
    return {
        "x_prompt": nrm(ks[0], (BATCH, SEQ, D_MODEL), 1.0),
        "x_sample": nrm(ks[1], (DEC_BATCH, DEC_SEQ, D_MODEL), 1.0),
        "g_mix": gain(ks[2], (DEPTH, D_MODEL)),
        "w_in": nrm(ks[3], (DEPTH, D_MODEL, IN_WIDTH), D_MODEL ** -0.5),
        "b_gate": nrm(ks[4], (DEPTH, N_BRANCH * D_MODEL), 0.02),
        "lambda_q": nrm(ks[5], (DEPTH, 2, HEAD_DIM), 0.1),
        "lambda_k": nrm(ks[6], (DEPTH, 2, HEAD_DIM), 0.1),
        "g_subln": gain(ks[7], (DEPTH, V_DIM)),
        "w_attn_proj": nrm(ks[8], (DEPTH, V_WIDTH, D_MODEL), V_WIDTH ** -0.5),
        "w_dw": nrm(ks[9], (DEPTH, CONV_WIDTH, 1, CONV_CH), CONV_WIDTH ** -0.5),
        "b_dw": nrm(ks[10], (DEPTH, CONV_CH), 0.02),
        "g_conv_ln": gain(ks[11], (DEPTH, CONV_CH)),
        "b_conv_ln": nrm(ks[12], (DEPTH, CONV_CH), 0.02),
        "w_conv_proj": nrm(ks[13], (DEPTH, CONV_CH, D_MODEL), CONV_CH ** -0.5),
        "b_conv_proj": nrm(ks[14], (DEPTH, D_MODEL), 0.02),
        "w_out": nrm(ks[15], (DEPTH, D_MODEL, D_MODEL), D_MODEL ** -0.5 * res_scale),
        "g_ffn": gain(ks[16], (DEPTH, D_MODEL)),
        "w_ffn_in": nrm(ks[17], (DEPTH, D_MODEL, 2 * D_FF), D_MODEL ** -0.5),
        "w_ffn_out": nrm(ks[18], (DEPTH, D_FF, D_MODEL), D_FF ** -0.5 * res_scale),
        "g_final": gain(ks[19], (D_MODEL,)),
    }


def reference(x_prompt, x_sample, g_mix, w_in, b_gate, lambda_q, lambda_k, g_subln,
              w_attn_proj, w_dw, b_dw, g_conv_ln, b_conv_ln, w_conv_proj, b_conv_proj,
              w_out, g_ffn, w_ffn_in, w_ffn_out, g_final):
    y_prompt = _trunk(x_prompt, g_mix, w_in, b_gate, lambda_q, lambda_k, g_subln,
                      w_attn_proj, w_dw, b_dw, g_conv_ln, b_conv_ln, w_conv_proj,
                      b_conv_proj, w_out, g_ffn, w_ffn_in, w_ffn_out, g_final)
    y_sample = _trunk(x_sample, g_mix, w_in, b_gate, lambda_q, lambda_k, g_subln,
                      w_attn_proj, w_dw, b_dw, g_conv_ln, b_conv_ln, w_conv_proj,
                      b_conv_proj, w_out, g_ffn, w_ffn_in, w_ffn_out, g_final)
    return (y_prompt, y_sample)
```

```python
import numpy as np
from contextlib import ExitStack
import concourse.bass as bass
import concourse.mybir as mybir
from concourse.bass_utils import run_bass_kernel_spmd

F32 = mybir.dt.float32
BF16 = mybir.dt.bfloat16
U8 = mybir.dt.uint8
AF = mybir.ActivationFunctionType
ALU = mybir.AluOpType

NORM_EPS = 1e-6
LN_EPS = 1e-5
MASK_NEG = -30000.0
SB_BYTES = 192 * 1024


class Cfg:
    def __init__(self, T=4096, D=2048, H=8, DFF=5632, DEPTH=4, CW=31):
        self.T, self.D, self.H, self.DFF, self.DEPTH, self.CW = T, D, H, DFF, DEPTH, CW
        self.HD = 128
        self.VD = 256
        self.QK = H * 2 * self.HD
        self.VW = H * self.VD
        self.CC = D
        self.IN_W = 2 * self.QK + self.VW + 2 * self.CC + 2 * D
        self.DC = D // 128
        self.CCc = self.CC // 128
        self.NK = T // 128
        self.NQ = T // 512
        self.AOFF = T - 128
        self.AW = (self.NQ - 1) * 512 + 512 + self.AOFF
        c = 0
        self.c_gmix = c; c += self.DC
        self.c_bgate = c; c += 2 * self.DC
        self.c_bdw = c; c += self.CCc
        self.c_gcln = c; c += self.CCc
        self.c_bcln = c; c += self.CCc
        self.c_bcproj = c; c += self.DC
        self.c_gffn = c; c += self.DC
        self.c_wdw = c; c += self.CCc * CW
        self.c_gsub = c; c += self.VD
        self.c_lq = c; c += 2 * self.HD
        self.c_lk = c; c += 2 * self.HD
        self.NCL = c
        g = 0
        self.g_gfinal = g; g += self.DC
        self.g_ident = g; g += 128
        self.g_kmask = g; g += self.NK
        self.NCG = g


def lambda_init(layer):
    return 0.8 - 0.6 * float(np.exp(-0.3 * layer))


class DmaSem:
    def __init__(self, sem):
        self.sem = sem
        self.count = 0


class Prog:
    ENG = ('pe', 'act', 'dve', 'pool', 'sp')

    def __init__(self, nc, stack):
        self.nc = nc
        self.stack = stack
        self.streams = {k: [] for k in self.ENG}
        self.cnt = {k: 0 for k in self.ENG}
        self.esem = {k: stack.enter_context(nc.semaphore("s_" + k)) for k in self.ENG}
        self.dsems = {}
        self.pending = {k: [] for k in self.ENG}
        self.waited = {k: {} for k in self.ENG}

    def ds(self, name):
        if name not in self.dsems:
            self.dsems[name] = DmaSem(self.stack.enter_context(self.nc.semaphore("d_" + name)))
        return self.dsems[name]

    def _w(self, eng, waits):
        w = []

        def fl(x):
            if x is None:
                return
            if isinstance(x, list):
                for y in x:
                    fl(y)
            else:
                w.append(x)
        fl(list(waits))
        if self.pending[eng]:
            w = self.pending[eng] + w
            self.pending[eng] = []
        return w

    def op(self, eng, fn, waits=()):
        self.cnt[eng] += 1
        self.streams[eng].append((fn, self._w(eng, waits), None))
        return (eng, self.cnt[eng])

    def dma(self, eng, fn, ds, waits=()):
        ds.count += 16
        self.streams[eng].append((fn, self._w(eng, waits), ds))
        return (ds, ds.count)

    def last(self, eng):
        return (eng, self.cnt[eng]) if self.cnt[eng] else None

    def barrier(self):
        toks = [(k, self.cnt[k]) for k in self.ENG if self.cnt[k]]
        toks += [(d, d.count) for d in self.dsems.values() if d.count]
        for k in self.ENG:
            self.pending[k] = list(toks)

    def finish(self):
        self.barrier()
        for k in self.ENG:
            w = self._w(k, [])
            self.streams[k].append((None, w, None))

    def flush(self):
        nc = self.nc
        P = self
        with nc.Block() as block:
            @block.tensor
            def _(e):
                P.replay('pe', e)

            @block.scalar
            def _(e):
                P.replay('act', e)

            @block.vector
            def _(e):
                P.replay('dve', e)

            @block.gpsimd
            def _(e):
                P.replay('pool', e)

            @block.sync
            def _(e):
                P.replay('sp', e)
        for k in self.ENG:
            self.streams[k] = []

    def replay(self, eng, e):
        waited = self.waited[eng]
        for fn, waits, ds in self.streams[eng]:
            for key, val in waits:
                kk = key if isinstance(key, str) else id(key)
                if waited.get(kk, 0) >= val:
                    continue
                waited[kk] = val
                sem = self.esem[key] if isinstance(key, str) else key.sem
                e.wait_ge(sem, val)
            if fn is None:
                continue
            inst = fn(e)
            if ds is None:
                inst.then_inc(self.esem[eng], 1)
            else:
                inst.then_inc(ds.sem, 16)


class Alloc:
    def __init__(self, sb, base, limit):
        self.sb, self.base, self.off, self.limit = sb, base, base, limit

    def reset(self):
        self.off = self.base

    def get(self, shape, dt):
        esz = 4 if dt == F32 else 2
        n = int(np.prod(shape[1:]))
        nb = (n * esz + 63) // 64 * 64
        o = self.off
        self.off += nb
        assert self.off <= self.limit, ("SBUF overflow", self.off, self.limit)
        v = self.sb[:, o:o + n * esz].bitcast(dt)
        if len(shape) == 3:
            v = v.rearrange("p (a b) -> p a b", b=shape[2])
        return v


def fm(ap):
    return ap.rearrange("(c p) t -> p c t", p=128)


class Builder:
    def __init__(self, cfg):
        self.cfg = cfg

    def build(self):
        cfg = self.cfg
        T, D, DEPTH = cfg.T, cfg.D, cfg.DEPTH
        nc = bass.Bass("TRN2", target_bir_lowering=False)
        self.nc = nc
        dt_in = lambda name, shape: nc.dram_tensor(name, shape, F32, kind="ExternalInput").ap()
        self.xT_in = dt_in("xT", [D, T])
        self.cl_in = dt_in("cl", [DEPTH, 128, cfg.NCL])
        self.cg_in = dt_in("cg", [128, cfg.NCG])
        self.atab_in = dt_in("atab", [128, cfg.AW])
        self.tmask_in = dt_in("tmask", [128, T])
        self.w_in = dt_in("w_in", [DEPTH, D, cfg.IN_W])
        self.w_ap = dt_in("w_attn_proj", [DEPTH, cfg.VW, D])
        self.w_cp = dt_in("w_conv_proj", [DEPTH, cfg.CC, D])
        self.w_out = dt_in("w_out", [DEPTH, D, D])
        self.w_fi = dt_in("w_ffn_in", [DEPTH, D, 2 * cfg.DFF])
        self.w_fo = dt_in("w_ffn_out", [DEPTH, cfg.DFF, D])
        self.yT_out = nc.dram_tensor("yT", [D, T], F32, kind="ExternalOutput").ap()
        scr = lambda name, shape, dt: nc.dram_tensor(name, shape, dt).ap()
        self.XT = scr("s_XT", [D, T], F32)
        self.HT = scr("s_HT", [D, T], BF16)
        self.QKT = scr("s_QKT", [2 * cfg.QK, T], BF16)
        self.V = scr("s_V", [T, cfg.VW], BF16)
        self.YT = scr("s_YT", [cfg.CC, T], BF16)
        self.GT = scr("s_GT", [2 * D, T], BF16)
        self.ONT = scr("s_ONT", [cfg.VW, T], BF16)
        self.ZT = scr("s_ZT", [cfg.CC, T], F32)
        self.CT = scr("s_CT", [cfg.CC, T], BF16)
        self.MT = scr("s_MT", [D, T], BF16)
        self.AT = scr("s_AT", [cfg.DFF, T], BF16)

        with ExitStack() as stack:
            P = Prog(nc, stack)
            self.P = P
            GB = 16 * 1024
            gsb = stack.enter_context(nc.sbuf_tensor("gsb", [128, GB], U8))
            G = Alloc(gsb, 0, GB)
            self.cg = G.get([128, cfg.NCG], F32)
            self.cl = G.get([128, cfg.NCL], F32)
            self.ident_bf = G.get([128, 128], BF16)
            self.ones_bf = G.get([128, 128], BF16)
            self.neglam = G.get([128, 2], F32)
            self.gsub = G.get([128, cfg.VD], F32)
            self.lamtmp = G.get([128, 2 * cfg.HD], F32)
            self.lamred = G.get([128, 4], F32)
            self.PH_BYTES = SB_BYTES - GB
            self.phase_no = 0
            self.ph = None
            self.A = None
            self.banks = None
            t0 = P.dma('sp', lambda e: e.dma_start(out=self.cg, in_=self.cg_in), P.ds("c0"))
            t1 = P.op('dve', lambda e: e.tensor_copy(out=self.ident_bf, in_=self.cg[:, cfg.g_ident:cfg.g_ident + 128]), [t0])
            t2 = P.op('pool', lambda e: e.memset(self.ones_bf, 1.0))
            P.barrier()

            xsrc = self.xT_in
            for l in range(DEPTH):
                self.layer_consts(l)
                self.norm_phase(xsrc, cfg.c_gmix, self.HT, final=False)
                self.inproj_phase(l)
                self.attn_phase(l)
                self.conv1_phase(l)
                self.conv2_phase(l)
                self.merge_phase(l)
                self.resid_phase(l, self.w_out[l], self.MT, cfg.D // 128, xsrc, TS=min(2048, T))
                xsrc = self.XT
                self.norm_phase(xsrc, cfg.c_gffn, self.HT, final=False)
                self.ffnin_phase(l)
                self.resid_phase(l, self.w_fo[l], self.AT, cfg.DFF // 128, xsrc, TS=min(1024, T))
            self.norm_phase(xsrc, None, self.yT_out, final=True)
            P.finish()
            P.flush()
        return nc

    def phase_begin(self):
        nc = self.nc
        self.P.barrier()
        self.ph = ExitStack()
        self.phase_no += 1
        sbp = self.ph.enter_context(nc.sbuf_tensor("sb%d" % self.phase_no, [128, self.PH_BYTES], U8))
        psp = self.ph.enter_context(nc.psum_tensor("ps%d" % self.phase_no, [128, 8 * 512], F32))
        self.banks = [psp[:, b * 512:(b + 1) * 512] for b in range(8)]
        self.A = Alloc(sbp, 0, self.PH_BYTES)

    def phase_end(self):
        self.P.barrier()
        self.P.flush()
        self.ph.close()
        self.ph = None

    def layer_consts(self, l):
        cfg, P = self.cfg, self.P
        HD = cfg.HD
        P.barrier()
        t0 = P.dma('sp', lambda e: e.dma_start(out=self.cl, in_=self.cl_in[l]), P.ds("c0"))
        lq = self.cl[:, cfg.c_lq:cfg.c_lq + 2 * HD]
        lk = self.cl[:, cfg.c_lk:cfg.c_lk + 2 * HD]
        t1 = P.op('dve', lambda e: e.tensor_tensor(out=self.lamtmp, in0=lq, in1=lk, op=ALU.mult), [t0])
        t2 = P.op('dve', lambda e: e.tensor_reduce(
            out=self.lamred[:, 0:2], in_=self.lamtmp.rearrange("p (a b) -> p a b", b=HD),
            axis=mybir.AxisListType.X, op=ALU.add), [t1])
        t3 = P.op('act', lambda e: e.activation(out=self.lamred[:, 2:4], in_=self.lamred[:, 0:2], func=AF.Exp), [t2])
        li = lambda_init(l)
        t4 = P.op('dve', lambda e: e.tensor_tensor(out=self.neglam[:, 0:1], in0=self.lamred[:, 3:4],
                                                   in1=self.lamred[:, 2:3], op=ALU.subtract), [t3])
        t5 = P.op('dve', lambda e: e.tensor_scalar(out=self.neglam[:, 1:2], in0=self.neglam[:, 0:1],
                                                   scalar1=-li, scalar2=None, op0=ALU.add), [t4])
        t6 = P.op('dve', lambda e: e.tensor_scalar(out=self.gsub, in0=self.cl[:, cfg.c_gsub:cfg.c_gsub + cfg.VD],
                                                   scalar1=(1.0 - li), scalar2=None, op0=ALU.mult), [t5])
        P.barrier()

    def norm_phase(self, src, gcol, dst, final):
        cfg, P, A = self.cfg, self.P, self.A
        DC, D = cfg.DC, cfg.D
        self.phase_begin()
        A = self.A
        odt = F32 if final else BF16
        xin = [A.get([128, DC, 512], F32) for _ in range(2)]
        sq = [A.get([128, DC, 512], BF16) for _ in range(2)]
        hout = [A.get([128, DC, 512], odt) for _ in range(2)]
        rstd = [A.get([128, 512], F32) for _ in range(2)]
        gsrc = self.cg if final else self.cl
        gc = cfg.g_gfinal if final else gcol
        dsi = [P.ds("in0"), P.ds("in1")]
        dso = [P.ds("st0"), P.ds("st1")]
        hdone = [None, None]
        mmdone = [None, None]
        stdone = [None, None]
        r2done = [None, None]
        for i in range(cfg.NQ):
            b = i % 2
            ts = slice(i * 512, (i + 1) * 512)
            ld = P.dma('sp', lambda e, b=b, ts=ts: e.dma_start(out=xin[b], in_=fm(src)[:, :, ts]), dsi[b],
                       (hdone[b] or []))
            sqt = P.op('act', lambda e, b=b: e.activation(out=sq[b], in_=xin[b], func=AF.Square), [ld, mmdone[b]])
            mm = None
            for c in range(DC):
                mm = P.op('pe', lambda e, b=b, c=c: e.matmul(self.banks[b], self.ones_bf, sq[b][:, c, :],
                                                             start=(c == 0), stop=(c == DC - 1)),
                          [sqt, r2done[b]] if c == 0 else [])
            mmdone[b] = mm
            r1 = P.op('dve', lambda e, b=b: e.tensor_scalar(out=rstd[b], in0=self.banks[b], scalar1=1.0 / D,
                                                            scalar2=NORM_EPS, op0=ALU.mult, op1=ALU.add),
                      [mm] + (hdone[b] or []))
            r2a = P.op('act', lambda e, b=b: e.activation(out=rstd[b], in_=rstd[b], func=AF.Sqrt), [r1])
            r2 = P.op('dve', lambda e, b=b: e.reciprocal(out=rstd[b], in_=rstd[b]), [r2a])
            r2done[b] = r1
            lastd = lastp = None
            for c in range(DC):
                eng = 'dve'
                tk = P.op(eng, lambda e, b=b, c=c: e.scalar_tensor_tensor(
                    out=hout[b][:, c, :], in0=xin[b][:, c, :], scalar=gsrc[:, gc + c:gc + c + 1],
                    in1=rstd[b], op0=ALU.mult, op1=ALU.mult), [r2, stdone[b]])
                if eng == 'dve':
                    lastd = tk
                else:
                    lastp = tk
            hdone[b] = [lastd, lastp]
            stdone[b] = P.dma('sp', lambda e, b=b, ts=ts: e.dma_start(out=fm(dst)[:, :, ts], in_=hout[b]), dso[b],
                              [lastd, lastp])
        self.phase_end()

    def linear(self, ins, TS, blocks):
        cfg, P, A = self.cfg, self.P, self.A
        T = cfg.T
        NS = T // TS
        NHALF = TS // 1024
        KCmax = max(kc for _, kc in ins)
        FBmax = max(sum(p[2] for p in b['pieces']) for b in blocks)
        in_sb = [A.get([128, kc, TS], BF16) for _, kc in ins]
        wbuf = [A.get([128, KCmax, FBmax], BF16) for _ in range(2)]
        dsw = [P.ds("w0"), P.ds("w1")]
        dsin = [P.ds("in0"), P.ds("in1")]
        wfree = [None, None]
        bankfree = [[None] * 4, [None] * 4]
        seq = [(s, bi) for s in range(NS) for bi in range(len(blocks))]
        wtok = {}

        def issue_w(n):
            s, bi = seq[n]
            wb = n % 2
            c0 = 0
            tk = None
            for (w2d, col0, ncols, in_idx) in blocks[bi]['pieces']:
                kc = ins[in_idx][1]
                tk = P.dma('pool', lambda e, wb=wb, c0=c0, w2d=w2d, col0=col0, ncols=ncols, kc=kc: e.dma_start(
                    out=wbuf[wb][:, 0:kc, c0:c0 + ncols],
                    in_=w2d.rearrange("(c p) f -> p c f", p=128)[:, :, col0:col0 + ncols]), dsw[wb], [wfree[wb]])
                c0 += ncols
            wtok[n] = tk

        issue_w(0)
        ucount = 0
        intok = [None] * len(ins)
        lastpe_all = None
        for n, (s, bi) in enumerate(seq):
            blk = blocks[bi]
            if bi == 0:
                for ii, (src, kc) in enumerate(ins):
                    tk = None
                    nsp = max(1, kc // 8)
                    for q in range(nsp):
                        cs = slice(q * kc // nsp, (q + 1) * kc // nsp)
                        tk = P.dma('sp', lambda e, ii=ii, cs=cs, s=s, src=src: e.dma_start(
                            out=in_sb[ii][:, cs, :], in_=fm(src)[:, cs, s * TS:(s + 1) * TS]), dsin[ii], [lastpe_all])
                    intok[ii] = tk
            if n + 1 < len(seq):
                issue_w(n + 1)
            wb = n % 2
            FB = sum(p[2] for p in blk['pieces'])
            kind = blk['kind']
            lastpe = None
            if kind == 'vtok':
                kc = ins[0][1]
                for g in range(TS // 512):
                    bs = ucount % 2
                    ucount += 1
                    bk = self.banks[bs * 4:bs * 4 + 4]
                    tok0 = s * TS + g * 512
                    for k in range(kc):
                        for j in range(4):
                            w_ = [wtok[n], intok[0], bankfree[bs][j]] if k == 0 else []
                            lastpe = P.op('pe', lambda e, k=k, j=j, g=g, wb=wb, bk=bk, FB=FB: e.matmul(
                                bk[j][:, 0:FB], in_sb[0][:, k, g * 512 + j * 128:g * 512 + (j + 1) * 128],
                                wbuf[wb][:, k, 0:FB], start=(k == 0), stop=(k == kc - 1)), w_)
                    rel = blk['epi'](0, tok0, bk, lastpe, None)
                    bankfree[bs] = rel
            else:
                nunits = FB // 256
                for u in range(nunits):
                    if kind == 'pair':
                        cols = [u * 128, FB // 2 + u * 128]
                        iidx = [blk['pieces'][0][3], blk['pieces'][1][3]]
                    else:
                        cols = [2 * u * 128, (2 * u + 1) * 128]
                        iidx = [blk['pieces'][0][3]] * 2
                    for h in range(NHALF):
                        bs = ucount % 2
                        ucount += 1
                        bk = self.banks[bs * 4:bs * 4 + 4]
                        tok0 = s * TS + h * 1024
                        pretoks = blk['pre'](u, tok0) if blk.get('pre') else None
                        kcs = [ins[iidx[0]][1], ins[iidx[1]][1]]
                        kc = max(kcs)
                        for k in range(kc):
                            for ci in range(2):
                                if k >= kcs[ci]:
                                    continue
                                for t in range(2):
                                    w_ = [wtok[n], intok[iidx[ci]], bankfree[bs][ci * 2 + t]] if k == 0 else []
                                    lastpe = P.op('pe', lambda e, k=k, ci=ci, t=t, h=h, wb=wb, bk=bk, cols=cols, iidx=iidx, kcs=kcs: e.matmul(
                                        bk[ci * 2 + t], wbuf[wb][:, k, cols[ci]:cols[ci] + 128],
                                        in_sb[iidx[ci]][:, k, h * 1024 + t * 512:h * 1024 + (t + 1) * 512],
                                        start=(k == 0), stop=(k == kcs[ci] - 1)), w_)
                        rel = blk['epi'](u, tok0, bk, lastpe, pretoks)
                        bankfree[bs] = rel
            wfree[wb] = lastpe
            lastpe_all = lastpe

    def stager(self, name, shape, dt, n=2):
        bufs = [self.A.get(shape, dt) for _ in range(n)]
        return {'bufs': bufs, 'tok': [None] * n, 'i': 0, 'ds': [self.P.ds("%s%d" % (name, i)) for i in range(n)]}

    def inproj_phase(self, l):
        cfg, P, A = self.cfg, self.P, self.A
        D, QK, VW, CC = cfg.D, cfg.QK, cfg.VW, cfg.CC
        self.phase_begin()
        A = self.A
        w = self.w_in[l]
        TS = min(2048, cfg.T)
        stg = self.stager("st", [128, 2, 1024], BF16)
        sig = [A.get([128, 1024], F32) for _ in range(2)]
        vst = self.stager("sv", [128, 4, 512], BF16)
        sigtok = [None, None]
        cnt = [0]
        bgc = cfg.c_bgate
        banks = self.banks

        def store2(s, row0s, dst, tok0, evs):
            b = s['i']
            buf = s['bufs'][b]
            t = None
            for ci, r0 in enumerate(row0s):
                t = P.dma('sp', lambda e, ci=ci, r0=r0, buf=buf: e.dma_start(
                    out=dst[r0:r0 + 128, tok0:tok0 + 1024], in_=buf[:, ci, :]), s['ds'][b], evs)
            s['tok'][b] = t
            s['i'] = (b + 1) % len(s['bufs'])

        def epi_copy(row_of_unit, dst):
            def f(u, tok0, bk, lastpe, pre):
                b = stg['i']
                buf = stg['bufs'][b]
                evs = []
                for ci in range(2):
                    for t in range(2):
                        eng = 'act' if (ci + t) % 2 == 0 else 'dve'
                        if eng == 'act':
                            tk = P.op('act', lambda e, ci=ci, t=t, buf=buf, bk=bk: e.activation(
                                out=buf[:, ci, t * 512:(t + 1) * 512], in_=bk[ci * 2 + t], func=AF.Copy),
                                [lastpe, stg['tok'][b]])
                        else:
                            tk = P.op('dve', lambda e, ci=ci, t=t, buf=buf, bk=bk: e.tensor_copy(
                                out=buf[:, ci, t * 512:(t + 1) * 512], in_=bk[ci * 2 + t]),
                                [lastpe, stg['tok'][b]])
                        evs.append(tk)
                r0 = row_of_unit(u)
                store2(stg, [r0, r0 + 128], dst, tok0, evs)
                return evs
            return f

        def epi_gates(row_of_unit):
            def f(u, tok0, bk, lastpe, pre):
                b = stg['i']
                buf = stg['bufs'][b]
                evs = []
                r0 = row_of_unit(u)
                for ci in range(2):
                    col = bgc + (r0 + ci * 128) // 128
                    for t in range(2):
                        tk = P.op('act', lambda e, ci=ci, t=t, buf=buf, bk=bk, col=col: e.activation(
                            out=buf[:, ci, t * 512:(t + 1) * 512], in_=bk[ci * 2 + t], func=AF.Sigmoid,
                            bias=self.cl[:, col:col + 1]), [lastpe, stg['tok'][b]])
                        evs.append(tk)
                store2(stg, [r0, r0 + 128], self.GT, tok0, evs)
                return evs
            return f

        def epi_glu(row_of_unit):
            def f(u, tok0, bk, lastpe, pre):
                b = stg['i']
                buf = stg['bufs'][b]
                sb_ = cnt[0] % 2
                cnt[0] += 1
                rel = [None] * 4
                evs = []
                for t in range(2):
                    ta = P.op('act', lambda e, t=t, bk=bk, sb_=sb_: e.activation(
                        out=sig[sb_][:, t * 512:(t + 1) * 512], in_=bk[2 + t], func=AF.Sigmoid),
                        [lastpe, sigtok[sb_]])
                    td = P.op('dve', lambda e, t=t, bk=bk, buf=buf, sb_=sb_: e.tensor_tensor(
                        out=buf[:, 0, t * 512:(t + 1) * 512], in0=bk[t], in1=sig[sb_][:, t * 512:(t + 1) * 512],
                        op=ALU.mult), [ta, stg['tok'][b]])
                    rel[2 + t] = ta
                    rel[t] = td
                    evs.append(td)
                sigtok[sb_] = evs[-1]
                r0 = row_of_unit(u)
                store2(stg, [r0], self.YT, tok0, evs)
                return rel
            return f

        def epi_v(col0):
            def f(u, tok0, bk, lastpe, pre):
                b = vst['i']
                buf = vst['bufs'][b]
                FB = min(512, VW)
                evs = []
                for j in range(4):
                    if j % 2 == 0:
                        tk = P.op('act', lambda e, j=j, buf=buf, bk=bk: e.activation(
                            out=buf[:, j, 0:FB], in_=bk[j][:, 0:FB], func=AF.Copy), [lastpe, vst['tok'][b]])
                    else:
                        tk = P.op('dve', lambda e, j=j, buf=buf, bk=bk: e.tensor_copy(
                            out=buf[:, j, 0:FB], in_=bk[j][:, 0:FB]), [lastpe, vst['tok'][b]])
                    evs.append(tk)
                vst['tok'][b] = P.dma('sp', lambda e, buf=buf: e.dma_start(
                    out=self.V[tok0:tok0 + 512, col0:col0 + FB].rearrange("(j p) f -> p j f", p=128),
                    in_=buf[:, :, 0:FB]), vst['ds'][b], evs)
                vst['i'] = (b + 1) % 2
                return evs
            return f

        blocks = []
        for c0 in range(0, 2 * QK, 512):
            blocks.append(dict(pieces=[(w, c0, 512, 0)], kind='single',
                               epi=epi_copy(lambda u, c0=c0: c0 + u * 256, self.QKT)))
        FBv = min(512, VW)
        for c0 in range(0, VW, FBv):
            blocks.append(dict(pieces=[(w, 2 * QK + c0, FBv, 0)], kind='vtok', epi=epi_v(c0)))
        ub = 2 * QK + VW
        for c0 in range(0, CC, 256):
            blocks.append(dict(pieces=[(w, ub + c0, 256, 0), (w, ub + CC + c0, 256, 0)], kind='pair',
                               epi=epi_glu(lambda u, c0=c0: c0 + u * 128)))
        gb = ub + 2 * CC
        for c0 in range(0, 2 * D, 512):
            blocks.append(dict(pieces=[(w, gb + c0, 512, 0)], kind='single',
                               epi=epi_gates(lambda u, c0=c0: c0 + u * 256)))
        self.linear([(self.HT, cfg.DC)], TS, blocks)
        self.phase_end()

    def merge_phase(self, l):
        cfg, P, A = self.cfg, self.P, self.A
        D = cfg.D
        self.phase_begin()
        A = self.A
        TS = min(1024, cfg.T)
        stg = self.stager("st", [128, 1, 1024], BF16)
        gbuf = [[A.get([128, 1024], BF16) for _ in range(2)] for _ in range(2)]
        gds = [P.ds("g0"), P.ds("g1")]
        gfree = [None, None]
        t1 = [A.get([128, 1024], F32) for _ in range(2)]
        t2 = [A.get([128, 1024], F32) for _ in range(2)]
        tfree = [None, None]
        cnt = [0]
        slot_of = {}

        def pre(row0):
            def f(u, tok0):
                s_ = cnt[0] % 2
                cnt[0] += 1
                r = row0 + u * 128
                tk = None
                for gi in range(2):
                    tk = P.dma('sp', lambda e, gi=gi, r=r, s_=s_: e.dma_start(
                        out=gbuf[s_][gi], in_=self.GT[gi * D + r:gi * D + r + 128, tok0:tok0 + 1024]),
                        gds[s_], [gfree[s_]])
                return (s_, tk)
            return f

        def epi(row0):
            def f(u, tok0, bk, lastpe, pretoks):
                s_, gtok = pretoks
                b = stg['i']
                buf = stg['bufs'][b]
                r = row0 + u * 128
                col = cfg.c_bcproj + r // 128
                rel = [None] * 4
                evs = []
                for t in range(2):
                    sl = slice(t * 512, (t + 1) * 512)
                    ta = P.op('dve', lambda e, t=t, sl=sl, bk=bk, s_=s_: e.tensor_tensor(
                        out=t1[s_][:, sl], in0=bk[t], in1=gbuf[s_][0][:, sl], op=ALU.mult),
                        [lastpe, gtok, tfree[s_]])
                    tb = P.op('dve', lambda e, t=t, sl=sl, bk=bk, s_=s_, col=col: e.scalar_tensor_tensor(
                        out=t2[s_][:, sl], in0=bk[2 + t], scalar=self.cl[:, col:col + 1], in1=gbuf[s_][1][:, sl],
                        op0=ALU.add, op1=ALU.mult), [lastpe, gtok, tfree[s_]])
                    tc = P.op('pool', lambda e, sl=sl, buf=buf, s_=s_: e.tensor_tensor(
                        out=buf[:, 0, sl], in0=t1[s_][:, sl], in1=t2[s_][:, sl], op=ALU.add),
                        [ta, tb, stg['tok'][b]])
                    rel[t] = ta
                    rel[2 + t] = tb
                    evs.append(tc)
                tfree[s_] = evs[-1]
                gfree[s_] = rel[3]
                stg['tok'][b] = P.dma('sp', lambda e, buf=buf, r=r: e.dma_start(
                    out=self.MT[r:r + 128, tok0:tok0 + 1024], in_=buf[:, 0, :]), stg['ds'][b], evs)
                stg['i'] = (b + 1) % 2
                return rel
            return f

        blocks = []
        for c0 in range(0, D, 256):
            blocks.append(dict(pieces=[(self.w_ap[l], c0, 256, 0), (self.w_cp[l], c0, 256, 1)], kind='pair',
                               pre=pre(c0), epi=epi(c0)))
        self.linear([(self.ONT, cfg.VW // 128), (self.CT, cfg.CCc)], TS, blocks)
        self.phase_end()

    def resid_phase(self, l, w2d, src, KC, xsrc, TS):
        cfg, P, A = self.cfg, self.P, self.A
        D = cfg.D
        self.phase_begin()
        A = self.A
        xold = [[A.get([128, 1024], F32) for _ in range(2)] for _ in range(2)]
        xds = [P.ds("g0"), P.ds("g1")]
        xfree = [None, None]
        xnew = self.stager("st", [128, 2, 1024], F32)
        tmp = [A.get([128, 1024], F32) for _ in range(2)]
        tmpfree = [None, None]
        cnt = [0]

        def pre(row0):
            def f(u, tok0):
                s_ = cnt[0] % 2
                cnt[0] += 1
                tk = None
                for ci in range(2):
                    r = row0 + u * 256 + ci * 128
                    tk = P.dma('sp', lambda e, ci=ci, r=r, s_=s_: e.dma_start(
                        out=xold[s_][ci], in_=xsrc[r:r + 128, tok0:tok0 + 1024]), xds[s_], [xfree[s_]])
                return (s_, tk)
            return f

        def epi(row0):
            def f(u, tok0, bk, lastpe, pretoks):
                s_, xtok = pretoks
                b = xnew['i']
                buf = xnew['bufs'][b]
                rel = [None] * 4
                evs = []
                for t in range(2):
                    sl = slice(t * 512, (t + 1) * 512)
                    ta = P.op('dve', lambda e, t=t, sl=sl, bk=bk, buf=buf, s_=s_: e.tensor_tensor(
                        out=buf[:, 0, sl], in0=bk[t], in1=xold[s_][0][:, sl], op=ALU.add),
                        [lastpe, xtok, xnew['tok'][b]])
                    tb = P.op('act', lambda e, t=t, sl=sl, bk=bk, s_=s_: e.activation(
                        out=tmp[s_][:, sl], in_=bk[2 + t], func=AF.Copy), [lastpe, tmpfree[s_]])
                    tc = P.op('pool', lambda e, sl=sl, buf=buf, s_=s_: e.tensor_tensor(
                        out=buf[:, 1, sl], in0=tmp[s_][:, sl], in1=xold[s_][1][:, sl], op=ALU.add),
                        [tb, xtok, xnew['tok'][b]])
                    rel[t] = ta
                    rel[2 + t] = tb
                    evs += [ta, tc]
                tmpfree[s_] = evs[-1]
                xfree[s_] = [evs[-1], evs[-2]]
                t_ = None
                for ci in range(2):
                    r = row0 + u * 256 + ci * 128
                    t_ = P.dma('sp', lambda e, ci=ci, r=r, buf=buf: e.dma_start(
                        out=self.XT[r:r + 128, tok0:tok0 + 1024], in_=buf[:, ci, :]), xnew['ds'][b], evs)
                xnew['tok'][b] = t_
                xnew['i'] = (b + 1) % 2
                return rel
            return f

        blocks = []
        FB = 256 if KC > 16 else min(512, D)
        for c0 in range(0, D, FB):
            blocks.append(dict(pieces=[(w2d, c0, FB, 0)], kind='single', pre=pre(c0), epi=epi(c0)))
        self._flatten_fix = True
        self.linear([(src, KC)], TS, blocks)
        self.phase_end()

    def ffnin_phase(self, l):
        cfg, P, A = self.cfg, self.P, self.A
        DFF = cfg.DFF
        self.phase_begin()
        A = self.A
        TS = min(2048, cfg.T)
        stg = self.stager("st", [128, 1, 1024], BF16)
        sil = [A.get([128, 1024], F32) for _ in range(2)]
        silfree = [None, None]
        cnt = [0]
        w = self.w_fi[l]

        def epi(row0):
            def f(u, tok0, bk, lastpe, pre):
                b = stg['i']
                buf = stg['bufs'][b]
                s_ = cnt[0] % 2
                cnt[0] += 1
                rel = [None] * 4
                evs = []
                for t in range(2):
                    sl = slice(t * 512, (t + 1) * 512)
                    ta = P.op('act', lambda e, t=t, sl=sl, bk=bk, s_=s_: e.activation(
                        out=sil[s_][:, sl], in_=bk[t], func=AF.Silu), [lastpe, silfree[s_]])
                    td = P.op('dve', lambda e, t=t, sl=sl, bk=bk, buf=buf, s_=s_: e.tensor_tensor(
                        out=buf[:, 0, sl], in0=bk[2 + t], in1=sil[s_][:, sl], op=ALU.mult), [ta, stg['tok'][b]])
                    rel[t] = ta
                    rel[2 + t] = td
                    evs.append(td)
                silfree[s_] = evs[-1]
                r = row0 + u * 128
                stg['tok'][b] = P.dma('sp', lambda e, buf=buf, r=r: e.dma_start(
                    out=self.AT[r:r + 128, tok0:tok0 + 1024], in_=buf[:, 0, :]), stg['ds'][b], evs)
                stg['i'] = (b + 1) % 2
                return rel
            return f

        blocks = []
        for c0 in range(0, DFF, 256):
            blocks.append(dict(pieces=[(w, c0, 256, 0), (w, DFF + c0, 256, 0)], kind='pair', epi=epi(c0)))
        self.linear([(self.HT, cfg.DC)], TS, blocks)
        self.phase_end()

    def attn_phase(self, l):
        cfg, P, A = self.cfg, self.P, self.A
        T, H, HD, VD, NK, NQ = cfg.T, cfg.H, cfg.HD, cfg.VD, cfg.NK, cfg.NQ
        self.phase_begin()
        A = self.A
        banks = self.banks
        scale = HD ** -0.5
        atab = A.get([128, cfg.AW], F32)
        qb = [A.get([128, 2, T], BF16) for _ in range(2)]
        kb = [A.get([128, 2, T], BF16) for _ in range(2)]
        vb = [A.get([128, NK, VD + 1], BF16) for _ in range(2)]
        tmp = [A.get([128, 512], F32) for _ in range(2)]
        pt = [A.get([128, 512], BF16) for _ in range(3)]
        Om = [A.get([128, 4, VD], F32) for _ in range(2)]
        o = A.get([128, 4, VD], F32)
        junk = A.get([128, VD], F32)
        on = A.get([128, 4, VD], BF16)
        ost = [A.get([128, VD // 128, 512], BF16) for _ in range(2)]
        rec = A.get([128, 8], F32)
        ssq = A.get([128, 4], F32)
        rs2 = A.get([128, 4], F32)
        tb = banks[6].bitcast(BF16)
        ta = P.dma('sp', lambda e: e.dma_start(out=atab, in_=self.atab_in), P.ds("c0"))
        ones_tok = []
        for i in range(2):
            ones_tok.append(P.op('pool', lambda e, i=i: e.memset(vb[i][:, :, VD:VD + 1], 1.0)))
        dsl = [P.ds("in0"), P.ds("in1")]
        dso = [P.ds("st0"), P.ds("st1")]
        loadtok = [None, None]
        headdone = [None, None]

        def load_head(h):
            b = h % 2
            w_ = [headdone[b], ones_tok[b]]
            P.dma('sp', lambda e: e.dma_start(out=qb[b], in_=fm(self.QKT)[:, 2 * h:2 * h + 2, :]), dsl[b], w_)
            P.dma('sp', lambda e: e.dma_start(out=kb[b], in_=fm(self.QKT)[:, cfg.QK // 128 + 2 * h:cfg.QK // 128 + 2 * h + 2, :]),
                  dsl[b], w_)
            loadtok[b] = P.dma('sp', lambda e: e.dma_start(
                out=vb[b][:, :, 0:VD], in_=self.V.rearrange("(k p) f -> p k f", p=128)[:, :, h * VD:(h + 1) * VD]),
                dsl[b], w_)

        load_head(0)
        st = {'n': 0, 'sdve': [None, None], 'tact': [None, None], 'ptpe': [None, None, None],
              'accfree': [None] * 4, 'omfree': [None, None], 'ofree': None, 'onfree': None,
              'ostfree': [None, None], 'tbfree': None, 'osti': 0, 'ssqfree': None}
        deferred = []

        def run_head(h):
            hb = h % 2
            if h + 1 < H:
                load_head(h + 1)
            slope = 2.0 ** (-8.0 * (h + 1) / H)
            steps = [(qc, m, kt) for qc in range(NQ) for m in range(2) for kt in range(NK)]
            S_tok = {}
            P_tok = {}

            def emit_S(i):
                qc, m, kt = steps[i]
                n = st['n'] + i
                sbk = 4 + n % 2
                s_tok = P.op('pe', lambda e: e.matmul(
                    banks[sbk], kb[hb][:, m, kt * 128:(kt + 1) * 128], qb[hb][:, m, qc * 512:(qc + 1) * 512],
                    start=True, stop=True), [loadtok[hb], st['sdve'][n % 2]])
                u0 = 512 * qc - 128 * kt + cfg.AOFF
                t_tok = P.op('dve', lambda e: e.scalar_tensor_tensor(
                    out=tmp[n % 2], in0=atab[:, u0:u0 + 512], scalar=slope / scale, in1=banks[sbk],
                    op0=ALU.mult, op1=ALU.add), [s_tok, st['tact'][n % 2], ta])
                st['sdve'][n % 2] = t_tok
                kcol = cfg.g_kmask + kt
                p_tok = P.op('act', lambda e: e.activation(
                    out=pt[n % 3], in_=tmp[n % 2], func=AF.Exp, bias=self.cg[:, kcol:kcol + 1], scale=scale),
                    [t_tok, st['ptpe'][n % 3]])
                st['tact'][n % 2] = p_tok
                P_tok[i] = p_tok

            def emit_PV(i):
                qc, m, kt = steps[i]
                n = st['n'] + i
                last = None
                for qs in range(4):
                    w_ = [P_tok[i]]
                    if kt == 0:
                        w_.append(st['accfree'][qs])
                    last = P.op('pe', lambda e, qs=qs: e.matmul(
                        banks[qs][:, 0:VD + 1], pt[n % 3][:, qs * 128:(qs + 1) * 128], vb[hb][:, kt, :],
                        start=(kt == 0), stop=(kt == NK - 1)), w_)
                st['ptpe'][n % 3] = last
                if kt == NK - 1:
                    finish_group(qc, m, last)
                return last

            def finish_group(qc, m, lastpe):
                r_tok = []
                for qs in range(4):
                    c_ = P.op('dve', lambda e, qs=qs: e.tensor_scalar(
                        out=rec[:, m * 4 + qs:m * 4 + qs + 1], in0=banks[qs][:, VD:VD + 1], scalar1=1e-30,
                        scalar2=None, op0=ALU.max), [lastpe, st['omfree'][m]])
                    r_tok.append(P.op('dve', lambda e, qs=qs: e.reciprocal(
                        out=rec[:, m * 4 + qs:m * 4 + qs + 1], in_=rec[:, m * 4 + qs:m * 4 + qs + 1]), [c_]))
                ev = []
                for qs in range(4):
                    tk = P.op('dve', lambda e, qs=qs: e.tensor_scalar(
                        out=Om[m][:, qs, :], in0=banks[qs][:, 0:VD], scalar1=rec[:, m * 4 + qs:m * 4 + qs + 1],
                        scalar2=None, op0=ALU.mult), [r_tok[qs], st['omfree'][m]])
                    st['accfree'][qs] = tk
                    ev.append(tk)
                if m == 0:
                    st['om0'] = ev[-1]
                    return
                c_tok = P.op('dve', lambda e: e.scalar_tensor_tensor(
                    out=o, in0=Om[1], scalar=self.neglam[:, 1:2], in1=Om[0], op0=ALU.mult, op1=ALU.add),
                    [ev[-1], st['om0'], st['ofree']])
                st['omfree'] = [c_tok, c_tok]
                z_tok = P.op('pool', lambda e: e.memset(ssq, 0.0), [st['ssqfree']])
                sq_tok = None
                for qs in range(4):
                    sq_tok = P.op('act', lambda e, qs=qs: e.activation(
                        out=junk, in_=o[:, qs, :], func=AF.Square, accum_out=ssq[:, qs:qs + 1]),
                        [c_tok, z_tok, sq_tok])
                r1 = P.op('dve', lambda e: e.tensor_scalar(out=rs2, in0=ssq, scalar1=1.0 / VD, scalar2=NORM_EPS,
                                                           op0=ALU.mult, op1=ALU.add), [sq_tok, st['onfree']])
                r2a = P.op('act', lambda e: e.activation(out=rs2, in_=rs2, func=AF.Sqrt), [r1])
                r2 = P.op('dve', lambda e: e.reciprocal(out=rs2, in_=rs2), [r2a])
                st['ssqfree'] = r1
                n_tok = None
                for qs in range(4):
                    n_tok = P.op('dve', lambda e, qs=qs: e.scalar_tensor_tensor(
                        out=on[:, qs, :], in0=o[:, qs, :], scalar=rs2[:, qs:qs + 1], in1=self.gsub,
                        op0=ALU.mult, op1=ALU.mult), [r2, st['onfree']])
                st['ofree'] = n_tok

                def do_transposes(n_tok=n_tok, qc=qc, h=h):
                    tl = None
                    for vc in range(VD // 128):
                        for qs in range(4):
                            i_ = vc * 4 + qs
                            tl = P.op('pe', lambda e, vc=vc, qs=qs, i_=i_: e.transpose(
                                out=tb[:, i_ * 128:(i_ + 1) * 128], in_=on[:, qs, vc * 128:(vc + 1) * 128],
                                identity=self.ident_bf), [n_tok, st['tbfree']])
                    ob = st['osti']
                    st['osti'] = 1 - ob
                    c_ = P.op('act', lambda e: e.activation(
                        out=ost[ob].rearrange("p a b -> p (a b)"), in_=tb, func=AF.Copy), [tl, st['ostfree'][ob]])
                    st['tbfree'] = c_
                    st['onfree'] = tl
                    st['ostfree'][ob] = P.dma('sp', lambda e: e.dma_start(
                        out=fm(self.ONT)[:, h * (VD // 128):(h + 1) * (VD // 128), qc * 512:(qc + 1) * 512],
                        in_=ost[ob]), dso[ob], [c_])
                deferred.append([6, do_transposes])

            ns = len(steps)
            emit_S(0)
            if ns > 1:
                emit_S(1)
            lastpv = None
            for i in range(ns):
                lastpv = emit_PV(i)
                if i + 2 < ns:
                    emit_S(i + 2)
                for d in deferred:
                    d[0] -= 1
                for d in [d for d in deferred if d[0] <= 0]:
                    d[1]()
                    deferred.remove(d)
            st['n'] += ns
            headdone[hb] = lastpv

        for h in range(H):
            run_head(h)
        for d in deferred:
            d[1]()
        self.phase_end()

    def conv1_phase(self, l):
        cfg, P, A = self.cfg, self.P, self.A
        T, CW, CCc = cfg.T, cfg.CW, cfg.CCc
        PAD = (CW - 1) // 2
        self.phase_begin()
        A = self.A
        banks = self.banks
        ypad = [A.get([128, T + 2 * PAD + 2], BF16) for _ in range(2)]
        diag = [A.get([128, CW, 128], BF16) for _ in range(2)]
        zst = [A.get([128, T], F32) for _ in range(2)]
        dsl = [P.ds("in0"), P.ds("in1")]
        dso = [P.ds("st0"), P.ds("st1")]
        halo = []
        for i in range(2):
            halo.append(P.op('pool', lambda e, i=i: e.memset(ypad[i][:, 0:PAD], 0.0)))
            halo.append(P.op('pool', lambda e, i=i: e.memset(ypad[i][:, PAD + T:PAD + T + PAD], 0.0)))
        pedone = [None, None]
        stdone = [None, None]
        bankfree = [None] * 8
        bn = 0
        ldtok = {}
        tm = A.get([128, T], BF16)
        tmtok = P.dma("pool", lambda e: e.dma_start(out=tm, in_=self.tmask_in), P.ds("pm"))

        def load(c):
            b = c % 2
            t_ = P.dma('sp', lambda e: e.dma_start(out=ypad[b][:, PAD:PAD + T], in_=self.YT[c * 128:(c + 1) * 128, :]),
                       dsl[b], [pedone[b]] + halo)
            ldtok[c] = P.op('pool' if c % 2 else 'dve', lambda e: e.tensor_tensor(
                out=ypad[b][:, PAD:PAD + T], in0=ypad[b][:, PAD:PAD + T], in1=tm, op=ALU.mult), [t_, tmtok])

        load(0)
        for c in range(CCc):
            b = c % 2
            if c + 1 < CCc:
                load(c + 1)
            dl = [None, None]
            for j in range(CW):
                eng = 'dve' if j % 2 == 0 else 'pool'
                col = cfg.c_wdw + c * CW + j
                dl[j % 2] = P.op(eng, lambda e, j=j, col=col, b=b: e.tensor_scalar(
                    out=diag[b][:, j, :], in0=self.ident_bf, scalar1=self.cl[:, col:col + 1], scalar2=None,
                    op0=ALU.mult), [pedone[b]])
            evs = []
            last = None
            for tt in range(T // 512):
                bk = bn % 8
                bn += 1
                for j in range(CW):
                    w_ = [ldtok[c], dl[0], dl[1], bankfree[bk]] if j == 0 else []
                    last = P.op('pe', lambda e, j=j, tt=tt, bk=bk, b=b: e.matmul(
                        banks[bk], diag[b][:, j, :], ypad[b][:, tt * 512 + j:tt * 512 + j + 512],
                        start=(j == 0), stop=(j == CW - 1)), w_)
                bcol = cfg.c_bdw + c
                if tt % 2 == 0:
                    tk = P.op('act', lambda e, tt=tt, bk=bk, bcol=bcol, b=b: e.activation(
                        out=zst[b][:, tt * 512:(tt + 1) * 512], in_=banks[bk], func=AF.Identity,
                        bias=self.cl[:, bcol:bcol + 1]), [last, stdone[b]])
                else:
                    tk = P.op('dve', lambda e, tt=tt, bk=bk, bcol=bcol, b=b: e.tensor_scalar(
                        out=zst[b][:, tt * 512:(tt + 1) * 512], in0=banks[bk], scalar1=self.cl[:, bcol:bcol + 1],
                        scalar2=None, op0=ALU.add), [last, stdone[b]])
                bankfree[bk] = tk
                evs.append(tk)
            pedone[b] = last
            stdone[b] = P.dma('sp', lambda e, c=c, b=b: e.dma_start(out=self.ZT[c * 128:(c + 1) * 128, :], in_=zst[b]),
                              dso[b], evs)
        self.phase_end()

    def conv2_phase(self, l):
        cfg, P, A = self.cfg, self.P, self.A
        T, CCc, CC = cfg.T, cfg.CCc, cfg.CC
        self.phase_begin()
        A = self.A
        banks = self.banks
        z = [A.get([128, CCc, 512], F32) for _ in range(2)]
        zb = A.get([128, CCc, 512], BF16)
        zs = A.get([128, CCc, 512], BF16)
        mean = A.get([128, 512], F32)
        msq = A.get([128, 512], F32)
        rstd = A.get([128, 512], F32)
        t1 = [A.get([128, 512], F32) for _ in range(4)]
        outb = [A.get([128, CCc, 512], BF16) for _ in range(2)]
        dsl = [P.ds("in0"), P.ds("in1")]
        dso = [P.ds("st0"), P.ds("st1")]
        zfree = [None, None]
        stdone = [None, None]
        pedone = None
        statfree = None
        t1free = [None] * 4
        ld = {}

        def load(i):
            b = i % 2
            ld[i] = P.dma('sp', lambda e: e.dma_start(out=z[b], in_=fm(self.ZT)[:, :, i * 512:(i + 1) * 512]),
                          dsl[b], zfree[b] or [])

        load(0)
        nt = 0
        for i in range(T // 512):
            b = i % 2
            if i + 1 < T // 512:
                load(i + 1)
            a1 = P.op('act', lambda e, b=b: e.activation(out=zb, in_=z[b], func=AF.Copy), [ld[i], pedone])
            a2 = P.op('act', lambda e, b=b: e.activation(out=zs, in_=z[b], func=AF.Square), [ld[i], pedone])
            mm = None
            for c in range(CCc):
                mm = P.op('pe', lambda e, c=c: e.matmul(banks[0], self.ones_bf, zb[:, c, :], start=(c == 0),
                                                        stop=(c == CCc - 1)), [a1, statfree] if c == 0 else [])
            for c in range(CCc):
                mm = P.op('pe', lambda e, c=c: e.matmul(banks[1], self.ones_bf, zs[:, c, :], start=(c == 0),
                                                        stop=(c == CCc - 1)), [a2] if c == 0 else [])
            pedone = mm
            s1 = P.op('dve', lambda e: e.tensor_scalar(out=mean, in0=banks[0], scalar1=1.0 / CC, scalar2=None,
                                                       op0=ALU.mult), [mm] + (zfree[1 - b] or []))
            s2 = P.op('dve', lambda e: e.tensor_tensor(out=msq, in0=mean, in1=mean, op=ALU.mult), [s1])
            s3 = P.op('dve', lambda e: e.scalar_tensor_tensor(out=rstd, in0=banks[1], scalar=1.0 / CC, in1=msq,
                                                              op0=ALU.mult, op1=ALU.subtract), [s2])
            s4a = P.op('dve', lambda e: e.tensor_scalar(out=rstd, in0=rstd, scalar1=LN_EPS, scalar2=None,
                                                        op0=ALU.add), [s3])
            s4b = P.op('act', lambda e: e.activation(out=rstd, in_=rstd, func=AF.Sqrt), [s4a])
            s4 = P.op('dve', lambda e: e.reciprocal(out=rstd, in_=rstd), [s4b])
            statfree = s3
            lasts = []
            for c in range(CCc):
                eng = 'dve' if c % 3 != 2 else 'pool'
                tb_ = nt % 4
                nt += 1
                u1 = P.op(eng, lambda e, c=c, tb_=tb_, b=b: e.tensor_tensor(out=t1[tb_], in0=z[b][:, c, :], in1=mean,
                                                                       op=ALU.subtract), [s4, t1free[tb_]])
                u2 = P.op(eng, lambda e, c=c, tb_=tb_: e.tensor_tensor(out=t1[tb_], in0=t1[tb_], in1=rstd,
                                                                       op=ALU.mult), [u1])
                gcol = cfg.c_gcln + c
                bcol = cfg.c_bcln + c
                u3 = P.op('act', lambda e, c=c, tb_=tb_, gcol=gcol, bcol=bcol, b=b: e.activation(
                    out=outb[b][:, c, :], in_=t1[tb_], func=AF.Silu, bias=self.cl[:, bcol:bcol + 1],
                    scale=self.cl[:, gcol:gcol + 1]), [u2, stdone[b]])
                t1free[tb_] = u3
                lasts.append(u3)
            zfree[b] = [lasts[-1]]
            stdone[b] = P.dma('sp', lambda e, i=i, b=b: e.dma_start(out=fm(self.CT)[:, :, i * 512:(i + 1) * 512], in_=outb[b]),
                              dso[b], [lasts[-1]])
        self.phase_end()


def _consts(cfg, inp):
    DEPTH = cfg.DEPTH
    cl = np.zeros((DEPTH, 128, cfg.NCL), np.float32)

    def pc(v):
        return np.ascontiguousarray(v.reshape(-1, 128).T)

    for l in range(DEPTH):
        c = cl[l]
        c[:, cfg.c_gmix:cfg.c_gmix + cfg.DC] = pc(inp["g_mix"][l])
        c[:, cfg.c_bgate:cfg.c_bgate + 2 * cfg.DC] = pc(inp["b_gate"][l])
        c[:, cfg.c_bdw:cfg.c_bdw + cfg.CCc] = pc(inp["b_dw"][l])
        c[:, cfg.c_gcln:cfg.c_gcln + cfg.CCc] = pc(inp["g_conv_ln"][l])
        c[:, cfg.c_bcln:cfg.c_bcln + cfg.CCc] = pc(inp["b_conv_ln"][l])
        c[:, cfg.c_bcproj:cfg.c_bcproj + cfg.DC] = pc(inp["b_conv_proj"][l])
        c[:, cfg.c_gffn:cfg.c_gffn + cfg.DC] = pc(inp["g_ffn"][l])
        wd = inp["w_dw"][l][:, 0, :]
        wd = wd.T.reshape(cfg.CCc, 128, cfg.CW).transpose(1, 0, 2).reshape(128, cfg.CCc * cfg.CW)
        c[:, cfg.c_wdw:cfg.c_wdw + cfg.CCc * cfg.CW] = wd
        c[:, cfg.c_gsub:cfg.c_gsub + cfg.VD] = inp["g_subln"][l][None, :]
        c[:, cfg.c_lq:cfg.c_lq + 2 * cfg.HD] = inp["lambda_q"][l].reshape(1, -1)
        c[:, cfg.c_lk:cfg.c_lk + 2 * cfg.HD] = inp["lambda_k"][l].reshape(1, -1)
    return cl


def _globals(cfg, g_final, nvalid):
    cg = np.zeros((128, cfg.NCG), np.float32)
    cg[:, cfg.g_gfinal:cfg.g_gfinal + cfg.DC] = g_final.reshape(-1, 128).T
    cg[:, cfg.g_ident:cfg.g_ident + 128] = np.eye(128, dtype=np.float32)
    kpos = np.arange(cfg.T).reshape(cfg.NK, 128).T
    cg[:, cfg.g_kmask:cfg.g_kmask + cfg.NK] = np.where(kpos < nvalid, 0.0, MASK_NEG)
    return cg


def _atab(cfg):
    p = np.arange(128)[:, None]
    u = np.arange(cfg.AW)[None, :]
    return (-np.abs(u - cfg.AOFF - p)).astype(np.float32)


_NC_CACHE = {}


def run_trunk(cfg, seqs, inp, n_cores=8):
    key = (cfg.T, cfg.D, cfg.H, cfg.DFF, cfg.DEPTH)
    if key not in _NC_CACHE:
        _NC_CACHE[key] = Builder(cfg).build()
    nc = _NC_CACHE[key]
    cl = _consts(cfg, inp)
    atab = _atab(cfg)
    wmap = {k: np.ascontiguousarray(inp[k], dtype=np.float32) for k in
            ("w_in", "w_attn_proj", "w_conv_proj", "w_out", "w_ffn_in", "w_ffn_out")}
    in_maps = []
    for c in range(n_cores):
        s = seqs[c] if c < len(seqs) else seqs[0]
        S = s.shape[0]
        xT = np.zeros((cfg.D, cfg.T), np.float32)
        xT[:, :S] = s.T
        tmask = np.zeros((128, cfg.T), np.float32)
        tmask[:, :S] = 1.0
        m = {"xT": xT, "cl": cl, "cg": _globals(cfg, inp["g_final"], S), "atab": atab, "tmask": tmask}
        m.update(wmap)
        in_maps.append(m)
    res = run_bass_kernel_spmd(nc, in_maps, core_ids=list(range(n_cores)))
    outs = []
    for c in range(len(seqs)):
        S = seqs[c].shape[0]
        outs.append(np.ascontiguousarray(res.results[c]["yT"][:, :S].T))
    return outs


def kernel(x_prompt, x_sample, g_mix, w_in, b_gate, lambda_q, lambda_k, g_subln,
           w_attn_proj, w_dw, b_dw, g_conv_ln, b_conv_ln, w_conv_proj, b_conv_proj,
           w_out, g_ffn, w_ffn_in, w_ffn_out, g_final):
    inp = dict(g_mix=g_mix, w_in=w_in, b_gate=b_gate, lambda_q=lambda_q, lambda_k=lambda_k, g_subln=g_subln,
               w_attn_proj=w_attn_proj, w_dw=w_dw, b_dw=b_dw, g_conv_ln=g_conv_ln, b_conv_ln=b_conv_ln,
               w_conv_proj=w_conv_proj, b_conv_proj=b_conv_proj, w_out=w_out, g_ffn=g_ffn,
               w_ffn_in=w_ffn_in, w_ffn_out=w_ffn_out, g_final=g_final)
    inp = {k: np.asarray(v, dtype=np.float32) for k, v in inp.items()}
    x_prompt = np.asarray(x_prompt, dtype=np.float32)
    x_sample = np.asarray(x_sample, dtype=np.float32)
    cfg = Cfg()
    seqs = [x_prompt[b] for b in range(x_prompt.shape[0])] + [x_sample[b] for b in range(x_sample.shape[0])]
    outs = run_trunk(cfg, seqs, inp)
    nb = x_prompt.shape[0]
    y_prompt = np.stack(outs[:nb]).astype(np.float32)
    y_sample = np.stack(outs[nb:]).astype(np.float32)
    return (y_prompt, y_sample)
```

```python
import numpy as np
from contextlib import ExitStack
import concourse.bass as bass
import concourse.mybir as mybir
from concourse.bass_utils import run_bass_kernel_spmd

F32 = mybir.dt.float32
BF16 = mybir.dt.bfloat16
U8 = mybir.dt.uint8
AF = mybir.ActivationFunctionType
ALU = mybir.AluOpType

NORM_EPS = 1e-6
LN_EPS = 1e-5
MASK_NEG = -30000.0
SB_BYTES = 192 * 1024


class Cfg:
    def __init__(self, T=4096, D=2048, H=8, DFF=5632, DEPTH=4, CW=31):
        self.T, self.D, self.H, self.DFF, self.DEPTH, self.CW = T, D, H, DFF, DEPTH, CW
        self.HD = 128
        self.VD = 256
        self.QK = H * 2 * self.HD
        self.VW = H * self.VD
        self.CC = D
        self.IN_W = 2 * self.QK + self.VW + 2 * self.CC + 2 * D
        self.DC = D // 128
        self.CCc = self.CC // 128
        self.NK = T // 128
        self.NQ = T // 512
        self.AOFF = T - 128
        self.AW = (self.NQ - 1) * 512 + 512 + self.AOFF
        c = 0
        self.c_gmix = c; c += self.DC
        self.c_bgate = c; c += 2 * self.DC
        self.c_bdw = c; c += self.CCc
        self.c_gcln = c; c += self.CCc
        self.c_bcln = c; c += self.CCc
        self.c_bcproj = c; c += self.DC
        self.c_gffn = c; c += self.DC
        self.c_wdw = c; c += self.CCc * CW
        self.c_gsub = c; c += self.VD
        self.c_lq = c; c += 2 * self.HD
        self.c_lk = c; c += 2 * self.HD
        self.NCL = c
        g = 0
        self.g_gfinal = g; g += self.DC
        self.g_ident = g; g += 128
        self.g_kmask = g; g += self.NK
        self.NCG = g


def lambda_init(layer):
    return 0.8 - 0.6 * float(np.exp(-0.3 * layer))


class DmaSem:
    def __init__(self, sem):
        self.sem = sem
        self.count = 0


class Prog:
    ENG = ('pe', 'act', 'dve', 'pool', 'sp')

    def __init__(self, nc, stack):
        self.nc = nc
        self.stack = stack
        self.streams = {k: [] for k in self.ENG}
        self.cnt = {k: 0 for k in self.ENG}
        self.esem = {k: stack.enter_context(nc.semaphore("s_" + k)) for k in self.ENG}
        self.dsems = {}
        self.pending = {k: [] for k in self.ENG}
        self.waited = {k: {} for k in self.ENG}

    def ds(self, name):
        if name not in self.dsems:
            self.dsems[name] = DmaSem(self.stack.enter_context(self.nc.semaphore("d_" + name)))
        return self.dsems[name]

    def _w(self, eng, waits):
        w = []

        def fl(x):
            if x is None:
                return
            if isinstance(x, list):
                for y in x:
                    fl(y)
            else:
                w.append(x)
        fl(list(waits))
        if self.pending[eng]:
            w = self.pending[eng] + w
            self.pending[eng] = []
        return w

    def op(self, eng, fn, waits=()):
        self.cnt[eng] += 1
        self.streams[eng].append((fn, self._w(eng, waits), None))
        return (eng, self.cnt[eng])

    def dma(self, eng, fn, ds, waits=()):
        ds.count += 16
        self.streams[eng].append((fn, self._w(eng, waits), ds))
        return (ds, ds.count)

    def last(self, eng):
        return (eng, self.cnt[eng]) if self.cnt[eng] else None

    def barrier(self):
        toks = [(k, self.cnt[k]) for k in self.ENG if self.cnt[k]]
        toks += [(d, d.count) for d in self.dsems.values() if d.count]
        for k in self.ENG:
            self.pending[k] = list(toks)

    def finish(self):
        self.barrier()
        for k in self.ENG:
            w = self._w(k, [])
            self.streams[k].append((None, w, None))

    def flush(self):
        nc = self.nc
        P = self
        with nc.Block() as block:
            @block.tensor
            def _(e):
                P.replay('pe', e)

            @block.scalar
            def _(e):
                P.replay('act', e)

            @block.vector
            def _(e):
                P.replay('dve', e)

            @block.gpsimd
            def _(e):
                P.replay('pool', e)

            @block.sync
            def _(e):
                P.replay('sp', e)
        for k in self.ENG:
            self.streams[k] = []

    def replay(self, eng, e):
        waited = self.waited[eng]
        for fn, waits, ds in self.streams[eng]:
            for key, val in waits:
                kk = key if isinstance(key, str) else id(key)
                if waited.get(kk, 0) >= val:
                    continue
                waited[kk] = val
                sem = self.esem[key] if isinstance(key, str) else key.sem
                e.wait_ge(sem, val)
            if fn is None:
                continue
            inst = fn(e)
            if ds is None:
                inst.then_inc(self.esem[eng], 1)
            else:
                inst.then_inc(ds.sem, 16)


class Alloc:
    def __init__(self, sb, base, limit):
        self.sb, self.base, self.off, self.limit = sb, base, base, limit

    def reset(self):
        self.off = self.base

    def get(self, shape, dt):
        esz = 4 if dt == F32 else 2
        n = int(np.prod(shape[1:]))
        nb = (n * esz + 63) // 64 * 64
        o = self.off
        self.off += nb
        assert self.off <= self.limit, ("SBUF overflow", self.off, self.limit)
        v = self.sb[:, o:o + n * esz].bitcast(dt)
        if len(shape) == 3:
            v = v.rearrange("p (a b) -> p a b", b=shape[2])
        return v


def fm(ap):
    return ap.rearrange("(c p) t -> p c t", p=128)


class Builder:
    def __init__(self, cfg):
        self.cfg = cfg

    def build(self):
        cfg = self.cfg
        T, D, DEPTH = cfg.T, cfg.D, cfg.DEPTH
        nc = bass.Bass("TRN2", target_bir_lowering=False)
        self.nc = nc
        dt_in = lambda name, shape: nc.dram_tensor(name, shape, F32, kind="ExternalInput").ap()
        self.xT_in = dt_in("xT", [D, T])
        self.cl_in = dt_in("cl", [DEPTH, 128, cfg.NCL])
        self.cg_in = dt_in("cg", [128, cfg.NCG])
        self.atab_in = dt_in("atab", [128, cfg.AW])
        self.tmask_in = dt_in("tmask", [128, T])
        self.w_in = dt_in("w_in", [DEPTH, D, cfg.IN_W])
        self.w_ap = dt_in("w_attn_proj", [DEPTH, cfg.VW, D])
        self.w_cp = dt_in("w_conv_proj", [DEPTH, cfg.CC, D])
        self.w_out = dt_in("w_out", [DEPTH, D, D])
        self.w_fi = dt_in("w_ffn_in", [DEPTH, D, 2 * cfg.DFF])
        self.w_fo = dt_in("w_ffn_out", [DEPTH, cfg.DFF, D])
        self.yT_out = nc.dram_tensor("yT", [D, T], F32, kind="ExternalOutput").ap()
        scr = lambda name, shape, dt: nc.dram_tensor(name, shape, dt).ap()
        self.XT = scr("s_XT", [D, T], F32)
        self.HT = scr("s_HT", [D, T], BF16)
        self.QKT = scr("s_QKT", [2 * cfg.QK, T], BF16)
        self.V = scr("s_V", [T, cfg.VW], BF16)
        self.YT = scr("s_YT", [cfg.CC, T], BF16)
        self.GT = scr("s_GT", [2 * D, T], BF16)
        self.ONT = scr("s_ONT", [cfg.VW, T], BF16)
        self.ZT = scr("s_ZT", [cfg.CC, T], F32)
        self.CT = scr("s_CT", [cfg.CC, T], BF16)
        self.MT = scr("s_MT", [D, T], BF16)
        self.AT = scr("s_AT", [cfg.DFF, T], BF16)

        with ExitStack() as stack:
            P = Prog(nc, stack)
            self.P = P
            GB = 16 * 1024
            gsb = stack.enter_context(nc.sbuf_tensor("gsb", [128, GB], U8))
            G = Alloc(gsb, 0, GB)
            self.cg = G.get([128, cfg.NCG], F32)
            self.cl = G.get([128, cfg.NCL], F32)
            self.ident_bf = G.get([128, 128], BF16)
            self.ones_bf = G.get([128, 128], BF16)
            self.neglam = G.get([128, 2], F32)
            self.gsub = G.get([128, cfg.VD], F32)
            self.lamtmp = G.get([128, 2 * cfg.HD], F32)
            self.lamred = G.get([128, 4], F32)
            self.PH_BYTES = SB_BYTES - GB
            self.phase_no = 0
            self.ph = None
            self.A = None
            self.banks = None
            t0 = P.dma('sp', lambda e: e.dma_start(out=self.cg, in_=self.cg_in), P.ds("c0"))
            t1 = P.op('dve', lambda e: e.tensor_copy(out=self.ident_bf, in_=self.cg[:, cfg.g_ident:cfg.g_ident + 128]), [t0])
            t2 = P.op('pool', lambda e: e.memset(self.ones_bf, 1.0))
            P.barrier()

            xsrc = self.xT_in
            for l in range(DEPTH):
                self.layer_consts(l)
                self.norm_phase(xsrc, cfg.c_gmix, self.HT, final=False)
                self.inproj_phase(l)
                self.attn_phase(l)
                self.conv1_phase(l)
                self.conv2_phase(l)
                self.merge_phase(l)
                self.resid_phase(l, self.w_out[l], self.MT, cfg.D // 128, xsrc, TS=min(2048, T))
                xsrc = self.XT
                self.norm_phase(xsrc, cfg.c_gffn, self.HT, final=False)
                self.ffnin_phase(l)
                self.resid_phase(l, self.w_fo[l], self.AT, cfg.DFF // 128, xsrc, TS=min(1024, T))
            self.norm_phase(xsrc, None, self.yT_out, final=True)
            P.finish()
            P.flush()
        return nc

    def phase_begin(self):
        nc = self.nc
        self.P.barrier()
        self.ph = ExitStack()
        self.phase_no += 1
        sbp = self.ph.enter_context(nc.sbuf_tensor("sb%d" % self.phase_no, [128, self.PH_BYTES], U8))
        psp = self.ph.enter_context(nc.psum_tensor("ps%d" % self.phase_no, [128, 8 * 512], F32))
        self.banks = [psp[:, b * 512:(b + 1) * 512] for b in range(8)]
        self.A = Alloc(sbp, 0, self.PH_BYTES)

    def phase_end(self):
        self.P.barrier()
        self.P.flush()
        self.ph.close()
        self.ph = None

    def layer_consts(self, l):
        cfg, P = self.cfg, self.P
        HD = cfg.HD
        P.barrier()
        t0 = P.dma('sp', lambda e: e.dma_start(out=self.cl, in_=self.cl_in[l]), P.ds("c0"))
        lq = self.cl[:, cfg.c_lq:cfg.c_lq + 2 * HD]
        lk = self.cl[:, cfg.c_lk:cfg.c_lk + 2 * HD]
        t1 = P.op('dve', lambda e: e.tensor_tensor(out=self.lamtmp, in0=lq, in1=lk, op=ALU.mult), [t0])
        t2 = P.op('dve', lambda e: e.tensor_reduce(
            out=self.lamred[:, 0:2], in_=self.lamtmp.rearrange("p (a b) -> p a b", b=HD),
            axis=mybir.AxisListType.X, op=ALU.add), [t1])
        t3 = P.op('act', lambda e: e.activation(out=self.lamred[:, 2:4], in_=self.lamred[:, 0:2], func=AF.Exp), [t2])
        li = lambda_init(l)
        t4 = P.op('dve', lambda e: e.tensor_tensor(out=self.neglam[:, 0:1], in0=self.lamred[:, 3:4],
                                                   in1=self.lamred[:, 2:3], op=ALU.subtract), [t3])
        t5 = P.op('dve', lambda e: e.tensor_scalar(out=self.neglam[:, 1:2], in0=self.neglam[:, 0:1],
                                                   scalar1=-li, scalar2=None, op0=ALU.add), [t4])
        t6 = P.op('dve', lambda e: e.tensor_scalar(out=self.gsub, in0=self.cl[:, cfg.c_gsub:cfg.c_gsub + cfg.VD],
                                                   scalar1=(1.0 - li), scalar2=None, op0=ALU.mult), [t5])
        P.barrier()

    def norm_phase(self, src, gcol, dst, final):
        cfg, P, A = self.cfg, self.P, self.A
        DC, D = cfg.DC, cfg.D
        self.phase_begin()
        A = self.A
        odt = F32 if final else BF16
        xin = [A.get([128, DC, 512], F32) for _ in range(2)]
        sq = [A.get([128, DC, 512], BF16) for _ in range(2)]
        hout = [A.get([128, DC, 512], odt) for _ in range(2)]
        rstd = [A.get([128, 512], F32) for _ in range(2)]
        gsrc = self.cg if final else self.cl
        gc = cfg.g_gfinal if final else gcol
        dsi = [P.ds("in0"), P.ds("in1")]
        dso = [P.ds("st0"), P.ds("st1")]
        hdone = [None, None]
        mmdone = [None, None]
        stdone = [None, None]
        r2done = [None, None]
        for i in range(cfg.NQ):
            b = i % 2
            ts = slice(i * 512, (i + 1) * 512)
            ld = P.dma('sp', lambda e, b=b, ts=ts: e.dma_start(out=xin[b], in_=fm(src)[:, :, ts]), dsi[b],
                       (hdone[b] or []))
            sqt = P.op('act', lambda e, b=b: e.activation(out=sq[b], in_=xin[b], func=AF.Square), [ld, mmdone[b]])
            mm = None
            for c in range(DC):
                mm = P.op('pe', lambda e, b=b, c=c: e.matmul(self.banks[b], self.ones_bf, sq[b][:, c, :],
                                                             start=(c == 0), stop=(c == DC - 1)),
                          [sqt, r2done[b]] if c == 0 else [])
            mmdone[b] = mm
            r1 = P.op('dve', lambda e, b=b: e.tensor_scalar(out=rstd[b], in0=self.banks[b], scalar1=1.0 / D,
                                                            scalar2=NORM_EPS, op0=ALU.mult, op1=ALU.add),
                      [mm] + (hdone[b] or []))
            r2a = P.op('act', lambda e, b=b: e.activation(out=rstd[b], in_=rstd[b], func=AF.Sqrt), [r1])
            r2 = P.op('dve', lambda e, b=b: e.reciprocal(out=rstd[b], in_=rstd[b]), [r2a])
            r2done[b] = r1
            lastd = lastp = None
            for c in range(DC):
                eng = 'dve'
                tk = P.op(eng, lambda e, b=b, c=c: e.scalar_tensor_tensor(
                    out=hout[b][:, c, :], in0=xin[b][:, c, :], scalar=gsrc[:, gc + c:gc + c + 1],
                    in1=rstd[b], op0=ALU.mult, op1=ALU.mult), [r2, stdone[b]])
                if eng == 'dve':
                    lastd = tk
                else:
                    lastp = tk
            hdone[b] = [lastd, lastp]
            stdone[b] = P.dma('sp', lambda e, b=b, ts=ts: e.dma_start(out=fm(dst)[:, :, ts], in_=hout[b]), dso[b],
                              [lastd, lastp])
        self.phase_end()

    def linear(self, ins, TS, blocks):
        cfg, P, A = self.cfg, self.P, self.A
        T = cfg.T
        NS = T // TS
        NHALF = TS // 1024
        KCmax = max(kc for _, kc in ins)
        FBmax = max(sum(p[2] for p in b['pieces']) for b in blocks)
        in_sb = [A.get([128, kc, TS], BF16) for _, kc in ins]
        wbuf = [A.get([128, KCmax, FBmax], BF16) for _ in range(2)]
        dsw = [P.ds("w0"), P.ds("w1")]
        dsin = [P.ds("in0"), P.ds("in1")]
        wfree = [None, None]
        bankfree = [[None] * 4, [None] * 4]
        seq = [(s, bi) for s in range(NS) for bi in range(len(blocks))]
        wtok = {}

        def issue_w(n):
            s, bi = seq[n]
            wb = n % 2
            c0 = 0
            tk = None
            for (w2d, col0, ncols, in_idx) in blocks[bi]['pieces']:
                kc = ins[in_idx][1]
                tk = P.dma('pool', lambda e, wb=wb, c0=c0, w2d=w2d, col0=col0, ncols=ncols, kc=kc: e.dma_start(
                    out=wbuf[wb][:, 0:kc, c0:c0 + ncols],
                    in_=w2d.rearrange("(c p) f -> p c f", p=128)[:, :, col0:col0 + ncols]), dsw[wb], [wfree[wb]])
                c0 += ncols
            wtok[n] = tk

        issue_w(0)
        ucount = 0
        intok = [None] * len(ins)
        lastpe_all = None
        for n, (s, bi) in enumerate(seq):
            blk = blocks[bi]
            if bi == 0:
                for ii, (src, kc) in enumerate(ins):
                    tk = None
                    nsp = max(1, kc // 8)
                    for q in range(nsp):
                        cs = slice(q * kc // nsp, (q + 1) * kc // nsp)
                        tk = P.dma('sp', lambda e, ii=ii, cs=cs, s=s, src=src: e.dma_start(
                            out=in_sb[ii][:, cs, :], in_=fm(src)[:, cs, s * TS:(s + 1) * TS]), dsin[ii], [lastpe_all])
                    intok[ii] = tk
            if n + 1 < len(seq):
                issue_w(n + 1)
            wb = n % 2
            FB = sum(p[2] for p in blk['pieces'])
            kind = blk['kind']
            lastpe = None
            if kind == 'vtok':
                kc = ins[0][1]
                for g in range(TS // 512):
                    bs = ucount % 2
                    ucount += 1
                    bk = self.banks[bs * 4:bs * 4 + 4]
                    tok0 = s * TS + g * 512
                    for k in range(kc):
                        for j in range(4):
                            w_ = [wtok[n], intok[0], bankfree[bs][j]] if k == 0 else []
                            lastpe = P.op('pe', lambda e, k=k, j=j, g=g, wb=wb, bk=bk, FB=FB: e.matmul(
                                bk[j][:, 0:FB], in_sb[0][:, k, g * 512 + j * 128:g * 512 + (j + 1) * 128],
                                wbuf[wb][:, k, 0:FB], start=(k == 0), stop=(k == kc - 1)), w_)
                    rel = blk['epi'](0, tok0, bk, lastpe, None)
                    bankfree[bs] = rel
            else:
                nunits = FB // 256
                for u in range(nunits):
                    if kind == 'pair':
                        cols = [u * 128, FB // 2 + u * 128]
                        iidx = [blk['pieces'][0][3], blk['pieces'][1][3]]
                    else:
                        cols = [2 * u * 128, (2 * u + 1) * 128]
                        iidx = [blk['pieces'][0][3]] * 2
                    for h in range(NHALF):
                        bs = ucount % 2
                        ucount += 1
                        bk = self.banks[bs * 4:bs * 4 + 4]
                        tok0 = s * TS + h * 1024
                        pretoks = blk['pre'](u, tok0) if blk.get('pre') else None
                        kcs = [ins[iidx[0]][1], ins[iidx[1]][1]]
                        kc = max(kcs)
                        for k in range(kc):
                            for ci in range(2):
                                if k >= kcs[ci]:
                                    continue
                                for t in range(2):
                                    w_ = [wtok[n], intok[iidx[ci]], bankfree[bs][ci * 2 + t]] if k == 0 else []
                                    lastpe = P.op('pe', lambda e, k=k, ci=ci, t=t, h=h, wb=wb, bk=bk, cols=cols, iidx=iidx, kcs=kcs: e.matmul(
                                        bk[ci * 2 + t], wbuf[wb][:, k, cols[ci]:cols[ci] + 128],
                                        in_sb[iidx[ci]][:, k, h * 1024 + t * 512:h * 1024 + (t + 1) * 512],
                                        start=(k == 0), stop=(k == kcs[ci] - 1)), w_)
                        rel = blk['epi'](u, tok0, bk, lastpe, pretoks)
                        bankfree[bs] = rel
            wfree[wb] = lastpe
            lastpe_all = lastpe

    def stager(self, name, shape, dt, n=2):
        bufs = [self.A.get(shape, dt) for _ in range(n)]
        return {'bufs': bufs, 'tok': [None] * n, 'i': 0, 'ds': [self.P.ds("%s%d" % (name, i)) for i in range(n)]}

    def inproj_phase(self, l):
        cfg, P, A = self.cfg, self.P, self.A
        D, QK, VW, CC = cfg.D, cfg.QK, cfg.VW, cfg.CC
        self.phase_begin()
        A = self.A
        w = self.w_in[l]
        TS = min(2048, cfg.T)
        stg = self.stager("st", [128, 2, 1024], BF16)
        sig = [A.get([128, 1024], F32) for _ in range(2)]
        vst = self.stager("sv", [128, 4, 512], BF16)
        sigtok = [None, None]
        cnt = [0]
        bgc = cfg.c_bgate
        banks = self.banks

        def store2(s, row0s, dst, tok0, evs):
            b = s['i']
            buf = s['bufs'][b]
            t = None
            for ci, r0 in enumerate(row0s):
                t = P.dma('sp', lambda e, ci=ci, r0=r0, buf=buf: e.dma_start(
                    out=dst[r0:r0 + 128, tok0:tok0 + 1024], in_=buf[:, ci, :]), s['ds'][b], evs)
            s['tok'][b] = t
            s['i'] = (b + 1) % len(s['bufs'])

        def epi_copy(row_of_unit, dst):
            def f(u, tok0, bk, lastpe, pre):
                b = stg['i']
                buf = stg['bufs'][b]
                evs = []
                for ci in range(2):
                    for t in range(2):
                        eng = 'act' if (ci + t) % 2 == 0 else 'dve'
                        if eng == 'act':
                            tk = P.op('act', lambda e, ci=ci, t=t, buf=buf, bk=bk: e.activation(
                                out=buf[:, ci, t * 512:(t + 1) * 512], in_=bk[ci * 2 + t], func=AF.Copy),
                                [lastpe, stg['tok'][b]])
                        else:
                            tk = P.op('dve', lambda e, ci=ci, t=t, buf=buf, bk=bk: e.tensor_copy(
                                out=buf[:, ci, t * 512:(t + 1) * 512], in_=bk[ci * 2 + t]),
                                [lastpe, stg['tok'][b]])
                        evs.append(tk)
                r0 = row_of_unit(u)
                store2(stg, [r0, r0 + 128], dst, tok0, evs)
                return evs
            return f

        def epi_gates(row_of_unit):
            def f(u, tok0, bk, lastpe, pre):
                b = stg['i']
                buf = stg['bufs'][b]
                evs = []
                r0 = row_of_unit(u)
                for ci in range(2):
                    col = bgc + (r0 + ci * 128) // 128
                    for t in range(2):
                        tk = P.op('act', lambda e, ci=ci, t=t, buf=buf, bk=bk, col=col: e.activation(
                            out=buf[:, ci, t * 512:(t + 1) * 512], in_=bk[ci * 2 + t], func=AF.Sigmoid,
                            bias=self.cl[:, col:col + 1]), [lastpe, stg['tok'][b]])
                        evs.append(tk)
                store2(stg, [r0, r0 + 128], self.GT, tok0, evs)
                return evs
            return f

        def epi_glu(row_of_unit):
            def f(u, tok0, bk, lastpe, pre):
                b = stg['i']
                buf = stg['bufs'][b]
                sb_ = cnt[0] % 2
                cnt[0] += 1
                rel = [None] * 4
                evs = []
                for t in range(2):
                    ta = P.op('act', lambda e, t=t, bk=bk, sb_=sb_: e.activation(
                        out=sig[sb_][:, t * 512:(t + 1) * 512], in_=bk[2 + t], func=AF.Sigmoid),
                        [lastpe, sigtok[sb_]])
                    td = P.op('dve', lambda e, t=t, bk=bk, buf=buf, sb_=sb_: e.tensor_tensor(
                        out=buf[:, 0, t * 512:(t + 1) * 512], in0=bk[t], in1=sig[sb_][:, t * 512:(t + 1) * 512],
                        op=ALU.mult), [ta, stg['tok'][b]])
                    rel[2 + t] = ta
                    rel[t] = td
                    evs.append(td)
                sigtok[sb_] = evs[-1]
                r0 = row_of_unit(u)
                store2(stg, [r0], self.YT, tok0, evs)
                return rel
            return f

        def epi_v(col0):
            def f(u, tok0, bk, lastpe, pre):
                b = vst['i']
                buf = vst['bufs'][b]
                FB = min(512, VW)
                evs = []
                for j in range(4):
                    if j % 2 == 0:
                        tk = P.op('act', lambda e, j=j, buf=buf, bk=bk: e.activation(
                            out=buf[:, j, 0:FB], in_=bk[j][:, 0:FB], func=AF.Copy), [lastpe, vst['tok'][b]])
                    else:
                        tk = P.op('dve', lambda e, j=j, buf=buf, bk=bk: e.tensor_copy(
                            out=buf[:, j, 0:FB], in_=bk[j][:, 0:FB]), [lastpe, vst['tok'][b]])
                    evs.append(tk)
                vst['tok'][b] = P.dma('sp', lambda e, buf=buf: e.dma_start(
                    out=self.V[tok0:tok0 + 512, col0:col0 + FB].rearrange("(j p) f -> p j f", p=128),
                    in_=buf[:, :, 0:FB]), vst['ds'][b], evs)
                vst['i'] = (b + 1) % 2
                return evs
            return f

        blocks = []
        for c0 in range(0, 2 * QK, 512):
            blocks.append(dict(pieces=[(w, c0, 512, 0)], kind='single',
                               epi=epi_copy(lambda u, c0=c0: c0 + u * 256, self.QKT)))
        FBv = min(512, VW)
        for c0 in range(0, VW, FBv):
            blocks.append(dict(pieces=[(w, 2 * QK + c0, FBv, 0)], kind='vtok', epi=epi_v(c0)))
        ub = 2 * QK + VW
        for c0 in range(0, CC, 256):
            blocks.append(dict(pieces=[(w, ub + c0, 256, 0), (w, ub + CC + c0, 256, 0)], kind='pair',
                               epi=epi_glu(lambda u, c0=c0: c0 + u * 128)))
        gb = ub + 2 * CC
        for c0 in range(0, 2 * D, 512):
            blocks.append(dict(pieces=[(w, gb + c0, 512, 0)], kind='single',
                               epi=epi_gates(lambda u, c0=c0: c0 + u * 256)))
        self.linear([(self.HT, cfg.DC)], TS, blocks)
        self.phase_end()

    def merge_phase(self, l):
        cfg, P, A = self.cfg, self.P, self.A
        D = cfg.D
        self.phase_begin()
        A = self.A
        TS = min(1024, cfg.T)
        stg = self.stager("st", [128, 1, 1024], BF16)
        gbuf = [[A.get([128, 1024], BF16) for _ in range(2)] for _ in range(2)]
        gds = [P.ds("g0"), P.ds("g1")]
        gfree = [None, None]
        t1 = [A.get([128, 1024], F32) for _ in range(2)]
        t2 = [A.get([128, 1024], F32) for _ in range(2)]
        tfree = [None, None]
        cnt = [0]
        slot_of = {}

        def pre(row0):
            def f(u, tok0):
                s_ = cnt[0] % 2
                cnt[0] += 1
                r = row0 + u * 128
                tk = None
                for gi in range(2):
                    tk = P.dma('sp', lambda e, gi=gi, r=r, s_=s_: e.dma_start(
                        out=gbuf[s_][gi], in_=self.GT[gi * D + r:gi * D + r + 128, tok0:tok0 + 1024]),
                        gds[s_], [gfree[s_]])
                return (s_, tk)
            return f

        def epi(row0):
            def f(u, tok0, bk, lastpe, pretoks):
                s_, gtok = pretoks
                b = stg['i']
                buf = stg['bufs'][b]
                r = row0 + u * 128
                col = cfg.c_bcproj + r // 128
                rel = [None] * 4
                evs = []
                for t in range(2):
                    sl = slice(t * 512, (t + 1) * 512)
                    ta = P.op('dve', lambda e, t=t, sl=sl, bk=bk, s_=s_: e.tensor_tensor(
                        out=t1[s_][:, sl], in0=bk[t], in1=gbuf[s_][0][:, sl], op=ALU.mult),
                        [lastpe, gtok, tfree[s_]])
                    tb = P.op('dve', lambda e, t=t, sl=sl, bk=bk, s_=s_, col=col: e.scalar_tensor_tensor(
                        out=t2[s_][:, sl], in0=bk[2 + t], scalar=self.cl[:, col:col + 1], in1=gbuf[s_][1][:, sl],
                        op0=ALU.add, op1=ALU.mult), [lastpe, gtok, tfree[s_]])
                    tc = P.op('pool', lambda e, sl=sl, buf=buf, s_=s_: e.tensor_tensor(
                        out=buf[:, 0, sl], in0=t1[s_][:, sl], in1=t2[s_][:, sl], op=ALU.add),
                        [ta, tb, stg['tok'][b]])
                    rel[t] = ta
                    rel[2 + t] = tb
                    evs.append(tc)
                tfree[s_] = evs[-1]
                gfree[s_] = rel[3]
                stg['tok'][b] = P.dma('sp', lambda e, buf=buf, r=r: e.dma_start(
                    out=self.MT[r:r + 128, tok0:tok0 + 1024], in_=buf[:, 0, :]), stg['ds'][b], evs)
                stg['i'] = (b + 1) % 2
                return rel
            return f

        blocks = []
        for c0 in range(0, D, 256):
            blocks.append(dict(pieces=[(self.w_ap[l], c0, 256, 0), (self.w_cp[l], c0, 256, 1)], kind='pair',
                               pre=pre(c0), epi=epi(c0)))
        self.linear([(self.ONT, cfg.VW // 128), (self.CT, cfg.CCc)], TS, blocks)
        self.phase_end()

    def resid_phase(self, l, w2d, src, KC, xsrc, TS):
        cfg, P, A = self.cfg, self.P, self.A
        D = cfg.D
        self.phase_begin()
        A = self.A
        xold = [[A.get([128, 1024], F32) for _ in range(2)] for _ in range(2)]
        xds = [P.ds("g0"), P.ds("g1")]
        xfree = [None, None]
        xnew = self.stager("st", [128, 2, 1024], F32)
        tmp = [A.get([128, 1024], F32) for _ in range(2)]
        tmpfree = [None, None]
        cnt = [0]

        def pre(row0):
            def f(u, tok0):
                s_ = cnt[0] % 2
                cnt[0] += 1
                tk = None
                for ci in range(2):
                    r = row0 + u * 256 + ci * 128
                    tk = P.dma('sp', lambda e, ci=ci, r=r, s_=s_: e.dma_start(
                        out=xold[s_][ci], in_=xsrc[r:r + 128, tok0:tok0 + 1024]), xds[s_], [xfree[s_]])
                return (s_, tk)
            return f

        def epi(row0):
            def f(u, tok0, bk, lastpe, pretoks):
                s_, xtok = pretoks
                b = xnew['i']
                buf = xnew['bufs'][b]
                rel = [None] * 4
                evs = []
                for t in range(2):
                    sl = slice(t * 512, (t + 1) * 512)
                    ta = P.op('dve', lambda e, t=t, sl=sl, bk=bk, buf=buf, s_=s_: e.tensor_tensor(
                        out=buf[:, 0, sl], in0=bk[t], in1=xold[s_][0][:, sl], op=ALU.add),
                        [lastpe, xtok, xnew['tok'][b]])
                    tb = P.op('act', lambda e, t=t, sl=sl, bk=bk, s_=s_: e.activation(
                        out=tmp[s_][:, sl], in_=bk[2 + t], func=AF.Copy), [lastpe, tmpfree[s_]])
                    tc = P.op('pool', lambda e, sl=sl, buf=buf, s_=s_: e.tensor_tensor(
                        out=buf[:, 1, sl], in0=tmp[s_][:, sl], in1=xold[s_][1][:, sl], op=ALU.add),
                        [tb, xtok, xnew['tok'][b]])
                    rel[t] = ta
                    rel[2 + t] = tb
                    evs += [ta, tc]
                tmpfree[s_] = evs[-1]
                xfree[s_] = [evs[-1], evs[-2]]
                t_ = None
                for ci in range(2):
                    r = row0 + u * 256 + ci * 128
                    t_ = P.dma('sp', lambda e, ci=ci, r=r, buf=buf: e.dma_start(
                        out=self.XT[r:r + 128, tok0:tok0 + 1024], in_=buf[:, ci, :]), xnew['ds'][b], evs)
                xnew['tok'][b] = t_
                xnew['i'] = (b + 1) % 2
                return rel
            return f

        blocks = []
        FB = 256 if KC > 16 else min(512, D)
        for c0 in range(0, D, FB):
            blocks.append(dict(pieces=[(w2d, c0, FB, 0)], kind='single', pre=pre(c0), epi=epi(c0)))
        self._flatten_fix = True
        self.linear([(src, KC)], TS, blocks)
        self.phase_end()

    def ffnin_phase(self, l):
        cfg, P, A = self.cfg, self.P, self.A
        DFF = cfg.DFF
        self.phase_begin()
        A = self.A
        TS = min(2048, cfg.T)
        stg = self.stager("st", [128, 1, 1024], BF16)
        sil = [A.get([128, 1024], F32) for _ in range(2)]
        silfree = [None, None]
        cnt = [0]
        w = self.w_fi[l]

        def epi(row0):
            def f(u, tok0, bk, lastpe, pre):
                b = stg['i']
                buf = stg['bufs'][b]
                s_ = cnt[0] % 2
                cnt[0] += 1
                rel = [None] * 4
                evs = []
                for t in range(2):
                    sl = slice(t * 512, (t + 1) * 512)
                    ta = P.op('act', lambda e, t=t, sl=sl, bk=bk, s_=s_: e.activation(
                        out=sil[s_][:, sl], in_=bk[t], func=AF.Silu), [lastpe, silfree[s_]])
                    td = P.op('dve', lambda e, t=t, sl=sl, bk=bk, buf=buf, s_=s_: e.tensor_tensor(
                        out=buf[:, 0, sl], in0=bk[2 + t], in1=sil[s_][:, sl], op=ALU.mult), [ta, stg['tok'][b]])
                    rel[t] = ta
                    rel[2 + t] = td
                    evs.append(td)
                silfree[s_] = evs[-1]
                r = row0 + u * 128
                stg['tok'][b] = P.dma('sp', lambda e, buf=buf, r=r: e.dma_start(
                    out=self.AT[r:r + 128, tok0:tok0 + 1024], in_=buf[:, 0, :]), stg['ds'][b], evs)
                stg['i'] = (b + 1) % 2
                return rel
            return f

        blocks = []
        for c0 in range(0, DFF, 256):
            blocks.append(dict(pieces=[(w, c0, 256, 0), (w, DFF + c0, 256, 0)], kind='pair', epi=epi(c0)))
        self.linear([(self.HT, cfg.DC)], TS, blocks)
        self.phase_end()

    def attn_phase(self, l):
        cfg, P, A = self.cfg, self.P, self.A
        T, H, HD, VD, NK, NQ = cfg.T, cfg.H, cfg.HD, cfg.VD, cfg.NK, cfg.NQ
        self.phase_begin()
        A = self.A
        banks = self.banks
        scale = HD ** -0.5
        atab = A.get([128, cfg.AW], F32)
        qb = [A.get([128, 2, T], BF16) for _ in range(2)]
        kb = [A.get([128, 2, T], BF16) for _ in range(2)]
        vb = [A.get([128, NK, VD + 1], BF16) for _ in range(2)]
        NSB = 3
        NPT = 4
        sbanks = [4, 5, 7]
        tmp = [A.get([128, 512], F32) for _ in range(NSB)]
        pt = [A.get([128, 512], BF16) for _ in range(NPT)]
        Om = [A.get([128, 4, VD], F32) for _ in range(2)]
        o = A.get([128, 4, VD], F32)
        junk = A.get([128, VD], F32)
        on = A.get([128, 4, VD], BF16)
        ost = [A.get([128, VD // 128, 512], BF16) for _ in range(2)]
        rec = A.get([128, 8], F32)
        ssq = A.get([128, 4], F32)
        rs2 = A.get([128, 4], F32)
        tb = banks[6].bitcast(BF16)
        ta = P.dma('sp', lambda e: e.dma_start(out=atab, in_=self.atab_in), P.ds("c0"))
        ones_tok = []
        for i in range(2):
            ones_tok.append(P.op('pool', lambda e, i=i: e.memset(vb[i][:, :, VD:VD + 1], 1.0)))
        dsl = [P.ds("in0"), P.ds("in1")]
        dso = [P.ds("st0"), P.ds("st1")]
        loadtok = [None, None]
        headdone = [None, None]

        def load_head(h):
            b = h % 2
            w_ = [headdone[b], ones_tok[b]]
            P.dma('sp', lambda e: e.dma_start(out=qb[b], in_=fm(self.QKT)[:, 2 * h:2 * h + 2, :]), dsl[b], w_)
            P.dma('sp', lambda e: e.dma_start(out=kb[b], in_=fm(self.QKT)[:, cfg.QK // 128 + 2 * h:cfg.QK // 128 + 2 * h + 2, :]),
                  dsl[b], w_)
            loadtok[b] = P.dma('sp', lambda e: e.dma_start(
                out=vb[b][:, :, 0:VD], in_=self.V.rearrange("(k p) f -> p k f", p=128)[:, :, h * VD:(h + 1) * VD]),
                dsl[b], w_)

        load_head(0)
        st = {'n': 0, 'sdve': [None] * NSB, 'tact': [None] * NSB, 'ptpe': [None] * NPT,
              'accfree': [None] * 4, 'omfree': [None, None], 'ofree': None, 'onfree': None,
              'ostfree': [None, None], 'tbfree': None, 'osti': 0, 'ssqfree': None}
        deferred = []

        def run_head(h):
            hb = h % 2
            if h + 1 < H:
                load_head(h + 1)
            slope = 2.0 ** (-8.0 * (h + 1) / H)
            steps = [(qc, m, kt) for qc in range(NQ) for m in range(2) for kt in range(NK)]
            S_tok = {}
            P_tok = {}

            def emit_S(i):
                qc, m, kt = steps[i]
                n = st['n'] + i
                sbk = sbanks[n % NSB]
                s_tok = P.op('pe', lambda e: e.matmul(
                    banks[sbk], kb[hb][:, m, kt * 128:(kt + 1) * 128], qb[hb][:, m, qc * 512:(qc + 1) * 512],
                    start=True, stop=True), [loadtok[hb], st['sdve'][n % NSB]])
                u0 = 512 * qc - 128 * kt + cfg.AOFF
                t_tok = P.op('dve', lambda e: e.scalar_tensor_tensor(
                    out=tmp[n % NSB], in0=atab[:, u0:u0 + 512], scalar=slope / scale, in1=banks[sbk],
                    op0=ALU.mult, op1=ALU.add), [s_tok, st['tact'][n % NSB], ta])
                st['sdve'][n % NSB] = t_tok
                kcol = cfg.g_kmask + kt
                p_tok = P.op('act', lambda e: e.activation(
                    out=pt[n % NPT], in_=tmp[n % NSB], func=AF.Exp, bias=self.cg[:, kcol:kcol + 1], scale=scale),
                    [t_tok, st['ptpe'][n % NPT]])
                st['tact'][n % NSB] = p_tok
                P_tok[i] = p_tok

            def emit_PV(i):
                qc, m, kt = steps[i]
                n = st['n'] + i
                last = None
                for qs in range(4):
                    w_ = [P_tok[i]]
                    if kt == 0:
                        w_.append(st['accfree'][qs])
                    last = P.op('pe', lambda e, qs=qs: e.matmul(
                        banks[qs][:, 0:VD + 1], pt[n % NPT][:, qs * 128:(qs + 1) * 128], vb[hb][:, kt, :],
                        start=(kt == 0), stop=(kt == NK - 1)), w_)
                st['ptpe'][n % NPT] = last
                if kt == NK - 1:
                    finish_group(qc, m, last)
                return last

            def finish_group(qc, m, lastpe):
                r_tok = []
                for qs in range(4):
                    c_ = P.op('dve', lambda e, qs=qs: e.tensor_scalar(
                        out=rec[:, m * 4 + qs:m * 4 + qs + 1], in0=banks[qs][:, VD:VD + 1], scalar1=1e-30,
                        scalar2=None, op0=ALU.max), [lastpe, st['omfree'][m]])
                    r_tok.append(P.op('dve', lambda e, qs=qs: e.reciprocal(
                        out=rec[:, m * 4 + qs:m * 4 + qs + 1], in_=rec[:, m * 4 + qs:m * 4 + qs + 1]), [c_]))
                ev = []
                for qs in range(4):
                    tk = P.op('dve', lambda e, qs=qs: e.tensor_scalar(
                        out=Om[m][:, qs, :], in0=banks[qs][:, 0:VD], scalar1=rec[:, m * 4 + qs:m * 4 + qs + 1],
                        scalar2=None, op0=ALU.mult), [r_tok[qs], st['omfree'][m]])
                    st['accfree'][qs] = tk
                    ev.append(tk)
                if m == 0:
                    st['om0'] = ev[-1]
                    return
                c_tok = P.op('dve', lambda e: e.scalar_tensor_tensor(
                    out=o, in0=Om[1], scalar=self.neglam[:, 1:2], in1=Om[0], op0=ALU.mult, op1=ALU.add),
                    [ev[-1], st['om0'], st['ofree']])
                st['omfree'] = [c_tok, c_tok]
                z_tok = P.op('pool', lambda e: e.memset(ssq, 0.0), [st['ssqfree']])
                sq_tok = None
                for qs in range(4):
                    sq_tok = P.op('act', lambda e, qs=qs: e.activation(
                        out=junk, in_=o[:, qs, :], func=AF.Square, accum_out=ssq[:, qs:qs + 1]),
                        [c_tok, z_tok, sq_tok])
                r1 = P.op('dve', lambda e: e.tensor_scalar(out=rs2, in0=ssq, scalar1=1.0 / VD, scalar2=NORM_EPS,
                                                           op0=ALU.mult, op1=ALU.add), [sq_tok, st['onfree']])
                r2a = P.op('act', lambda e: e.activation(out=rs2, in_=rs2, func=AF.Sqrt), [r1])
                r2 = P.op('dve', lambda e: e.reciprocal(out=rs2, in_=rs2), [r2a])
                st['ssqfree'] = r1
                n_tok = None
                for qs in range(4):
                    n_tok = P.op('dve', lambda e, qs=qs: e.scalar_tensor_tensor(
                        out=on[:, qs, :], in0=o[:, qs, :], scalar=rs2[:, qs:qs + 1], in1=self.gsub,
                        op0=ALU.mult, op1=ALU.mult), [r2, st['onfree']])
                st['ofree'] = n_tok

                def do_transposes(n_tok=n_tok, qc=qc, h=h):
                    tl = None
                    for vc in range(VD // 128):
                        for qs in range(4):
                            i_ = vc * 4 + qs
                            tl = P.op('pe', lambda e, vc=vc, qs=qs, i_=i_: e.transpose(
                                out=tb[:, i_ * 128:(i_ + 1) * 128], in_=on[:, qs, vc * 128:(vc + 1) * 128],
                                identity=self.ident_bf), [n_tok, st['tbfree']])
                    ob = st['osti']
                    st['osti'] = 1 - ob
                    c_ = P.op('act', lambda e: e.activation(
                        out=ost[ob].rearrange("p a b -> p (a b)"), in_=tb, func=AF.Copy), [tl, st['ostfree'][ob]])
                    st['tbfree'] = c_
                    st['onfree'] = tl
                    st['ostfree'][ob] = P.dma('sp', lambda e: e.dma_start(
                        out=fm(self.ONT)[:, h * (VD // 128):(h + 1) * (VD // 128), qc * 512:(qc + 1) * 512],
                        in_=ost[ob]), dso[ob], [c_])
                deferred.append([6, do_transposes])

            ns = len(steps)
            for i0 in range(min(NSB, ns)):
                emit_S(i0)
            lastpv = None
            for i in range(ns):
                lastpv = emit_PV(i)
                if i + NSB < ns:
                    emit_S(i + NSB)
                for d in deferred:
                    d[0] -= 1
                for d in [d for d in deferred if d[0] <= 0]:
                    d[1]()
                    deferred.remove(d)
            st['n'] += ns
            headdone[hb] = lastpv

        for h in range(H):
            run_head(h)
        for d in deferred:
            d[1]()
        self.phase_end()

    def conv1_phase(self, l):
        cfg, P, A = self.cfg, self.P, self.A
        T, CW, CCc = cfg.T, cfg.CW, cfg.CCc
        PAD = (CW - 1) // 2
        self.phase_begin()
        A = self.A
        banks = self.banks
        ypad = [A.get([128, T + 2 * PAD + 2], BF16) for _ in range(2)]
        diag = [A.get([128, CW, 128], BF16) for _ in range(2)]
        zst = [A.get([128, T], F32) for _ in range(2)]
        dsl = [P.ds("in0"), P.ds("in1")]
        dso = [P.ds("st0"), P.ds("st1")]
        halo = []
        for i in range(2):
            halo.append(P.op('pool', lambda e, i=i: e.memset(ypad[i][:, 0:PAD], 0.0)))
            halo.append(P.op('pool', lambda e, i=i: e.memset(ypad[i][:, PAD + T:PAD + T + PAD], 0.0)))
        pedone = [None, None]
        stdone = [None, None]
        bankfree = [None] * 8
        bn = 0
        ldtok = {}
        tm = A.get([128, T], BF16)
        tmtok = P.dma("pool", lambda e: e.dma_start(out=tm, in_=self.tmask_in), P.ds("pm"))

        def load(c):
            b = c % 2
            t_ = P.dma('sp', lambda e: e.dma_start(out=ypad[b][:, PAD:PAD + T], in_=self.YT[c * 128:(c + 1) * 128, :]),
                       dsl[b], [pedone[b]] + halo)
            ldtok[c] = P.op('pool' if c % 2 else 'dve', lambda e: e.tensor_tensor(
                out=ypad[b][:, PAD:PAD + T], in0=ypad[b][:, PAD:PAD + T], in1=tm, op=ALU.mult), [t_, tmtok])

        load(0)
        for c in range(CCc):
            b = c % 2
            if c + 1 < CCc:
                load(c + 1)
            dl = [None, None]
            for j in range(CW):
                eng = 'dve' if j % 2 == 0 else 'pool'
                col = cfg.c_wdw + c * CW + j
                dl[j % 2] = P.op(eng, lambda e, j=j, col=col, b=b: e.tensor_scalar(
                    out=diag[b][:, j, :], in0=self.ident_bf, scalar1=self.cl[:, col:col + 1], scalar2=None,
                    op0=ALU.mult), [pedone[b]])
            evs = []
            last = None
            for tt in range(T // 512):
                bk = bn % 8
                bn += 1
                for j in range(CW):
                    w_ = [ldtok[c], dl[0], dl[1], bankfree[bk]] if j == 0 else []
                    last = P.op('pe', lambda e, j=j, tt=tt, bk=bk, b=b: e.matmul(
                        banks[bk], diag[b][:, j, :], ypad[b][:, tt * 512 + j:tt * 512 + j + 512],
                        start=(j == 0), stop=(j == CW - 1)), w_)
                bcol = cfg.c_bdw + c
                if tt % 2 == 0:
                    tk = P.op('act', lambda e, tt=tt, bk=bk, bcol=bcol, b=b: e.activation(
                        out=zst[b][:, tt * 512:(tt + 1) * 512], in_=banks[bk], func=AF.Identity,
                        bias=self.cl[:, bcol:bcol + 1]), [last, stdone[b]])
                else:
                    tk = P.op('dve', lambda e, tt=tt, bk=bk, bcol=bcol, b=b: e.tensor_scalar(
                        out=zst[b][:, tt * 512:(tt + 1) * 512], in0=banks[bk], scalar1=self.cl[:, bcol:bcol + 1],
                        scalar2=None, op0=ALU.add), [last, stdone[b]])
                bankfree[bk] = tk
                evs.append(tk)
            pedone[b] = last
            stdone[b] = P.dma('sp', lambda e, c=c, b=b: e.dma_start(out=self.ZT[c * 128:(c + 1) * 128, :], in_=zst[b]),
                              dso[b], evs)
        self.phase_end()

    def conv2_phase(self, l):
        cfg, P, A = self.cfg, self.P, self.A
        T, CCc, CC = cfg.T, cfg.CCc, cfg.CC
        self.phase_begin()
        A = self.A
        banks = self.banks
        z = [A.get([128, CCc, 512], F32) for _ in range(2)]
        zb = A.get([128, CCc, 512], BF16)
        zs = A.get([128, CCc, 512], BF16)
        mean = A.get([128, 512], F32)
        msq = A.get([128, 512], F32)
        rstd = A.get([128, 512], F32)
        t1 = [A.get([128, 512], F32) for _ in range(4)]
        outb = [A.get([128, CCc, 512], BF16) for _ in range(2)]
        dsl = [P.ds("in0"), P.ds("in1")]
        dso = [P.ds("st0"), P.ds("st1")]
        zfree = [None, None]
        stdone = [None, None]
        pedone = None
        statfree = None
        t1free = [None] * 4
        ld = {}

        def load(i):
            b = i % 2
            ld[i] = P.dma('sp', lambda e: e.dma_start(out=z[b], in_=fm(self.ZT)[:, :, i * 512:(i + 1) * 512]),
                          dsl[b], zfree[b] or [])

        load(0)
        nt = 0
        for i in range(T // 512):
            b = i % 2
            if i + 1 < T // 512:
                load(i + 1)
            a1 = P.op('act', lambda e, b=b: e.activation(out=zb, in_=z[b], func=AF.Copy), [ld[i], pedone])
            a2 = P.op('act', lambda e, b=b: e.activation(out=zs, in_=z[b], func=AF.Square), [ld[i], pedone])
            mm = None
            for c in range(CCc):
                mm = P.op('pe', lambda e, c=c: e.matmul(banks[0], self.ones_bf, zb[:, c, :], start=(c == 0),
                                                        stop=(c == CCc - 1)), [a1, statfree] if c == 0 else [])
            for c in range(CCc):
                mm = P.op('pe', lambda e, c=c: e.matmul(banks[1], self.ones_bf, zs[:, c, :], start=(c == 0),
                                                        stop=(c == CCc - 1)), [a2] if c == 0 else [])
            pedone = mm
            s1 = P.op('dve', lambda e: e.tensor_scalar(out=mean, in0=banks[0], scalar1=1.0 / CC, scalar2=None,
                                                       op0=ALU.mult), [mm] + (zfree[1 - b] or []))
            s2 = P.op('dve', lambda e: e.tensor_tensor(out=msq, in0=mean, in1=mean, op=ALU.mult), [s1])
            s3 = P.op('dve', lambda e: e.scalar_tensor_tensor(out=rstd, in0=banks[1], scalar=1.0 / CC, in1=msq,
                                                              op0=ALU.mult, op1=ALU.subtract), [s2])
            s4a = P.op('dve', lambda e: e.tensor_scalar(out=rstd, in0=rstd, scalar1=LN_EPS, scalar2=None,
                                                        op0=ALU.add), [s3])
            s4b = P.op('act', lambda e: e.activation(out=rstd, in_=rstd, func=AF.Sqrt), [s4a])
            s4 = P.op('dve', lambda e: e.reciprocal(out=rstd, in_=rstd), [s4b])
            statfree = s3
            lasts = []
            for c in range(CCc):
                eng = 'dve' if c % 3 != 2 else 'pool'
                tb_ = nt % 4
                nt += 1
                u1 = P.op(eng, lambda e, c=c, tb_=tb_, b=b: e.tensor_tensor(out=t1[tb_], in0=z[b][:, c, :], in1=mean,
                                                                       op=ALU.subtract), [s4, t1free[tb_]])
                u2 = P.op(eng, lambda e, c=c, tb_=tb_: e.tensor_tensor(out=t1[tb_], in0=t1[tb_], in1=rstd,
                                                                       op=ALU.mult), [u1])
                gcol = cfg.c_gcln + c
                bcol = cfg.c_bcln + c
                u3 = P.op('act', lambda e, c=c, tb_=tb_, gcol=gcol, bcol=bcol, b=b: e.activation(
                    out=outb[b][:, c, :], in_=t1[tb_], func=AF.Silu, bias=self.cl[:, bcol:bcol + 1],
                    scale=self.cl[:, gcol:gcol + 1]), [u2, stdone[b]])
                t1free[tb_] = u3
                lasts.append(u3)
            zfree[b] = [lasts[-1]]
            stdone[b] = P.dma('sp', lambda e, i=i, b=b: e.dma_start(out=fm(self.CT)[:, :, i * 512:(i + 1) * 512], in_=outb[b]),
                              dso[b], [lasts[-1]])
        self.phase_end()


def _consts(cfg, inp):
    DEPTH = cfg.DEPTH
    cl = np.zeros((DEPTH, 128, cfg.NCL), np.float32)

    def pc(v):
        return np.ascontiguousarray(v.reshape(-1, 128).T)

    for l in range(DEPTH):
        c = cl[l]
        c[:, cfg.c_gmix:cfg.c_gmix + cfg.DC] = pc(inp["g_mix"][l])
        c[:, cfg.c_bgate:cfg.c_bgate + 2 * cfg.DC] = pc(inp["b_gate"][l])
        c[:, cfg.c_bdw:cfg.c_bdw + cfg.CCc] = pc(inp["b_dw"][l])
        c[:, cfg.c_gcln:cfg.c_gcln + cfg.CCc] = pc(inp["g_conv_ln"][l])
        c[:, cfg.c_bcln:cfg.c_bcln + cfg.CCc] = pc(inp["b_conv_ln"][l])
        c[:, cfg.c_bcproj:cfg.c_bcproj + cfg.DC] = pc(inp["b_conv_proj"][l])
        c[:, cfg.c_gffn:cfg.c_gffn + cfg.DC] = pc(inp["g_ffn"][l])
        wd = inp["w_dw"][l][:, 0, :]
        wd = wd.T.reshape(cfg.CCc, 128, cfg.CW).transpose(1, 0, 2).reshape(128, cfg.CCc * cfg.CW)
        c[:, cfg.c_wdw:cfg.c_wdw + cfg.CCc * cfg.CW] = wd
        c[:, cfg.c_gsub:cfg.c_gsub + cfg.VD] = inp["g_subln"][l][None, :]
        c[:, cfg.c_lq:cfg.c_lq + 2 * cfg.HD] = inp["lambda_q"][l].reshape(1, -1)
        c[:, cfg.c_lk:cfg.c_lk + 2 * cfg.HD] = inp["lambda_k"][l].reshape(1, -1)
    return cl


def _globals(cfg, g_final, nvalid):
    cg = np.zeros((128, cfg.NCG), np.float32)
    cg[:, cfg.g_gfinal:cfg.g_gfinal + cfg.DC] = g_final.reshape(-1, 128).T
    cg[:, cfg.g_ident:cfg.g_ident + 128] = np.eye(128, dtype=np.float32)
    kpos = np.arange(cfg.T).reshape(cfg.NK, 128).T
    cg[:, cfg.g_kmask:cfg.g_kmask + cfg.NK] = np.where(kpos < nvalid, 0.0, MASK_NEG)
    return cg


def _atab(cfg):
    p = np.arange(128)[:, None]
    u = np.arange(cfg.AW)[None, :]
    return (-np.abs(u - cfg.AOFF - p)).astype(np.float32)


_NC_CACHE = {}


def run_trunk(cfg, seqs, inp, n_cores=8):
    key = (cfg.T, cfg.D, cfg.H, cfg.DFF, cfg.DEPTH)
    if key not in _NC_CACHE:
        _NC_CACHE[key] = Builder(cfg).build()
    nc = _NC_CACHE[key]
    cl = _consts(cfg, inp)
    atab = _atab(cfg)
    wmap = {k: np.ascontiguousarray(inp[k], dtype=np.float32) for k in
            ("w_in", "w_attn_proj", "w_conv_proj", "w_out", "w_ffn_in", "w_ffn_out")}
    in_maps = []
    if n_cores == 8 and len(seqs) == 6:
        core_of_seq = [0, 4, 1, 2, 5, 6]
    else:
        core_of_seq = list(range(len(seqs)))
    seq_of_core = {c: i for i, c in enumerate(core_of_seq)}
    for c in range(n_cores):
        xT = np.zeros((cfg.D, cfg.T), np.float32)
        if c in seq_of_core:
            s = seqs[seq_of_core[c]]
            S = s.shape[0]
            xT[:, :S] = s.T
        else:
            S = cfg.T
        tmask = np.zeros((128, cfg.T), np.float32)
        tmask[:, :S] = 1.0
        m = {"xT": xT, "cl": cl, "cg": _globals(cfg, inp["g_final"], S), "atab": atab, "tmask": tmask}
        m.update(wmap)
        in_maps.append(m)
    res = run_bass_kernel_spmd(nc, in_maps, core_ids=list(range(n_cores)))
    outs = []
    for i in range(len(seqs)):
        S = seqs[i].shape[0]
        outs.append(np.ascontiguousarray(res.results[core_of_seq[i]]["yT"][:, :S].T))
    return outs


def kernel(x_prompt, x_sample, g_mix, w_in, b_gate, lambda_q, lambda_k, g_subln,
           w_attn_proj, w_dw, b_dw, g_conv_ln, b_conv_ln, w_conv_proj, b_conv_proj,
           w_out, g_ffn, w_ffn_in, w_ffn_out, g_final):
    inp = dict(g_mix=g_mix, w_in=w_in, b_gate=b_gate, lambda_q=lambda_q, lambda_k=lambda_k, g_subln=g_subln,
               w_attn_proj=w_attn_proj, w_dw=w_dw, b_dw=b_dw, g_conv_ln=g_conv_ln, b_conv_ln=b_conv_ln,
               w_conv_proj=w_conv_proj, b_conv_proj=b_conv_proj, w_out=w_out, g_ffn=g_ffn,
               w_ffn_in=w_ffn_in, w_ffn_out=w_ffn_out, g_final=g_final)
    inp = {k: np.asarray(v, dtype=np.float32) for k, v in inp.items()}
    x_prompt = np.asarray(x_prompt, dtype=np.float32)
    x_sample = np.asarray(x_sample, dtype=np.float32)
    cfg = Cfg()
    seqs = [x_prompt[b] for b in range(x_prompt.shape[0])] + [x_sample[b] for b in range(x_sample.shape[0])]
    outs = run_trunk(cfg, seqs, inp)
    nb = x_prompt.shape[0]
    y_prompt = np.stack(outs[:nb]).astype(np.float32)
    y_sample = np.stack(outs[nb:]).astype(np.float32)
    return (y_prompt, y_sample)
```

```python
import numpy as np
from contextlib import ExitStack
import concourse.bass as bass
import concourse.mybir as mybir
from concourse.bass_utils import run_bass_kernel_spmd

F32 = mybir.dt.float32
BF16 = mybir.dt.bfloat16
U8 = mybir.dt.uint8
AF = mybir.ActivationFunctionType
ALU = mybir.AluOpType

NORM_EPS = 1e-6
LN_EPS = 1e-5
MASK_NEG = -30000.0
SB_BYTES = 192 * 1024


class Cfg:
    def __init__(self, T=4096, D=2048, H=8, DFF=5632, DEPTH=4, CW=31):
        self.T, self.D, self.H, self.DFF, self.DEPTH, self.CW = T, D, H, DFF, DEPTH, CW
        self.HD = 128
        self.VD = 256
        self.QK = H * 2 * self.HD
        self.VW = H * self.VD
        self.CC = D
        self.IN_W = 2 * self.QK + self.VW + 2 * self.CC + 2 * D
        self.DC = D // 128
        self.CCc = self.CC // 128
        self.NK = T // 128
        self.NQ = T // 512
        self.AOFF = T - 128
        self.AW = (self.NQ - 1) * 512 + 512 + self.AOFF
        c = 0
        self.c_gmix = c; c += self.DC
        self.c_bgate = c; c += 2 * self.DC
        self.c_bdw = c; c += self.CCc
        self.c_gcln = c; c += self.CCc
        self.c_bcln = c; c += self.CCc
        self.c_bcproj = c; c += self.DC
        self.c_gffn = c; c += self.DC
        self.c_wdw = c; c += self.CCc * CW
        self.c_gsub = c; c += self.VD
        self.c_lq = c; c += 2 * self.HD
        self.c_lk = c; c += 2 * self.HD
        self.NCL = c
        g = 0
        self.g_gfinal = g; g += self.DC
        self.g_ident = g; g += 128
        self.g_kmask = g; g += self.NK
        self.NCG = g


def lambda_init(layer):
    return 0.8 - 0.6 * float(np.exp(-0.3 * layer))


class DmaSem:
    def __init__(self, sem):
        self.sem = sem
        self.count = 0


class Prog:
    ENG = ('pe', 'act', 'dve', 'pool', 'sp')

    def __init__(self, nc, stack):
        self.nc = nc
        self.stack = stack
        self.streams = {k: [] for k in self.ENG}
        self.cnt = {k: 0 for k in self.ENG}
        self.esem = {k: stack.enter_context(nc.semaphore("s_" + k)) for k in self.ENG}
        self.dsems = {}
        self.pending = {k: [] for k in self.ENG}
        self.waited = {k: {} for k in self.ENG}

    def ds(self, name):
        if name not in self.dsems:
            self.dsems[name] = DmaSem(self.stack.enter_context(self.nc.semaphore("d_" + name)))
        return self.dsems[name]

    def _w(self, eng, waits):
        w = []

        def fl(x):
            if x is None:
                return
            if isinstance(x, list):
                for y in x:
                    fl(y)
            else:
                w.append(x)
        fl(list(waits))
        if self.pending[eng]:
            w = self.pending[eng] + w
            self.pending[eng] = []
        return w

    def op(self, eng, fn, waits=()):
        self.cnt[eng] += 1
        self.streams[eng].append((fn, self._w(eng, waits), None))
        return (eng, self.cnt[eng])

    def dma(self, eng, fn, ds, waits=()):
        ds.count += 16
        self.streams[eng].append((fn, self._w(eng, waits), ds))
        return (ds, ds.count)

    def last(self, eng):
        return (eng, self.cnt[eng]) if self.cnt[eng] else None

    def barrier(self):
        toks = [(k, self.cnt[k]) for k in self.ENG if self.cnt[k]]
        toks += [(d, d.count) for d in self.dsems.values() if d.count]
        for k in self.ENG:
            self.pending[k] = list(toks)

    def finish(self):
        self.barrier()
        for k in self.ENG:
            w = self._w(k, [])
            self.streams[k].append((None, w, None))

    def flush(self):
        nc = self.nc
        P = self
        with nc.Block() as block:
            @block.tensor
            def _(e):
                P.replay('pe', e)

            @block.scalar
            def _(e):
                P.replay('act', e)

            @block.vector
            def _(e):
                P.replay('dve', e)

            @block.gpsimd
            def _(e):
                P.replay('pool', e)

            @block.sync
            def _(e):
                P.replay('sp', e)
        for k in self.ENG:
            self.streams[k] = []

    def replay(self, eng, e):
        waited = self.waited[eng]
        for fn, waits, ds in self.streams[eng]:
            for key, val in waits:
                kk = key if isinstance(key, str) else id(key)
                if waited.get(kk, 0) >= val:
                    continue
                waited[kk] = val
                sem = self.esem[key] if isinstance(key, str) else key.sem
                e.wait_ge(sem, val)
            if fn is None:
                continue
            inst = fn(e)
            if ds is None:
                inst.then_inc(self.esem[eng], 1)
            else:
                inst.then_inc(ds.sem, 16)


class Alloc:
    def __init__(self, sb, base, limit):
        self.sb, self.base, self.off, self.limit = sb, base, base, limit

    def reset(self):
        self.off = self.base

    def get(self, shape, dt):
        esz = 4 if dt == F32 else 2
        n = int(np.prod(shape[1:]))
        nb = (n * esz + 63) // 64 * 64
        o = self.off
        self.off += nb
        assert self.off <= self.limit, ("SBUF overflow", self.off, self.limit)
        v = self.sb[:, o:o + n * esz].bitcast(dt)
        if len(shape) == 3:
            v = v.rearrange("p (a b) -> p a b", b=shape[2])
        return v


def fm(ap):
    return ap.rearrange("(c p) t -> p c t", p=128)


class Builder:
    def __init__(self, cfg):
        self.cfg = cfg

    def build(self):
        cfg = self.cfg
        T, D, DEPTH = cfg.T, cfg.D, cfg.DEPTH
        nc = bass.Bass("TRN2", target_bir_lowering=False)
        self.nc = nc
        dt_in = lambda name, shape: nc.dram_tensor(name, shape, F32, kind="ExternalInput").ap()
        self.xT_in = dt_in("xT", [D, T])
        self.cl_in = dt_in("cl", [DEPTH, 128, cfg.NCL])
        self.cg_in = dt_in("cg", [128, cfg.NCG])
        self.atab_in = dt_in("atab", [128, cfg.AW])
        self.tmask_in = dt_in("tmask", [128, T])
        self.w_in = dt_in("w_in", [DEPTH, D, cfg.IN_W])
        self.w_ap = dt_in("w_attn_proj", [DEPTH, cfg.VW, D])
        self.w_cp = dt_in("w_conv_proj", [DEPTH, cfg.CC, D])
        self.w_out = dt_in("w_out", [DEPTH, D, D])
        self.w_fi = dt_in("w_ffn_in", [DEPTH, D, 2 * cfg.DFF])
        self.w_fo = dt_in("w_ffn_out", [DEPTH, cfg.DFF, D])
        self.yT_out = nc.dram_tensor("yT", [D, T], F32, kind="ExternalOutput").ap()
        scr = lambda name, shape, dt: nc.dram_tensor(name, shape, dt).ap()
        self.XT = scr("s_XT", [D, T], F32)
        self.HT = scr("s_HT", [D, T], BF16)
        self.QKT = scr("s_QKT", [2 * cfg.QK, T], BF16)
        self.V = scr("s_V", [T, cfg.VW], BF16)
        self.YT = scr("s_YT", [cfg.CC, T], BF16)
        self.GT = scr("s_GT", [2 * D, T], BF16)
        self.ONT = scr("s_ONT", [cfg.VW, T], BF16)
        self.ZT = scr("s_ZT", [cfg.CC, T], F32)
        self.CT = scr("s_CT", [cfg.CC, T], BF16)
        self.MT = scr("s_MT", [D, T], BF16)
        self.AT = scr("s_AT", [cfg.DFF, T], BF16)

        with ExitStack() as stack:
            P = Prog(nc, stack)
            self.P = P
            GB = 16 * 1024
            gsb = stack.enter_context(nc.sbuf_tensor("gsb", [128, GB], U8))
            G = Alloc(gsb, 0, GB)
            self.cg = G.get([128, cfg.NCG], F32)
            self.cl = G.get([128, cfg.NCL], F32)
            self.ident_bf = G.get([128, 128], BF16)
            self.ones_bf = G.get([128, 128], BF16)
            self.neglam = G.get([128, 2], F32)
            self.gsub = G.get([128, cfg.VD], F32)
            self.lamtmp = G.get([128, 2 * cfg.HD], F32)
            self.lamred = G.get([128, 4], F32)
            self.PH_BYTES = SB_BYTES - GB
            self.phase_no = 0
            self.ph = None
            self.A = None
            self.banks = None
            t0 = P.dma('sp', lambda e: e.dma_start(out=self.cg, in_=self.cg_in), P.ds("c0"))
            t1 = P.op('dve', lambda e: e.tensor_copy(out=self.ident_bf, in_=self.cg[:, cfg.g_ident:cfg.g_ident + 128]), [t0])
            t2 = P.op('pool', lambda e: e.memset(self.ones_bf, 1.0))
            P.barrier()

            xsrc = self.xT_in
            for l in range(DEPTH):
                self.layer_consts(l)
                self.norm_phase(xsrc, cfg.c_gmix, self.HT, final=False)
                self.inproj_phase(l)
                self.attn_phase(l)
                self.conv1_phase(l)
                self.conv2_phase(l)
                self.merge_phase(l)
                self.resid_phase(l, self.w_out[l], self.MT, cfg.D // 128, xsrc, TS=min(2048, T))
                xsrc = self.XT
                self.norm_phase(xsrc, cfg.c_gffn, self.HT, final=False)
                self.ffnin_phase(l)
                self.resid_phase(l, self.w_fo[l], self.AT, cfg.DFF // 128, xsrc, TS=min(1024, T))
            self.norm_phase(xsrc, None, self.yT_out, final=True)
            P.finish()
            P.flush()
        return nc

    def phase_begin(self):
        nc = self.nc
        self.P.barrier()
        self.ph = ExitStack()
        self.phase_no += 1
        sbp = self.ph.enter_context(nc.sbuf_tensor("sb%d" % self.phase_no, [128, self.PH_BYTES], U8))
        psp = self.ph.enter_context(nc.psum_tensor("ps%d" % self.phase_no, [128, 8 * 512], F32))
        self.banks = [psp[:, b * 512:(b + 1) * 512] for b in range(8)]
        self.A = Alloc(sbp, 0, self.PH_BYTES)

    def phase_end(self):
        self.P.barrier()
        self.P.flush()
        self.ph.close()
        self.ph = None

    def layer_consts(self, l):
        cfg, P = self.cfg, self.P
        HD = cfg.HD
        P.barrier()
        t0 = P.dma('sp', lambda e: e.dma_start(out=self.cl, in_=self.cl_in[l]), P.ds("c0"))
        lq = self.cl[:, cfg.c_lq:cfg.c_lq + 2 * HD]
        lk = self.cl[:, cfg.c_lk:cfg.c_lk + 2 * HD]
        t1 = P.op('dve', lambda e: e.tensor_tensor(out=self.lamtmp, in0=lq, in1=lk, op=ALU.mult), [t0])
        t2 = P.op('dve', lambda e: e.tensor_reduce(
            out=self.lamred[:, 0:2], in_=self.lamtmp.rearrange("p (a b) -> p a b", b=HD),
            axis=mybir.AxisListType.X, op=ALU.add), [t1])
        t3 = P.op('act', lambda e: e.activation(out=self.lamred[:, 2:4], in_=self.lamred[:, 0:2], func=AF.Exp), [t2])
        li = lambda_init(l)
        t4 = P.op('dve', lambda e: e.tensor_tensor(out=self.neglam[:, 0:1], in0=self.lamred[:, 3:4],
                                                   in1=self.lamred[:, 2:3], op=ALU.subtract), [t3])
        t5 = P.op('dve', lambda e: e.tensor_scalar(out=self.neglam[:, 1:2], in0=self.neglam[:, 0:1],
                                                   scalar1=-li, scalar2=None, op0=ALU.add), [t4])
        t6 = P.op('dve', lambda e: e.tensor_scalar(out=self.gsub, in0=self.cl[:, cfg.c_gsub:cfg.c_gsub + cfg.VD],
                                                   scalar1=(1.0 - li), scalar2=None, op0=ALU.mult), [t5])
        P.barrier()

    def norm_phase(self, src, gcol, dst, final):
        cfg, P, A = self.cfg, self.P, self.A
        DC, D = cfg.DC, cfg.D
        self.phase_begin()
        A = self.A
        odt = F32 if final else BF16
        xin = [A.get([128, DC, 512], F32) for _ in range(2)]
        sq = [A.get([128, DC, 512], BF16) for _ in range(2)]
        hout = [A.get([128, DC, 512], odt) for _ in range(2)]
        rstd = [A.get([128, 512], F32) for _ in range(2)]
        gsrc = self.cg if final else self.cl
        gc = cfg.g_gfinal if final else gcol
        dsi = [P.ds("in0"), P.ds("in1")]
        dso = [P.ds("st0"), P.ds("st1")]
        hdone = [None, None]
        mmdone = [None, None]
        stdone = [None, None]
        r2done = [None, None]
        for i in range(cfg.NQ):
            b = i % 2
            ts = slice(i * 512, (i + 1) * 512)
            ld = P.dma('sp', lambda e, b=b, ts=ts: e.dma_start(out=xin[b], in_=fm(src)[:, :, ts]), dsi[b],
                       (hdone[b] or []))
            sqt = P.op('act', lambda e, b=b: e.activation(out=sq[b], in_=xin[b], func=AF.Square), [ld, mmdone[b]])
            mm = None
            for c in range(DC):
                mm = P.op('pe', lambda e, b=b, c=c: e.matmul(self.banks[b], self.ones_bf, sq[b][:, c, :],
                                                             start=(c == 0), stop=(c == DC - 1)),
                          [sqt, r2done[b]] if c == 0 else [])
            mmdone[b] = mm
            r1 = P.op('dve', lambda e, b=b: e.tensor_scalar(out=rstd[b], in0=self.banks[b], scalar1=1.0 / D,
                                                            scalar2=NORM_EPS, op0=ALU.mult, op1=ALU.add),
                      [mm] + (hdone[b] or []))
            r2a = P.op('act', lambda e, b=b: e.activation(out=rstd[b], in_=rstd[b], func=AF.Sqrt), [r1])
            r2 = P.op('dve', lambda e, b=b: e.reciprocal(out=rstd[b], in_=rstd[b]), [r2a])
            r2done[b] = r1
            lastd = lastp = None
            for c in range(DC):
                eng = 'dve'
                tk = P.op(eng, lambda e, b=b, c=c: e.scalar_tensor_tensor(
                    out=hout[b][:, c, :], in0=xin[b][:, c, :], scalar=gsrc[:, gc + c:gc + c + 1],
                    in1=rstd[b], op0=ALU.mult, op1=ALU.mult), [r2, stdone[b]])
                if eng == 'dve':
                    lastd = tk
                else:
                    lastp = tk
            hdone[b] = [lastd, lastp]
            stdone[b] = P.dma('sp', lambda e, b=b, ts=ts: e.dma_start(out=fm(dst)[:, :, ts], in_=hout[b]), dso[b],
                              [lastd, lastp])
        self.phase_end()

    def linear(self, ins, TS, blocks):
        cfg, P, A = self.cfg, self.P, self.A
        T = cfg.T
        NS = T // TS
        NHALF = TS // 1024
        KCmax = max(kc for _, kc in ins)
        FBmax = max(sum(p[2] for p in b['pieces']) for b in blocks)
        in_sb = [A.get([128, kc, TS], BF16) for _, kc in ins]
        wbuf = [A.get([128, KCmax, FBmax], BF16) for _ in range(2)]
        dsw = [P.ds("w0"), P.ds("w1")]
        dsin = [P.ds("in0"), P.ds("in1")]
        wfree = [None, None]
        bankfree = [[None] * 4, [None] * 4]
        seq = [(s, bi) for s in range(NS) for bi in range(len(blocks))]
        wtok = {}

        def issue_w(n):
            s, bi = seq[n]
            wb = n % 2
            c0 = 0
            tk = None
            for (w2d, col0, ncols, in_idx) in blocks[bi]['pieces']:
                kc = ins[in_idx][1]
                tk = P.dma('pool', lambda e, wb=wb, c0=c0, w2d=w2d, col0=col0, ncols=ncols, kc=kc: e.dma_start(
                    out=wbuf[wb][:, 0:kc, c0:c0 + ncols],
                    in_=w2d.rearrange("(c p) f -> p c f", p=128)[:, :, col0:col0 + ncols]), dsw[wb], [wfree[wb]])
                c0 += ncols
            wtok[n] = tk

        issue_w(0)
        ucount = 0
        intok = [None] * len(ins)
        lastpe_all = None
        for n, (s, bi) in enumerate(seq):
            blk = blocks[bi]
            if bi == 0:
                for ii, (src, kc) in enumerate(ins):
                    tk = None
                    nsp = max(1, kc // 8)
                    for q in range(nsp):
                        cs = slice(q * kc // nsp, (q + 1) * kc // nsp)
                        tk = P.dma('sp', lambda e, ii=ii, cs=cs, s=s, src=src: e.dma_start(
                            out=in_sb[ii][:, cs, :], in_=fm(src)[:, cs, s * TS:(s + 1) * TS]), dsin[ii], [lastpe_all])
                    intok[ii] = tk
            if n + 1 < len(seq):
                issue_w(n + 1)
            wb = n % 2
            FB = sum(p[2] for p in blk['pieces'])
            kind = blk['kind']
            lastpe = None
            if kind == 'vtok':
                kc = ins[0][1]
                for g in range(TS // 512):
                    bs = ucount % 2
                    ucount += 1
                    bk = self.banks[bs * 4:bs * 4 + 4]
                    tok0 = s * TS + g * 512
                    for k in range(kc):
                        for j in range(4):
                            w_ = [wtok[n], intok[0], bankfree[bs][j]] if k == 0 else []
                            lastpe = P.op('pe', lambda e, k=k, j=j, g=g, wb=wb, bk=bk, FB=FB: e.matmul(
                                bk[j][:, 0:FB], in_sb[0][:, k, g * 512 + j * 128:g * 512 + (j + 1) * 128],
                                wbuf[wb][:, k, 0:FB], start=(k == 0), stop=(k == kc - 1)), w_)
                    rel = blk['epi'](0, tok0, bk, lastpe, None)
                    bankfree[bs] = rel
            else:
                nunits = FB // 256
                for u in range(nunits):
                    if kind == 'pair':
                        cols = [u * 128, FB // 2 + u * 128]
                        iidx = [blk['pieces'][0][3], blk['pieces'][1][3]]
                    else:
                        cols = [2 * u * 128, (2 * u + 1) * 128]
                        iidx = [blk['pieces'][0][3]] * 2
                    for h in range(NHALF):
                        bs = ucount % 2
                        ucount += 1
                        bk = self.banks[bs * 4:bs * 4 + 4]
                        tok0 = s * TS + h * 1024
                        pretoks = blk['pre'](u, tok0) if blk.get('pre') else None
                        kcs = [ins[iidx[0]][1], ins[iidx[1]][1]]
                        kc = max(kcs)
                        for k in range(kc):
                            for ci in range(2):
                                if k >= kcs[ci]:
                                    continue
                                for t in range(2):
                                    w_ = [wtok[n], intok[iidx[ci]], bankfree[bs][ci * 2 + t]] if k == 0 else []
                                    lastpe = P.op('pe', lambda e, k=k, ci=ci, t=t, h=h, wb=wb, bk=bk, cols=cols, iidx=iidx, kcs=kcs: e.matmul(
                                        bk[ci * 2 + t], wbuf[wb][:, k, cols[ci]:cols[ci] + 128],
                                        in_sb[iidx[ci]][:, k, h * 1024 + t * 512:h * 1024 + (t + 1) * 512],
                                        start=(k == 0), stop=(k == kcs[ci] - 1)), w_)
                        rel = blk['epi'](u, tok0, bk, lastpe, pretoks)
                        bankfree[bs] = rel
            wfree[wb] = lastpe
            lastpe_all = lastpe

    def stager(self, name, shape, dt, n=2):
        bufs = [self.A.get(shape, dt) for _ in range(n)]
        return {'bufs': bufs, 'tok': [None] * n, 'i': 0, 'ds': [self.P.ds("%s%d" % (name, i)) for i in range(n)]}

    def inproj_phase(self, l):
        cfg, P, A = self.cfg, self.P, self.A
        D, QK, VW, CC = cfg.D, cfg.QK, cfg.VW, cfg.CC
        self.phase_begin()
        A = self.A
        w = self.w_in[l]
        TS = min(2048, cfg.T)
        stg = self.stager("st", [128, 2, 1024], BF16)
        sig = [A.get([128, 1024], F32) for _ in range(2)]
        vst = self.stager("sv", [128, 4, 512], BF16)
        sigtok = [None, None]
        cnt = [0]
        bgc = cfg.c_bgate
        banks = self.banks

        def store2(s, row0s, dst, tok0, evs):
            b = s['i']
            buf = s['bufs'][b]
            t = None
            for ci, r0 in enumerate(row0s):
                t = P.dma('sp', lambda e, ci=ci, r0=r0, buf=buf: e.dma_start(
                    out=dst[r0:r0 + 128, tok0:tok0 + 1024], in_=buf[:, ci, :]), s['ds'][b], evs)
            s['tok'][b] = t
            s['i'] = (b + 1) % len(s['bufs'])

        def epi_copy(row_of_unit, dst):
            def f(u, tok0, bk, lastpe, pre):
                b = stg['i']
                buf = stg['bufs'][b]
                evs = []
                for ci in range(2):
                    for t in range(2):
                        eng = 'act' if (ci + t) % 2 == 0 else 'dve'
                        if eng == 'act':
                            tk = P.op('act', lambda e, ci=ci, t=t, buf=buf, bk=bk: e.activation(
                                out=buf[:, ci, t * 512:(t + 1) * 512], in_=bk[ci * 2 + t], func=AF.Copy),
                                [lastpe, stg['tok'][b]])
                        else:
                            tk = P.op('dve', lambda e, ci=ci, t=t, buf=buf, bk=bk: e.tensor_copy(
                                out=buf[:, ci, t * 512:(t + 1) * 512], in_=bk[ci * 2 + t]),
                                [lastpe, stg['tok'][b]])
                        evs.append(tk)
                r0 = row_of_unit(u)
                store2(stg, [r0, r0 + 128], dst, tok0, evs)
                return evs
            return f

        def epi_gates(row_of_unit):
            def f(u, tok0, bk, lastpe, pre):
                b = stg['i']
                buf = stg['bufs'][b]
                evs = []
                r0 = row_of_unit(u)
                for ci in range(2):
                    col = bgc + (r0 + ci * 128) // 128
                    for t in range(2):
                        tk = P.op('act', lambda e, ci=ci, t=t, buf=buf, bk=bk, col=col: e.activation(
                            out=buf[:, ci, t * 512:(t + 1) * 512], in_=bk[ci * 2 + t], func=AF.Sigmoid,
                            bias=self.cl[:, col:col + 1]), [lastpe, stg['tok'][b]])
                        evs.append(tk)
                store2(stg, [r0, r0 + 128], self.GT, tok0, evs)
                return evs
            return f

        def epi_glu(row_of_unit):
            def f(u, tok0, bk, lastpe, pre):
                b = stg['i']
                buf = stg['bufs'][b]
                sb_ = cnt[0] % 2
                cnt[0] += 1
                rel = [None] * 4
                evs = []
                for t in range(2):
                    ta = P.op('act', lambda e, t=t, bk=bk, sb_=sb_: e.activation(
                        out=sig[sb_][:, t * 512:(t + 1) * 512], in_=bk[2 + t], func=AF.Sigmoid),
                        [lastpe, sigtok[sb_]])
                    td = P.op('dve', lambda e, t=t, bk=bk, buf=buf, sb_=sb_: e.tensor_tensor(
                        out=buf[:, 0, t * 512:(t + 1) * 512], in0=bk[t], in1=sig[sb_][:, t * 512:(t + 1) * 512],
                        op=ALU.mult), [ta, stg['tok'][b]])
                    rel[2 + t] = ta
                    rel[t] = td
                    evs.append(td)
                sigtok[sb_] = evs[-1]
                r0 = row_of_unit(u)
                store2(stg, [r0], self.YT, tok0, evs)
                return rel
            return f

        def epi_v(col0):
            def f(u, tok0, bk, lastpe, pre):
                b = vst['i']
                buf = vst['bufs'][b]
                FB = min(512, VW)
                evs = []
                for j in range(4):
                    if j % 2 == 0:
                        tk = P.op('act', lambda e, j=j, buf=buf, bk=bk: e.activation(
                            out=buf[:, j, 0:FB], in_=bk[j][:, 0:FB], func=AF.Copy), [lastpe, vst['tok'][b]])
                    else:
                        tk = P.op('dve', lambda e, j=j, buf=buf, bk=bk: e.tensor_copy(
                            out=buf[:, j, 0:FB], in_=bk[j][:, 0:FB]), [lastpe, vst['tok'][b]])
                    evs.append(tk)
                vst['tok'][b] = P.dma('sp', lambda e, buf=buf: e.dma_start(
                    out=self.V[tok0:tok0 + 512, col0:col0 + FB].rearrange("(j p) f -> p j f", p=128),
                    in_=buf[:, :, 0:FB]), vst['ds'][b], evs)
                vst['i'] = (b + 1) % 2
                return evs
            return f

        blocks = []
        for c0 in range(0, 2 * QK, 512):
            blocks.append(dict(pieces=[(w, c0, 512, 0)], kind='single',
                               epi=epi_copy(lambda u, c0=c0: c0 + u * 256, self.QKT)))
        FBv = min(512, VW)
        for c0 in range(0, VW, FBv):
            blocks.append(dict(pieces=[(w, 2 * QK + c0, FBv, 0)], kind='vtok', epi=epi_v(c0)))
        ub = 2 * QK + VW
        for c0 in range(0, CC, 256):
            blocks.append(dict(pieces=[(w, ub + c0, 256, 0), (w, ub + CC + c0, 256, 0)], kind='pair',
                               epi=epi_glu(lambda u, c0=c0: c0 + u * 128)))
        gb = ub + 2 * CC
        for c0 in range(0, 2 * D, 512):
            blocks.append(dict(pieces=[(w, gb + c0, 512, 0)], kind='single',
                               epi=epi_gates(lambda u, c0=c0: c0 + u * 256)))
        self.linear([(self.HT, cfg.DC)], TS, blocks)
        self.phase_end()

    def merge_phase(self, l):
        cfg, P, A = self.cfg, self.P, self.A
        D = cfg.D
        self.phase_begin()
        A = self.A
        TS = min(1024, cfg.T)
        stg = self.stager("st", [128, 1, 1024], BF16)
        gbuf = [[A.get([128, 1024], BF16) for _ in range(2)] for _ in range(2)]
        gds = [P.ds("g0"), P.ds("g1")]
        gfree = [None, None]
        t1 = [A.get([128, 1024], F32) for _ in range(2)]
        t2 = [A.get([128, 1024], F32) for _ in range(2)]
        tfree = [None, None]
        cnt = [0]
        slot_of = {}

        def pre(row0):
            def f(u, tok0):
                s_ = cnt[0] % 2
                cnt[0] += 1
                r = row0 + u * 128
                tk = None
                for gi in range(2):
                    tk = P.dma('sp', lambda e, gi=gi, r=r, s_=s_: e.dma_start(
                        out=gbuf[s_][gi], in_=self.GT[gi * D + r:gi * D + r + 128, tok0:tok0 + 1024]),
                        gds[s_], [gfree[s_]])
                return (s_, tk)
            return f

        def epi(row0):
            def f(u, tok0, bk, lastpe, pretoks):
                s_, gtok = pretoks
                b = stg['i']
                buf = stg['bufs'][b]
                r = row0 + u * 128
                col = cfg.c_bcproj + r // 128
                rel = [None] * 4
                evs = []
                for t in range(2):
                    sl = slice(t * 512, (t + 1) * 512)
                    ta = P.op('dve', lambda e, t=t, sl=sl, bk=bk, s_=s_: e.tensor_tensor(
                        out=t1[s_][:, sl], in0=bk[t], in1=gbuf[s_][0][:, sl], op=ALU.mult),
                        [lastpe, gtok, tfree[s_]])
                    tb = P.op('dve', lambda e, t=t, sl=sl, bk=bk, s_=s_, col=col: e.scalar_tensor_tensor(
                        out=t2[s_][:, sl], in0=bk[2 + t], scalar=self.cl[:, col:col + 1], in1=gbuf[s_][1][:, sl],
                        op0=ALU.add, op1=ALU.mult), [lastpe, gtok, tfree[s_]])
                    tc = P.op('pool', lambda e, sl=sl, buf=buf, s_=s_: e.tensor_tensor(
                        out=buf[:, 0, sl], in0=t1[s_][:, sl], in1=t2[s_][:, sl], op=ALU.add),
                        [ta, tb, stg['tok'][b]])
                    rel[t] = ta
                    rel[2 + t] = tb
                    evs.append(tc)
                tfree[s_] = evs[-1]
                gfree[s_] = rel[3]
                stg['tok'][b] = P.dma('sp', lambda e, buf=buf, r=r: e.dma_start(
                    out=self.MT[r:r + 128, tok0:tok0 + 1024], in_=buf[:, 0, :]), stg['ds'][b], evs)
                stg['i'] = (b + 1) % 2
                return rel
            return f

        blocks = []
        for c0 in range(0, D, 256):
            blocks.append(dict(pieces=[(self.w_ap[l], c0, 256, 0), (self.w_cp[l], c0, 256, 1)], kind='pair',
                               pre=pre(c0), epi=epi(c0)))
        self.linear([(self.ONT, cfg.VW // 128), (self.CT, cfg.CCc)], TS, blocks)
        self.phase_end()

    def resid_phase(self, l, w2d, src, KC, xsrc, TS):
        cfg, P, A = self.cfg, self.P, self.A
        D = cfg.D
        self.phase_begin()
        A = self.A
        xold = [[A.get([128, 1024], F32) for _ in range(2)] for _ in range(2)]
        xds = [P.ds("g0"), P.ds("g1")]
        xfree = [None, None]
        xnew = self.stager("st", [128, 2, 1024], F32)
        tmp = [A.get([128, 1024], F32) for _ in range(2)]
        tmpfree = [None, None]
        cnt = [0]

        def pre(row0):
            def f(u, tok0):
                s_ = cnt[0] % 2
                cnt[0] += 1
                tk = None
                for ci in range(2):
                    r = row0 + u * 256 + ci * 128
                    tk = P.dma('sp', lambda e, ci=ci, r=r, s_=s_: e.dma_start(
                        out=xold[s_][ci], in_=xsrc[r:r + 128, tok0:tok0 + 1024]), xds[s_], [xfree[s_]])
                return (s_, tk)
            return f

        def epi(row0):
            def f(u, tok0, bk, lastpe, pretoks):
                s_, xtok = pretoks
                b = xnew['i']
                buf = xnew['bufs'][b]
                rel = [None] * 4
                evs = []
                for t in range(2):
                    sl = slice(t * 512, (t + 1) * 512)
                    ta = P.op('dve', lambda e, t=t, sl=sl, bk=bk, buf=buf, s_=s_: e.tensor_tensor(
                        out=buf[:, 0, sl], in0=bk[t], in1=xold[s_][0][:, sl], op=ALU.add),
                        [lastpe, xtok, xnew['tok'][b]])
                    tb = P.op('act', lambda e, t=t, sl=sl, bk=bk, s_=s_: e.activation(
                        out=tmp[s_][:, sl], in_=bk[2 + t], func=AF.Copy), [lastpe, tmpfree[s_]])
                    tc = P.op('pool', lambda e, sl=sl, buf=buf, s_=s_: e.tensor_tensor(
                        out=buf[:, 1, sl], in0=tmp[s_][:, sl], in1=xold[s_][1][:, sl], op=ALU.add),
                        [tb, xtok, xnew['tok'][b]])
                    rel[t] = ta
                    rel[2 + t] = tb
                    evs += [ta, tc]
                tmpfree[s_] = evs[-1]
                xfree[s_] = [evs[-1], evs[-2]]
                t_ = None
                for ci in range(2):
                    r = row0 + u * 256 + ci * 128
                    t_ = P.dma('sp', lambda e, ci=ci, r=r, buf=buf: e.dma_start(
                        out=self.XT[r:r + 128, tok0:tok0 + 1024], in_=buf[:, ci, :]), xnew['ds'][b], evs)
                xnew['tok'][b] = t_
                xnew['i'] = (b + 1) % 2
                return rel
            return f

        blocks = []
        FB = 256 if KC > 16 else min(512, D)
        for c0 in range(0, D, FB):
            blocks.append(dict(pieces=[(w2d, c0, FB, 0)], kind='single', pre=pre(c0), epi=epi(c0)))
        self._flatten_fix = True
        self.linear([(src, KC)], TS, blocks)
        self.phase_end()

    def ffnin_phase(self, l):
        cfg, P, A = self.cfg, self.P, self.A
        DFF = cfg.DFF
        self.phase_begin()
        A = self.A
        TS = min(2048, cfg.T)
        stg = self.stager("st", [128, 1, 1024], BF16)
        sil = [A.get([128, 1024], F32) for _ in range(2)]
        silfree = [None, None]
        cnt = [0]
        w = self.w_fi[l]

        def epi(row0):
            def f(u, tok0, bk, lastpe, pre):
                b = stg['i']
                buf = stg['bufs'][b]
                s_ = cnt[0] % 2
                cnt[0] += 1
                rel = [None] * 4
                evs = []
                for t in range(2):
                    sl = slice(t * 512, (t + 1) * 512)
                    ta = P.op('act', lambda e, t=t, sl=sl, bk=bk, s_=s_: e.activation(
                        out=sil[s_][:, sl], in_=bk[t], func=AF.Silu), [lastpe, silfree[s_]])
                    td = P.op('dve', lambda e, t=t, sl=sl, bk=bk, buf=buf, s_=s_: e.tensor_tensor(
                        out=buf[:, 0, sl], in0=bk[2 + t], in1=sil[s_][:, sl], op=ALU.mult), [ta, stg['tok'][b]])
                    rel[t] = ta
                    rel[2 + t] = td
                    evs.append(td)
                silfree[s_] = evs[-1]
                r = row0 + u * 128
                stg['tok'][b] = P.dma('sp', lambda e, buf=buf, r=r: e.dma_start(
                    out=self.AT[r:r + 128, tok0:tok0 + 1024], in_=buf[:, 0, :]), stg['ds'][b], evs)
                stg['i'] = (b + 1) % 2
                return rel
            return f

        blocks = []
        for c0 in range(0, DFF, 256):
            blocks.append(dict(pieces=[(w, c0, 256, 0), (w, DFF + c0, 256, 0)], kind='pair', epi=epi(c0)))
        self.linear([(self.HT, cfg.DC)], TS, blocks)
        self.phase_end()

    def attn_phase(self, l):
        cfg, P, A = self.cfg, self.P, self.A
        T, H, HD, VD, NK, NQ = cfg.T, cfg.H, cfg.HD, cfg.VD, cfg.NK, cfg.NQ
        self.phase_begin()
        A = self.A
        banks = self.banks
        scale = HD ** -0.5
        atab = A.get([128, cfg.AW], F32)
        qb = [A.get([128, 2, T], BF16) for _ in range(2)]
        kb = [A.get([128, 2, T], BF16) for _ in range(2)]
        vb = [A.get([128, NK, VD + 1], BF16) for _ in range(2)]
        NSB = 3
        NPT = 4
        sbanks = [4, 5, 7]
        tmp = [A.get([128, 512], F32) for _ in range(NSB)]
        pt = [A.get([128, 512], BF16) for _ in range(NPT)]
        Om = [A.get([128, 4, VD], F32) for _ in range(2)]
        o = A.get([128, 4, VD], F32)
        junk = A.get([128, VD], F32)
        on = A.get([128, 4, VD], BF16)
        ost = [A.get([128, VD // 128, 512], BF16) for _ in range(2)]
        rec = A.get([128, 8], F32)
        ssq = A.get([128, 4], F32)
        rs2 = A.get([128, 4], F32)
        tb = banks[6].bitcast(BF16)
        ta = P.dma('sp', lambda e: e.dma_start(out=atab, in_=self.atab_in), P.ds("c0"))
        ones_tok = []
        for i in range(2):
            ones_tok.append(P.op('pool', lambda e, i=i: e.memset(vb[i][:, :, VD:VD + 1], 1.0)))
        dsl = [P.ds("in0"), P.ds("in1")]
        dso = [P.ds("st0"), P.ds("st1")]
        loadtok = [None, None]
        headdone = [None, None]

        def load_head(h):
            b = h % 2
            w_ = [headdone[b], ones_tok[b]]
            P.dma('sp', lambda e: e.dma_start(out=qb[b], in_=fm(self.QKT)[:, 2 * h:2 * h + 2, :]), dsl[b], w_)
            P.dma('sp', lambda e: e.dma_start(out=kb[b], in_=fm(self.QKT)[:, cfg.QK // 128 + 2 * h:cfg.QK // 128 + 2 * h + 2, :]),
                  dsl[b], w_)
            loadtok[b] = P.dma('sp', lambda e: e.dma_start(
                out=vb[b][:, :, 0:VD], in_=self.V.rearrange("(k p) f -> p k f", p=128)[:, :, h * VD:(h + 1) * VD]),
                dsl[b], w_)

        load_head(0)
        st = {'n': 0, 'sdve': [None] * NSB, 'tact': [None] * NSB, 'ptpe': [None] * NPT,
              'accfree': [None] * 4, 'omfree': [None, None], 'ofree': None, 'onfree': None,
              'ostfree': [None, None], 'tbfree': None, 'osti': 0, 'ssqfree': None}
        deferred = []

        def run_head(h):
            hb = h % 2
            if h + 1 < H:
                load_head(h + 1)
            slope = 2.0 ** (-8.0 * (h + 1) / H)
            dmin = 80.0 / slope
            kts_of = {}
            for qc in range(NQ):
                keep = []
                for kt in range(NK):
                    dist = max(0, kt * 128 - (qc * 512 + 511), qc * 512 - (kt * 128 + 127))
                    if dist < dmin:
                        keep.append(kt)
                kts_of[qc] = keep
            steps = [(qc, m, kt) for qc in range(NQ) for m in range(2) for kt in kts_of[qc]]
            S_tok = {}
            P_tok = {}

            def emit_S(i):
                qc, m, kt = steps[i]
                n = st['n'] + i
                sbk = sbanks[n % NSB]
                s_tok = P.op('pe', lambda e: e.matmul(
                    banks[sbk], kb[hb][:, m, kt * 128:(kt + 1) * 128], qb[hb][:, m, qc * 512:(qc + 1) * 512],
                    start=True, stop=True), [loadtok[hb], st['sdve'][n % NSB]])
                u0 = 512 * qc - 128 * kt + cfg.AOFF
                t_tok = P.op('dve', lambda e: e.scalar_tensor_tensor(
                    out=tmp[n % NSB], in0=atab[:, u0:u0 + 512], scalar=slope / scale, in1=banks[sbk],
                    op0=ALU.mult, op1=ALU.add), [s_tok, st['tact'][n % NSB], ta])
                st['sdve'][n % NSB] = t_tok
                kcol = cfg.g_kmask + kt
                p_tok = P.op('act', lambda e: e.activation(
                    out=pt[n % NPT], in_=tmp[n % NSB], func=AF.Exp, bias=self.cg[:, kcol:kcol + 1], scale=scale),
                    [t_tok, st['ptpe'][n % NPT]])
                st['tact'][n % NSB] = p_tok
                P_tok[i] = p_tok

            def emit_PV(i):
                qc, m, kt = steps[i]
                n = st['n'] + i
                last = None
                for qs in range(4):
                    w_ = [P_tok[i]]
                    first = (kt == kts_of[qc][0])
                    lastk = (kt == kts_of[qc][-1])
                    if first:
                        w_.append(st['accfree'][qs])
                    last = P.op('pe', lambda e, qs=qs, first=first, lastk=lastk: e.matmul(
                        banks[qs][:, 0:VD + 1], pt[n % NPT][:, qs * 128:(qs + 1) * 128], vb[hb][:, kt, :],
                        start=first, stop=lastk), w_)
                st['ptpe'][n % NPT] = last
                if kt == kts_of[qc][-1]:
                    finish_group(qc, m, last)
                return last

            def finish_group(qc, m, lastpe):
                r_tok = []
                for qs in range(4):
                    c_ = P.op('dve', lambda e, qs=qs: e.tensor_scalar(
                        out=rec[:, m * 4 + qs:m * 4 + qs + 1], in0=banks[qs][:, VD:VD + 1], scalar1=1e-30,
                        scalar2=None, op0=ALU.max), [lastpe, st['omfree'][m]])
                    r_tok.append(P.op('dve', lambda e, qs=qs: e.reciprocal(
                        out=rec[:, m * 4 + qs:m * 4 + qs + 1], in_=rec[:, m * 4 + qs:m * 4 + qs + 1]), [c_]))
                ev = []
                for qs in range(4):
                    tk = P.op('dve', lambda e, qs=qs: e.tensor_scalar(
                        out=Om[m][:, qs, :], in0=banks[qs][:, 0:VD], scalar1=rec[:, m * 4 + qs:m * 4 + qs + 1],
                        scalar2=None, op0=ALU.mult), [r_tok[qs], st['omfree'][m]])
                    st['accfree'][qs] = tk
                    ev.append(tk)
                if m == 0:
                    st['om0'] = ev[-1]
                    return
                c_tok = P.op('dve', lambda e: e.scalar_tensor_tensor(
                    out=o, in0=Om[1], scalar=self.neglam[:, 1:2], in1=Om[0], op0=ALU.mult, op1=ALU.add),
                    [ev[-1], st['om0'], st['ofree']])
                st['omfree'] = [c_tok, c_tok]
                z_tok = P.op('pool', lambda e: e.memset(ssq, 0.0), [st['ssqfree']])
                sq_tok = None
                for qs in range(4):
                    sq_tok = P.op('act', lambda e, qs=qs: e.activation(
                        out=junk, in_=o[:, qs, :], func=AF.Square, accum_out=ssq[:, qs:qs + 1]),
                        [c_tok, z_tok, sq_tok])
                r1 = P.op('dve', lambda e: e.tensor_scalar(out=rs2, in0=ssq, scalar1=1.0 / VD, scalar2=NORM_EPS,
                                                           op0=ALU.mult, op1=ALU.add), [sq_tok, st['onfree']])
                r2a = P.op('act', lambda e: e.activation(out=rs2, in_=rs2, func=AF.Sqrt), [r1])
                r2 = P.op('dve', lambda e: e.reciprocal(out=rs2, in_=rs2), [r2a])
                st['ssqfree'] = r1
                n_tok = None
                for qs in range(4):
                    n_tok = P.op('dve', lambda e, qs=qs: e.scalar_tensor_tensor(
                        out=on[:, qs, :], in0=o[:, qs, :], scalar=rs2[:, qs:qs + 1], in1=self.gsub,
                        op0=ALU.mult, op1=ALU.mult), [r2, st['onfree']])
                st['ofree'] = n_tok

                def do_transposes(n_tok=n_tok, qc=qc, h=h):
                    tl = None
                    for vc in range(VD // 128):
                        for qs in range(4):
                            i_ = vc * 4 + qs
                            tl = P.op('pe', lambda e, vc=vc, qs=qs, i_=i_: e.transpose(
                                out=tb[:, i_ * 128:(i_ + 1) * 128], in_=on[:, qs, vc * 128:(vc + 1) * 128],
                                identity=self.ident_bf), [n_tok, st['tbfree']])
                    ob = st['osti']
                    st['osti'] = 1 - ob
                    c_ = P.op('act', lambda e: e.activation(
                        out=ost[ob].rearrange("p a b -> p (a b)"), in_=tb, func=AF.Copy), [tl, st['ostfree'][ob]])
                    st['tbfree'] = c_
                    st['onfree'] = tl
                    st['ostfree'][ob] = P.dma('sp', lambda e: e.dma_start(
                        out=fm(self.ONT)[:, h * (VD // 128):(h + 1) * (VD // 128), qc * 512:(qc + 1) * 512],
                        in_=ost[ob]), dso[ob], [c_])
                deferred.append([6, do_transposes])

            ns = len(steps)
            for i0 in range(min(NSB, ns)):
                emit_S(i0)
            lastpv = None
            for i in range(ns):
                lastpv = emit_PV(i)
                if i + NSB < ns:
                    emit_S(i + NSB)
                for d in deferred:
                    d[0] -= 1
                for d in [d for d in deferred if d[0] <= 0]:
                    d[1]()
                    deferred.remove(d)
            st['n'] += ns
            headdone[hb] = lastpv

        for h in range(H):
            run_head(h)
        for d in deferred:
            d[1]()
        self.phase_end()

    def conv1_phase(self, l):
        cfg, P, A = self.cfg, self.P, self.A
        T, CW, CCc = cfg.T, cfg.CW, cfg.CCc
        PAD = (CW - 1) // 2
        self.phase_begin()
        A = self.A
        banks = self.banks
        ypad = [A.get([128, T + 2 * PAD + 2], BF16) for _ in range(2)]
        diag = [A.get([128, CW, 128], BF16) for _ in range(2)]
        zst = [A.get([128, T], F32) for _ in range(2)]
        dsl = [P.ds("in0"), P.ds("in1")]
        dso = [P.ds("st0"), P.ds("st1")]
        halo = []
        for i in range(2):
            halo.append(P.op('pool', lambda e, i=i: e.memset(ypad[i][:, 0:PAD], 0.0)))
            halo.append(P.op('pool', lambda e, i=i: e.memset(ypad[i][:, PAD + T:PAD + T + PAD], 0.0)))
        pedone = [None, None]
        stdone = [None, None]
        bankfree = [None] * 8
        bn = 0
        ldtok = {}
        tm = A.get([128, T], BF16)
        tmtok = P.dma("pool", lambda e: e.dma_start(out=tm, in_=self.tmask_in), P.ds("pm"))

        def load(c):
            b = c % 2
            t_ = P.dma('sp', lambda e: e.dma_start(out=ypad[b][:, PAD:PAD + T], in_=self.YT[c * 128:(c + 1) * 128, :]),
                       dsl[b], [pedone[b]] + halo)
            ldtok[c] = P.op('pool' if c % 2 else 'dve', lambda e: e.tensor_tensor(
                out=ypad[b][:, PAD:PAD + T], in0=ypad[b][:, PAD:PAD + T], in1=tm, op=ALU.mult), [t_, tmtok])

        dltok = {}

        def build_diag(c):
            b = c % 2
            dl = [None, None]
            for j in range(CW):
                eng = 'dve' if j % 3 != 2 else 'pool'
                col = cfg.c_wdw + c * CW + j
                dl[0 if eng == 'dve' else 1] = P.op(eng, lambda e, j=j, col=col, b=b: e.tensor_scalar(
                    out=diag[b][:, j, :], in0=self.ident_bf, scalar1=self.cl[:, col:col + 1], scalar2=None,
                    op0=ALU.mult), [pedone[b]])
            dltok[c] = dl

        load(0)
        build_diag(0)
        for c in range(CCc):
            b = c % 2
            if c + 1 < CCc:
                load(c + 1)
                build_diag(c + 1)
            dl = dltok[c]
            evs = []
            last = None
            for tt in range(T // 512):
                bk = bn % 8
                bn += 1
                for j in range(CW):
                    w_ = [ldtok[c], dl[0], dl[1], bankfree[bk]] if j == 0 else []
                    last = P.op('pe', lambda e, j=j, tt=tt, bk=bk, b=b: e.matmul(
                        banks[bk], diag[b][:, j, :], ypad[b][:, tt * 512 + j:tt * 512 + j + 512],
                        start=(j == 0), stop=(j == CW - 1)), w_)
                bcol = cfg.c_bdw + c
                if tt % 2 == 0:
                    tk = P.op('act', lambda e, tt=tt, bk=bk, bcol=bcol, b=b: e.activation(
                        out=zst[b][:, tt * 512:(tt + 1) * 512], in_=banks[bk], func=AF.Identity,
                        bias=self.cl[:, bcol:bcol + 1]), [last, stdone[b]])
                else:
                    tk = P.op('dve', lambda e, tt=tt, bk=bk, bcol=bcol, b=b: e.tensor_scalar(
                        out=zst[b][:, tt * 512:(tt + 1) * 512], in0=banks[bk], scalar1=self.cl[:, bcol:bcol + 1],
                        scalar2=None, op0=ALU.add), [last, stdone[b]])
                bankfree[bk] = tk
                evs.append(tk)
            pedone[b] = last
            stdone[b] = P.dma('sp', lambda e, c=c, b=b: e.dma_start(out=self.ZT[c * 128:(c + 1) * 128, :], in_=zst[b]),
                              dso[b], evs)
        self.phase_end()

    def conv2_phase(self, l):
        cfg, P, A = self.cfg, self.P, self.A
        T, CCc, CC = cfg.T, cfg.CCc, cfg.CC
        self.phase_begin()
        A = self.A
        banks = self.banks
        z = [A.get([128, CCc, 512], F32) for _ in range(2)]
        zb = A.get([128, CCc, 512], BF16)
        zs = A.get([128, CCc, 512], BF16)
        mean = A.get([128, 512], F32)
        msq = A.get([128, 512], F32)
        rstd = A.get([128, 512], F32)
        t1 = [A.get([128, 512], F32) for _ in range(4)]
        outb = [A.get([128, CCc, 512], BF16) for _ in range(2)]
        dsl = [P.ds("in0"), P.ds("in1")]
        dso = [P.ds("st0"), P.ds("st1")]
        zfree = [None, None]
        stdone = [None, None]
        pedone = None
        statfree = None
        t1free = [None] * 4
        ld = {}

        def load(i):
            b = i % 2
            ld[i] = P.dma('sp', lambda e: e.dma_start(out=z[b], in_=fm(self.ZT)[:, :, i * 512:(i + 1) * 512]),
                          dsl[b], zfree[b] or [])

        load(0)
        nt = 0
        for i in range(T // 512):
            b = i % 2
            if i + 1 < T // 512:
                load(i + 1)
            a1 = P.op('act', lambda e, b=b: e.activation(out=zb, in_=z[b], func=AF.Copy), [ld[i], pedone])
            a2 = P.op('act', lambda e, b=b: e.activation(out=zs, in_=z[b], func=AF.Square), [ld[i], pedone])
            mm = None
            for c in range(CCc):
                mm = P.op('pe', lambda e, c=c: e.matmul(banks[0], self.ones_bf, zb[:, c, :], start=(c == 0),
                                                        stop=(c == CCc - 1)), [a1, statfree] if c == 0 else [])
            for c in range(CCc):
                mm = P.op('pe', lambda e, c=c: e.matmul(banks[1], self.ones_bf, zs[:, c, :], start=(c == 0),
                                                        stop=(c == CCc - 1)), [a2] if c == 0 else [])
            pedone = mm
            s1 = P.op('dve', lambda e: e.tensor_scalar(out=mean, in0=banks[0], scalar1=1.0 / CC, scalar2=None,
                                                       op0=ALU.mult), [mm] + (zfree[1 - b] or []))
            s2 = P.op('dve', lambda e: e.tensor_tensor(out=msq, in0=mean, in1=mean, op=ALU.mult), [s1])
            s3 = P.op('dve', lambda e: e.scalar_tensor_tensor(out=rstd, in0=banks[1], scalar=1.0 / CC, in1=msq,
                                                              op0=ALU.mult, op1=ALU.subtract), [s2])
            s4a = P.op('dve', lambda e: e.tensor_scalar(out=rstd, in0=rstd, scalar1=LN_EPS, scalar2=None,
                                                        op0=ALU.add), [s3])
            s4b = P.op('act', lambda e: e.activation(out=rstd, in_=rstd, func=AF.Sqrt), [s4a])
            s4 = P.op('dve', lambda e: e.reciprocal(out=rstd, in_=rstd), [s4b])
            statfree = s3
            lasts = []
            for c in range(CCc):
                eng = 'dve' if c % 3 != 2 else 'pool'
                tb_ = nt % 4
                nt += 1
                u1 = P.op(eng, lambda e, c=c, tb_=tb_, b=b: e.tensor_tensor(out=t1[tb_], in0=z[b][:, c, :], in1=mean,
                                                                       op=ALU.subtract), [s4, t1free[tb_]])
                u2 = P.op(eng, lambda e, c=c, tb_=tb_: e.tensor_tensor(out=t1[tb_], in0=t1[tb_], in1=rstd,
                                                                       op=ALU.mult), [u1])
                gcol = cfg.c_gcln + c
                bcol = cfg.c_bcln + c
                u3 = P.op('act', lambda e, c=c, tb_=tb_, gcol=gcol, bcol=bcol, b=b: e.activation(
                    out=outb[b][:, c, :], in_=t1[tb_], func=AF.Silu, bias=self.cl[:, bcol:bcol + 1],
                    scale=self.cl[:, gcol:gcol + 1]), [u2, stdone[b]])
                t1free[tb_] = u3
                lasts.append(u3)
            zfree[b] = [lasts[-1]]
            stdone[b] = P.dma('sp', lambda e, i=i, b=b: e.dma_start(out=fm(self.CT)[:, :, i * 512:(i + 1) * 512], in_=outb[b]),
                              dso[b], [lasts[-1]])
        self.phase_end()


def _consts(cfg, inp):
    DEPTH = cfg.DEPTH
    cl = np.zeros((DEPTH, 128, cfg.NCL), np.float32)

    def pc(v):
        return np.ascontiguousarray(v.reshape(-1, 128).T)

    for l in range(DEPTH):
        c = cl[l]
        c[:, cfg.c_gmix:cfg.c_gmix + cfg.DC] = pc(inp["g_mix"][l])
        c[:, cfg.c_bgate:cfg.c_bgate + 2 * cfg.DC] = pc(inp["b_gate"][l])
        c[:, cfg.c_bdw:cfg.c_bdw + cfg.CCc] = pc(inp["b_dw"][l])
        c[:, cfg.c_gcln:cfg.c_gcln + cfg.CCc] = pc(inp["g_conv_ln"][l])
        c[:, cfg.c_bcln:cfg.c_bcln + cfg.CCc] = pc(inp["b_conv_ln"][l])
        c[:, cfg.c_bcproj:cfg.c_bcproj + cfg.DC] = pc(inp["b_conv_proj"][l])
        c[:, cfg.c_gffn:cfg.c_gffn + cfg.DC] = pc(inp["g_ffn"][l])
        wd = inp["w_dw"][l][:, 0, :]
        wd = wd.T.reshape(cfg.CCc, 128, cfg.CW).transpose(1, 0, 2).reshape(128, cfg.CCc * cfg.CW)
        c[:, cfg.c_wdw:cfg.c_wdw + cfg.CCc * cfg.CW] = wd
        c[:, cfg.c_gsub:cfg.c_gsub + cfg.VD] = inp["g_subln"][l][None, :]
        c[:, cfg.c_lq:cfg.c_lq + 2 * cfg.HD] = inp["lambda_q"][l].reshape(1, -1)
        c[:, cfg.c_lk:cfg.c_lk + 2 * cfg.HD] = inp["lambda_k"][l].reshape(1, -1)
    return cl


def _globals(cfg, g_final, nvalid):
    cg = np.zeros((128, cfg.NCG), np.float32)
    cg[:, cfg.g_gfinal:cfg.g_gfinal + cfg.DC] = g_final.reshape(-1, 128).T
    cg[:, cfg.g_ident:cfg.g_ident + 128] = np.eye(128, dtype=np.float32)
    kpos = np.arange(cfg.T).reshape(cfg.NK, 128).T
    cg[:, cfg.g_kmask:cfg.g_kmask + cfg.NK] = np.where(kpos < nvalid, 0.0, MASK_NEG)
    return cg


def _atab(cfg):
    p = np.arange(128)[:, None]
    u = np.arange(cfg.AW)[None, :]
    return (-np.abs(u - cfg.AOFF - p)).astype(np.float32)


_NC_CACHE = {}


def run_trunk(cfg, seqs, inp, n_cores=8):
    key = (cfg.T, cfg.D, cfg.H, cfg.DFF, cfg.DEPTH)
    if key not in _NC_CACHE:
        _NC_CACHE[key] = Builder(cfg).build()
    nc = _NC_CACHE[key]
    cl = _consts(cfg, inp)
    atab = _atab(cfg)
    wmap = {k: np.ascontiguousarray(inp[k], dtype=np.float32) for k in
            ("w_in", "w_attn_proj", "w_conv_proj", "w_out", "w_ffn_in", "w_ffn_out")}
    in_maps = []
    if n_cores == 8 and len(seqs) == 6:
        core_of_seq = [0, 4, 1, 2, 5, 6]
    else:
        core_of_seq = list(range(len(seqs)))
    seq_of_core = {c: i for i, c in enumerate(core_of_seq)}
    for c in range(n_cores):
        xT = np.zeros((cfg.D, cfg.T), np.float32)
        if c in seq_of_core:
            s = seqs[seq_of_core[c]]
            S = s.shape[0]
            xT[:, :S] = s.T
        else:
            S = cfg.T
        tmask = np.zeros((128, cfg.T), np.float32)
        tmask[:, :S] = 1.0
        m = {"xT": xT, "cl": cl, "cg": _globals(cfg, inp["g_final"], S), "atab": atab, "tmask": tmask}
        m.update(wmap)
        in_maps.append(m)
    res = run_bass_kernel_spmd(nc, in_maps, core_ids=list(range(n_cores)))
    outs = []
    for i in range(len(seqs)):
        S = seqs[i].shape[0]
        outs.append(np.ascontiguousarray(res.results[core_of_seq[i]]["yT"][:, :S].T))
    return outs


def kernel(x_prompt, x_sample, g_mix, w_in, b_gate, lambda_q, lambda_k, g_subln,
           w_attn_proj, w_dw, b_dw, g_conv_ln, b_conv_ln, w_conv_proj, b_conv_proj,
           w_out, g_ffn, w_ffn_in, w_ffn_out, g_final):
    inp = dict(g_mix=g_mix, w_in=w_in, b_gate=b_gate, lambda_q=lambda_q, lambda_k=lambda_k, g_subln=g_subln,
               w_attn_proj=w_attn_proj, w_dw=w_dw, b_dw=b_dw, g_conv_ln=g_conv_ln, b_conv_ln=b_conv_ln,
               w_conv_proj=w_conv_proj, b_conv_proj=b_conv_proj, w_out=w_out, g_ffn=g_ffn,
               w_ffn_in=w_ffn_in, w_ffn_out=w_ffn_out, g_final=g_final)
    inp = {k: np.asarray(v, dtype=np.float32) for k, v in inp.items()}
    x_prompt = np.asarray(x_prompt, dtype=np.float32)
    x_sample = np.asarray(x_sample, dtype=np.float32)
    cfg = Cfg()
    seqs = [x_prompt[b] for b in range(x_prompt.shape[0])] + [x_sample[b] for b in range(x_sample.shape[0])]
    outs = run_trunk(cfg, seqs, inp)
    nb = x_prompt.shape[0]
    y_prompt = np.stack(outs[:nb]).astype(np.float32)
    y_sample = np.stack(outs[nb:]).astype(np.float32)
    return (y_prompt, y_sample)
```

```python
import numpy as np
from contextlib import ExitStack
import concourse.bass as bass
import concourse.mybir as mybir
from concourse.bass_utils import run_bass_kernel_spmd

F32 = mybir.dt.float32
BF16 = mybir.dt.bfloat16
U8 = mybir.dt.uint8
AF = mybir.ActivationFunctionType
ALU = mybir.AluOpType

NORM_EPS = 1e-6
LN_EPS = 1e-5
MASK_NEG = -30000.0
SB_BYTES = 192 * 1024


class Cfg:
    def __init__(self, T=4096, D=2048, H=8, DFF=5632, DEPTH=4, CW=31):
        self.T, self.D, self.H, self.DFF, self.DEPTH, self.CW = T, D, H, DFF, DEPTH, CW
        self.HD = 128
        self.VD = 256
        self.QK = H * 2 * self.HD
        self.VW = H * self.VD
        self.CC = D
        self.IN_W = 2 * self.QK + self.VW + 2 * self.CC + 2 * D
        self.DC = D // 128
        self.CCc = self.CC // 128
        self.NK = T // 128
        self.NQ = T // 512
        self.AOFF = T - 128
        self.AW = (self.NQ - 1) * 512 + 512 + self.AOFF
        c = 0
        self.c_gmix = c; c += self.DC
        self.c_bgate = c; c += 2 * self.DC
        self.c_bdw = c; c += self.CCc
        self.c_gcln = c; c += self.CCc
        self.c_bcln = c; c += self.CCc
        self.c_bcproj = c; c += self.DC
        self.c_gffn = c; c += self.DC
        self.c_wdw = c; c += self.CCc * CW
        self.c_gsub = c; c += self.VD
        self.c_lq = c; c += 2 * self.HD
        self.c_lk = c; c += 2 * self.HD
        self.NCL = c
        g = 0
        self.g_gfinal = g; g += self.DC
        self.g_ident = g; g += 128
        self.g_kmask = g; g += self.NK
        self.NCG = g


def lambda_init(layer):
    return 0.8 - 0.6 * float(np.exp(-0.3 * layer))


class DmaSem:
    def __init__(self, sem):
        self.sem = sem
        self.count = 0


class Prog:
    ENG = ('pe', 'act', 'dve', 'pool', 'sp')

    def __init__(self, nc, stack):
        self.nc = nc
        self.stack = stack
        self.streams = {k: [] for k in self.ENG}
        self.cnt = {k: 0 for k in self.ENG}
        self.esem = {k: stack.enter_context(nc.semaphore("s_" + k)) for k in self.ENG}
        self.dsems = {}
        self.pending = {k: [] for k in self.ENG}
        self.waited = {k: {} for k in self.ENG}

    def ds(self, name):
        if name not in self.dsems:
            self.dsems[name] = DmaSem(self.stack.enter_context(self.nc.semaphore("d_" + name)))
        return self.dsems[name]

    def _w(self, eng, waits):
        w = []

        def fl(x):
            if x is None:
                return
            if isinstance(x, list):
                for y in x:
                    fl(y)
            else:
                w.append(x)
        fl(list(waits))
        if self.pending[eng]:
            w = self.pending[eng] + w
            self.pending[eng] = []
        return w

    def op(self, eng, fn, waits=()):
        self.cnt[eng] += 1
        self.streams[eng].append((fn, self._w(eng, waits), None))
        return (eng, self.cnt[eng])

    def dma(self, eng, fn, ds, waits=()):
        ds.count += 16
        self.streams[eng].append((fn, self._w(eng, waits), ds))
        return (ds, ds.count)

    def last(self, eng):
        return (eng, self.cnt[eng]) if self.cnt[eng] else None

    def barrier(self):
        toks = [(k, self.cnt[k]) for k in self.ENG if self.cnt[k]]
        toks += [(d, d.count) for d in self.dsems.values() if d.count]
        for k in self.ENG:
            self.pending[k] = list(toks)

    def finish(self):
        self.barrier()
        for k in self.ENG:
            w = self._w(k, [])
            self.streams[k].append((None, w, None))

    def flush(self):
        nc = self.nc
        P = self
        with nc.Block() as block:
            @block.tensor
            def _(e):
                P.replay('pe', e)

            @block.scalar
            def _(e):
                P.replay('act', e)

            @block.vector
            def _(e):
                P.replay('dve', e)

            @block.gpsimd
            def _(e):
                P.replay('pool', e)

            @block.sync
            def _(e):
                P.replay('sp', e)
        for k in self.ENG:
            self.streams[k] = []

    def replay(self, eng, e):
        waited = self.waited[eng]
        for fn, waits, ds in self.streams[eng]:
            for key, val in waits:
                kk = key if isinstance(key, str) else id(key)
                if waited.get(kk, 0) >= val:
                    continue
                waited[kk] = val
                sem = self.esem[key] if isinstance(key, str) else key.sem
                e.wait_ge(sem, val)
            if fn is None:
                continue
            inst = fn(e)
            if ds is None:
                inst.then_inc(self.esem[eng], 1)
            else:
                inst.then_inc(ds.sem, 16)


class Alloc:
    def __init__(self, sb, base, limit):
        self.sb, self.base, self.off, self.limit = sb, base, base, limit

    def reset(self):
        self.off = self.base

    def get(self, shape, dt):
        esz = 4 if dt == F32 else 2
        n = int(np.prod(shape[1:]))
        nb = (n * esz + 63) // 64 * 64
        o = self.off
        self.off += nb
        assert self.off <= self.limit, ("SBUF overflow", self.off, self.limit)
        v = self.sb[:, o:o + n * esz].bitcast(dt)
        if len(shape) == 3:
            v = v.rearrange("p (a b) -> p a b", b=shape[2])
        return v


def fm(ap):
    return ap.rearrange("(c p) t -> p c t", p=128)


class Builder:
    def __init__(self, cfg):
        self.cfg = cfg

    def build(self):
        cfg = self.cfg
        T, D, DEPTH = cfg.T, cfg.D, cfg.DEPTH
        nc = bass.Bass("TRN2", target_bir_lowering=False)
        self.nc = nc
        dt_in = lambda name, shape: nc.dram_tensor(name, shape, F32, kind="ExternalInput").ap()
        self.xT_in = dt_in("xT", [D, T])
        self.cl_in = dt_in("cl", [DEPTH, 128, cfg.NCL])
        self.cg_in = dt_in("cg", [128, cfg.NCG])
        self.atab_in = dt_in("atab", [128, cfg.AW])
        self.tmask_in = dt_in("tmask", [128, T])
        self.w_in = dt_in("w_in", [DEPTH, D, cfg.IN_W])
        self.w_ap = dt_in("w_attn_proj", [DEPTH, cfg.VW, D])
        self.w_cp = dt_in("w_conv_proj", [DEPTH, cfg.CC, D])
        self.w_out = dt_in("w_out", [DEPTH, D, D])
        self.w_fi = dt_in("w_ffn_in", [DEPTH, D, 2 * cfg.DFF])
        self.w_fo = dt_in("w_ffn_out", [DEPTH, cfg.DFF, D])
        self.yT_out = nc.dram_tensor("yT", [D, T], F32, kind="ExternalOutput").ap()
        scr = lambda name, shape, dt: nc.dram_tensor(name, shape, dt).ap()
        self.XT = scr("s_XT", [D, T], F32)
        self.HT = scr("s_HT", [D, T], BF16)
        self.QKT = scr("s_QKT", [2 * cfg.QK, T], BF16)
        self.V = scr("s_V", [T, cfg.VW], BF16)
        self.YT = scr("s_YT", [cfg.CC, T], BF16)
        self.GT = scr("s_GT", [2 * D, T], BF16)
        self.ONT = scr("s_ONT", [cfg.VW, T], BF16)
        self.ZT = scr("s_ZT", [cfg.CC, T], F32)
        self.CT = scr("s_CT", [cfg.CC, T], BF16)
        self.MT = scr("s_MT", [D, T], BF16)
        self.AT = scr("s_AT", [cfg.DFF, T], BF16)

        with ExitStack() as stack:
            P = Prog(nc, stack)
            self.P = P
            GB = 16 * 1024
            gsb = stack.enter_context(nc.sbuf_tensor("gsb", [128, GB], U8))
            G = Alloc(gsb, 0, GB)
            self.cg = G.get([128, cfg.NCG], F32)
            self.cl = G.get([128, cfg.NCL], F32)
            self.ident_bf = G.get([128, 128], BF16)
            self.ones_bf = G.get([128, 128], BF16)
            self.neglam = G.get([128, 2], F32)
            self.gsub = G.get([128, cfg.VD], F32)
            self.lamtmp = G.get([128, 2 * cfg.HD], F32)
            self.lamred = G.get([128, 4], F32)
            self.PH_BYTES = SB_BYTES - GB
            self.phase_no = 0
            self.ph = None
            self.A = None
            self.banks = None
            t0 = P.dma('sp', lambda e: e.dma_start(out=self.cg, in_=self.cg_in), P.ds("c0"))
            t1 = P.op('dve', lambda e: e.tensor_copy(out=self.ident_bf, in_=self.cg[:, cfg.g_ident:cfg.g_ident + 128]), [t0])
            t2 = P.op('pool', lambda e: e.memset(self.ones_bf, 1.0))
            P.barrier()

            xsrc = self.xT_in
            for l in range(DEPTH):
                self.layer_consts(l)
                self.norm_phase(xsrc, cfg.c_gmix, self.HT, final=False)
                self.inproj_phase(l)
                self.attn_phase(l)
                self.conv1_phase(l)
                self.conv2_phase(l)
                self.merge_phase(l)
                self.resid_phase(l, self.w_out[l], self.MT, cfg.D // 128, xsrc, TS=min(2048, T))
                xsrc = self.XT
                self.norm_phase(xsrc, cfg.c_gffn, self.HT, final=False)
                self.ffnin_phase(l)
                self.resid_phase(l, self.w_fo[l], self.AT, cfg.DFF // 128, xsrc, TS=min(1024, T))
            self.norm_phase(xsrc, None, self.yT_out, final=True)
            P.finish()
            P.flush()
        return nc

    def phase_begin(self):
        nc = self.nc
        self.P.barrier()
        self.ph = ExitStack()
        self.phase_no += 1
        sbp = self.ph.enter_context(nc.sbuf_tensor("sb%d" % self.phase_no, [128, self.PH_BYTES], U8))
        psp = self.ph.enter_context(nc.psum_tensor("ps%d" % self.phase_no, [128, 8 * 512], F32))
        self.banks = [psp[:, b * 512:(b + 1) * 512] for b in range(8)]
        self.A = Alloc(sbp, 0, self.PH_BYTES)

    def phase_end(self):
        self.P.barrier()
        self.P.flush()
        self.ph.close()
        self.ph = None

    def layer_consts(self, l):
        cfg, P = self.cfg, self.P
        HD = cfg.HD
        P.barrier()
        t0 = P.dma('sp', lambda e: e.dma_start(out=self.cl, in_=self.cl_in[l]), P.ds("c0"))
        lq = self.cl[:, cfg.c_lq:cfg.c_lq + 2 * HD]
        lk = self.cl[:, cfg.c_lk:cfg.c_lk + 2 * HD]
        t1 = P.op('dve', lambda e: e.tensor_tensor(out=self.lamtmp, in0=lq, in1=lk, op=ALU.mult), [t0])
        t2 = P.op('dve', lambda e: e.tensor_reduce(
            out=self.lamred[:, 0:2], in_=self.lamtmp.rearrange("p (a b) -> p a b", b=HD),
            axis=mybir.AxisListType.X, op=ALU.add), [t1])
        t3 = P.op('act', lambda e: e.activation(out=self.lamred[:, 2:4], in_=self.lamred[:, 0:2], func=AF.Exp), [t2])
        li = lambda_init(l)
        t4 = P.op('dve', lambda e: e.tensor_tensor(out=self.neglam[:, 0:1], in0=self.lamred[:, 3:4],
                                                   in1=self.lamred[:, 2:3], op=ALU.subtract), [t3])
        t5 = P.op('dve', lambda e: e.tensor_scalar(out=self.neglam[:, 1:2], in0=self.neglam[:, 0:1],
                                                   scalar1=-li, scalar2=None, op0=ALU.add), [t4])
        t6 = P.op('dve', lambda e: e.tensor_scalar(out=self.gsub, in0=self.cl[:, cfg.c_gsub:cfg.c_gsub + cfg.VD],
                                                   scalar1=(1.0 - li), scalar2=None, op0=ALU.mult), [t5])
        P.barrier()

    def norm_phase(self, src, gcol, dst, final):
        cfg, P, A = self.cfg, self.P, self.A
        DC, D = cfg.DC, cfg.D
        self.phase_begin()
        A = self.A
        odt = F32 if final else BF16
        xin = [A.get([128, DC, 512], F32) for _ in range(2)]
        sq = [A.get([128, DC, 512], BF16) for _ in range(2)]
        hout = [A.get([128, DC, 512], odt) for _ in range(2)]
        rstd = [A.get([128, 512], F32) for _ in range(2)]
        gsrc = self.cg if final else self.cl
        gc = cfg.g_gfinal if final else gcol
        dsi = [P.ds("in0"), P.ds("in1")]
        dso = [P.ds("st0"), P.ds("st1")]
        hdone = [None, None]
        mmdone = [None, None]
        stdone = [None, None]
        r2done = [None, None]
        for i in range(cfg.NQ):
            b = i % 2
            ts = slice(i * 512, (i + 1) * 512)
            ld = P.dma('sp', lambda e, b=b, ts=ts: e.dma_start(out=xin[b], in_=fm(src)[:, :, ts]), dsi[b],
                       (hdone[b] or []))
            sqt = P.op('act', lambda e, b=b: e.activation(out=sq[b], in_=xin[b], func=AF.Square), [ld, mmdone[b]])
            mm = None
            for c in range(DC):
                mm = P.op('pe', lambda e, b=b, c=c: e.matmul(self.banks[b], self.ones_bf, sq[b][:, c, :],
                                                             start=(c == 0), stop=(c == DC - 1)),
                          [sqt, r2done[b]] if c == 0 else [])
            mmdone[b] = mm
            r1 = P.op('dve', lambda e, b=b: e.tensor_scalar(out=rstd[b], in0=self.banks[b], scalar1=1.0 / D,
                                                            scalar2=NORM_EPS, op0=ALU.mult, op1=ALU.add),
                      [mm] + (hdone[b] or []))
            r2a = P.op('act', lambda e, b=b: e.activation(out=rstd[b], in_=rstd[b], func=AF.Sqrt), [r1])
            r2 = P.op('dve', lambda e, b=b: e.reciprocal(out=rstd[b], in_=rstd[b]), [r2a])
            r2done[b] = r1
            lastd = lastp = None
            for c in range(DC):
                eng = 'dve'
                tk = P.op(eng, lambda e, b=b, c=c: e.scalar_tensor_tensor(
                    out=hout[b][:, c, :], in0=xin[b][:, c, :], scalar=gsrc[:, gc + c:gc + c + 1],
                    in1=rstd[b], op0=ALU.mult, op1=ALU.mult), [r2, stdone[b]])
                if eng == 'dve':
                    lastd = tk
                else:
                    lastp = tk
            hdone[b] = [lastd, lastp]
            stdone[b] = P.dma('sp', lambda e, b=b, ts=ts: e.dma_start(out=fm(dst)[:, :, ts], in_=hout[b]), dso[b],
                              [lastd, lastp])
        self.phase_end()

    def linear(self, ins, TS, blocks):
        cfg, P, A = self.cfg, self.P, self.A
        T = cfg.T
        NS = T // TS
        NHALF = TS // 1024
        KCmax = max(kc for _, kc in ins)
        FBmax = max(sum(p[2] for p in b['pieces']) for b in blocks)
        in_sb = [A.get([128, kc, TS], BF16) for _, kc in ins]
        wbuf = [A.get([128, KCmax, FBmax], BF16) for _ in range(2)]
        dsw = [P.ds("w0"), P.ds("w1")]
        dsin = [P.ds("in0"), P.ds("in1")]
        wfree = [None, None]
        bankfree = [[None] * 4, [None] * 4]
        seq = [(s, bi) for s in range(NS) for bi in range(len(blocks))]
        wtok = {}

        def issue_w(n):
            s, bi = seq[n]
            wb = n % 2
            c0 = 0
            tk = None
            for (w2d, col0, ncols, in_idx) in blocks[bi]['pieces']:
                kc = ins[in_idx][1]
                tk = P.dma('pool', lambda e, wb=wb, c0=c0, w2d=w2d, col0=col0, ncols=ncols, kc=kc: e.dma_start(
                    out=wbuf[wb][:, 0:kc, c0:c0 + ncols],
                    in_=w2d.rearrange("(c p) f -> p c f", p=128)[:, :, col0:col0 + ncols]), dsw[wb], [wfree[wb]])
                c0 += ncols
            wtok[n] = tk

        issue_w(0)
        ucount = 0
        intok = [None] * len(ins)
        lastpe_all = None
        for n, (s, bi) in enumerate(seq):
            blk = blocks[bi]
            if bi == 0:
                for ii, (src, kc) in enumerate(ins):
                    tk = None
                    nsp = max(1, kc // 8)
                    for q in range(nsp):
                        cs = slice(q * kc // nsp, (q + 1) * kc // nsp)
                        tk = P.dma('sp', lambda e, ii=ii, cs=cs, s=s, src=src: e.dma_start(
                            out=in_sb[ii][:, cs, :], in_=fm(src)[:, cs, s * TS:(s + 1) * TS]), dsin[ii], [lastpe_all])
                    intok[ii] = tk
            if n + 1 < len(seq):
                issue_w(n + 1)
            wb = n % 2
            FB = sum(p[2] for p in blk['pieces'])
            kind = blk['kind']
            lastpe = None
            if kind == 'vtok':
                kc = ins[0][1]
                for g in range(TS // 512):
                    bs = ucount % 2
                    ucount += 1
                    bk = self.banks[bs * 4:bs * 4 + 4]
                    tok0 = s * TS + g * 512
                    for k in range(kc):
                        for j in range(4):
                            w_ = [wtok[n], intok[0], bankfree[bs][j]] if k == 0 else []
                            lastpe = P.op('pe', lambda e, k=k, j=j, g=g, wb=wb, bk=bk, FB=FB: e.matmul(
                                bk[j][:, 0:FB], in_sb[0][:, k, g * 512 + j * 128:g * 512 + (j + 1) * 128],
                                wbuf[wb][:, k, 0:FB], start=(k == 0), stop=(k == kc - 1)), w_)
                    rel = blk['epi'](0, tok0, bk, lastpe, None)
                    bankfree[bs] = rel
            else:
                nunits = FB // 256
                for u in range(nunits):
                    if kind == 'pair':
                        cols = [u * 128, FB // 2 + u * 128]
                        iidx = [blk['pieces'][0][3], blk['pieces'][1][3]]
                    else:
                        cols = [2 * u * 128, (2 * u + 1) * 128]
                        iidx = [blk['pieces'][0][3]] * 2
                    for h in range(NHALF):
                        bs = ucount % 2
                        ucount += 1
                        bk = self.banks[bs * 4:bs * 4 + 4]
                        tok0 = s * TS + h * 1024
                        pretoks = blk['pre'](u, tok0) if blk.get('pre') else None
                        kcs = [ins[iidx[0]][1], ins[iidx[1]][1]]
                        kc = max(kcs)
                        for k in range(kc):
                            for ci in range(2):
                                if k >= kcs[ci]:
                                    continue
                                for t in range(2):
                                    w_ = [wtok[n], intok[iidx[ci]], bankfree[bs][ci * 2 + t]] if k == 0 else []
                                    lastpe = P.op('pe', lambda e, k=k, ci=ci, t=t, h=h, wb=wb, bk=bk, cols=cols, iidx=iidx, kcs=kcs: e.matmul(
                                        bk[ci * 2 + t], wbuf[wb][:, k, cols[ci]:cols[ci] + 128],
                                        in_sb[iidx[ci]][:, k, h * 1024 + t * 512:h * 1024 + (t + 1) * 512],
                                        start=(k == 0), stop=(k == kcs[ci] - 1)), w_)
                        rel = blk['epi'](u, tok0, bk, lastpe, pretoks)
                        bankfree[bs] = rel
            wfree[wb] = lastpe
            lastpe_all = lastpe

    def stager(self, name, shape, dt, n=2):
        bufs = [self.A.get(shape, dt) for _ in range(n)]
        return {'bufs': bufs, 'tok': [None] * n, 'i': 0, 'ds': [self.P.ds("%s%d" % (name, i)) for i in range(n)]}

    def inproj_phase(self, l):
        cfg, P, A = self.cfg, self.P, self.A
        D, QK, VW, CC = cfg.D, cfg.QK, cfg.VW, cfg.CC
        self.phase_begin()
        A = self.A
        w = self.w_in[l]
        TS = min(2048, cfg.T)
        stg = self.stager("st", [128, 2, 1024], BF16)
        sig = [A.get([128, 1024], F32) for _ in range(2)]
        vst = self.stager("sv", [128, 4, 512], BF16)
        sigtok = [None, None]
        cnt = [0]
        bgc = cfg.c_bgate
        banks = self.banks

        def store2(s, row0s, dst, tok0, evs):
            b = s['i']
            buf = s['bufs'][b]
            t = None
            for ci, r0 in enumerate(row0s):
                t = P.dma('sp', lambda e, ci=ci, r0=r0, buf=buf: e.dma_start(
                    out=dst[r0:r0 + 128, tok0:tok0 + 1024], in_=buf[:, ci, :]), s['ds'][b], evs)
            s['tok'][b] = t
            s['i'] = (b + 1) % len(s['bufs'])

        def epi_copy(row_of_unit, dst):
            def f(u, tok0, bk, lastpe, pre):
                b = stg['i']
                buf = stg['bufs'][b]
                evs = []
                for ci in range(2):
                    for t in range(2):
                        eng = 'act' if (ci + t) % 2 == 0 else 'dve'
                        if eng == 'act':
                            tk = P.op('act', lambda e, ci=ci, t=t, buf=buf, bk=bk: e.activation(
                                out=buf[:, ci, t * 512:(t + 1) * 512], in_=bk[ci * 2 + t], func=AF.Copy),
                                [lastpe, stg['tok'][b]])
                        else:
                            tk = P.op('dve', lambda e, ci=ci, t=t, buf=buf, bk=bk: e.tensor_copy(
                                out=buf[:, ci, t * 512:(t + 1) * 512], in_=bk[ci * 2 + t]),
                                [lastpe, stg['tok'][b]])
                        evs.append(tk)
                r0 = row_of_unit(u)
                store2(stg, [r0, r0 + 128], dst, tok0, evs)
                return evs
            return f

        def epi_gates(row_of_unit):
            def f(u, tok0, bk, lastpe, pre):
                b = stg['i']
                buf = stg['bufs'][b]
                evs = []
                r0 = row_of_unit(u)
                for ci in range(2):
                    col = bgc + (r0 + ci * 128) // 128
                    for t in range(2):
                        tk = P.op('act', lambda e, ci=ci, t=t, buf=buf, bk=bk, col=col: e.activation(
                            out=buf[:, ci, t * 512:(t + 1) * 512], in_=bk[ci * 2 + t], func=AF.Sigmoid,
                            bias=self.cl[:, col:col + 1]), [lastpe, stg['tok'][b]])
                        evs.append(tk)
                store2(stg, [r0, r0 + 128], self.GT, tok0, evs)
                return evs
            return f

        def epi_glu(row_of_unit):
            def f(u, tok0, bk, lastpe, pre):
                b = stg['i']
                buf = stg['bufs'][b]
                sb_ = cnt[0] % 2
                cnt[0] += 1
                rel = [None] * 4
                evs = []
                for t in range(2):
                    ta = P.op('act', lambda e, t=t, bk=bk, sb_=sb_: e.activation(
                        out=sig[sb_][:, t * 512:(t + 1) * 512], in_=bk[2 + t], func=AF.Sigmoid),
                        [lastpe, sigtok[sb_]])
                    td = P.op('dve', lambda e, t=t, bk=bk, buf=buf, sb_=sb_: e.tensor_tensor(
                        out=buf[:, 0, t * 512:(t + 1) * 512], in0=bk[t], in1=sig[sb_][:, t * 512:(t + 1) * 512],
                        op=ALU.mult), [ta, stg['tok'][b]])
                    rel[2 + t] = ta
                    rel[t] = td
                    evs.append(td)
                sigtok[sb_] = evs[-1]
                r0 = row_of_unit(u)
                store2(stg, [r0], self.YT, tok0, evs)
                return rel
            return f

        def epi_v(col0):
            def f(u, tok0, bk, lastpe, pre):
                b = vst['i']
                buf = vst['bufs'][b]
                FB = min(512, VW)
                evs = []
                for j in range(4):
                    if j % 2 == 0:
                        tk = P.op('act', lambda e, j=j, buf=buf, bk=bk: e.activation(
                            out=buf[:, j, 0:FB], in_=bk[j][:, 0:FB], func=AF.Copy), [lastpe, vst['tok'][b]])
                    else:
                        tk = P.op('dve', lambda e, j=j, buf=buf, bk=bk: e.tensor_copy(
                            out=buf[:, j, 0:FB], in_=bk[j][:, 0:FB]), [lastpe, vst['tok'][b]])
                    evs.append(tk)
                vst['tok'][b] = P.dma('sp', lambda e, buf=buf: e.dma_start(
                    out=self.V[tok0:tok0 + 512, col0:col0 + FB].rearrange("(j p) f -> p j f", p=128),
                    in_=buf[:, :, 0:FB]), vst['ds'][b], evs)
                vst['i'] = (b + 1) % 2
                return evs
            return f

        blocks = []
        for c0 in range(0, 2 * QK, 512):
            blocks.append(dict(pieces=[(w, c0, 512, 0)], kind='single',
                               epi=epi_copy(lambda u, c0=c0: c0 + u * 256, self.QKT)))
        FBv = min(512, VW)
        for c0 in range(0, VW, FBv):
            blocks.append(dict(pieces=[(w, 2 * QK + c0, FBv, 0)], kind='vtok', epi=epi_v(c0)))
        ub = 2 * QK + VW
        for c0 in range(0, CC, 256):
            blocks.append(dict(pieces=[(w, ub + c0, 256, 0), (w, ub + CC + c0, 256, 0)], kind='pair',
                               epi=epi_glu(lambda u, c0=c0: c0 + u * 128)))
        gb = ub + 2 * CC
        for c0 in range(0, 2 * D, 512):
            blocks.append(dict(pieces=[(w, gb + c0, 512, 0)], kind='single',
                               epi=epi_gates(lambda u, c0=c0: c0 + u * 256)))
        self.linear([(self.HT, cfg.DC)], TS, blocks)
        self.phase_end()

    def merge_phase(self, l):
        cfg, P, A = self.cfg, self.P, self.A
        D = cfg.D
        self.phase_begin()
        A = self.A
        TS = min(1024, cfg.T)
        stg = self.stager("st", [128, 1, 1024], BF16)
        gbuf = [[A.get([128, 1024], BF16) for _ in range(2)] for _ in range(2)]
        gds = [P.ds("g0"), P.ds("g1")]
        gfree = [None, None]
        t1 = [A.get([128, 1024], F32) for _ in range(2)]
        t2 = [A.get([128, 1024], F32) for _ in range(2)]
        tfree = [None, None]
        cnt = [0]
        slot_of = {}

        def pre(row0):
            def f(u, tok0):
                s_ = cnt[0] % 2
                cnt[0] += 1
                r = row0 + u * 128
                tk = None
                for gi in range(2):
                    tk = P.dma('sp', lambda e, gi=gi, r=r, s_=s_: e.dma_start(
                        out=gbuf[s_][gi], in_=self.GT[gi * D + r:gi * D + r + 128, tok0:tok0 + 1024]),
                        gds[s_], [gfree[s_]])
                return (s_, tk)
            return f

        def epi(row0):
            def f(u, tok0, bk, lastpe, pretoks):
                s_, gtok = pretoks
                b = stg['i']
                buf = stg['bufs'][b]
                r = row0 + u * 128
                col = cfg.c_bcproj + r // 128
                rel = [None] * 4
                evs = []
                for t in range(2):
                    sl = slice(t * 512, (t + 1) * 512)
                    ta = P.op('dve', lambda e, t=t, sl=sl, bk=bk, s_=s_: e.tensor_tensor(
                        out=t1[s_][:, sl], in0=bk[t], in1=gbuf[s_][0][:, sl], op=ALU.mult),
                        [lastpe, gtok, tfree[s_]])
                    tb = P.op('dve', lambda e, t=t, sl=sl, bk=bk, s_=s_, col=col: e.scalar_tensor_tensor(
                        out=t2[s_][:, sl], in0=bk[2 + t], scalar=self.cl[:, col:col + 1], in1=gbuf[s_][1][:, sl],
                        op0=ALU.add, op1=ALU.mult), [lastpe, gtok, tfree[s_]])
                    tc = P.op('pool', lambda e, sl=sl, buf=buf, s_=s_: e.tensor_tensor(
                        out=buf[:, 0, sl], in0=t1[s_][:, sl], in1=t2[s_][:, sl], op=ALU.add),
                        [ta, tb, stg['tok'][b]])
                    rel[t] = ta
                    rel[2 + t] = tb
                    evs.append(tc)
                tfree[s_] = evs[-1]
                gfree[s_] = rel[3]
                stg['tok'][b] = P.dma('sp', lambda e, buf=buf, r=r: e.dma_start(
                    out=self.MT[r:r + 128, tok0:tok0 + 1024], in_=buf[:, 0, :]), stg['ds'][b], evs)
                stg['i'] = (b + 1) % 2
                return rel
            return f

        blocks = []
        for c0 in range(0, D, 256):
            blocks.append(dict(pieces=[(self.w_ap[l], c0, 256, 0), (self.w_cp[l], c0, 256, 1)], kind='pair',
                               pre=pre(c0), epi=epi(c0)))
        self.linear([(self.ONT, cfg.VW // 128), (self.CT, cfg.CCc)], TS, blocks)
        self.phase_end()

    def resid_phase(self, l, w2d, src, KC, xsrc, TS):
        cfg, P, A = self.cfg, self.P, self.A
        D = cfg.D
        self.phase_begin()
        A = self.A
        xold = [[A.get([128, 1024], F32) for _ in range(2)] for _ in range(2)]
        xds = [P.ds("g0"), P.ds("g1")]
        xfree = [None, None]
        xnew = self.stager("st", [128, 2, 1024], F32)
        tmp = [A.get([128, 1024], F32) for _ in range(2)]
        tmpfree = [None, None]
        cnt = [0]

        def pre(row0):
            def f(u, tok0):
                s_ = cnt[0] % 2
                cnt[0] += 1
                tk = None
                for ci in range(2):
                    r = row0 + u * 256 + ci * 128
                    tk = P.dma('sp', lambda e, ci=ci, r=r, s_=s_: e.dma_start(
                        out=xold[s_][ci], in_=xsrc[r:r + 128, tok0:tok0 + 1024]), xds[s_], [xfree[s_]])
                return (s_, tk)
            return f

        def epi(row0):
            def f(u, tok0, bk, lastpe, pretoks):
                s_, xtok = pretoks
                b = xnew['i']
                buf = xnew['bufs'][b]
                rel = [None] * 4
                evs = []
                for t in range(2):
                    sl = slice(t * 512, (t + 1) * 512)
                    ta = P.op('dve', lambda e, t=t, sl=sl, bk=bk, buf=buf, s_=s_: e.tensor_tensor(
                        out=buf[:, 0, sl], in0=bk[t], in1=xold[s_][0][:, sl], op=ALU.add),
                        [lastpe, xtok, xnew['tok'][b]])
                    tb = P.op('act', lambda e, t=t, sl=sl, bk=bk, s_=s_: e.activation(
                        out=tmp[s_][:, sl], in_=bk[2 + t], func=AF.Copy), [lastpe, tmpfree[s_]])
                    tc = P.op('pool', lambda e, sl=sl, buf=buf, s_=s_: e.tensor_tensor(
                        out=buf[:, 1, sl], in0=tmp[s_][:, sl], in1=xold[s_][1][:, sl], op=ALU.add),
                        [tb, xtok, xnew['tok'][b]])
                    rel[t] = ta
                    rel[2 + t] = tb
                    evs += [ta, tc]
                tmpfree[s_] = evs[-1]
                xfree[s_] = [evs[-1], evs[-2]]
                t_ = None
                for ci in range(2):
                    r = row0 + u * 256 + ci * 128
                    t_ = P.dma('sp', lambda e, ci=ci, r=r, buf=buf: e.dma_start(
                        out=self.XT[r:r + 128, tok0:tok0 + 1024], in_=buf[:, ci, :]), xnew['ds'][b], evs)
                xnew['tok'][b] = t_
                xnew['i'] = (b + 1) % 2
                return rel
            return f

        blocks = []
        FB = 256 if KC > 16 else min(512, D)
        for c0 in range(0, D, FB):
            blocks.append(dict(pieces=[(w2d, c0, FB, 0)], kind='single', pre=pre(c0), epi=epi(c0)))
        self._flatten_fix = True
        self.linear([(src, KC)], TS, blocks)
        self.phase_end()

    def ffnin_phase(self, l):
        cfg, P, A = self.cfg, self.P, self.A
        DFF = cfg.DFF
        self.phase_begin()
        A = self.A
        TS = min(2048, cfg.T)
        stg = self.stager("st", [128, 1, 1024], BF16)
        sil = [A.get([128, 1024], F32) for _ in range(2)]
        silfree = [None, None]
        cnt = [0]
        w = self.w_fi[l]

        def epi(row0):
            def f(u, tok0, bk, lastpe, pre):
                b = stg['i']
                buf = stg['bufs'][b]
                s_ = cnt[0] % 2
                cnt[0] += 1
                rel = [None] * 4
                evs = []
                for t in range(2):
                    sl = slice(t * 512, (t + 1) * 512)
                    ta = P.op('act', lambda e, t=t, sl=sl, bk=bk, s_=s_: e.activation(
                        out=sil[s_][:, sl], in_=bk[t], func=AF.Silu), [lastpe, silfree[s_]])
                    td = P.op('dve', lambda e, t=t, sl=sl, bk=bk, buf=buf, s_=s_: e.tensor_tensor(
                        out=buf[:, 0, sl], in0=bk[2 + t], in1=sil[s_][:, sl], op=ALU.mult), [ta, stg['tok'][b]])
                    rel[t] = ta
                    rel[2 + t] = td
                    evs.append(td)
                silfree[s_] = evs[-1]
                r = row0 + u * 128
                stg['tok'][b] = P.dma('sp', lambda e, buf=buf, r=r: e.dma_start(
                    out=self.AT[r:r + 128, tok0:tok0 + 1024], in_=buf[:, 0, :]), stg['ds'][b], evs)
                stg['i'] = (b + 1) % 2
                return rel
            return f

        blocks = []
        for c0 in range(0, DFF, 256):
            blocks.append(dict(pieces=[(w, c0, 256, 0), (w, DFF + c0, 256, 0)], kind='pair', epi=epi(c0)))
        self.linear([(self.HT, cfg.DC)], TS, blocks)
        self.phase_end()

    def attn_phase(self, l):
        cfg, P, A = self.cfg, self.P, self.A
        T, H, HD, VD, NK, NQ = cfg.T, cfg.H, cfg.HD, cfg.VD, cfg.NK, cfg.NQ
        self.phase_begin()
        A = self.A
        banks = self.banks
        scale = HD ** -0.5
        atab = A.get([128, cfg.AW], F32)
        qb = [A.get([128, 2, T], BF16) for _ in range(2)]
        kb = [A.get([128, 2, T], BF16) for _ in range(2)]
        vb = [A.get([128, NK, VD + 1], BF16) for _ in range(2)]
        NSB = 3
        NTMP = 6
        NPT = 8
        LA = 6
        sbanks = [4, 5, 7]
        tmp = [A.get([128, 512], F32) for _ in range(NTMP)]
        pt = [A.get([128, 512], BF16) for _ in range(NPT)]
        Om = [A.get([128, 4, VD], F32) for _ in range(2)]
        o = A.get([128, 4, VD], F32)
        junk = A.get([128, VD], F32)
        on = A.get([128, 4, VD], BF16)
        ost = [A.get([128, VD // 128, 512], BF16) for _ in range(2)]
        rec = A.get([128, 8], F32)
        ssq = A.get([128, 4], F32)
        rs2 = A.get([128, 4], F32)
        tb = banks[6].bitcast(BF16)
        ta = P.dma('sp', lambda e: e.dma_start(out=atab, in_=self.atab_in), P.ds("c0"))
        ones_tok = []
        for i in range(2):
            ones_tok.append(P.op('pool', lambda e, i=i: e.memset(vb[i][:, :, VD:VD + 1], 1.0)))
        dsl = [P.ds("in0"), P.ds("in1")]
        dso = [P.ds("st0"), P.ds("st1")]
        loadtok = [None, None]
        headdone = [None, None]

        def load_head(h):
            b = h % 2
            w_ = [headdone[b], ones_tok[b]]
            P.dma('sp', lambda e: e.dma_start(out=qb[b], in_=fm(self.QKT)[:, 2 * h:2 * h + 2, :]), dsl[b], w_)
            P.dma('sp', lambda e: e.dma_start(out=kb[b], in_=fm(self.QKT)[:, cfg.QK // 128 + 2 * h:cfg.QK // 128 + 2 * h + 2, :]),
                  dsl[b], w_)
            loadtok[b] = P.dma('sp', lambda e: e.dma_start(
                out=vb[b][:, :, 0:VD], in_=self.V.rearrange("(k p) f -> p k f", p=128)[:, :, h * VD:(h + 1) * VD]),
                dsl[b], w_)

        load_head(0)
        st = {'n': 0, 'sdve': [None] * NSB, 'tact': [None] * NTMP, 'ptpe': [None] * NPT,
              'accfree': [None] * 4, 'omfree': [None, None], 'ofree': None, 'onfree': None,
              'ostfree': [None, None], 'tbfree': None, 'osti': 0, 'ssqfree': None}
        deferred = []

        def run_head(h):
            hb = h % 2
            if h + 1 < H:
                load_head(h + 1)
            slope = 2.0 ** (-8.0 * (h + 1) / H)
            dmin = 80.0 / slope
            kts_of = {}
            for qc in range(NQ):
                keep = []
                for kt in range(NK):
                    dist = max(0, kt * 128 - (qc * 512 + 511), qc * 512 - (kt * 128 + 127))
                    if dist < dmin:
                        keep.append(kt)
                kts_of[qc] = keep
            steps = [(qc, m, kt) for qc in range(NQ) for m in range(2) for kt in kts_of[qc]]
            S_tok = {}
            P_tok = {}

            def emit_S(i):
                qc, m, kt = steps[i]
                n = st['n'] + i
                sbk = sbanks[n % NSB]
                s_tok = P.op('pe', lambda e: e.matmul(
                    banks[sbk], kb[hb][:, m, kt * 128:(kt + 1) * 128], qb[hb][:, m, qc * 512:(qc + 1) * 512],
                    start=True, stop=True), [loadtok[hb], st['sdve'][n % NSB]])
                u0 = 512 * qc - 128 * kt + cfg.AOFF
                t_tok = P.op('dve', lambda e: e.scalar_tensor_tensor(
                    out=tmp[n % NTMP], in0=atab[:, u0:u0 + 512], scalar=slope / scale, in1=banks[sbk],
                    op0=ALU.mult, op1=ALU.add), [s_tok, st['tact'][n % NTMP], ta])
                st['sdve'][n % NSB] = t_tok
                kcol = cfg.g_kmask + kt
                p_tok = P.op('act', lambda e: e.activation(
                    out=pt[n % NPT], in_=tmp[n % NTMP], func=AF.Exp, bias=self.cg[:, kcol:kcol + 1], scale=scale),
                    [t_tok, st['ptpe'][n % NPT]])
                st['tact'][n % NTMP] = p_tok
                P_tok[i] = p_tok

            def emit_PV(i):
                qc, m, kt = steps[i]
                n = st['n'] + i
                last = None
                for qs in range(4):
                    w_ = [P_tok[i]]
                    first = (kt == kts_of[qc][0])
                    lastk = (kt == kts_of[qc][-1])
                    if first:
                        w_.append(st['accfree'][qs])
                    last = P.op('pe', lambda e, qs=qs, first=first, lastk=lastk: e.matmul(
                        banks[qs][:, 0:VD + 1], pt[n % NPT][:, qs * 128:(qs + 1) * 128], vb[hb][:, kt, :],
                        start=first, stop=lastk), w_)
                st['ptpe'][n % NPT] = last
                if kt == kts_of[qc][-1]:
                    finish_group(qc, m, last)
                return last

            def finish_group(qc, m, lastpe):
                r_tok = []
                for qs in range(4):
                    c_ = P.op('dve', lambda e, qs=qs: e.tensor_scalar(
                        out=rec[:, m * 4 + qs:m * 4 + qs + 1], in0=banks[qs][:, VD:VD + 1], scalar1=1e-30,
                        scalar2=None, op0=ALU.max), [lastpe, st['omfree'][m]])
                    r_tok.append(P.op('dve', lambda e, qs=qs: e.reciprocal(
                        out=rec[:, m * 4 + qs:m * 4 + qs + 1], in_=rec[:, m * 4 + qs:m * 4 + qs + 1]), [c_]))
                ev = []
                for qs in range(4):
                    tk = P.op('dve', lambda e, qs=qs: e.tensor_scalar(
                        out=Om[m][:, qs, :], in0=banks[qs][:, 0:VD], scalar1=rec[:, m * 4 + qs:m * 4 + qs + 1],
                        scalar2=None, op0=ALU.mult), [r_tok[qs], st['omfree'][m]])
                    st['accfree'][qs] = tk
                    ev.append(tk)
                if m == 0:
                    st['om0'] = ev[-1]
                    return
                c_tok = P.op('dve', lambda e: e.scalar_tensor_tensor(
                    out=o, in0=Om[1], scalar=self.neglam[:, 1:2], in1=Om[0], op0=ALU.mult, op1=ALU.add),
                    [ev[-1], st['om0'], st['ofree']])
                st['omfree'] = [c_tok, c_tok]
                z_tok = P.op('pool', lambda e: e.memset(ssq, 0.0), [st['ssqfree']])
                sq_tok = None
                for qs in range(4):
                    sq_tok = P.op('act', lambda e, qs=qs: e.activation(
                        out=junk, in_=o[:, qs, :], func=AF.Square, accum_out=ssq[:, qs:qs + 1]),
                        [c_tok, z_tok, sq_tok])
                r1 = P.op('dve', lambda e: e.tensor_scalar(out=rs2, in0=ssq, scalar1=1.0 / VD, scalar2=NORM_EPS,
                                                           op0=ALU.mult, op1=ALU.add), [sq_tok, st['onfree']])
                r2a = P.op('act', lambda e: e.activation(out=rs2, in_=rs2, func=AF.Sqrt), [r1])
                r2 = P.op('dve', lambda e: e.reciprocal(out=rs2, in_=rs2), [r2a])
                st['ssqfree'] = r1
                n_tok = None
                for qs in range(4):
                    n_tok = P.op('dve', lambda e, qs=qs: e.scalar_tensor_tensor(
                        out=on[:, qs, :], in0=o[:, qs, :], scalar=rs2[:, qs:qs + 1], in1=self.gsub,
                        op0=ALU.mult, op1=ALU.mult), [r2, st['onfree']])
                st['ofree'] = n_tok

                def do_transposes(n_tok=n_tok, qc=qc, h=h):
                    tl = None
                    for vc in range(VD // 128):
                        for qs in range(4):
                            i_ = vc * 4 + qs
                            tl = P.op('pe', lambda e, vc=vc, qs=qs, i_=i_: e.transpose(
                                out=tb[:, i_ * 128:(i_ + 1) * 128], in_=on[:, qs, vc * 128:(vc + 1) * 128],
                                identity=self.ident_bf), [n_tok, st['tbfree']])
                    ob = st['osti']
                    st['osti'] = 1 - ob
                    c_ = P.op('act', lambda e: e.activation(
                        out=ost[ob].rearrange("p a b -> p (a b)"), in_=tb, func=AF.Copy), [tl, st['ostfree'][ob]])
                    st['tbfree'] = c_
                    st['onfree'] = tl
                    st['ostfree'][ob] = P.dma('sp', lambda e: e.dma_start(
                        out=fm(self.ONT)[:, h * (VD // 128):(h + 1) * (VD // 128), qc * 512:(qc + 1) * 512],
                        in_=ost[ob]), dso[ob], [c_])
                deferred.append([6, do_transposes])

            ns = len(steps)
            for i0 in range(min(LA, ns)):
                emit_S(i0)
            lastpv = None
            for i in range(ns):
                lastpv = emit_PV(i)
                if i + LA < ns:
                    emit_S(i + LA)
                for d in deferred:
                    d[0] -= 1
                for d in [d for d in deferred if d[0] <= 0]:
                    d[1]()
                    deferred.remove(d)
            st['n'] += ns
            headdone[hb] = lastpv

        for h in range(H):
            run_head(h)
        for d in deferred:
            d[1]()
        self.phase_end()

    def conv1_phase(self, l):
        cfg, P, A = self.cfg, self.P, self.A
        T, CW, CCc = cfg.T, cfg.CW, cfg.CCc
        PAD = (CW - 1) // 2
        self.phase_begin()
        A = self.A
        banks = self.banks
        ypad = [A.get([128, T + 2 * PAD + 2], BF16) for _ in range(2)]
        diag = [A.get([128, CW, 128], BF16) for _ in range(2)]
        zst = [A.get([128, T], F32) for _ in range(2)]
        dsl = [P.ds("in0"), P.ds("in1")]
        dso = [P.ds("st0"), P.ds("st1")]
        halo = []
        for i in range(2):
            halo.append(P.op('pool', lambda e, i=i: e.memset(ypad[i][:, 0:PAD], 0.0)))
            halo.append(P.op('pool', lambda e, i=i: e.memset(ypad[i][:, PAD + T:PAD + T + PAD], 0.0)))
        pedone = [None, None]
        stdone = [None, None]
        bankfree = [None] * 8
        bn = 0
        ldtok = {}
        tm = A.get([128, T], BF16)
        tmtok = P.dma("pool", lambda e: e.dma_start(out=tm, in_=self.tmask_in), P.ds("pm"))

        def load(c):
            b = c % 2
            t_ = P.dma('sp', lambda e: e.dma_start(out=ypad[b][:, PAD:PAD + T], in_=self.YT[c * 128:(c + 1) * 128, :]),
                       dsl[b], [pedone[b]] + halo)
            ldtok[c] = P.op('pool' if c % 2 else 'dve', lambda e: e.tensor_tensor(
                out=ypad[b][:, PAD:PAD + T], in0=ypad[b][:, PAD:PAD + T], in1=tm, op=ALU.mult), [t_, tmtok])

        dltok = {}

        def build_diag(c):
            b = c % 2
            dl = [None, None]
            for j in range(CW):
                eng = 'dve' if j % 3 != 2 else 'pool'
                col = cfg.c_wdw + c * CW + j
                dl[0 if eng == 'dve' else 1] = P.op(eng, lambda e, j=j, col=col, b=b: e.tensor_scalar(
                    out=diag[b][:, j, :], in0=self.ident_bf, scalar1=self.cl[:, col:col + 1], scalar2=None,
                    op0=ALU.mult), [pedone[b]])
            dltok[c] = dl

        load(0)
        build_diag(0)
        for c in range(CCc):
            b = c % 2
            if c + 1 < CCc:
                load(c + 1)
                build_diag(c + 1)
            dl = dltok[c]
            evs = []
            last = None
            for tt in range(T // 512):
                bk = bn % 8
                bn += 1
                for j in range(CW):
                    w_ = [ldtok[c], dl[0], dl[1], bankfree[bk]] if j == 0 else []
                    last = P.op('pe', lambda e, j=j, tt=tt, bk=bk, b=b: e.matmul(
                        banks[bk], diag[b][:, j, :], ypad[b][:, tt * 512 + j:tt * 512 + j + 512],
                        start=(j == 0), stop=(j == CW - 1)), w_)
                bcol = cfg.c_bdw + c
                if tt % 2 == 0:
                    tk = P.op('act', lambda e, tt=tt, bk=bk, bcol=bcol, b=b: e.activation(
                        out=zst[b][:, tt * 512:(tt + 1) * 512], in_=banks[bk], func=AF.Identity,
                        bias=self.cl[:, bcol:bcol + 1]), [last, stdone[b]])
                else:
                    tk = P.op('dve', lambda e, tt=tt, bk=bk, bcol=bcol, b=b: e.tensor_scalar(
                        out=zst[b][:, tt * 512:(tt + 1) * 512], in0=banks[bk], scalar1=self.cl[:, bcol:bcol + 1],
                        scalar2=None, op0=ALU.add), [last, stdone[b]])
                bankfree[bk] = tk
                evs.append(tk)
            pedone[b] = last
            stdone[b] = P.dma('sp', lambda e, c=c, b=b: e.dma_start(out=self.ZT[c * 128:(c + 1) * 128, :], in_=zst[b]),
                              dso[b], evs)
        self.phase_end()

    def conv2_phase(self, l):
        cfg, P, A = self.cfg, self.P, self.A
        T, CCc, CC = cfg.T, cfg.CCc, cfg.CC
        self.phase_begin()
        A = self.A
        banks = self.banks
        z = [A.get([128, CCc, 512], F32) for _ in range(2)]
        zb = A.get([128, CCc, 512], BF16)
        zs = A.get([128, CCc, 512], BF16)
        mean = A.get([128, 512], F32)
        msq = A.get([128, 512], F32)
        rstd = A.get([128, 512], F32)
        t1 = [A.get([128, 512], F32) for _ in range(4)]
        outb = [A.get([128, CCc, 512], BF16) for _ in range(2)]
        dsl = [P.ds("in0"), P.ds("in1")]
        dso = [P.ds("st0"), P.ds("st1")]
        zfree = [None, None]
        stdone = [None, None]
        pedone = None
        statfree = None
        t1free = [None] * 4
        ld = {}

        def load(i):
            b = i % 2
            ld[i] = P.dma('sp', lambda e: e.dma_start(out=z[b], in_=fm(self.ZT)[:, :, i * 512:(i + 1) * 512]),
                          dsl[b], zfree[b] or [])

        load(0)
        nt = 0
        for i in range(T // 512):
            b = i % 2
            if i + 1 < T // 512:
                load(i + 1)
            a1 = P.op('act', lambda e, b=b: e.activation(out=zb, in_=z[b], func=AF.Copy), [ld[i], pedone])
            a2 = P.op('act', lambda e, b=b: e.activation(out=zs, in_=z[b], func=AF.Square), [ld[i], pedone])
            mm = None
            for c in range(CCc):
                mm = P.op('pe', lambda e, c=c: e.matmul(banks[0], self.ones_bf, zb[:, c, :], start=(c == 0),
                                                        stop=(c == CCc - 1)), [a1, statfree] if c == 0 else [])
            for c in range(CCc):
                mm = P.op('pe', lambda e, c=c: e.matmul(banks[1], self.ones_bf, zs[:, c, :], start=(c == 0),
                                                        stop=(c == CCc - 1)), [a2] if c == 0 else [])
            pedone = mm
            s1 = P.op('dve', lambda e: e.tensor_scalar(out=mean, in0=banks[0], scalar1=1.0 / CC, scalar2=None,
                                                       op0=ALU.mult), [mm] + (zfree[1 - b] or []))
            s2 = P.op('dve', lambda e: e.tensor_tensor(out=msq, in0=mean, in1=mean, op=ALU.mult), [s1])
            s3 = P.op('dve', lambda e: e.scalar_tensor_tensor(out=rstd, in0=banks[1], scalar=1.0 / CC, in1=msq,
                                                              op0=ALU.mult, op1=ALU.subtract), [s2])
            s4a = P.op('dve', lambda e: e.tensor_scalar(out=rstd, in0=rstd, scalar1=LN_EPS, scalar2=None,
                                                        op0=ALU.add), [s3])
            s4b = P.op('act', lambda e: e.activation(out=rstd, in_=rstd, func=AF.Sqrt), [s4a])
            s4 = P.op('dve', lambda e: e.reciprocal(out=rstd, in_=rstd), [s4b])
            statfree = s3
            lasts = []
            for c in range(CCc):
                eng = 'dve' if c % 3 != 2 else 'pool'
                tb_ = nt % 4
                nt += 1
                u1 = P.op(eng, lambda e, c=c, tb_=tb_, b=b: e.tensor_tensor(out=t1[tb_], in0=z[b][:, c, :], in1=mean,
                                                                       op=ALU.subtract), [s4, t1free[tb_]])
                u2 = P.op(eng, lambda e, c=c, tb_=tb_: e.tensor_tensor(out=t1[tb_], in0=t1[tb_], in1=rstd,
                                                                       op=ALU.mult), [u1])
                gcol = cfg.c_gcln + c
                bcol = cfg.c_bcln + c
                u3 = P.op('act', lambda e, c=c, tb_=tb_, gcol=gcol, bcol=bcol, b=b: e.activation(
                    out=outb[b][:, c, :], in_=t1[tb_], func=AF.Silu, bias=self.cl[:, bcol:bcol + 1],
                    scale=self.cl[:, gcol:gcol + 1]), [u2, stdone[b]])
                t1free[tb_] = u3
                lasts.append(u3)
            zfree[b] = [lasts[-1]]
            stdone[b] = P.dma('sp', lambda e, i=i, b=b: e.dma_start(out=fm(self.CT)[:, :, i * 512:(i + 1) * 512], in_=outb[b]),
                              dso[b], [lasts[-1]])
        self.phase_end()


def _consts(cfg, inp):
    DEPTH = cfg.DEPTH
    cl = np.zeros((DEPTH, 128, cfg.NCL), np.float32)

    def pc(v):
        return np.ascontiguousarray(v.reshape(-1, 128).T)

    for l in range(DEPTH):
        c = cl[l]
        c[:, cfg.c_gmix:cfg.c_gmix + cfg.DC] = pc(inp["g_mix"][l])
        c[:, cfg.c_bgate:cfg.c_bgate + 2 * cfg.DC] = pc(inp["b_gate"][l])
        c[:, cfg.c_bdw:cfg.c_bdw + cfg.CCc] = pc(inp["b_dw"][l])
        c[:, cfg.c_gcln:cfg.c_gcln + cfg.CCc] = pc(inp["g_conv_ln"][l])
        c[:, cfg.c_bcln:cfg.c_bcln + cfg.CCc] = pc(inp["b_conv_ln"][l])
        c[:, cfg.c_bcproj:cfg.c_bcproj + cfg.DC] = pc(inp["b_conv_proj"][l])
        c[:, cfg.c_gffn:cfg.c_gffn + cfg.DC] = pc(inp["g_ffn"][l])
        wd = inp["w_dw"][l][:, 0, :]
        wd = wd.T.reshape(cfg.CCc, 128, cfg.CW).transpose(1, 0, 2).reshape(128, cfg.CCc * cfg.CW)
        c[:, cfg.c_wdw:cfg.c_wdw + cfg.CCc * cfg.CW] = wd
        c[:, cfg.c_gsub:cfg.c_gsub + cfg.VD] = inp["g_subln"][l][None, :]
        c[:, cfg.c_lq:cfg.c_lq + 2 * cfg.HD] = inp["lambda_q"][l].reshape(1, -1)
        c[:, cfg.c_lk:cfg.c_lk + 2 * cfg.HD] = inp["lambda_k"][l].reshape(1, -1)
    return cl


def _globals(cfg, g_final, nvalid):
    cg = np.zeros((128, cfg.NCG), np.float32)
    cg[:, cfg.g_gfinal:cfg.g_gfinal + cfg.DC] = g_final.reshape(-1, 128).T
    cg[:, cfg.g_ident:cfg.g_ident + 128] = np.eye(128, dtype=np.float32)
    kpos = np.arange(cfg.T).reshape(cfg.NK, 128).T
    cg[:, cfg.g_kmask:cfg.g_kmask + cfg.NK] = np.where(kpos < nvalid, 0.0, MASK_NEG)
    return cg


def _atab(cfg):
    p = np.arange(128)[:, None]
    u = np.arange(cfg.AW)[None, :]
    return (-np.abs(u - cfg.AOFF - p)).astype(np.float32)


_NC_CACHE = {}


def run_trunk(cfg, seqs, inp, n_cores=8):
    key = (cfg.T, cfg.D, cfg.H, cfg.DFF, cfg.DEPTH)
    if key not in _NC_CACHE:
        _NC_CACHE[key] = Builder(cfg).build()
    nc = _NC_CACHE[key]
    cl = _consts(cfg, inp)
    atab = _atab(cfg)
    wmap = {k: np.ascontiguousarray(inp[k], dtype=np.float32) for k in
            ("w_in", "w_attn_proj", "w_conv_proj", "w_out", "w_ffn_in", "w_ffn_out")}
    in_maps = []
    if n_cores == 8 and len(seqs) == 6:
        core_of_seq = [0, 4, 1, 2, 5, 6]
    else:
        core_of_seq = list(range(len(seqs)))
    seq_of_core = {c: i for i, c in enumerate(core_of_seq)}
    for c in range(n_cores):
        xT = np.zeros((cfg.D, cfg.T), np.float32)
        if c in seq_of_core:
            s = seqs[seq_of_core[c]]
            S = s.shape[0]
            xT[:, :S] = s.T
        else:
            S = cfg.T
        tmask = np.zeros((128, cfg.T), np.float32)
        tmask[:, :S] = 1.0
        m = {"xT": xT, "cl": cl, "cg": _globals(cfg, inp["g_final"], S), "atab": atab, "tmask": tmask}
        m.update(wmap)
        in_maps.append(m)
    res = run_bass_kernel_spmd(nc, in_maps, core_ids=list(range(n_cores)))
    outs = []
    for i in range(len(seqs)):
        S = seqs[i].shape[0]
        outs.append(np.ascontiguousarray(res.results[core_of_seq[i]]["yT"][:, :S].T))
    return outs


def kernel(x_prompt, x_sample, g_mix, w_in, b_gate, lambda_q, lambda_k, g_subln,
           w_attn_proj, w_dw, b_dw, g_conv_ln, b_conv_ln, w_conv_proj, b_conv_proj,
           w_out, g_ffn, w_ffn_in, w_ffn_out, g_final):
    inp = dict(g_mix=g_mix, w_in=w_in, b_gate=b_gate, lambda_q=lambda_q, lambda_k=lambda_k, g_subln=g_subln,
               w_attn_proj=w_attn_proj, w_dw=w_dw, b_dw=b_dw, g_conv_ln=g_conv_ln, b_conv_ln=b_conv_ln,
               w_conv_proj=w_conv_proj, b_conv_proj=b_conv_proj, w_out=w_out, g_ffn=g_ffn,
               w_ffn_in=w_ffn_in, w_ffn_out=w_ffn_out, g_final=g_final)
    inp = {k: np.asarray(v, dtype=np.float32) for k, v in inp.items()}
    x_prompt = np.asarray(x_prompt, dtype=np.float32)
    x_sample = np.asarray(x_sample, dtype=np.float32)
    cfg = Cfg()
    seqs = [x_prompt[b] for b in range(x_prompt.shape[0])] + [x_sample[b] for b in range(x_sample.shape[0])]
    outs = run_trunk(cfg, seqs, inp)
    nb = x_prompt.shape[0]
    y_prompt = np.stack(outs[:nb]).astype(np.float32)
    y_sample = np.stack(outs[nb:]).astype(np.float32)
    return (y_prompt, y_sample)
```

```python
import numpy as np
from contextlib import ExitStack
import concourse.bass as bass
import concourse.mybir as mybir
from concourse.bass_utils import run_bass_kernel_spmd

F32 = mybir.dt.float32
BF16 = mybir.dt.bfloat16
U8 = mybir.dt.uint8
AF = mybir.ActivationFunctionType
ALU = mybir.AluOpType

NORM_EPS = 1e-6
LN_EPS = 1e-5
MASK_NEG = -30000.0
SB_BYTES = 192 * 1024


class Cfg:
    def __init__(self, T=4096, D=2048, H=8, DFF=5632, DEPTH=4, CW=31):
        self.T, self.D, self.H, self.DFF, self.DEPTH, self.CW = T, D, H, DFF, DEPTH, CW
        self.HD = 128
        self.VD = 256
        self.QK = H * 2 * self.HD
        self.VW = H * self.VD
        self.CC = D
        self.IN_W = 2 * self.QK + self.VW + 2 * self.CC + 2 * D
        self.DC = D // 128
        self.CCc = self.CC // 128
        self.NK = T // 128
        self.NQ = T // 512
        self.AOFF = T - 128
        self.AW = (self.NQ - 1) * 512 + 512 + self.AOFF
        c = 0
        self.c_gmix = c; c += self.DC
        self.c_bgate = c; c += 2 * self.DC
        self.c_bdw = c; c += self.CCc
        self.c_gcln = c; c += self.CCc
        self.c_bcln = c; c += self.CCc
        self.c_bcproj = c; c += self.DC
        self.c_gffn = c; c += self.DC
        self.c_wdw = c; c += self.CCc * CW
        self.c_gsub = c; c += self.VD
        self.c_lq = c; c += 2 * self.HD
        self.c_lk = c; c += 2 * self.HD
        self.NCL = c
        g = 0
        self.g_gfinal = g; g += self.DC
        self.g_ident = g; g += 128
        self.g_kmask = g; g += self.NK
        self.NCG = g


def lambda_init(layer):
    return 0.8 - 0.6 * float(np.exp(-0.3 * layer))


class DmaSem:
    def __init__(self, sem):
        self.sem = sem
        self.count = 0


class Prog:
    ENG = ('pe', 'act', 'dve', 'pool', 'sp')

    def __init__(self, nc, stack):
        self.nc = nc
        self.stack = stack
        self.streams = {k: [] for k in self.ENG}
        self.cnt = {k: 0 for k in self.ENG}
        self.esem = {k: stack.enter_context(nc.semaphore("s_" + k)) for k in self.ENG}
        self.dsems = {}
        self.pending = {k: [] for k in self.ENG}
        self.waited = {k: {} for k in self.ENG}

    def ds(self, name):
        if name not in self.dsems:
            self.dsems[name] = DmaSem(self.stack.enter_context(self.nc.semaphore("d_" + name)))
        return self.dsems[name]

    def _w(self, eng, waits):
        w = []

        def fl(x):
            if x is None:
                return
            if isinstance(x, list):
                for y in x:
                    fl(y)
            else:
                w.append(x)
        fl(list(waits))
        if self.pending[eng]:
            w = self.pending[eng] + w
            self.pending[eng] = []
        return w

    def op(self, eng, fn, waits=()):
        self.cnt[eng] += 1
        self.streams[eng].append((fn, self._w(eng, waits), None))
        return (eng, self.cnt[eng])

    def dma(self, eng, fn, ds, waits=()):
        ds.count += 16
        self.streams[eng].append((fn, self._w(eng, waits), ds))
        return (ds, ds.count)

    def last(self, eng):
        return (eng, self.cnt[eng]) if self.cnt[eng] else None

    def barrier(self):
        toks = [(k, self.cnt[k]) for k in self.ENG if self.cnt[k]]
        toks += [(d, d.count) for d in self.dsems.values() if d.count]
        for k in self.ENG:
            self.pending[k] = list(toks)

    def finish(self):
        self.barrier()
        for k in self.ENG:
            w = self._w(k, [])
            self.streams[k].append((None, w, None))

    def flush(self):
        nc = self.nc
        P = self
        with nc.Block() as block:
            @block.tensor
            def _(e):
                P.replay('pe', e)

            @block.scalar
            def _(e):
                P.replay('act', e)

            @block.vector
            def _(e):
                P.replay('dve', e)

            @block.gpsimd
            def _(e):
                P.replay('pool', e)

            @block.sync
            def _(e):
                P.replay('sp', e)
        for k in self.ENG:
            self.streams[k] = []

    def replay(self, eng, e):
        waited = self.waited[eng]
        for fn, waits, ds in self.streams[eng]:
            for key, val in waits:
                kk = key if isinstance(key, str) else id(key)
                if waited.get(kk, 0) >= val:
                    continue
                waited[kk] = val
                sem = self.esem[key] if isinstance(key, str) else key.sem
                e.wait_ge(sem, val)
            if fn is None:
                continue
            inst = fn(e)
            if ds is None:
                inst.then_inc(self.esem[eng], 1)
            else:
                inst.then_inc(ds.sem, 16)


class Alloc:
    def __init__(self, sb, base, limit):
        self.sb, self.base, self.off, self.limit = sb, base, base, limit

    def reset(self):
        self.off = self.base

    def get(self, shape, dt):
        esz = 4 if dt == F32 else 2
        n = int(np.prod(shape[1:]))
        nb = (n * esz + 63) // 64 * 64
        o = self.off
        self.off += nb
        assert self.off <= self.limit, ("SBUF overflow", self.off, self.limit)
        v = self.sb[:, o:o + n * esz].bitcast(dt)
        if len(shape) == 3:
            v = v.rearrange("p (a b) -> p a b", b=shape[2])
        return v


def fm(ap):
    return ap.rearrange("(c p) t -> p c t", p=128)


class Builder:
    def __init__(self, cfg):
        self.cfg = cfg

    def build(self):
        cfg = self.cfg
        T, D, DEPTH = cfg.T, cfg.D, cfg.DEPTH
        nc = bass.Bass("TRN2", target_bir_lowering=False)
        self.nc = nc
        dt_in = lambda name, shape: nc.dram_tensor(name, shape, F32, kind="ExternalInput").ap()
        self.xT_in = dt_in("xT", [D, T])
        self.cl_in = dt_in("cl", [DEPTH, 128, cfg.NCL])
        self.cg_in = dt_in("cg", [128, cfg.NCG])
        self.atab_in = dt_in("atab", [128, cfg.AW])
        self.tmask_in = dt_in("tmask", [128, T])
        self.w_in = dt_in("w_in", [DEPTH, D, cfg.IN_W])
        self.w_ap = dt_in("w_attn_proj", [DEPTH, cfg.VW, D])
        self.w_cp = dt_in("w_conv_proj", [DEPTH, cfg.CC, D])
        self.w_out = dt_in("w_out", [DEPTH, D, D])
        self.w_fi = dt_in("w_ffn_in", [DEPTH, D, 2 * cfg.DFF])
        self.w_fo = dt_in("w_ffn_out", [DEPTH, cfg.DFF, D])
        self.yT_out = nc.dram_tensor("yT", [D, T], F32, kind="ExternalOutput").ap()
        scr = lambda name, shape, dt: nc.dram_tensor(name, shape, dt).ap()
        self.XT = scr("s_XT", [D, T], F32)
        self.HT = scr("s_HT", [D, T], BF16)
        self.QKT = scr("s_QKT", [2 * cfg.QK, T], BF16)
        self.V = scr("s_V", [T, cfg.VW], BF16)
        self.YT = scr("s_YT", [cfg.CC, T], BF16)
        self.GT = scr("s_GT", [2 * D, T], BF16)
        self.ONT = scr("s_ONT", [cfg.VW, T], BF16)
        self.ZT = scr("s_ZT", [cfg.CC, T], F32)
        self.CT = scr("s_CT", [cfg.CC, T], BF16)
        self.MT = scr("s_MT", [D, T], BF16)
        self.AT = scr("s_AT", [cfg.DFF, T], BF16)

        with ExitStack() as stack:
            P = Prog(nc, stack)
            self.P = P
            GB = 16 * 1024
            gsb = stack.enter_context(nc.sbuf_tensor("gsb", [128, GB], U8))
            G = Alloc(gsb, 0, GB)
            self.cg = G.get([128, cfg.NCG], F32)
            self.cl = G.get([128, cfg.NCL], F32)
            self.ident_bf = G.get([128, 128], BF16)
            self.ones_bf = G.get([128, 128], BF16)
            self.neglam = G.get([128, 2], F32)
            self.gsub = G.get([128, cfg.VD], F32)
            self.lamtmp = G.get([128, 2 * cfg.HD], F32)
            self.lamred = G.get([128, 4], F32)
            self.PH_BYTES = SB_BYTES - GB
            self.phase_no = 0
            self.ph = None
            self.A = None
            self.banks = None
            t0 = P.dma('sp', lambda e: e.dma_start(out=self.cg, in_=self.cg_in), P.ds("c0"))
            t1 = P.op('dve', lambda e: e.tensor_copy(out=self.ident_bf, in_=self.cg[:, cfg.g_ident:cfg.g_ident + 128]), [t0])
            t2 = P.op('pool', lambda e: e.memset(self.ones_bf, 1.0))
            P.barrier()

            xsrc = self.xT_in
            for l in range(DEPTH):
                self.layer_consts(l)
                self.norm_phase(xsrc, cfg.c_gmix, self.HT, final=False)
                self.inproj_phase(l)
                self.attn_phase(l)
                self.conv1_phase(l)
                self.conv2_phase(l)
                self.merge_phase(l)
                self.resid_phase(l, self.w_out[l], self.MT, cfg.D // 128, xsrc, TS=min(2048, T))
                xsrc = self.XT
                self.norm_phase(xsrc, cfg.c_gffn, self.HT, final=False)
                self.ffnin_phase(l)
                self.resid_phase(l, self.w_fo[l], self.AT, cfg.DFF // 128, xsrc, TS=min(1024, T))
            self.norm_phase(xsrc, None, self.yT_out, final=True)
            P.finish()
            P.flush()
        return nc

    def phase_begin(self):
        nc = self.nc
        self.P.barrier()
        self.ph = ExitStack()
        self.phase_no += 1
        sbp = self.ph.enter_context(nc.sbuf_tensor("sb%d" % self.phase_no, [128, self.PH_BYTES], U8))
        psp = self.ph.enter_context(nc.psum_tensor("ps%d" % self.phase_no, [128, 8 * 512], F32))
        self.banks = [psp[:, b * 512:(b + 1) * 512] for b in range(8)]
        self.A = Alloc(sbp, 0, self.PH_BYTES)

    def phase_end(self):
        self.P.barrier()
        self.P.flush()
        self.ph.close()
        self.ph = None

    def layer_consts(self, l):
        cfg, P = self.cfg, self.P
        HD = cfg.HD
        P.barrier()
        t0 = P.dma('sp', lambda e: e.dma_start(out=self.cl, in_=self.cl_in[l]), P.ds("c0"))
        lq = self.cl[:, cfg.c_lq:cfg.c_lq + 2 * HD]
        lk = self.cl[:, cfg.c_lk:cfg.c_lk + 2 * HD]
        t1 = P.op('dve', lambda e: e.tensor_tensor(out=self.lamtmp, in0=lq, in1=lk, op=ALU.mult), [t0])
        t2 = P.op('dve', lambda e: e.tensor_reduce(
            out=self.lamred[:, 0:2], in_=self.lamtmp.rearrange("p (a b) -> p a b", b=HD),
            axis=mybir.AxisListType.X, op=ALU.add), [t1])
        t3 = P.op('act', lambda e: e.activation(out=self.lamred[:, 2:4], in_=self.lamred[:, 0:2], func=AF.Exp), [t2])
        li = lambda_init(l)
        t4 = P.op('dve', lambda e: e.tensor_tensor(out=self.neglam[:, 0:1], in0=self.lamred[:, 3:4],
                                                   in1=self.lamred[:, 2:3], op=ALU.subtract), [t3])
        t5 = P.op('dve', lambda e: e.tensor_scalar(out=self.neglam[:, 1:2], in0=self.neglam[:, 0:1],
                                                   scalar1=-li, scalar2=None, op0=ALU.add), [t4])
        t6 = P.op('dve', lambda e: e.tensor_scalar(out=self.gsub, in0=self.cl[:, cfg.c_gsub:cfg.c_gsub + cfg.VD],
                                                   scalar1=(1.0 - li), scalar2=None, op0=ALU.mult), [t5])
        P.barrier()

    def norm_phase(self, src, gcol, dst, final):
        cfg, P, A = self.cfg, self.P, self.A
        DC, D = cfg.DC, cfg.D
        self.phase_begin()
        A = self.A
        odt = F32 if final else BF16
        xin = [A.get([128, DC, 512], F32) for _ in range(2)]
        sq = [A.get([128, DC, 512], BF16) for _ in range(2)]
        hout = [A.get([128, DC, 512], odt) for _ in range(2)]
        rstd = [A.get([128, 512], F32) for _ in range(2)]
        gsrc = self.cg if final else self.cl
        gc = cfg.g_gfinal if final else gcol
        dsi = [P.ds("in0"), P.ds("in1")]
        dso = [P.ds("st0"), P.ds("st1")]
        hdone = [None, None]
        mmdone = [None, None]
        stdone = [None, None]
        r2done = [None, None]
        ldt = {}

        def issue_ld(i):
            b = i % 2
            ts = slice(i * 512, (i + 1) * 512)
            ldt[i] = P.dma('sp', lambda e: e.dma_start(out=xin[b], in_=fm(src)[:, :, ts]), dsi[b],
                           (hdone[b] or []))

        issue_ld(0)
        for i in range(cfg.NQ):
            b = i % 2
            ts = slice(i * 512, (i + 1) * 512)
            if i + 1 < cfg.NQ:
                issue_ld(i + 1)
            ld = ldt[i]
            sqt = P.op('act', lambda e, b=b: e.activation(out=sq[b], in_=xin[b], func=AF.Square), [ld, mmdone[b]])
            mm = None
            for c in range(DC):
                mm = P.op('pe', lambda e, b=b, c=c: e.matmul(self.banks[b], self.ones_bf, sq[b][:, c, :],
                                                             start=(c == 0), stop=(c == DC - 1)),
                          [sqt, r2done[b]] if c == 0 else [])
            mmdone[b] = mm
            r1 = P.op('dve', lambda e, b=b: e.tensor_scalar(out=rstd[b], in0=self.banks[b], scalar1=1.0 / D,
                                                            scalar2=NORM_EPS, op0=ALU.mult, op1=ALU.add),
                      [mm] + (hdone[b] or []))
            r2a = P.op('act', lambda e, b=b: e.activation(out=rstd[b], in_=rstd[b], func=AF.Sqrt), [r1])
            r2 = P.op('dve', lambda e, b=b: e.reciprocal(out=rstd[b], in_=rstd[b]), [r2a])
            r2done[b] = r1
            lastd = lastp = None
            for c in range(DC):
                eng = 'dve'
                tk = P.op(eng, lambda e, b=b, c=c: e.scalar_tensor_tensor(
                    out=hout[b][:, c, :], in0=xin[b][:, c, :], scalar=gsrc[:, gc + c:gc + c + 1],
                    in1=rstd[b], op0=ALU.mult, op1=ALU.mult), [r2, stdone[b]])
                if eng == 'dve':
                    lastd = tk
                else:
                    lastp = tk
            hdone[b] = [lastd, lastp]
            stdone[b] = P.dma('sp', lambda e, b=b, ts=ts: e.dma_start(out=fm(dst)[:, :, ts], in_=hout[b]), dso[b],
                              [lastd, lastp])
        self.phase_end()

    def linear(self, ins, TS, blocks):
        cfg, P, A = self.cfg, self.P, self.A
        T = cfg.T
        NS = T // TS
        NHALF = TS // 1024
        KCmax = max(kc for _, kc in ins)
        FBmax = max(sum(p[2] for p in b['pieces']) for b in blocks)
        in_sb = [A.get([128, kc, TS], BF16) for _, kc in ins]
        wbuf = [A.get([128, KCmax, FBmax], BF16) for _ in range(2)]
        dsw = [P.ds("w0"), P.ds("w1")]
        dsin = [P.ds("in0"), P.ds("in1")]
        wfree = [None, None]
        bankfree = [[None] * 4, [None] * 4]
        seq = [(s, bi) for s in range(NS) for bi in range(len(blocks))]
        wtok = {}

        def issue_w(n):
            s, bi = seq[n]
            wb = n % 2
            c0 = 0
            tk = None
            for (w2d, col0, ncols, in_idx) in blocks[bi]['pieces']:
                kc = ins[in_idx][1]
                tk = P.dma('pool', lambda e, wb=wb, c0=c0, w2d=w2d, col0=col0, ncols=ncols, kc=kc: e.dma_start(
                    out=wbuf[wb][:, 0:kc, c0:c0 + ncols],
                    in_=w2d.rearrange("(c p) f -> p c f", p=128)[:, :, col0:col0 + ncols]), dsw[wb], [wfree[wb]])
                c0 += ncols
            wtok[n] = tk

        issue_w(0)
        ucount = 0
        intok = [None] * len(ins)
        lastpe_all = None
        for n, (s, bi) in enumerate(seq):
            blk = blocks[bi]
            if bi == 0:
                for ii, (src, kc) in enumerate(ins):
                    tk = None
                    nsp = max(1, kc // 8)
                    for q in range(nsp):
                        cs = slice(q * kc // nsp, (q + 1) * kc // nsp)
                        tk = P.dma('sp', lambda e, ii=ii, cs=cs, s=s, src=src: e.dma_start(
                            out=in_sb[ii][:, cs, :], in_=fm(src)[:, cs, s * TS:(s + 1) * TS]), dsin[ii], [lastpe_all])
                    intok[ii] = tk
            if n + 1 < len(seq):
                issue_w(n + 1)
            wb = n % 2
            FB = sum(p[2] for p in blk['pieces'])
            kind = blk['kind']
            lastpe = None
            if kind == 'vtok':
                kc = ins[0][1]
                for g in range(TS // 512):
                    bs = ucount % 2
                    ucount += 1
                    bk = self.banks[bs * 4:bs * 4 + 4]
                    tok0 = s * TS + g * 512
                    for k in range(kc):
                        for j in range(4):
                            w_ = [wtok[n], intok[0], bankfree[bs][j]] if k == 0 else []
                            lastpe = P.op('pe', lambda e, k=k, j=j, g=g, wb=wb, bk=bk, FB=FB: e.matmul(
                                bk[j][:, 0:FB], in_sb[0][:, k, g * 512 + j * 128:g * 512 + (j + 1) * 128],
                                wbuf[wb][:, k, 0:FB], start=(k == 0), stop=(k == kc - 1)), w_)
                    rel = blk['epi'](0, tok0, bk, lastpe, None)
                    bankfree[bs] = rel
            else:
                nunits = FB // 256
                for u in range(nunits):
                    if kind == 'pair':
                        cols = [u * 128, FB // 2 + u * 128]
                        iidx = [blk['pieces'][0][3], blk['pieces'][1][3]]
                    else:
                        cols = [2 * u * 128, (2 * u + 1) * 128]
                        iidx = [blk['pieces'][0][3]] * 2
                    for h in range(NHALF):
                        bs = ucount % 2
                        ucount += 1
                        bk = self.banks[bs * 4:bs * 4 + 4]
                        tok0 = s * TS + h * 1024
                        pretoks = blk['pre'](u, tok0) if blk.get('pre') else None
                        kcs = [ins[iidx[0]][1], ins[iidx[1]][1]]
                        kc = max(kcs)
                        for k in range(kc):
                            for ci in range(2):
                                if k >= kcs[ci]:
                                    continue
                                for t in range(2):
                                    w_ = [wtok[n], intok[iidx[ci]], bankfree[bs][ci * 2 + t]] if k == 0 else []
                                    lastpe = P.op('pe', lambda e, k=k, ci=ci, t=t, h=h, wb=wb, bk=bk, cols=cols, iidx=iidx, kcs=kcs: e.matmul(
                                        bk[ci * 2 + t], wbuf[wb][:, k, cols[ci]:cols[ci] + 128],
                                        in_sb[iidx[ci]][:, k, h * 1024 + t * 512:h * 1024 + (t + 1) * 512],
                                        start=(k == 0), stop=(k == kcs[ci] - 1)), w_)
                        rel = blk['epi'](u, tok0, bk, lastpe, pretoks)
                        bankfree[bs] = rel
            wfree[wb] = lastpe
            lastpe_all = lastpe

    def stager(self, name, shape, dt, n=2):
        bufs = [self.A.get(shape, dt) for _ in range(n)]
        return {'bufs': bufs, 'tok': [None] * n, 'i': 0, 'ds': [self.P.ds("%s%d" % (name, i)) for i in range(n)]}

    def inproj_phase(self, l):
        cfg, P, A = self.cfg, self.P, self.A
        D, QK, VW, CC = cfg.D, cfg.QK, cfg.VW, cfg.CC
        self.phase_begin()
        A = self.A
        w = self.w_in[l]
        TS = min(2048, cfg.T)
        stg = self.stager("st", [128, 2, 1024], BF16)
        sig = [A.get([128, 1024], F32) for _ in range(2)]
        vst = self.stager("sv", [128, 4, 512], BF16)
        sigtok = [None, None]
        cnt = [0]
        bgc = cfg.c_bgate
        banks = self.banks

        def store2(s, row0s, dst, tok0, evs):
            b = s['i']
            buf = s['bufs'][b]
            t = None
            for ci, r0 in enumerate(row0s):
                t = P.dma('sp', lambda e, ci=ci, r0=r0, buf=buf: e.dma_start(
                    out=dst[r0:r0 + 128, tok0:tok0 + 1024], in_=buf[:, ci, :]), s['ds'][b], evs)
            s['tok'][b] = t
            s['i'] = (b + 1) % len(s['bufs'])

        def epi_copy(row_of_unit, dst):
            def f(u, tok0, bk, lastpe, pre):
                b = stg['i']
                buf = stg['bufs'][b]
                evs = []
                for ci in range(2):
                    for t in range(2):
                        eng = 'act' if (ci + t) % 2 == 0 else 'dve'
                        if eng == 'act':
                            tk = P.op('act', lambda e, ci=ci, t=t, buf=buf, bk=bk: e.activation(
                                out=buf[:, ci, t * 512:(t + 1) * 512], in_=bk[ci * 2 + t], func=AF.Copy),
                                [lastpe, stg['tok'][b]])
                        else:
                            tk = P.op('dve', lambda e, ci=ci, t=t, buf=buf, bk=bk: e.tensor_copy(
                                out=buf[:, ci, t * 512:(t + 1) * 512], in_=bk[ci * 2 + t]),
                                [lastpe, stg['tok'][b]])
                        evs.append(tk)
                r0 = row_of_unit(u)
                store2(stg, [r0, r0 + 128], dst, tok0, evs)
                return evs
            return f

        def epi_gates(row_of_unit):
            def f(u, tok0, bk, lastpe, pre):
                b = stg['i']
                buf = stg['bufs'][b]
                evs = []
                r0 = row_of_unit(u)
                for ci in range(2):
                    col = bgc + (r0 + ci * 128) // 128
                    for t in range(2):
                        tk = P.op('act', lambda e, ci=ci, t=t, buf=buf, bk=bk, col=col: e.activation(
                            out=buf[:, ci, t * 512:(t + 1) * 512], in_=bk[ci * 2 + t], func=AF.Sigmoid,
                            bias=self.cl[:, col:col + 1]), [lastpe, stg['tok'][b]])
                        evs.append(tk)
                store2(stg, [r0, r0 + 128], self.GT, tok0, evs)
                return evs
            return f

        def epi_glu(row_of_unit):
            def f(u, tok0, bk, lastpe, pre):
                b = stg['i']
                buf = stg['bufs'][b]
                sb_ = cnt[0] % 2
                cnt[0] += 1
                rel = [None] * 4
                evs = []
                for t in range(2):
                    ta = P.op('act', lambda e, t=t, bk=bk, sb_=sb_: e.activation(
                        out=sig[sb_][:, t * 512:(t + 1) * 512], in_=bk[2 + t], func=AF.Sigmoid),
                        [lastpe, sigtok[sb_]])
                    td = P.op('dve', lambda e, t=t, bk=bk, buf=buf, sb_=sb_: e.tensor_tensor(
                        out=buf[:, 0, t * 512:(t + 1) * 512], in0=bk[t], in1=sig[sb_][:, t * 512:(t + 1) * 512],
                        op=ALU.mult), [ta, stg['tok'][b]])
                    rel[2 + t] = ta
                    rel[t] = td
                    evs.append(td)
                sigtok[sb_] = evs[-1]
                r0 = row_of_unit(u)
                store2(stg, [r0], self.YT, tok0, evs)
                return rel
            return f

        def epi_v(col0):
            def f(u, tok0, bk, lastpe, pre):
                b = vst['i']
                buf = vst['bufs'][b]
                FB = min(512, VW)
                evs = []
                for j in range(4):
                    if j % 2 == 0:
                        tk = P.op('act', lambda e, j=j, buf=buf, bk=bk: e.activation(
                            out=buf[:, j, 0:FB], in_=bk[j][:, 0:FB], func=AF.Copy), [lastpe, vst['tok'][b]])
                    else:
                        tk = P.op('dve', lambda e, j=j, buf=buf, bk=bk: e.tensor_copy(
                            out=buf[:, j, 0:FB], in_=bk[j][:, 0:FB]), [lastpe, vst['tok'][b]])
                    evs.append(tk)
                vst['tok'][b] = P.dma('sp', lambda e, buf=buf: e.dma_start(
                    out=self.V[tok0:tok0 + 512, col0:col0 + FB].rearrange("(j p) f -> p j f", p=128),
                    in_=buf[:, :, 0:FB]), vst['ds'][b], evs)
                vst['i'] = (b + 1) % 2
                return evs
            return f

        blocks = []
        for c0 in range(0, 2 * QK, 512):
            blocks.append(dict(pieces=[(w, c0, 512, 0)], kind='single',
                               epi=epi_copy(lambda u, c0=c0: c0 + u * 256, self.QKT)))
        FBv = min(512, VW)
        for c0 in range(0, VW, FBv):
            blocks.append(dict(pieces=[(w, 2 * QK + c0, FBv, 0)], kind='vtok', epi=epi_v(c0)))
        ub = 2 * QK + VW
        for c0 in range(0, CC, 256):
            blocks.append(dict(pieces=[(w, ub + c0, 256, 0), (w, ub + CC + c0, 256, 0)], kind='pair',
                               epi=epi_glu(lambda u, c0=c0: c0 + u * 128)))
        gb = ub + 2 * CC
        for c0 in range(0, 2 * D, 512):
            blocks.append(dict(pieces=[(w, gb + c0, 512, 0)], kind='single',
                               epi=epi_gates(lambda u, c0=c0: c0 + u * 256)))
        self.linear([(self.HT, cfg.DC)], TS, blocks)
        self.phase_end()

    def merge_phase(self, l):
        cfg, P, A = self.cfg, self.P, self.A
        D = cfg.D
        self.phase_begin()
        A = self.A
        TS = min(1024, cfg.T)
        stg = self.stager("st", [128, 1, 1024], BF16)
        gbuf = [[A.get([128, 1024], BF16) for _ in range(2)] for _ in range(2)]
        gds = [P.ds("g0"), P.ds("g1")]
        gfree = [None, None]
        t1 = [A.get([128, 1024], F32) for _ in range(2)]
        t2 = [A.get([128, 1024], F32) for _ in range(2)]
        tfree = [None, None]
        cnt = [0]
        slot_of = {}

        def pre(row0):
            def f(u, tok0):
                s_ = cnt[0] % 2
                cnt[0] += 1
                r = row0 + u * 128
                tk = None
                for gi in range(2):
                    tk = P.dma('sp', lambda e, gi=gi, r=r, s_=s_: e.dma_start(
                        out=gbuf[s_][gi], in_=self.GT[gi * D + r:gi * D + r + 128, tok0:tok0 + 1024]),
                        gds[s_], [gfree[s_]])
                return (s_, tk)
            return f

        def epi(row0):
            def f(u, tok0, bk, lastpe, pretoks):
                s_, gtok = pretoks
                b = stg['i']
                buf = stg['bufs'][b]
                r = row0 + u * 128
                col = cfg.c_bcproj + r // 128
                rel = [None] * 4
                evs = []
                for t in range(2):
                    sl = slice(t * 512, (t + 1) * 512)
                    ta = P.op('dve', lambda e, t=t, sl=sl, bk=bk, s_=s_: e.tensor_tensor(
                        out=t1[s_][:, sl], in0=bk[t], in1=gbuf[s_][0][:, sl], op=ALU.mult),
                        [lastpe, gtok, tfree[s_]])
                    tb = P.op('dve', lambda e, t=t, sl=sl, bk=bk, s_=s_, col=col: e.scalar_tensor_tensor(
                        out=t2[s_][:, sl], in0=bk[2 + t], scalar=self.cl[:, col:col + 1], in1=gbuf[s_][1][:, sl],
                        op0=ALU.add, op1=ALU.mult), [lastpe, gtok, tfree[s_]])
                    tc = P.op('pool', lambda e, sl=sl, buf=buf, s_=s_: e.tensor_tensor(
                        out=buf[:, 0, sl], in0=t1[s_][:, sl], in1=t2[s_][:, sl], op=ALU.add),
                        [ta, tb, stg['tok'][b]])
                    rel[t] = ta
                    rel[2 + t] = tb
                    evs.append(tc)
                tfree[s_] = evs[-1]
                gfree[s_] = rel[3]
                stg['tok'][b] = P.dma('sp', lambda e, buf=buf, r=r: e.dma_start(
                    out=self.MT[r:r + 128, tok0:tok0 + 1024], in_=buf[:, 0, :]), stg['ds'][b], evs)
                stg['i'] = (b + 1) % 2
                return rel
            return f

        blocks = []
        for c0 in range(0, D, 256):
            blocks.append(dict(pieces=[(self.w_ap[l], c0, 256, 0), (self.w_cp[l], c0, 256, 1)], kind='pair',
                               pre=pre(c0), epi=epi(c0)))
        self.linear([(self.ONT, cfg.VW // 128), (self.CT, cfg.CCc)], TS, blocks)
        self.phase_end()

    def resid_phase(self, l, w2d, src, KC, xsrc, TS):
        cfg, P, A = self.cfg, self.P, self.A
        D = cfg.D
        self.phase_begin()
        A = self.A
        xold = [[A.get([128, 1024], F32) for _ in range(2)] for _ in range(2)]
        xds = [P.ds("g0"), P.ds("g1")]
        xfree = [None, None]
        xnew = self.stager("st", [128, 2, 1024], F32)
        tmp = [A.get([128, 1024], F32) for _ in range(2)]
        tmpfree = [None, None]
        cnt = [0]

        def pre(row0):
            def f(u, tok0):
                s_ = cnt[0] % 2
                cnt[0] += 1
                tk = None
                for ci in range(2):
                    r = row0 + u * 256 + ci * 128
                    tk = P.dma('sp', lambda e, ci=ci, r=r, s_=s_: e.dma_start(
                        out=xold[s_][ci], in_=xsrc[r:r + 128, tok0:tok0 + 1024]), xds[s_], [xfree[s_]])
                return (s_, tk)
            return f

        def epi(row0):
            def f(u, tok0, bk, lastpe, pretoks):
                s_, xtok = pretoks
                b = xnew['i']
                buf = xnew['bufs'][b]
                rel = [None] * 4
                evs = []
                for t in range(2):
                    sl = slice(t * 512, (t + 1) * 512)
                    ta = P.op('dve', lambda e, t=t, sl=sl, bk=bk, buf=buf, s_=s_: e.tensor_tensor(
                        out=buf[:, 0, sl], in0=bk[t], in1=xold[s_][0][:, sl], op=ALU.add),
                        [lastpe, xtok, xnew['tok'][b]])
                    tb = P.op('act', lambda e, t=t, sl=sl, bk=bk, s_=s_: e.activation(
                        out=tmp[s_][:, sl], in_=bk[2 + t], func=AF.Copy), [lastpe, tmpfree[s_]])
                    tc = P.op('pool', lambda e, sl=sl, buf=buf, s_=s_: e.tensor_tensor(
                        out=buf[:, 1, sl], in0=tmp[s_][:, sl], in1=xold[s_][1][:, sl], op=ALU.add),
                        [tb, xtok, xnew['tok'][b]])
                    rel[t] = ta
                    rel[2 + t] = tb
                    evs += [ta, tc]
                tmpfree[s_] = evs[-1]
                xfree[s_] = [evs[-1], evs[-2]]
                t_ = None
                for ci in range(2):
                    r = row0 + u * 256 + ci * 128
                    t_ = P.dma('sp', lambda e, ci=ci, r=r, buf=buf: e.dma_start(
                        out=self.XT[r:r + 128, tok0:tok0 + 1024], in_=buf[:, ci, :]), xnew['ds'][b], evs)
                xnew['tok'][b] = t_
                xnew['i'] = (b + 1) % 2
                return rel
            return f

        blocks = []
        FB = 256 if KC > 16 else min(512, D)
        for c0 in range(0, D, FB):
            blocks.append(dict(pieces=[(w2d, c0, FB, 0)], kind='single', pre=pre(c0), epi=epi(c0)))
        self._flatten_fix = True
        self.linear([(src, KC)], TS, blocks)
        self.phase_end()

    def ffnin_phase(self, l):
        cfg, P, A = self.cfg, self.P, self.A
        DFF = cfg.DFF
        self.phase_begin()
        A = self.A
        TS = min(2048, cfg.T)
        stg = self.stager("st", [128, 1, 1024], BF16)
        sil = [A.get([128, 1024], F32) for _ in range(2)]
        silfree = [None, None]
        cnt = [0]
        w = self.w_fi[l]

        def epi(row0):
            def f(u, tok0, bk, lastpe, pre):
                b = stg['i']
                buf = stg['bufs'][b]
                s_ = cnt[0] % 2
                cnt[0] += 1
                rel = [None] * 4
                evs = []
                for t in range(2):
                    sl = slice(t * 512, (t + 1) * 512)
                    ta = P.op('act', lambda e, t=t, sl=sl, bk=bk, s_=s_: e.activation(
                        out=sil[s_][:, sl], in_=bk[t], func=AF.Silu), [lastpe, silfree[s_]])
                    td = P.op('dve', lambda e, t=t, sl=sl, bk=bk, buf=buf, s_=s_: e.tensor_tensor(
                        out=buf[:, 0, sl], in0=bk[2 + t], in1=sil[s_][:, sl], op=ALU.mult), [ta, stg['tok'][b]])
                    rel[t] = ta
                    rel[2 + t] = td
                    evs.append(td)
                silfree[s_] = evs[-1]
                r = row0 + u * 128
                stg['tok'][b] = P.dma('sp', lambda e, buf=buf, r=r: e.dma_start(
                    out=self.AT[r:r + 128, tok0:tok0 + 1024], in_=buf[:, 0, :]), stg['ds'][b], evs)
                stg['i'] = (b + 1) % 2
                return rel
            return f

        blocks = []
        for c0 in range(0, DFF, 256):
            blocks.append(dict(pieces=[(w, c0, 256, 0), (w, DFF + c0, 256, 0)], kind='pair', epi=epi(c0)))
        self.linear([(self.HT, cfg.DC)], TS, blocks)
        self.phase_end()

    def attn_phase(self, l):
        cfg, P, A = self.cfg, self.P, self.A
        T, H, HD, VD, NK, NQ = cfg.T, cfg.H, cfg.HD, cfg.VD, cfg.NK, cfg.NQ
        self.phase_begin()
        A = self.A
        banks = self.banks
        scale = HD ** -0.5
        atab = A.get([128, cfg.AW], F32)
        qb = [A.get([128, 2, T], BF16) for _ in range(2)]
        kb = [A.get([128, 2, T], BF16) for _ in range(2)]
        vb = [A.get([128, NK, VD + 1], BF16) for _ in range(2)]
        NSB = 3
        NTMP = 6
        NPT = 8
        LA = 6
        sbanks = [4, 5, 7]
        tmp = [A.get([128, 512], F32) for _ in range(NTMP)]
        pt = [A.get([128, 512], BF16) for _ in range(NPT)]
        Om = [A.get([128, 4, VD], F32) for _ in range(2)]
        o = A.get([128, 4, VD], F32)
        junk = A.get([128, VD], F32)
        on = A.get([128, 4, VD], BF16)
        ost = [A.get([128, VD // 128, 512], BF16) for _ in range(2)]
        rec = A.get([128, 8], F32)
        ssq = A.get([128, 4], F32)
        rs2 = A.get([128, 4], F32)
        tb = banks[6].bitcast(BF16)
        ta = P.dma('sp', lambda e: e.dma_start(out=atab, in_=self.atab_in), P.ds("c0"))
        ones_tok = []
        for i in range(2):
            ones_tok.append(P.op('pool', lambda e, i=i: e.memset(vb[i][:, :, VD:VD + 1], 1.0)))
        dsl = [P.ds("in0"), P.ds("in1")]
        dso = [P.ds("st0"), P.ds("st1")]
        loadtok = [None, None]
        headdone = [None, None]

        def load_head(h):
            b = h % 2
            w_ = [headdone[b], ones_tok[b]]
            P.dma('sp', lambda e: e.dma_start(out=qb[b], in_=fm(self.QKT)[:, 2 * h:2 * h + 2, :]), dsl[b], w_)
            P.dma('sp', lambda e: e.dma_start(out=kb[b], in_=fm(self.QKT)[:, cfg.QK // 128 + 2 * h:cfg.QK // 128 + 2 * h + 2, :]),
                  dsl[b], w_)
            loadtok[b] = P.dma('sp', lambda e: e.dma_start(
                out=vb[b][:, :, 0:VD], in_=self.V.rearrange("(k p) f -> p k f", p=128)[:, :, h * VD:(h + 1) * VD]),
                dsl[b], w_)

        load_head(0)
        st = {'n': 0, 'sdve': [None] * NSB, 'tact': [None] * NTMP, 'ptpe': [None] * NPT,
              'accfree': [None] * 4, 'omfree': [None, None], 'ofree': None, 'onfree': None,
              'ostfree': [None, None], 'tbfree': None, 'osti': 0, 'ssqfree': None}
        deferred = []

        def run_head(h):
            hb = h % 2
            if h + 1 < H:
                load_head(h + 1)
            slope = 2.0 ** (-8.0 * (h + 1) / H)
            dmin = 80.0 / slope
            kts_of = {}
            for qc in range(NQ):
                keep = []
                for kt in range(NK):
                    dist = max(0, kt * 128 - (qc * 512 + 511), qc * 512 - (kt * 128 + 127))
                    if dist < dmin:
                        keep.append(kt)
                kts_of[qc] = keep
            steps = [(qc, m, kt) for qc in range(NQ) for m in range(2) for kt in kts_of[qc]]
            S_tok = {}
            P_tok = {}

            def emit_S(i):
                qc, m, kt = steps[i]
                n = st['n'] + i
                sbk = sbanks[n % NSB]
                s_tok = P.op('pe', lambda e: e.matmul(
                    banks[sbk], kb[hb][:, m, kt * 128:(kt + 1) * 128], qb[hb][:, m, qc * 512:(qc + 1) * 512],
                    start=True, stop=True), [loadtok[hb], st['sdve'][n % NSB]])
                u0 = 512 * qc - 128 * kt + cfg.AOFF
                t_tok = P.op('dve', lambda e: e.scalar_tensor_tensor(
                    out=tmp[n % NTMP], in0=atab[:, u0:u0 + 512], scalar=slope / scale, in1=banks[sbk],
                    op0=ALU.mult, op1=ALU.add), [s_tok, st['tact'][n % NTMP], ta])
                st['sdve'][n % NSB] = t_tok
                kcol = cfg.g_kmask + kt
                p_tok = P.op('act', lambda e: e.activation(
                    out=pt[n % NPT], in_=tmp[n % NTMP], func=AF.Exp, bias=self.cg[:, kcol:kcol + 1], scale=scale),
                    [t_tok, st['ptpe'][n % NPT]])
                st['tact'][n % NTMP] = p_tok
                P_tok[i] = p_tok

            def emit_PV(i):
                qc, m, kt = steps[i]
                n = st['n'] + i
                last = None
                for qs in range(4):
                    w_ = [P_tok[i]]
                    first = (kt == kts_of[qc][0])
                    lastk = (kt == kts_of[qc][-1])
                    if first:
                        w_.append(st['accfree'][qs])
                    last = P.op('pe', lambda e, qs=qs, first=first, lastk=lastk: e.matmul(
                        banks[qs][:, 0:VD + 1], pt[n % NPT][:, qs * 128:(qs + 1) * 128], vb[hb][:, kt, :],
                        start=first, stop=lastk), w_)
                st['ptpe'][n % NPT] = last
                if kt == kts_of[qc][-1]:
                    finish_group(qc, m, last)
                return last

            def finish_group(qc, m, lastpe):
                r_tok = []
                for qs in range(4):
                    c_ = P.op('dve', lambda e, qs=qs: e.tensor_scalar(
                        out=rec[:, m * 4 + qs:m * 4 + qs + 1], in0=banks[qs][:, VD:VD + 1], scalar1=1e-30,
                        scalar2=None, op0=ALU.max), [lastpe, st['omfree'][m]])
                    r_tok.append(P.op('dve', lambda e, qs=qs: e.reciprocal(
                        out=rec[:, m * 4 + qs:m * 4 + qs + 1], in_=rec[:, m * 4 + qs:m * 4 + qs + 1]), [c_]))
                ev = []
                for qs in range(4):
                    tk = P.op('dve', lambda e, qs=qs: e.tensor_scalar(
                        out=Om[m][:, qs, :], in0=banks[qs][:, 0:VD], scalar1=rec[:, m * 4 + qs:m * 4 + qs + 1],
                        scalar2=None, op0=ALU.mult), [r_tok[qs], st['omfree'][m]])
                    st['accfree'][qs] = tk
                    ev.append(tk)
                if m == 0:
                    st['om0'] = ev[-1]
                    return
                c_tok = P.op('dve', lambda e: e.scalar_tensor_tensor(
                    out=o, in0=Om[1], scalar=self.neglam[:, 1:2], in1=Om[0], op0=ALU.mult, op1=ALU.add),
                    [ev[-1], st['om0'], st['ofree']])
                st['omfree'] = [c_tok, c_tok]
                z_tok = P.op('pool', lambda e: e.memset(ssq, 0.0), [st['ssqfree']])
                sq_tok = None
                for qs in range(4):
                    sq_tok = P.op('act', lambda e, qs=qs: e.activation(
                        out=junk, in_=o[:, qs, :], func=AF.Square, accum_out=ssq[:, qs:qs + 1]),
                        [c_tok, z_tok, sq_tok])
                r1 = P.op('dve', lambda e: e.tensor_scalar(out=rs2, in0=ssq, scalar1=1.0 / VD, scalar2=NORM_EPS,
                                                           op0=ALU.mult, op1=ALU.add), [sq_tok, st['onfree']])
                r2a = P.op('act', lambda e: e.activation(out=rs2, in_=rs2, func=AF.Sqrt), [r1])
                r2 = P.op('dve', lambda e: e.reciprocal(out=rs2, in_=rs2), [r2a])
                st['ssqfree'] = r1
                n_tok = None
                for qs in range(4):
                    n_tok = P.op('dve', lambda e, qs=qs: e.scalar_tensor_tensor(
                        out=on[:, qs, :], in0=o[:, qs, :], scalar=rs2[:, qs:qs + 1], in1=self.gsub,
                        op0=ALU.mult, op1=ALU.mult), [r2, st['onfree']])
                st['ofree'] = n_tok

                def do_transposes(n_tok=n_tok, qc=qc, h=h):
                    tl = None
                    for vc in range(VD // 128):
                        for qs in range(4):
                            i_ = vc * 4 + qs
                            tl = P.op('pe', lambda e, vc=vc, qs=qs, i_=i_: e.transpose(
                                out=tb[:, i_ * 128:(i_ + 1) * 128], in_=on[:, qs, vc * 128:(vc + 1) * 128],
                                identity=self.ident_bf), [n_tok, st['tbfree']])
                    ob = st['osti']
                    st['osti'] = 1 - ob
                    c_ = P.op('act', lambda e: e.activation(
                        out=ost[ob].rearrange("p a b -> p (a b)"), in_=tb, func=AF.Copy), [tl, st['ostfree'][ob]])
                    st['tbfree'] = c_
                    st['onfree'] = tl
                    st['ostfree'][ob] = P.dma('sp', lambda e: e.dma_start(
                        out=fm(self.ONT)[:, h * (VD // 128):(h + 1) * (VD // 128), qc * 512:(qc + 1) * 512],
                        in_=ost[ob]), dso[ob], [c_])
                deferred.append([6, do_transposes])

            ns = len(steps)
            for i0 in range(min(LA, ns)):
                emit_S(i0)
            lastpv = None
            for i in range(ns):
                lastpv = emit_PV(i)
                if i + LA < ns:
                    emit_S(i + LA)
                for d in deferred:
                    d[0] -= 1
                for d in [d for d in deferred if d[0] <= 0]:
                    d[1]()
                    deferred.remove(d)
            st['n'] += ns
            headdone[hb] = lastpv

        for h in range(H):
            run_head(h)
        for d in deferred:
            d[1]()
        self.phase_end()

    def conv1_phase(self, l):
        cfg, P, A = self.cfg, self.P, self.A
        T, CW, CCc = cfg.T, cfg.CW, cfg.CCc
        PAD = (CW - 1) // 2
        self.phase_begin()
        A = self.A
        banks = self.banks
        ypad = [A.get([128, T + 2 * PAD + 2], BF16) for _ in range(2)]
        diag = [A.get([128, CW, 128], BF16) for _ in range(2)]
        zst = [A.get([128, T], F32) for _ in range(2)]
        dsl = [P.ds("in0"), P.ds("in1")]
        dso = [P.ds("st0"), P.ds("st1")]
        halo = []
        for i in range(2):
            halo.append(P.op('pool', lambda e, i=i: e.memset(ypad[i][:, 0:PAD], 0.0)))
            halo.append(P.op('pool', lambda e, i=i: e.memset(ypad[i][:, PAD + T:PAD + T + PAD], 0.0)))
        pedone = [None, None]
        stdone = [None, None]
        bankfree = [None] * 8
        bn = 0
        ldtok = {}
        tm = A.get([128, T], BF16)
        tmtok = P.dma("pool", lambda e: e.dma_start(out=tm, in_=self.tmask_in), P.ds("pm"))

        def load(c):
            b = c % 2
            t_ = P.dma('sp', lambda e: e.dma_start(out=ypad[b][:, PAD:PAD + T], in_=self.YT[c * 128:(c + 1) * 128, :]),
                       dsl[b], [pedone[b]] + halo)
            ldtok[c] = P.op('pool' if c % 2 else 'dve', lambda e: e.tensor_tensor(
                out=ypad[b][:, PAD:PAD + T], in0=ypad[b][:, PAD:PAD + T], in1=tm, op=ALU.mult), [t_, tmtok])

        dltok = {}

        def build_diag(c):
            b = c % 2
            dl = [None, None]
            for j in range(CW):
                eng = 'dve' if j % 3 != 2 else 'pool'
                col = cfg.c_wdw + c * CW + j
                dl[0 if eng == 'dve' else 1] = P.op(eng, lambda e, j=j, col=col, b=b: e.tensor_scalar(
                    out=diag[b][:, j, :], in0=self.ident_bf, scalar1=self.cl[:, col:col + 1], scalar2=None,
                    op0=ALU.mult), [pedone[b]])
            dltok[c] = dl

        load(0)
        build_diag(0)
        for c in range(CCc):
            b = c % 2
            if c + 1 < CCc:
                load(c + 1)
                build_diag(c + 1)
            dl = dltok[c]
            evs = []
            last = None
            for tt in range(T // 512):
                bk = bn % 8
                bn += 1
                for j in range(CW):
                    w_ = [ldtok[c], dl[0], dl[1], bankfree[bk]] if j == 0 else []
                    last = P.op('pe', lambda e, j=j, tt=tt, bk=bk, b=b: e.matmul(
                        banks[bk], diag[b][:, j, :], ypad[b][:, tt * 512 + j:tt * 512 + j + 512],
                        start=(j == 0), stop=(j == CW - 1)), w_)
                bcol = cfg.c_bdw + c
                if tt % 2 == 0:
                    tk = P.op('act', lambda e, tt=tt, bk=bk, bcol=bcol, b=b: e.activation(
                        out=zst[b][:, tt * 512:(tt + 1) * 512], in_=banks[bk], func=AF.Identity,
                        bias=self.cl[:, bcol:bcol + 1]), [last, stdone[b]])
                else:
                    tk = P.op('dve', lambda e, tt=tt, bk=bk, bcol=bcol, b=b: e.tensor_scalar(
                        out=zst[b][:, tt * 512:(tt + 1) * 512], in0=banks[bk], scalar1=self.cl[:, bcol:bcol + 1],
                        scalar2=None, op0=ALU.add), [last, stdone[b]])
                bankfree[bk] = tk
                evs.append(tk)
            pedone[b] = last
            stdone[b] = P.dma('sp', lambda e, c=c, b=b: e.dma_start(out=self.ZT[c * 128:(c + 1) * 128, :], in_=zst[b]),
                              dso[b], evs)
        self.phase_end()

    def conv2_phase(self, l):
        cfg, P, A = self.cfg, self.P, self.A
        T, CCc, CC = cfg.T, cfg.CCc, cfg.CC
        self.phase_begin()
        A = self.A
        banks = self.banks
        z = [A.get([128, CCc, 512], F32) for _ in range(2)]
        zb = A.get([128, CCc, 512], BF16)
        zs = A.get([128, CCc, 512], BF16)
        mean = A.get([128, 512], F32)
        msq = A.get([128, 512], F32)
        rstd = A.get([128, 512], F32)
        t1 = [A.get([128, 512], F32) for _ in range(4)]
        outb = [A.get([128, CCc, 512], BF16) for _ in range(2)]
        dsl = [P.ds("in0"), P.ds("in1")]
        dso = [P.ds("st0"), P.ds("st1")]
        zfree = [None, None]
        stdone = [None, None]
        pedone = None
        statfree = None
        t1free = [None] * 4
        ld = {}

        def load(i):
            b = i % 2
            ld[i] = P.dma('sp', lambda e: e.dma_start(out=z[b], in_=fm(self.ZT)[:, :, i * 512:(i + 1) * 512]),
                          dsl[b], zfree[b] or [])

        load(0)
        nt = 0
        for i in range(T // 512):
            b = i % 2
            if i + 1 < T // 512:
                load(i + 1)
            a1 = P.op('act', lambda e, b=b: e.activation(out=zb, in_=z[b], func=AF.Copy), [ld[i], pedone])
            a2 = P.op('act', lambda e, b=b: e.activation(out=zs, in_=z[b], func=AF.Square), [ld[i], pedone])
            mm = None
            for c in range(CCc):
                mm = P.op('pe', lambda e, c=c: e.matmul(banks[0], self.ones_bf, zb[:, c, :], start=(c == 0),
                                                        stop=(c == CCc - 1)), [a1, statfree] if c == 0 else [])
            for c in range(CCc):
                mm = P.op('pe', lambda e, c=c: e.matmul(banks[1], self.ones_bf, zs[:, c, :], start=(c == 0),
                                                        stop=(c == CCc - 1)), [a2] if c == 0 else [])
            pedone = mm
            s1 = P.op('dve', lambda e: e.tensor_scalar(out=mean, in0=banks[0], scalar1=1.0 / CC, scalar2=None,
                                                       op0=ALU.mult), [mm] + (zfree[1 - b] or []))
            s2 = P.op('dve', lambda e: e.tensor_tensor(out=msq, in0=mean, in1=mean, op=ALU.mult), [s1])
            s3 = P.op('dve', lambda e: e.scalar_tensor_tensor(out=rstd, in0=banks[1], scalar=1.0 / CC, in1=msq,
                                                              op0=ALU.mult, op1=ALU.subtract), [s2])
            s4a = P.op('dve', lambda e: e.tensor_scalar(out=rstd, in0=rstd, scalar1=LN_EPS, scalar2=None,
                                                        op0=ALU.add), [s3])
            s4b = P.op('act', lambda e: e.activation(out=rstd, in_=rstd, func=AF.Sqrt), [s4a])
            s4 = P.op('dve', lambda e: e.reciprocal(out=rstd, in_=rstd), [s4b])
            statfree = s3
            lasts = []
            for c in range(CCc):
                eng = 'dve' if c % 3 != 2 else 'pool'
                tb_ = nt % 4
                nt += 1
                u1 = P.op(eng, lambda e, c=c, tb_=tb_, b=b: e.tensor_tensor(out=t1[tb_], in0=z[b][:, c, :], in1=mean,
                                                                       op=ALU.subtract), [s4, t1free[tb_]])
                u2 = P.op(eng, lambda e, c=c, tb_=tb_: e.tensor_tensor(out=t1[tb_], in0=t1[tb_], in1=rstd,
                                                                       op=ALU.mult), [u1])
                gcol = cfg.c_gcln + c
                bcol = cfg.c_bcln + c
                u3 = P.op('act', lambda e, c=c, tb_=tb_, gcol=gcol, bcol=bcol, b=b: e.activation(
                    out=outb[b][:, c, :], in_=t1[tb_], func=AF.Silu, bias=self.cl[:, bcol:bcol + 1],
                    scale=self.cl[:, gcol:gcol + 1]), [u2, stdone[b]])
                t1free[tb_] = u3
                lasts.append(u3)
            zfree[b] = [lasts[-1]]
            stdone[b] = P.dma('sp', lambda e, i=i, b=b: e.dma_start(out=fm(self.CT)[:, :, i * 512:(i + 1) * 512], in_=outb[b]),
                              dso[b], [lasts[-1]])
        self.phase_end()


def _consts(cfg, inp):
    DEPTH = cfg.DEPTH
    cl = np.zeros((DEPTH, 128, cfg.NCL), np.float32)

    def pc(v):
        return np.ascontiguousarray(v.reshape(-1, 128).T)

    for l in range(DEPTH):
        c = cl[l]
        c[:, cfg.c_gmix:cfg.c_gmix + cfg.DC] = pc(inp["g_mix"][l])
        c[:, cfg.c_bgate:cfg.c_bgate + 2 * cfg.DC] = pc(inp["b_gate"][l])
        c[:, cfg.c_bdw:cfg.c_bdw + cfg.CCc] = pc(inp["b_dw"][l])
        c[:, cfg.c_gcln:cfg.c_gcln + cfg.CCc] = pc(inp["g_conv_ln"][l])
        c[:, cfg.c_bcln:cfg.c_bcln + cfg.CCc] = pc(inp["b_conv_ln"][l])
        c[:, cfg.c_bcproj:cfg.c_bcproj + cfg.DC] = pc(inp["b_conv_proj"][l])
        c[:, cfg.c_gffn:cfg.c_gffn + cfg.DC] = pc(inp["g_ffn"][l])
        wd = inp["w_dw"][l][:, 0, :]
        wd = wd.T.reshape(cfg.CCc, 128, cfg.CW).transpose(1, 0, 2).reshape(128, cfg.CCc * cfg.CW)
        c[:, cfg.c_wdw:cfg.c_wdw + cfg.CCc * cfg.CW] = wd
        c[:, cfg.c_gsub:cfg.c_gsub + cfg.VD] = inp["g_subln"][l][None, :]
        c[:, cfg.c_lq:cfg.c_lq + 2 * cfg.HD] = inp["lambda_q"][l].reshape(1, -1)
        c[:, cfg.c_lk:cfg.c_lk + 2 * cfg.HD] = inp["lambda_k"][l].reshape(1, -1)
    return cl


def _globals(cfg, g_final, nvalid):
    cg = np.zeros((128, cfg.NCG), np.float32)
    cg[:, cfg.g_gfinal:cfg.g_gfinal + cfg.DC] = g_final.reshape(-1, 128).T
    cg[:, cfg.g_ident:cfg.g_ident + 128] = np.eye(128, dtype=np.float32)
    kpos = np.arange(cfg.T).reshape(cfg.NK, 128).T
    cg[:, cfg.g_kmask:cfg.g_kmask + cfg.NK] = np.where(kpos < nvalid, 0.0, MASK_NEG)
    return cg


def _atab(cfg):
    p = np.arange(128)[:, None]
    u = np.arange(cfg.AW)[None, :]
    return (-np.abs(u - cfg.AOFF - p)).astype(np.float32)


_NC_CACHE = {}


def run_trunk(cfg, seqs, inp, n_cores=8):
    key = (cfg.T, cfg.D, cfg.H, cfg.DFF, cfg.DEPTH)
    if key not in _NC_CACHE:
        _NC_CACHE[key] = Builder(cfg).build()
    nc = _NC_CACHE[key]
    cl = _consts(cfg, inp)
    atab = _atab(cfg)
    wmap = {k: np.ascontiguousarray(inp[k], dtype=np.float32) for k in
            ("w_in", "w_attn_proj", "w_conv_proj", "w_out", "w_ffn_in", "w_ffn_out")}
    in_maps = []
    if n_cores == 8 and len(seqs) == 6:
        core_of_seq = [0, 4, 1, 2, 5, 6]
    else:
        core_of_seq = list(range(len(seqs)))
    seq_of_core = {c: i for i, c in enumerate(core_of_seq)}
    for c in range(n_cores):
        xT = np.zeros((cfg.D, cfg.T), np.float32)
        if c in seq_of_core:
            s = seqs[seq_of_core[c]]
            S = s.shape[0]
            xT[:, :S] = s.T
        else:
            S = cfg.T
        tmask = np.zeros((128, cfg.T), np.float32)
        tmask[:, :S] = 1.0
        m = {"xT": xT, "cl": cl, "cg": _globals(cfg, inp["g_final"], S), "atab": atab, "tmask": tmask}
        m.update(wmap)
        in_maps.append(m)
    res = run_bass_kernel_spmd(nc, in_maps, core_ids=list(range(n_cores)))
    outs = []
    for i in range(len(seqs)):
        S = seqs[i].shape[0]
        outs.append(np.ascontiguousarray(res.results[core_of_seq[i]]["yT"][:, :S].T))
    return outs


def kernel(x_prompt, x_sample, g_mix, w_in, b_gate, lambda_q, lambda_k, g_subln,
           w_attn_proj, w_dw, b_dw, g_conv_ln, b_conv_ln, w_conv_proj, b_conv_proj,
           w_out, g_ffn, w_ffn_in, w_ffn_out, g_final):
    inp = dict(g_mix=g_mix, w_in=w_in, b_gate=b_gate, lambda_q=lambda_q, lambda_k=lambda_k, g_subln=g_subln,
               w_attn_proj=w_attn_proj, w_dw=w_dw, b_dw=b_dw, g_conv_ln=g_conv_ln, b_conv_ln=b_conv_ln,
               w_conv_proj=w_conv_proj, b_conv_proj=b_conv_proj, w_out=w_out, g_ffn=g_ffn,
               w_ffn_in=w_ffn_in, w_ffn_out=w_ffn_out, g_final=g_final)
    inp = {k: np.asarray(v, dtype=np.float32) for k, v in inp.items()}
    x_prompt = np.asarray(x_prompt, dtype=np.float32)
    x_sample = np.asarray(x_sample, dtype=np.float32)
    cfg = Cfg()
    seqs = [x_prompt[b] for b in range(x_prompt.shape[0])] + [x_sample[b] for b in range(x_sample.shape[0])]
    outs = run_trunk(cfg, seqs, inp)
    nb = x_prompt.shape[0]
    y_prompt = np.stack(outs[:nb]).astype(np.float32)
    y_sample = np.stack(outs[nb:]).astype(np.float32)
    return (y_prompt, y_sample)
```

```python
import numpy as np
from contextlib import ExitStack
import concourse.bass as bass
import concourse.mybir as mybir
from concourse.bass_utils import run_bass_kernel_spmd

F32 = mybir.dt.float32
BF16 = mybir.dt.bfloat16
U8 = mybir.dt.uint8
AF = mybir.ActivationFunctionType
ALU = mybir.AluOpType

NORM_EPS = 1e-6
LN_EPS = 1e-5
MASK_NEG = -30000.0
SB_BYTES = 192 * 1024


class Cfg:
    def __init__(self, T=4096, D=2048, H=8, DFF=5632, DEPTH=4, CW=31):
        self.T, self.D, self.H, self.DFF, self.DEPTH, self.CW = T, D, H, DFF, DEPTH, CW
        self.HD = 128
        self.VD = 256
        self.QK = H * 2 * self.HD
        self.VW = H * self.VD
        self.CC = D
        self.IN_W = 2 * self.QK + self.VW + 2 * self.CC + 2 * D
        self.DC = D // 128
        self.CCc = self.CC // 128
        self.NK = T // 128
        self.NQ = T // 512
        self.AOFF = T - 128
        self.AW = (self.NQ - 1) * 512 + 512 + self.AOFF
        c = 0
        self.c_gmix = c; c += self.DC
        self.c_bgate = c; c += 2 * self.DC
        self.c_bdw = c; c += self.CCc
        self.c_gcln = c; c += self.CCc
        self.c_bcln = c; c += self.CCc
        self.c_bcproj = c; c += self.DC
        self.c_gffn = c; c += self.DC
        self.c_wdw = c; c += self.CCc * CW
        self.c_gsub = c; c += self.VD
        self.c_lq = c; c += 2 * self.HD
        self.c_lk = c; c += 2 * self.HD
        self.NCL = c
        g = 0
        self.g_gfinal = g; g += self.DC
        self.g_ident = g; g += 128
        self.g_kmask = g; g += self.NK
        self.NCG = g


def lambda_init(layer):
    return 0.8 - 0.6 * float(np.exp(-0.3 * layer))


class DmaSem:
    def __init__(self, sem):
        self.sem = sem
        self.count = 0


class Prog:
    ENG = ('pe', 'act', 'dve', 'pool', 'sp')

    def __init__(self, nc, stack):
        self.nc = nc
        self.stack = stack
        self.streams = {k: [] for k in self.ENG}
        self.cnt = {k: 0 for k in self.ENG}
        self.esem = {k: stack.enter_context(nc.semaphore("s_" + k)) for k in self.ENG}
        self.dsems = {}
        self.pending = {k: [] for k in self.ENG}
        self.waited = {k: {} for k in self.ENG}

    def ds(self, name):
        if name not in self.dsems:
            self.dsems[name] = DmaSem(self.stack.enter_context(self.nc.semaphore("d_" + name)))
        return self.dsems[name]

    def _w(self, eng, waits):
        w = []

        def fl(x):
            if x is None:
                return
            if isinstance(x, list):
                for y in x:
                    fl(y)
            else:
                w.append(x)
        fl(list(waits))
        if self.pending[eng]:
            w = self.pending[eng] + w
            self.pending[eng] = []
        return w

    def op(self, eng, fn, waits=()):
        self.cnt[eng] += 1
        self.streams[eng].append((fn, self._w(eng, waits), None))
        return (eng, self.cnt[eng])

    def dma(self, eng, fn, ds, waits=()):
        ds.count += 16
        self.streams[eng].append((fn, self._w(eng, waits), ds))
        return (ds, ds.count)

    def last(self, eng):
        return (eng, self.cnt[eng]) if self.cnt[eng] else None

    def barrier(self):
        toks = [(k, self.cnt[k]) for k in self.ENG if self.cnt[k]]
        toks += [(d, d.count) for d in self.dsems.values() if d.count]
        for k in self.ENG:
            self.pending[k] = list(toks)

    def finish(self):
        self.barrier()
        for k in self.ENG:
            w = self._w(k, [])
            self.streams[k].append((None, w, None))

    def flush(self):
        nc = self.nc
        P = self
        with nc.Block() as block:
            @block.tensor
            def _(e):
                P.replay('pe', e)

            @block.scalar
            def _(e):
                P.replay('act', e)

            @block.vector
            def _(e):
                P.replay('dve', e)

            @block.gpsimd
            def _(e):
                P.replay('pool', e)

            @block.sync
            def _(e):
                P.replay('sp', e)
        for k in self.ENG:
            self.streams[k] = []

    def replay(self, eng, e):
        waited = self.waited[eng]
        for fn, waits, ds in self.streams[eng]:
            for key, val in waits:
                kk = key if isinstance(key, str) else id(key)
                if waited.get(kk, 0) >= val:
                    continue
                waited[kk] = val
                sem = self.esem[key] if isinstance(key, str) else key.sem
                e.wait_ge(sem, val)
            if fn is None:
                continue
            inst = fn(e)
            if ds is None:
                inst.then_inc(self.esem[eng], 1)
            else:
                inst.then_inc(ds.sem, 16)


class Alloc:
    def __init__(self, sb, base, limit):
        self.sb, self.base, self.off, self.limit = sb, base, base, limit

    def reset(self):
        self.off = self.base

    def get(self, shape, dt):
        esz = 4 if dt == F32 else 2
        n = int(np.prod(shape[1:]))
        nb = (n * esz + 63) // 64 * 64
        o = self.off
        self.off += nb
        assert self.off <= self.limit, ("SBUF overflow", self.off, self.limit)
        v = self.sb[:, o:o + n * esz].bitcast(dt)
        if len(shape) == 3:
            v = v.rearrange("p (a b) -> p a b", b=shape[2])
        return v


def fm(ap):
    return ap.rearrange("(c p) t -> p c t", p=128)


class Builder:
    def __init__(self, cfg):
        self.cfg = cfg

    def build(self):
        cfg = self.cfg
        T, D, DEPTH = cfg.T, cfg.D, cfg.DEPTH
        nc = bass.Bass("TRN2", target_bir_lowering=False)
        self.nc = nc
        dt_in = lambda name, shape: nc.dram_tensor(name, shape, F32, kind="ExternalInput").ap()
        self.xT_in = dt_in("xT", [D, T])
        self.cl_in = dt_in("cl", [DEPTH, 128, cfg.NCL])
        self.cg_in = dt_in("cg", [128, cfg.NCG])
        self.atab_in = dt_in("atab", [128, cfg.AW])
        self.tmask_in = dt_in("tmask", [128, T])
        self.w_in = dt_in("w_in", [DEPTH, D, cfg.IN_W])
        self.w_ap = dt_in("w_attn_proj", [DEPTH, cfg.VW, D])
        self.w_cp = dt_in("w_conv_proj", [DEPTH, cfg.CC, D])
        self.w_out = dt_in("w_out", [DEPTH, D, D])
        self.w_fi = dt_in("w_ffn_in", [DEPTH, D, 2 * cfg.DFF])
        self.w_fo = dt_in("w_ffn_out", [DEPTH, cfg.DFF, D])
        self.yT_out = nc.dram_tensor("yT", [D, T], F32, kind="ExternalOutput").ap()
        scr = lambda name, shape, dt: nc.dram_tensor(name, shape, dt).ap()
        self.XT = scr("s_XT", [D, T], F32)
        self.HT = scr("s_HT", [D, T], BF16)
        self.QKT = scr("s_QKT", [2 * cfg.QK, T], BF16)
        self.V = scr("s_V", [T, cfg.VW], BF16)
        self.YT = scr("s_YT", [cfg.CC, T], BF16)
        self.GT = scr("s_GT", [2 * D, T], BF16)
        self.ONT = scr("s_ONT", [cfg.VW, T], BF16)
        self.ZT = scr("s_ZT", [cfg.CC, T], F32)
        self.CT = scr("s_CT", [cfg.CC, T], BF16)
        self.MT = scr("s_MT", [D, T], BF16)
        self.AT = scr("s_AT", [cfg.DFF, T], BF16)

        with ExitStack() as stack:
            P = Prog(nc, stack)
            self.P = P
            GB = 16 * 1024
            gsb = stack.enter_context(nc.sbuf_tensor("gsb", [128, GB], U8))
            G = Alloc(gsb, 0, GB)
            self.cg = G.get([128, cfg.NCG], F32)
            self.cl = G.get([128, cfg.NCL], F32)
            self.ident_bf = G.get([128, 128], BF16)
            self.ones_bf = G.get([128, 128], BF16)
            self.neglam = G.get([128, 2], F32)
            self.gsub = G.get([128, cfg.VD], F32)
            self.lamtmp = G.get([128, 2 * cfg.HD], F32)
            self.lamred = G.get([128, 4], F32)
            self.PH_BYTES = SB_BYTES - GB
            self.phase_no = 0
            self.ph = None
            self.A = None
            self.banks = None
            t0 = P.dma('sp', lambda e: e.dma_start(out=self.cg, in_=self.cg_in), P.ds("c0"))
            t1 = P.op('dve', lambda e: e.tensor_copy(out=self.ident_bf, in_=self.cg[:, cfg.g_ident:cfg.g_ident + 128]), [t0])
            t2 = P.op('pool', lambda e: e.memset(self.ones_bf, 1.0))
            P.barrier()

            xsrc = self.xT_in
            for l in range(DEPTH):
                self.layer_consts(l)
                self.norm_phase(xsrc, cfg.c_gmix, self.HT, final=False)
                self.inproj_phase(l)
                self.attn_phase(l)
                self.conv1_phase(l)
                self.conv2_phase(l)
                self.merge_phase(l)
                self.resid_phase(l, self.w_out[l], self.MT, cfg.D // 128, xsrc, TS=min(2048, T))
                xsrc = self.XT
                self.norm_phase(xsrc, cfg.c_gffn, self.HT, final=False)
                self.ffnin_phase(l)
                self.resid_phase(l, self.w_fo[l], self.AT, cfg.DFF // 128, xsrc, TS=min(1024, T))
            self.norm_phase(xsrc, None, self.yT_out, final=True)
            P.finish()
            P.flush()
        return nc

    def phase_begin(self):
        nc = self.nc
        self.P.barrier()
        self.ph = ExitStack()
        self.phase_no += 1
        sbp = self.ph.enter_context(nc.sbuf_tensor("sb%d" % self.phase_no, [128, self.PH_BYTES], U8))
        psp = self.ph.enter_context(nc.psum_tensor("ps%d" % self.phase_no, [128, 8 * 512], F32))
        self.banks = [psp[:, b * 512:(b + 1) * 512] for b in range(8)]
        self.A = Alloc(sbp, 0, self.PH_BYTES)

    def phase_end(self):
        self.P.barrier()
        self.P.flush()
        self.ph.close()
        self.ph = None

    def layer_consts(self, l):
        cfg, P = self.cfg, self.P
        HD = cfg.HD
        P.barrier()
        t0 = P.dma('sp', lambda e: e.dma_start(out=self.cl, in_=self.cl_in[l]), P.ds("c0"))
        lq = self.cl[:, cfg.c_lq:cfg.c_lq + 2 * HD]
        lk = self.cl[:, cfg.c_lk:cfg.c_lk + 2 * HD]
        t1 = P.op('dve', lambda e: e.tensor_tensor(out=self.lamtmp, in0=lq, in1=lk, op=ALU.mult), [t0])
        t2 = P.op('dve', lambda e: e.tensor_reduce(
            out=self.lamred[:, 0:2], in_=self.lamtmp.rearrange("p (a b) -> p a b", b=HD),
            axis=mybir.AxisListType.X, op=ALU.add), [t1])
        t3 = P.op('act', lambda e: e.activation(out=self.lamred[:, 2:4], in_=self.lamred[:, 0:2], func=AF.Exp), [t2])
        li = lambda_init(l)
        t4 = P.op('dve', lambda e: e.tensor_tensor(out=self.neglam[:, 0:1], in0=self.lamred[:, 3:4],
                                                   in1=self.lamred[:, 2:3], op=ALU.subtract), [t3])
        t5 = P.op('dve', lambda e: e.tensor_scalar(out=self.neglam[:, 1:2], in0=self.neglam[:, 0:1],
                                                   scalar1=-li, scalar2=None, op0=ALU.add), [t4])
        t6 = P.op('dve', lambda e: e.tensor_scalar(out=self.gsub, in0=self.cl[:, cfg.c_gsub:cfg.c_gsub + cfg.VD],
                                                   scalar1=(1.0 - li), scalar2=None, op0=ALU.mult), [t5])
        P.barrier()

    def norm_phase(self, src, gcol, dst, final):
        cfg, P, A = self.cfg, self.P, self.A
        DC, D = cfg.DC, cfg.D
        self.phase_begin()
        A = self.A
        odt = F32 if final else BF16
        xin = [A.get([128, DC, 512], F32) for _ in range(2)]
        sq = [A.get([128, DC, 512], BF16) for _ in range(2)]
        hout = [A.get([128, DC, 512], odt) for _ in range(2)]
        rstd = [A.get([128, 512], F32) for _ in range(2)]
        gsrc = self.cg if final else self.cl
        gc = cfg.g_gfinal if final else gcol
        dsi = [P.ds("in0"), P.ds("in1")]
        dso = [P.ds("st0"), P.ds("st1")]
        hdone = [None, None]
        mmdone = [None, None]
        stdone = [None, None]
        r2done = [None, None]
        ldt = {}

        def issue_ld(i):
            b = i % 2
            ts = slice(i * 512, (i + 1) * 512)
            ldt[i] = P.dma('sp', lambda e: e.dma_start(out=xin[b], in_=fm(src)[:, :, ts]), dsi[b],
                           (hdone[b] or []))

        issue_ld(0)
        for i in range(cfg.NQ):
            b = i % 2
            ts = slice(i * 512, (i + 1) * 512)
            if i + 1 < cfg.NQ:
                issue_ld(i + 1)
            ld = ldt[i]
            sqt = P.op('act', lambda e, b=b: e.activation(out=sq[b], in_=xin[b], func=AF.Square), [ld, mmdone[b]])
            mm = None
            for c in range(DC):
                mm = P.op('pe', lambda e, b=b, c=c: e.matmul(self.banks[b], self.ones_bf, sq[b][:, c, :],
                                                             start=(c == 0), stop=(c == DC - 1)),
                          [sqt, r2done[b]] if c == 0 else [])
            mmdone[b] = mm
            r1 = P.op('dve', lambda e, b=b: e.tensor_scalar(out=rstd[b], in0=self.banks[b], scalar1=1.0 / D,
                                                            scalar2=NORM_EPS, op0=ALU.mult, op1=ALU.add),
                      [mm] + (hdone[b] or []))
            r2a = P.op('act', lambda e, b=b: e.activation(out=rstd[b], in_=rstd[b], func=AF.Sqrt), [r1])
            r2 = P.op('dve', lambda e, b=b: e.reciprocal(out=rstd[b], in_=rstd[b]), [r2a])
            r2done[b] = r1
            lastd = lastp = None
            for c in range(DC):
                eng = 'dve'
                tk = P.op(eng, lambda e, b=b, c=c: e.scalar_tensor_tensor(
                    out=hout[b][:, c, :], in0=xin[b][:, c, :], scalar=gsrc[:, gc + c:gc + c + 1],
                    in1=rstd[b], op0=ALU.mult, op1=ALU.mult), [r2, stdone[b]])
                if eng == 'dve':
                    lastd = tk
                else:
                    lastp = tk
            hdone[b] = [lastd, lastp]
            stdone[b] = P.dma('sp', lambda e, b=b, ts=ts: e.dma_start(out=fm(dst)[:, :, ts], in_=hout[b]), dso[b],
                              [lastd, lastp])
        self.phase_end()

    def linear(self, ins, TS, blocks):
        cfg, P, A = self.cfg, self.P, self.A
        T = cfg.T
        NS = T // TS
        NHALF = TS // 1024
        KCmax = max(kc for _, kc in ins)
        FBmax = max(sum(p[2] for p in b['pieces']) for b in blocks)
        in_sb = [A.get([128, kc, TS], BF16) for _, kc in ins]
        wbuf = [A.get([128, KCmax, FBmax], BF16) for _ in range(2)]
        dsw = [P.ds("w0"), P.ds("w1")]
        dsin = [P.ds("in0"), P.ds("in1")]
        wfree = [None, None]
        bankfree = [[None] * 4, [None] * 4]
        seq = [(s, bi) for s in range(NS) for bi in range(len(blocks))]
        wtok = {}

        def issue_w(n):
            s, bi = seq[n]
            wb = n % 2
            c0 = 0
            tk = None
            for (w2d, col0, ncols, in_idx) in blocks[bi]['pieces']:
                kc = ins[in_idx][1]
                tk = P.dma('pool', lambda e, wb=wb, c0=c0, w2d=w2d, col0=col0, ncols=ncols, kc=kc: e.dma_start(
                    out=wbuf[wb][:, 0:kc, c0:c0 + ncols],
                    in_=w2d.rearrange("(c p) f -> p c f", p=128)[:, :, col0:col0 + ncols]), dsw[wb], [wfree[wb]])
                c0 += ncols
            wtok[n] = tk

        issue_w(0)
        ucount = 0
        intok = [None] * len(ins)
        lastpe_all = None
        for n, (s, bi) in enumerate(seq):
            blk = blocks[bi]
            if bi == 0:
                for ii, (src, kc) in enumerate(ins):
                    tk = None
                    nsp = max(1, kc // 8)
                    for q in range(nsp):
                        cs = slice(q * kc // nsp, (q + 1) * kc // nsp)
                        tk = P.dma('sp', lambda e, ii=ii, cs=cs, s=s, src=src: e.dma_start(
                            out=in_sb[ii][:, cs, :], in_=fm(src)[:, cs, s * TS:(s + 1) * TS]), dsin[ii], [lastpe_all])
                    intok[ii] = tk
            if n + 1 < len(seq):
                issue_w(n + 1)
            wb = n % 2
            FB = sum(p[2] for p in blk['pieces'])
            kind = blk['kind']
            lastpe = None
            if kind == 'vtok':
                kc = ins[0][1]
                for g in range(TS // 512):
                    bs = ucount % 2
                    ucount += 1
                    bk = self.banks[bs * 4:bs * 4 + 4]
                    tok0 = s * TS + g * 512
                    for k in range(kc):
                        for j in range(4):
                            w_ = [wtok[n], intok[0], bankfree[bs][j]] if k == 0 else []
                            lastpe = P.op('pe', lambda e, k=k, j=j, g=g, wb=wb, bk=bk, FB=FB: e.matmul(
                                bk[j][:, 0:FB], in_sb[0][:, k, g * 512 + j * 128:g * 512 + (j + 1) * 128],
                                wbuf[wb][:, k, 0:FB], start=(k == 0), stop=(k == kc - 1)), w_)
                    rel = blk['epi'](0, tok0, bk, lastpe, None)
                    bankfree[bs] = rel
            else:
                nunits = FB // 256
                for u in range(nunits):
                    if kind == 'pair':
                        cols = [u * 128, FB // 2 + u * 128]
                        iidx = [blk['pieces'][0][3], blk['pieces'][1][3]]
                    else:
                        cols = [2 * u * 128, (2 * u + 1) * 128]
                        iidx = [blk['pieces'][0][3]] * 2
                    for h in range(NHALF):
                        bs = ucount % 2
                        ucount += 1
                        bk = self.banks[bs * 4:bs * 4 + 4]
                        tok0 = s * TS + h * 1024
                        pretoks = blk['pre'](u, tok0) if blk.get('pre') else None
                        kcs = [ins[iidx[0]][1], ins[iidx[1]][1]]
                        kc = max(kcs)
                        for k in range(kc):
                            for ci in range(2):
                                if k >= kcs[ci]:
                                    continue
                                for t in range(2):
                                    w_ = [wtok[n], intok[iidx[ci]], bankfree[bs][ci * 2 + t]] if k == 0 else []
                                    lastpe = P.op('pe', lambda e, k=k, ci=ci, t=t, h=h, wb=wb, bk=bk, cols=cols, iidx=iidx, kcs=kcs: e.matmul(
                                        bk[ci * 2 + t], wbuf[wb][:, k, cols[ci]:cols[ci] + 128],
                                        in_sb[iidx[ci]][:, k, h * 1024 + t * 512:h * 1024 + (t + 1) * 512],
                                        start=(k == 0), stop=(k == kcs[ci] - 1)), w_)
                        rel = blk['epi'](u, tok0, bk, lastpe, pretoks)
                        bankfree[bs] = rel
            wfree[wb] = lastpe
            lastpe_all = lastpe

    def stager(self, name, shape, dt, n=2):
        bufs = [self.A.get(shape, dt) for _ in range(n)]
        return {'bufs': bufs, 'tok': [None] * n, 'i': 0, 'ds': [self.P.ds("%s%d" % (name, i)) for i in range(n)]}

    def inproj_phase(self, l):
        cfg, P, A = self.cfg, self.P, self.A
        D, QK, VW, CC = cfg.D, cfg.QK, cfg.VW, cfg.CC
        self.phase_begin()
        A = self.A
        w = self.w_in[l]
        TS = min(2048, cfg.T)
        stg = self.stager("st", [128, 2, 1024], BF16)
        sig = [A.get([128, 1024], F32) for _ in range(2)]
        vst = self.stager("sv", [128, 4, 512], BF16)
        sigtok = [None, None]
        cnt = [0]
        bgc = cfg.c_bgate
        banks = self.banks

        def store2(s, row0s, dst, tok0, evs):
            b = s['i']
            buf = s['bufs'][b]
            t = None
            for ci, r0 in enumerate(row0s):
                t = P.dma('sp', lambda e, ci=ci, r0=r0, buf=buf: e.dma_start(
                    out=dst[r0:r0 + 128, tok0:tok0 + 1024], in_=buf[:, ci, :]), s['ds'][b], evs)
            s['tok'][b] = t
            s['i'] = (b + 1) % len(s['bufs'])

        def epi_copy(row_of_unit, dst):
            def f(u, tok0, bk, lastpe, pre):
                b = stg['i']
                buf = stg['bufs'][b]
                evs = []
                for ci in range(2):
                    for t in range(2):
                        eng = 'act' if (ci + t) % 2 == 0 else 'dve'
                        if eng == 'act':
                            tk = P.op('act', lambda e, ci=ci, t=t, buf=buf, bk=bk: e.activation(
                                out=buf[:, ci, t * 512:(t + 1) * 512], in_=bk[ci * 2 + t], func=AF.Copy),
                                [lastpe, stg['tok'][b]])
                        else:
                            tk = P.op('dve', lambda e, ci=ci, t=t, buf=buf, bk=bk: e.tensor_copy(
                                out=buf[:, ci, t * 512:(t + 1) * 512], in_=bk[ci * 2 + t]),
                                [lastpe, stg['tok'][b]])
                        evs.append(tk)
                r0 = row_of_unit(u)
                store2(stg, [r0, r0 + 128], dst, tok0, evs)
                return evs
            return f

        def epi_gates(row_of_unit):
            def f(u, tok0, bk, lastpe, pre):
                b = stg['i']
                buf = stg['bufs'][b]
                evs = []
                r0 = row_of_unit(u)
                for ci in range(2):
                    col = bgc + (r0 + ci * 128) // 128
                    for t in range(2):
                        tk = P.op('act', lambda e, ci=ci, t=t, buf=buf, bk=bk, col=col: e.activation(
                            out=buf[:, ci, t * 512:(t + 1) * 512], in_=bk[ci * 2 + t], func=AF.Sigmoid,
                            bias=self.cl[:, col:col + 1]), [lastpe, stg['tok'][b]])
                        evs.append(tk)
                store2(stg, [r0, r0 + 128], self.GT, tok0, evs)
                return evs
            return f

        def epi_glu(row_of_unit):
            def f(u, tok0, bk, lastpe, pre):
                b = stg['i']
                buf = stg['bufs'][b]
                sb_ = cnt[0] % 2
                cnt[0] += 1
                rel = [None] * 4
                evs = []
                for t in range(2):
                    ta = P.op('act', lambda e, t=t, bk=bk, sb_=sb_: e.activation(
                        out=sig[sb_][:, t * 512:(t + 1) * 512], in_=bk[2 + t], func=AF.Sigmoid),
                        [lastpe, sigtok[sb_]])
                    td = P.op('dve', lambda e, t=t, bk=bk, buf=buf, sb_=sb_: e.tensor_tensor(
                        out=buf[:, 0, t * 512:(t + 1) * 512], in0=bk[t], in1=sig[sb_][:, t * 512:(t + 1) * 512],
                        op=ALU.mult), [ta, stg['tok'][b]])
                    rel[2 + t] = ta
                    rel[t] = td
                    evs.append(td)
                sigtok[sb_] = evs[-1]
                r0 = row_of_unit(u)
                store2(stg, [r0], self.YT, tok0, evs)
                return rel
            return f

        def epi_v(col0):
            def f(u, tok0, bk, lastpe, pre):
                b = vst['i']
                buf = vst['bufs'][b]
                FB = min(512, VW)
                evs = []
                for j in range(4):
                    if j % 2 == 0:
                        tk = P.op('act', lambda e, j=j, buf=buf, bk=bk: e.activation(
                            out=buf[:, j, 0:FB], in_=bk[j][:, 0:FB], func=AF.Copy), [lastpe, vst['tok'][b]])
                    else:
                        tk = P.op('dve', lambda e, j=j, buf=buf, bk=bk: e.tensor_copy(
                            out=buf[:, j, 0:FB], in_=bk[j][:, 0:FB]), [lastpe, vst['tok'][b]])
                    evs.append(tk)
                vst['tok'][b] = P.dma('sp', lambda e, buf=buf: e.dma_start(
                    out=self.V[tok0:tok0 + 512, col0:col0 + FB].rearrange("(j p) f -> p j f", p=128),
                    in_=buf[:, :, 0:FB]), vst['ds'][b], evs)
                vst['i'] = (b + 1) % 2
                return evs
            return f

        blocks = []
        for c0 in range(0, 2 * QK, 512):
            blocks.append(dict(pieces=[(w, c0, 512, 0)], kind='single',
                               epi=epi_copy(lambda u, c0=c0: c0 + u * 256, self.QKT)))
        FBv = min(512, VW)
        for c0 in range(0, VW, FBv):
            blocks.append(dict(pieces=[(w, 2 * QK + c0, FBv, 0)], kind='vtok', epi=epi_v(c0)))
        ub = 2 * QK + VW
        for c0 in range(0, CC, 256):
            blocks.append(dict(pieces=[(w, ub + c0, 256, 0), (w, ub + CC + c0, 256, 0)], kind='pair',
                               epi=epi_glu(lambda u, c0=c0: c0 + u * 128)))
        gb = ub + 2 * CC
        for c0 in range(0, 2 * D, 512):
            blocks.append(dict(pieces=[(w, gb + c0, 512, 0)], kind='single',
                               epi=epi_gates(lambda u, c0=c0: c0 + u * 256)))
        self.linear([(self.HT, cfg.DC)], TS, blocks)
        self.phase_end()

    def merge_phase(self, l):
        cfg, P, A = self.cfg, self.P, self.A
        D = cfg.D
        self.phase_begin()
        A = self.A
        TS = min(1024, cfg.T)
        stg = self.stager("st", [128, 1, 1024], BF16)
        gbuf = [[A.get([128, 1024], BF16) for _ in range(2)] for _ in range(2)]
        gds = [P.ds("g0"), P.ds("g1")]
        gfree = [None, None]
        t1 = [A.get([128, 1024], F32) for _ in range(2)]
        t2 = [A.get([128, 1024], F32) for _ in range(2)]
        tfree = [None, None]
        cnt = [0]
        slot_of = {}

        def pre(row0):
            def f(u, tok0):
                s_ = cnt[0] % 2
                cnt[0] += 1
                r = row0 + u * 128
                tk = None
                for gi in range(2):
                    tk = P.dma('sp', lambda e, gi=gi, r=r, s_=s_: e.dma_start(
                        out=gbuf[s_][gi], in_=self.GT[gi * D + r:gi * D + r + 128, tok0:tok0 + 1024]),
                        gds[s_], [gfree[s_]])
                return (s_, tk)
            return f

        def epi(row0):
            def f(u, tok0, bk, lastpe, pretoks):
                s_, gtok = pretoks
                b = stg['i']
                buf = stg['bufs'][b]
                r = row0 + u * 128
                col = cfg.c_bcproj + r // 128
                rel = [None] * 4
                evs = []
                for t in range(2):
                    sl = slice(t * 512, (t + 1) * 512)
                    ta = P.op('dve', lambda e, t=t, sl=sl, bk=bk, s_=s_: e.tensor_tensor(
                        out=t1[s_][:, sl], in0=bk[t], in1=gbuf[s_][0][:, sl], op=ALU.mult),
                        [lastpe, gtok, tfree[s_]])
                    tb = P.op('dve', lambda e, t=t, sl=sl, bk=bk, s_=s_, col=col: e.scalar_tensor_tensor(
                        out=t2[s_][:, sl], in0=bk[2 + t], scalar=self.cl[:, col:col + 1], in1=gbuf[s_][1][:, sl],
                        op0=ALU.add, op1=ALU.mult), [lastpe, gtok, tfree[s_]])
                    tc = P.op('pool', lambda e, sl=sl, buf=buf, s_=s_: e.tensor_tensor(
                        out=buf[:, 0, sl], in0=t1[s_][:, sl], in1=t2[s_][:, sl], op=ALU.add),
                        [ta, tb, stg['tok'][b]])
                    rel[t] = ta
                    rel[2 + t] = tb
                    evs.append(tc)
                tfree[s_] = evs[-1]
                gfree[s_] = rel[3]
                stg['tok'][b] = P.dma('sp', lambda e, buf=buf, r=r: e.dma_start(
                    out=self.MT[r:r + 128, tok0:tok0 + 1024], in_=buf[:, 0, :]), stg['ds'][b], evs)
                stg['i'] = (b + 1) % 2
                return rel
            return f

        blocks = []
        for c0 in range(0, D, 256):
            blocks.append(dict(pieces=[(self.w_ap[l], c0, 256, 0), (self.w_cp[l], c0, 256, 1)], kind='pair',
                               pre=pre(c0), epi=epi(c0)))
        self.linear([(self.ONT, cfg.VW // 128), (self.CT, cfg.CCc)], TS, blocks)
        self.phase_end()

    def resid_phase(self, l, w2d, src, KC, xsrc, TS):
        cfg, P, A = self.cfg, self.P, self.A
        D = cfg.D
        self.phase_begin()
        A = self.A
        xold = [[A.get([128, 1024], F32) for _ in range(2)] for _ in range(2)]
        xds = [P.ds("g0"), P.ds("g1")]
        xfree = [None, None]
        xnew = self.stager("st", [128, 2, 1024], F32)
        tmp = [A.get([128, 1024], F32) for _ in range(2)]
        tmpfree = [None, None]
        cnt = [0]

        def pre(row0):
            def f(u, tok0):
                s_ = cnt[0] % 2
                cnt[0] += 1
                tk = None
                for ci in range(2):
                    r = row0 + u * 256 + ci * 128
                    tk = P.dma('sp', lambda e, ci=ci, r=r, s_=s_: e.dma_start(
                        out=xold[s_][ci], in_=xsrc[r:r + 128, tok0:tok0 + 1024]), xds[s_], [xfree[s_]])
                return (s_, tk)
            return f

        def epi(row0):
            def f(u, tok0, bk, lastpe, pretoks):
                s_, xtok = pretoks
                b = xnew['i']
                buf = xnew['bufs'][b]
                rel = [None] * 4
                evs = []
                for t in range(2):
                    sl = slice(t * 512, (t + 1) * 512)
                    ta = P.op('dve', lambda e, t=t, sl=sl, bk=bk, buf=buf, s_=s_: e.tensor_tensor(
                        out=buf[:, 0, sl], in0=bk[t], in1=xold[s_][0][:, sl], op=ALU.add),
                        [lastpe, xtok, xnew['tok'][b]])
                    tb = P.op('act', lambda e, t=t, sl=sl, bk=bk, s_=s_: e.activation(
                        out=tmp[s_][:, sl], in_=bk[2 + t], func=AF.Copy), [lastpe, tmpfree[s_]])
                    tc = P.op('pool', lambda e, sl=sl, buf=buf, s_=s_: e.tensor_tensor(
                        out=buf[:, 1, sl], in0=tmp[s_][:, sl], in1=xold[s_][1][:, sl], op=ALU.add),
                        [tb, xtok, xnew['tok'][b]])
                    rel[t] = ta
                    rel[2 + t] = tb
                    evs += [ta, tc]
                tmpfree[s_] = evs[-1]
                xfree[s_] = [evs[-1], evs[-2]]
                t_ = None
                for ci in range(2):
                    r = row0 + u * 256 + ci * 128
                    t_ = P.dma('sp', lambda e, ci=ci, r=r, buf=buf: e.dma_start(
                        out=self.XT[r:r + 128, tok0:tok0 + 1024], in_=buf[:, ci, :]), xnew['ds'][b], evs)
                xnew['tok'][b] = t_
                xnew['i'] = (b + 1) % 2
                return rel
            return f

        blocks = []
        FB = 256 if KC > 16 else min(512, D)
        for c0 in range(0, D, FB):
            blocks.append(dict(pieces=[(w2d, c0, FB, 0)], kind='single', pre=pre(c0), epi=epi(c0)))
        self._flatten_fix = True
        self.linear([(src, KC)], TS, blocks)
        self.phase_end()

    def ffnin_phase(self, l):
        cfg, P, A = self.cfg, self.P, self.A
        DFF = cfg.DFF
        self.phase_begin()
        A = self.A
        TS = min(2048, cfg.T)
        stg = self.stager("st", [128, 1, 1024], BF16)
        sil = [A.get([128, 1024], F32) for _ in range(2)]
        silfree = [None, None]
        cnt = [0]
        w = self.w_fi[l]

        def epi(row0):
            def f(u, tok0, bk, lastpe, pre):
                b = stg['i']
                buf = stg['bufs'][b]
                s_ = cnt[0] % 2
                cnt[0] += 1
                rel = [None] * 4
                evs = []
                for t in range(2):
                    sl = slice(t * 512, (t + 1) * 512)
                    ta = P.op('act', lambda e, t=t, sl=sl, bk=bk, s_=s_: e.activation(
                        out=sil[s_][:, sl], in_=bk[t], func=AF.Silu), [lastpe, silfree[s_]])
                    td = P.op('dve', lambda e, t=t, sl=sl, bk=bk, buf=buf, s_=s_: e.tensor_tensor(
                        out=buf[:, 0, sl], in0=bk[2 + t], in1=sil[s_][:, sl], op=ALU.mult), [ta, stg['tok'][b]])
                    rel[t] = ta
                    rel[2 + t] = td
                    evs.append(td)
                silfree[s_] = evs[-1]
                r = row0 + u * 128
                stg['tok'][b] = P.dma('sp', lambda e, buf=buf, r=r: e.dma_start(
                    out=self.AT[r:r + 128, tok0:tok0 + 1024], in_=buf[:, 0, :]), stg['ds'][b], evs)
                stg['i'] = (b + 1) % 2
                return rel
            return f

        blocks = []
        for c0 in range(0, DFF, 256):
            blocks.append(dict(pieces=[(w, c0, 256, 0), (w, DFF + c0, 256, 0)], kind='pair', epi=epi(c0)))
        self.linear([(self.HT, cfg.DC)], TS, blocks)
        self.phase_end()

    def attn_phase(self, l):
        cfg, P, A = self.cfg, self.P, self.A
        T, H, HD, VD, NK, NQ = cfg.T, cfg.H, cfg.HD, cfg.VD, cfg.NK, cfg.NQ
        self.phase_begin()
        A = self.A
        banks = self.banks
        scale = HD ** -0.5
        atab = A.get([128, cfg.AW], F32)
        qb = [A.get([128, 2, T], BF16) for _ in range(2)]
        kb = [A.get([128, 2, T], BF16) for _ in range(2)]
        vb = [A.get([128, NK, VD + 1], BF16) for _ in range(2)]
        NSB = 3
        NTMP = 6
        NPT = 8
        LA = 6
        sbanks = [4, 5, 7]
        tmp = [A.get([128, 512], F32) for _ in range(NTMP)]
        pt = [A.get([128, 512], BF16) for _ in range(NPT)]
        Om = [A.get([128, 4, VD], F32) for _ in range(2)]
        o = A.get([128, 4, VD], F32)
        junk = A.get([128, VD], F32)
        on = A.get([128, 4, VD], BF16)
        ost = [A.get([128, VD // 128, 512], BF16) for _ in range(2)]
        rec = A.get([128, 8], F32)
        ssq = A.get([128, 4], F32)
        rs2 = A.get([128, 4], F32)
        tb = banks[6].bitcast(BF16)
        ta = P.dma('sp', lambda e: e.dma_start(out=atab, in_=self.atab_in), P.ds("c0"))
        ones_tok = []
        for i in range(2):
            ones_tok.append(P.op('pool', lambda e, i=i: e.memset(vb[i][:, :, VD:VD + 1], 1.0)))
        dsl = [P.ds("in0"), P.ds("in1")]
        dso = [P.ds("st0"), P.ds("st1")]
        loadtok = [None, None]
        headdone = [None, None]

        def load_head(h):
            b = h % 2
            w_ = [headdone[b], ones_tok[b]]
            P.dma('sp', lambda e: e.dma_start(out=qb[b], in_=fm(self.QKT)[:, 2 * h:2 * h + 2, :]), dsl[b], w_)
            P.dma('sp', lambda e: e.dma_start(out=kb[b], in_=fm(self.QKT)[:, cfg.QK // 128 + 2 * h:cfg.QK // 128 + 2 * h + 2, :]),
                  dsl[b], w_)
            loadtok[b] = P.dma('sp', lambda e: e.dma_start(
                out=vb[b][:, :, 0:VD], in_=self.V.rearrange("(k p) f -> p k f", p=128)[:, :, h * VD:(h + 1) * VD]),
                dsl[b], w_)

        load_head(0)
        st = {'n': 0, 'sdve': [None] * NSB, 'tact': [None] * NTMP, 'ptpe': [None] * NPT,
              'accfree': [None] * 4, 'omfree': [None, None], 'ofree': None, 'onfree': None,
              'ostfree': [None, None], 'tbfree': None, 'osti': 0, 'ssqfree': None}
        deferred = []

        def run_head(h):
            hb = h % 2
            if h + 1 < H:
                load_head(h + 1)
            slope = 2.0 ** (-8.0 * (h + 1) / H)
            dmin = 80.0 / slope
            kts_of = {}
            for qc in range(NQ):
                keep = []
                for kt in range(NK):
                    dist = max(0, kt * 128 - (qc * 512 + 511), qc * 512 - (kt * 128 + 127))
                    if dist < dmin:
                        keep.append(kt)
                kts_of[qc] = keep
            steps = [(qc, m, kt) for qc in range(NQ) for m in range(2) for kt in kts_of[qc]]
            S_tok = {}
            P_tok = {}

            def emit_S(i):
                qc, m, kt = steps[i]
                n = st['n'] + i
                sbk = sbanks[n % NSB]
                s_tok = P.op('pe', lambda e: e.matmul(
                    banks[sbk], kb[hb][:, m, kt * 128:(kt + 1) * 128], qb[hb][:, m, qc * 512:(qc + 1) * 512],
                    start=True, stop=True), [loadtok[hb], st['sdve'][n % NSB]])
                u0 = 512 * qc - 128 * kt + cfg.AOFF
                t_tok = P.op('dve', lambda e: e.scalar_tensor_tensor(
                    out=tmp[n % NTMP], in0=atab[:, u0:u0 + 512], scalar=slope / scale, in1=banks[sbk],
                    op0=ALU.mult, op1=ALU.add), [s_tok, st['tact'][n % NTMP], ta])
                st['sdve'][n % NSB] = t_tok
                kcol = cfg.g_kmask + kt
                p_tok = P.op('act', lambda e: e.activation(
                    out=pt[n % NPT], in_=tmp[n % NTMP], func=AF.Exp, bias=self.cg[:, kcol:kcol + 1], scale=scale),
                    [t_tok, st['ptpe'][n % NPT]])
                st['tact'][n % NTMP] = p_tok
                P_tok[i] = p_tok

            def emit_PV(i):
                qc, m, kt = steps[i]
                n = st['n'] + i
                last = None
                for qs in range(4):
                    w_ = [P_tok[i]]
                    first = (kt == kts_of[qc][0])
                    lastk = (kt == kts_of[qc][-1])
                    if first:
                        w_.append(st['accfree'][qs])
                    last = P.op('pe', lambda e, qs=qs, first=first, lastk=lastk: e.matmul(
                        banks[qs][:, 0:VD + 1], pt[n % NPT][:, qs * 128:(qs + 1) * 128], vb[hb][:, kt, :],
                        start=first, stop=lastk), w_)
                st['ptpe'][n % NPT] = last
                if kt == kts_of[qc][-1]:
                    finish_group(qc, m, last)
                return last

            def finish_group(qc, m, lastpe):
                r_tok = []
                for qs in range(4):
                    c_ = P.op('dve', lambda e, qs=qs: e.tensor_scalar(
                        out=rec[:, m * 4 + qs:m * 4 + qs + 1], in0=banks[qs][:, VD:VD + 1], scalar1=1e-30,
                        scalar2=None, op0=ALU.max), [lastpe, st['omfree'][m]])
                    r_tok.append(P.op('dve', lambda e, qs=qs: e.reciprocal(
                        out=rec[:, m * 4 + qs:m * 4 + qs + 1], in_=rec[:, m * 4 + qs:m * 4 + qs + 1]), [c_]))
                ev = []
                for qs in range(4):
                    if qs % 2 == 0:
                        tk = P.op('dve', lambda e, qs=qs: e.tensor_scalar(
                            out=Om[m][:, qs, :], in0=banks[qs][:, 0:VD], scalar1=rec[:, m * 4 + qs:m * 4 + qs + 1],
                            scalar2=None, op0=ALU.mult), [r_tok[qs], st['omfree'][m]])
                    else:
                        tk = P.op('act', lambda e, qs=qs: e.activation(
                            out=Om[m][:, qs, :], in_=banks[qs][:, 0:VD], func=AF.Identity,
                            scale=rec[:, m * 4 + qs:m * 4 + qs + 1]), [r_tok[qs], st['omfree'][m]])
                    st['accfree'][qs] = tk
                    ev.append(tk)
                if m == 0:
                    st['om0'] = list(ev)
                    return
                c_tok = P.op('dve', lambda e: e.scalar_tensor_tensor(
                    out=o, in0=Om[1], scalar=self.neglam[:, 1:2], in1=Om[0], op0=ALU.mult, op1=ALU.add),
                    [list(ev), st['om0'], st['ofree']])
                st['omfree'] = [c_tok, c_tok]
                z_tok = P.op('pool', lambda e: e.memset(ssq, 0.0), [st['ssqfree']])
                sq_tok = None
                for qs in range(4):
                    sq_tok = P.op('act', lambda e, qs=qs: e.activation(
                        out=junk, in_=o[:, qs, :], func=AF.Square, accum_out=ssq[:, qs:qs + 1]),
                        [c_tok, z_tok, sq_tok])
                r1 = P.op('dve', lambda e: e.tensor_scalar(out=rs2, in0=ssq, scalar1=1.0 / VD, scalar2=NORM_EPS,
                                                           op0=ALU.mult, op1=ALU.add), [sq_tok, st['onfree']])
                r2a = P.op('act', lambda e: e.activation(out=rs2, in_=rs2, func=AF.Sqrt), [r1])
                r2 = P.op('dve', lambda e: e.reciprocal(out=rs2, in_=rs2), [r2a])
                st['ssqfree'] = r1
                n_tok = None
                for qs in range(4):
                    n_tok = P.op('dve', lambda e, qs=qs: e.scalar_tensor_tensor(
                        out=on[:, qs, :], in0=o[:, qs, :], scalar=rs2[:, qs:qs + 1], in1=self.gsub,
                        op0=ALU.mult, op1=ALU.mult), [r2, st['onfree']])
                st['ofree'] = n_tok

                def do_transposes(n_tok=n_tok, qc=qc, h=h):
                    tl = None
                    for vc in range(VD // 128):
                        for qs in range(4):
                            i_ = vc * 4 + qs
                            tl = P.op('pe', lambda e, vc=vc, qs=qs, i_=i_: e.transpose(
                                out=tb[:, i_ * 128:(i_ + 1) * 128], in_=on[:, qs, vc * 128:(vc + 1) * 128],
                                identity=self.ident_bf), [n_tok, st['tbfree']])
                    ob = st['osti']
                    st['osti'] = 1 - ob
                    c_ = P.op('act', lambda e: e.activation(
                        out=ost[ob].rearrange("p a b -> p (a b)"), in_=tb, func=AF.Copy), [tl, st['ostfree'][ob]])
                    st['tbfree'] = c_
                    st['onfree'] = tl
                    st['ostfree'][ob] = P.dma('sp', lambda e: e.dma_start(
                        out=fm(self.ONT)[:, h * (VD // 128):(h + 1) * (VD // 128), qc * 512:(qc + 1) * 512],
                        in_=ost[ob]), dso[ob], [c_])
                deferred.append([6, do_transposes])

            ns = len(steps)
            for i0 in range(min(LA, ns)):
                emit_S(i0)
            lastpv = None
            for i in range(ns):
                lastpv = emit_PV(i)
                if i + LA < ns:
                    emit_S(i + LA)
                for d in deferred:
                    d[0] -= 1
                for d in [d for d in deferred if d[0] <= 0]:
                    d[1]()
                    deferred.remove(d)
            st['n'] += ns
            headdone[hb] = lastpv

        for h in range(H):
            run_head(h)
        for d in deferred:
            d[1]()
        self.phase_end()

    def conv1_phase(self, l):
        cfg, P, A = self.cfg, self.P, self.A
        T, CW, CCc = cfg.T, cfg.CW, cfg.CCc
        PAD = (CW - 1) // 2
        self.phase_begin()
        A = self.A
        banks = self.banks
        ypad = [A.get([128, T + 2 * PAD + 2], BF16) for _ in range(2)]
        diag = [A.get([128, CW, 128], BF16) for _ in range(2)]
        zst = [A.get([128, T], F32) for _ in range(2)]
        dsl = [P.ds("in0"), P.ds("in1")]
        dso = [P.ds("st0"), P.ds("st1")]
        halo = []
        for i in range(2):
            halo.append(P.op('pool', lambda e, i=i: e.memset(ypad[i][:, 0:PAD], 0.0)))
            halo.append(P.op('pool', lambda e, i=i: e.memset(ypad[i][:, PAD + T:PAD + T + PAD], 0.0)))
        pedone = [None, None]
        stdone = [None, None]
        bankfree = [None] * 8
        bn = 0
        ldtok = {}
        tm = A.get([128, T], BF16)
        tmtok = P.dma("pool", lambda e: e.dma_start(out=tm, in_=self.tmask_in), P.ds("pm"))

        def load(c):
            b = c % 2
            t_ = P.dma('sp', lambda e: e.dma_start(out=ypad[b][:, PAD:PAD + T], in_=self.YT[c * 128:(c + 1) * 128, :]),
                       dsl[b], [pedone[b]] + halo)
            ldtok[c] = P.op('pool' if c % 2 else 'dve', lambda e: e.tensor_tensor(
                out=ypad[b][:, PAD:PAD + T], in0=ypad[b][:, PAD:PAD + T], in1=tm, op=ALU.mult), [t_, tmtok])

        dltok = {}

        def build_diag(c):
            b = c % 2
            dl = [None, None]
            for j in range(CW):
                eng = 'dve' if j % 3 != 2 else 'pool'
                col = cfg.c_wdw + c * CW + j
                dl[0 if eng == 'dve' else 1] = P.op(eng, lambda e, j=j, col=col, b=b: e.tensor_scalar(
                    out=diag[b][:, j, :], in0=self.ident_bf, scalar1=self.cl[:, col:col + 1], scalar2=None,
                    op0=ALU.mult), [pedone[b]])
            dltok[c] = dl

        load(0)
        build_diag(0)
        for c in range(CCc):
            b = c % 2
            if c + 1 < CCc:
                load(c + 1)
                build_diag(c + 1)
            dl = dltok[c]
            evs = []
            last = None
            for tt in range(T // 512):
                bk = bn % 8
                bn += 1
                for j in range(CW):
                    w_ = [ldtok[c], dl[0], dl[1], bankfree[bk]] if j == 0 else []
                    last = P.op('pe', lambda e, j=j, tt=tt, bk=bk, b=b: e.matmul(
                        banks[bk], diag[b][:, j, :], ypad[b][:, tt * 512 + j:tt * 512 + j + 512],
                        start=(j == 0), stop=(j == CW - 1)), w_)
                bcol = cfg.c_bdw + c
                if tt % 2 == 0:
                    tk = P.op('act', lambda e, tt=tt, bk=bk, bcol=bcol, b=b: e.activation(
                        out=zst[b][:, tt * 512:(tt + 1) * 512], in_=banks[bk], func=AF.Identity,
                        bias=self.cl[:, bcol:bcol + 1]), [last, stdone[b]])
                else:
                    tk = P.op('dve', lambda e, tt=tt, bk=bk, bcol=bcol, b=b: e.tensor_scalar(
                        out=zst[b][:, tt * 512:(tt + 1) * 512], in0=banks[bk], scalar1=self.cl[:, bcol:bcol + 1],
                        scalar2=None, op0=ALU.add), [last, stdone[b]])
                bankfree[bk] = tk
                evs.append(tk)
            pedone[b] = last
            stdone[b] = P.dma('sp', lambda e, c=c, b=b: e.dma_start(out=self.ZT[c * 128:(c + 1) * 128, :], in_=zst[b]),
                              dso[b], evs)
        self.phase_end()

    def conv2_phase(self, l):
        cfg, P, A = self.cfg, self.P, self.A
        T, CCc, CC = cfg.T, cfg.CCc, cfg.CC
        self.phase_begin()
        A = self.A
        banks = self.banks
        z = [A.get([128, CCc, 512], F32) for _ in range(2)]
        zb = A.get([128, CCc, 512], BF16)
        zs = A.get([128, CCc, 512], BF16)
        mean = A.get([128, 512], F32)
        msq = A.get([128, 512], F32)
        rstd = A.get([128, 512], F32)
        t1 = [A.get([128, 512], F32) for _ in range(4)]
        outb = [A.get([128, CCc, 512], BF16) for _ in range(2)]
        dsl = [P.ds("in0"), P.ds("in1")]
        dso = [P.ds("st0"), P.ds("st1")]
        zfree = [None, None]
        stdone = [None, None]
        pedone = None
        statfree = None
        t1free = [None] * 4
        ld = {}

        def load(i):
            b = i % 2
            ld[i] = P.dma('sp', lambda e: e.dma_start(out=z[b], in_=fm(self.ZT)[:, :, i * 512:(i + 1) * 512]),
                          dsl[b], zfree[b] or [])

        load(0)
        nt = 0
        for i in range(T // 512):
            b = i % 2
            if i + 1 < T // 512:
                load(i + 1)
            a1 = P.op('act', lambda e, b=b: e.activation(out=zb, in_=z[b], func=AF.Copy), [ld[i], pedone])
            a2 = P.op('act', lambda e, b=b: e.activation(out=zs, in_=z[b], func=AF.Square), [ld[i], pedone])
            mm = None
            for c in range(CCc):
                mm = P.op('pe', lambda e, c=c: e.matmul(banks[0], self.ones_bf, zb[:, c, :], start=(c == 0),
                                                        stop=(c == CCc - 1)), [a1, statfree] if c == 0 else [])
            for c in range(CCc):
                mm = P.op('pe', lambda e, c=c: e.matmul(banks[1], self.ones_bf, zs[:, c, :], start=(c == 0),
                                                        stop=(c == CCc - 1)), [a2] if c == 0 else [])
            pedone = mm
            s1 = P.op('dve', lambda e: e.tensor_scalar(out=mean, in0=banks[0], scalar1=1.0 / CC, scalar2=None,
                                                       op0=ALU.mult), [mm] + (zfree[1 - b] or []))
            s2 = P.op('dve', lambda e: e.tensor_tensor(out=msq, in0=mean, in1=mean, op=ALU.mult), [s1])
            s3 = P.op('dve', lambda e: e.scalar_tensor_tensor(out=rstd, in0=banks[1], scalar=1.0 / CC, in1=msq,
                                                              op0=ALU.mult, op1=ALU.subtract), [s2])
            s4a = P.op('dve', lambda e: e.tensor_scalar(out=rstd, in0=rstd, scalar1=LN_EPS, scalar2=None,
                                                        op0=ALU.add), [s3])
            s4b = P.op('act', lambda e: e.activation(out=rstd, in_=rstd, func=AF.Sqrt), [s4a])
            s4 = P.op('dve', lambda e: e.reciprocal(out=rstd, in_=rstd), [s4b])
            statfree = s3
            lasts = []
            for c in range(CCc):
                eng = 'dve' if c % 3 != 2 else 'pool'
                tb_ = nt % 4
                nt += 1
                u1 = P.op(eng, lambda e, c=c, tb_=tb_, b=b: e.tensor_tensor(out=t1[tb_], in0=z[b][:, c, :], in1=mean,
                                                                       op=ALU.subtract), [s4, t1free[tb_]])
                u2 = P.op(eng, lambda e, c=c, tb_=tb_: e.tensor_tensor(out=t1[tb_], in0=t1[tb_], in1=rstd,
                                                                       op=ALU.mult), [u1])
                gcol = cfg.c_gcln + c
                bcol = cfg.c_bcln + c
                u3 = P.op('act', lambda e, c=c, tb_=tb_, gcol=gcol, bcol=bcol, b=b: e.activation(
                    out=outb[b][:, c, :], in_=t1[tb_], func=AF.Silu, bias=self.cl[:, bcol:bcol + 1],
                    scale=self.cl[:, gcol:gcol + 1]), [u2, stdone[b]])
                t1free[tb_] = u3
                lasts.append(u3)
            zfree[b] = [lasts[-1]]
            stdone[b] = P.dma('sp', lambda e, i=i, b=b: e.dma_start(out=fm(self.CT)[:, :, i * 512:(i + 1) * 512], in_=outb[b]),
                              dso[b], [lasts[-1]])
        self.phase_end()


def _consts(cfg, inp):
    DEPTH = cfg.DEPTH
    cl = np.zeros((DEPTH, 128, cfg.NCL), np.float32)

    def pc(v):
        return np.ascontiguousarray(v.reshape(-1, 128).T)

    for l in range(DEPTH):
        c = cl[l]
        c[:, cfg.c_gmix:cfg.c_gmix + cfg.DC] = pc(inp["g_mix"][l])
        c[:, cfg.c_bgate:cfg.c_bgate + 2 * cfg.DC] = pc(inp["b_gate"][l])
        c[:, cfg.c_bdw:cfg.c_bdw + cfg.CCc] = pc(inp["b_dw"][l])
        c[:, cfg.c_gcln:cfg.c_gcln + cfg.CCc] = pc(inp["g_conv_ln"][l])
        c[:, cfg.c_bcln:cfg.c_bcln + cfg.CCc] = pc(inp["b_conv_ln"][l])
        c[:, cfg.c_bcproj:cfg.c_bcproj + cfg.DC] = pc(inp["b_conv_proj"][l])
        c[:, cfg.c_gffn:cfg.c_gffn + cfg.DC] = pc(inp["g_ffn"][l])
        wd = inp["w_dw"][l][:, 0, :]
        wd = wd.T.reshape(cfg.CCc, 128, cfg.CW).transpose(1, 0, 2).reshape(128, cfg.CCc * cfg.CW)
        c[:, cfg.c_wdw:cfg.c_wdw + cfg.CCc * cfg.CW] = wd
        c[:, cfg.c_gsub:cfg.c_gsub + cfg.VD] = inp["g_subln"][l][None, :]
        c[:, cfg.c_lq:cfg.c_lq + 2 * cfg.HD] = inp["lambda_q"][l].reshape(1, -1)
        c[:, cfg.c_lk:cfg.c_lk + 2 * cfg.HD] = inp["lambda_k"][l].reshape(1, -1)
    return cl


def _globals(cfg, g_final, nvalid):
    cg = np.zeros((128, cfg.NCG), np.float32)
    cg[:, cfg.g_gfinal:cfg.g_gfinal + cfg.DC] = g_final.reshape(-1, 128).T
    cg[:, cfg.g_ident:cfg.g_ident + 128] = np.eye(128, dtype=np.float32)
    kpos = np.arange(cfg.T).reshape(cfg.NK, 128).T
    cg[:, cfg.g_kmask:cfg.g_kmask + cfg.NK] = np.where(kpos < nvalid, 0.0, MASK_NEG)
    return cg


def _atab(cfg):
    p = np.arange(128)[:, None]
    u = np.arange(cfg.AW)[None, :]
    return (-np.abs(u - cfg.AOFF - p)).astype(np.float32)


_NC_CACHE = {}


def run_trunk(cfg, seqs, inp, n_cores=8):
    key = (cfg.T, cfg.D, cfg.H, cfg.DFF, cfg.DEPTH)
    if key not in _NC_CACHE:
        _NC_CACHE[key] = Builder(cfg).build()
    nc = _NC_CACHE[key]
    cl = _consts(cfg, inp)
    atab = _atab(cfg)
    wmap = {k: np.ascontiguousarray(inp[k], dtype=np.float32) for k in
            ("w_in", "w_attn_proj", "w_conv_proj", "w_out", "w_ffn_in", "w_ffn_out")}
    in_maps = []
    if n_cores == 8 and len(seqs) == 6:
        core_of_seq = [0, 4, 1, 2, 5, 6]
    else:
        core_of_seq = list(range(len(seqs)))
    seq_of_core = {c: i for i, c in enumerate(core_of_seq)}
    for c in range(n_cores):
        xT = np.zeros((cfg.D, cfg.T), np.float32)
        if c in seq_of_core:
            s = seqs[seq_of_core[c]]
            S = s.shape[0]
            xT[:, :S] = s.T
        else:
            S = cfg.T
        tmask = np.zeros((128, cfg.T), np.float32)
        tmask[:, :S] = 1.0
        m = {"xT": xT, "cl": cl, "cg": _globals(cfg, inp["g_final"], S), "atab": atab, "tmask": tmask}
        m.update(wmap)
        in_maps.append(m)
    res = run_bass_kernel_spmd(nc, in_maps, core_ids=list(range(n_cores)))
    outs = []
    for i in range(len(seqs)):
        S = seqs[i].shape[0]
        outs.append(np.ascontiguousarray(res.results[core_of_seq[i]]["yT"][:, :S].T))
    return outs


def kernel(x_prompt, x_sample, g_mix, w_in, b_gate, lambda_q, lambda_k, g_subln,
           w_attn_proj, w_dw, b_dw, g_conv_ln, b_conv_ln, w_conv_proj, b_conv_proj,
           w_out, g_ffn, w_ffn_in, w_ffn_out, g_final):
    inp = dict(g_mix=g_mix, w_in=w_in, b_gate=b_gate, lambda_q=lambda_q, lambda_k=lambda_k, g_subln=g_subln,
               w_attn_proj=w_attn_proj, w_dw=w_dw, b_dw=b_dw, g_conv_ln=g_conv_ln, b_conv_ln=b_conv_ln,
               w_conv_proj=w_conv_proj, b_conv_proj=b_conv_proj, w_out=w_out, g_ffn=g_ffn,
               w_ffn_in=w_ffn_in, w_ffn_out=w_ffn_out, g_final=g_final)
    inp = {k: np.asarray(v, dtype=np.float32) for k, v in inp.items()}
    x_prompt = np.asarray(x_prompt, dtype=np.float32)
    x_sample = np.asarray(x_sample, dtype=np.float32)
    cfg = Cfg()
    seqs = [x_prompt[b] for b in range(x_prompt.shape[0])] + [x_sample[b] for b in range(x_sample.shape[0])]
    outs = run_trunk(cfg, seqs, inp)
    nb = x_prompt.shape[0]
    y_prompt = np.stack(outs[:nb]).astype(np.float32)
    y_sample = np.stack(outs[nb:]).astype(np.float32)
    return (y_prompt, y_sample)
```

```python
import numpy as np
from contextlib import ExitStack
import concourse.bass as bass
import concourse.mybir as mybir
from concourse.bass_utils import run_bass_kernel_spmd

F32 = mybir.dt.float32
BF16 = mybir.dt.bfloat16
U8 = mybir.dt.uint8
AF = mybir.ActivationFunctionType
ALU = mybir.AluOpType

NORM_EPS = 1e-6
LN_EPS = 1e-5
MASK_NEG = -30000.0
SB_BYTES = 192 * 1024


class Cfg:
    def __init__(self, T=4096, D=2048, H=8, DFF=5632, DEPTH=4, CW=31):
        self.T, self.D, self.H, self.DFF, self.DEPTH, self.CW = T, D, H, DFF, DEPTH, CW
        self.HD = 128
        self.VD = 256
        self.QK = H * 2 * self.HD
        self.VW = H * self.VD
        self.CC = D
        self.IN_W = 2 * self.QK + self.VW + 2 * self.CC + 2 * D
        self.DC = D // 128
        self.CCc = self.CC // 128
        self.NK = T // 128
        self.NQ = T // 512
        self.AOFF = T - 128
        self.AW = (self.NQ - 1) * 512 + 512 + self.AOFF
        c = 0
        self.c_gmix = c; c += self.DC
        self.c_bgate = c; c += 2 * self.DC
        self.c_bdw = c; c += self.CCc
        self.c_gcln = c; c += self.CCc
        self.c_bcln = c; c += self.CCc
        self.c_bcproj = c; c += self.DC
        self.c_gffn = c; c += self.DC
        self.c_wdw = c; c += self.CCc * CW
        self.c_gsub = c; c += self.VD
        self.c_lq = c; c += 2 * self.HD
        self.c_lk = c; c += 2 * self.HD
        self.NCL = c
        g = 0
        self.g_gfinal = g; g += self.DC
        self.g_ident = g; g += 128
        self.g_kmask = g; g += self.NK
        self.NCG = g


def lambda_init(layer):
    return 0.8 - 0.6 * float(np.exp(-0.3 * layer))


class DmaSem:
    def __init__(self, sem):
        self.sem = sem
        self.count = 0


class Prog:
    ENG = ('pe', 'act', 'dve', 'pool', 'sp')

    def __init__(self, nc, stack):
        self.nc = nc
        self.stack = stack
        self.streams = {k: [] for k in self.ENG}
        self.cnt = {k: 0 for k in self.ENG}
        self.esem = {k: stack.enter_context(nc.semaphore("s_" + k)) for k in self.ENG}
        self.dsems = {}
        self.pending = {k: [] for k in self.ENG}
        self.waited = {k: {} for k in self.ENG}

    def ds(self, name):
        if name not in self.dsems:
            self.dsems[name] = DmaSem(self.stack.enter_context(self.nc.semaphore("d_" + name)))
        return self.dsems[name]

    def _w(self, eng, waits):
        w = []

        def fl(x):
            if x is None:
                return
            if isinstance(x, list):
                for y in x:
                    fl(y)
            else:
                w.append(x)
        fl(list(waits))
        if self.pending[eng]:
            w = self.pending[eng] + w
            self.pending[eng] = []
        return w

    def op(self, eng, fn, waits=()):
        self.cnt[eng] += 1
        self.streams[eng].append((fn, self._w(eng, waits), None))
        return (eng, self.cnt[eng])

    def dma(self, eng, fn, ds, waits=()):
        ds.count += 16
        self.streams[eng].append((fn, self._w(eng, waits), ds))
        return (ds, ds.count)

    def last(self, eng):
        return (eng, self.cnt[eng]) if self.cnt[eng] else None

    def barrier(self):
        toks = [(k, self.cnt[k]) for k in self.ENG if self.cnt[k]]
        toks += [(d, d.count) for d in self.dsems.values() if d.count]
        for k in self.ENG:
            self.pending[k] = list(toks)

    def finish(self):
        self.barrier()
        for k in self.ENG:
            w = self._w(k, [])
            self.streams[k].append((None, w, None))

    def flush(self):
        nc = self.nc
        P = self
        with nc.Block() as block:
            @block.tensor
            def _(e):
                P.replay('pe', e)

            @block.scalar
            def _(e):
                P.replay('act', e)

            @block.vector
            def _(e):
                P.replay('dve', e)

            @block.gpsimd
            def _(e):
                P.replay('pool', e)

            @block.sync
            def _(e):
                P.replay('sp', e)
        for k in self.ENG:
            self.streams[k] = []

    def replay(self, eng, e):
        waited = self.waited[eng]
        for fn, waits, ds in self.streams[eng]:
            for key, val in waits:
                kk = key if isinstance(key, str) else id(key)
                if waited.get(kk, 0) >= val:
                    continue
                waited[kk] = val
                sem = self.esem[key] if isinstance(key, str) else key.sem
                e.wait_ge(sem, val)
            if fn is None:
                continue
            inst = fn(e)
            if ds is None:
                inst.then_inc(self.esem[eng], 1)
            else:
                inst.then_inc(ds.sem, 16)


class Alloc:
    def __init__(self, sb, base, limit):
        self.sb, self.base, self.off, self.limit = sb, base, base, limit

    def reset(self):
        self.off = self.base

    def get(self, shape, dt):
        esz = 4 if dt == F32 else 2
        n = int(np.prod(shape[1:]))
        nb = (n * esz + 63) // 64 * 64
        o = self.off
        self.off += nb
        assert self.off <= self.limit, ("SBUF overflow", self.off, self.limit)
        v = self.sb[:, o:o + n * esz].bitcast(dt)
        if len(shape) == 3:
            v = v.rearrange("p (a b) -> p a b", b=shape[2])
        return v


def fm(ap):
    return ap.rearrange("(c p) t -> p c t", p=128)


class Builder:
    def __init__(self, cfg):
        self.cfg = cfg

    def build(self):
        cfg = self.cfg
        T, D, DEPTH = cfg.T, cfg.D, cfg.DEPTH
        nc = bass.Bass("TRN2", target_bir_lowering=False)
        self.nc = nc
        dt_in = lambda name, shape: nc.dram_tensor(name, shape, F32, kind="ExternalInput").ap()
        self.xT_in = dt_in("xT", [D, T])
        self.cl_in = dt_in("cl", [DEPTH, 128, cfg.NCL])
        self.cg_in = dt_in("cg", [128, cfg.NCG])
        self.atab_in = dt_in("atab", [128, cfg.AW])
        self.tmask_in = dt_in("tmask", [128, T])
        self.w_in = dt_in("w_in", [DEPTH, D, cfg.IN_W])
        self.w_ap = dt_in("w_attn_proj", [DEPTH, cfg.VW, D])
        self.w_cp = dt_in("w_conv_proj", [DEPTH, cfg.CC, D])
        self.w_out = dt_in("w_out", [DEPTH, D, D])
        self.w_fi = dt_in("w_ffn_in", [DEPTH, D, 2 * cfg.DFF])
        self.w_fo = dt_in("w_ffn_out", [DEPTH, cfg.DFF, D])
        self.yT_out = nc.dram_tensor("yT", [D, T], F32, kind="ExternalOutput").ap()
        scr = lambda name, shape, dt: nc.dram_tensor(name, shape, dt).ap()
        self.XT = scr("s_XT", [D, T], F32)
        self.HT = scr("s_HT", [D, T], BF16)
        self.QKT = scr("s_QKT", [2 * cfg.QK, T], BF16)
        self.V = scr("s_V", [T, cfg.VW], BF16)
        self.YT = scr("s_YT", [cfg.CC, T], BF16)
        self.GT = scr("s_GT", [2 * D, T], BF16)
        self.ONT = scr("s_ONT", [cfg.VW, T], BF16)
        self.ZT = scr("s_ZT", [cfg.CC, T], F32)
        self.CT = scr("s_CT", [cfg.CC, T], BF16)
        self.MT = scr("s_MT", [D, T], BF16)
        self.AT = scr("s_AT", [cfg.DFF, T], BF16)

        with ExitStack() as stack:
            P = Prog(nc, stack)
            self.P = P
            GB = 16 * 1024
            gsb = stack.enter_context(nc.sbuf_tensor("gsb", [128, GB], U8))
            G = Alloc(gsb, 0, GB)
            self.cg = G.get([128, cfg.NCG], F32)
            self.cl = G.get([128, cfg.NCL], F32)
            self.ident_bf = G.get([128, 128], BF16)
            self.ones_bf = G.get([128, 128], BF16)
            self.neglam = G.get([128, 2], F32)
            self.gsub = G.get([128, cfg.VD], F32)
            self.lamtmp = G.get([128, 2 * cfg.HD], F32)
            self.lamred = G.get([128, 4], F32)
            self.PH_BYTES = SB_BYTES - GB
            self.phase_no = 0
            self.ph = None
            self.A = None
            self.banks = None
            t0 = P.dma('sp', lambda e: e.dma_start(out=self.cg, in_=self.cg_in), P.ds("c0"))
            t1 = P.op('dve', lambda e: e.tensor_copy(out=self.ident_bf, in_=self.cg[:, cfg.g_ident:cfg.g_ident + 128]), [t0])
            t2 = P.op('pool', lambda e: e.memset(self.ones_bf, 1.0))
            P.barrier()

            xsrc = self.xT_in
            for l in range(DEPTH):
                self.layer_consts(l)
                self.norm_phase(xsrc, cfg.c_gmix, self.HT, final=False)
                self.inproj_phase(l)
                self.attn_phase(l)
                self.conv1_phase(l)
                self.conv2_phase(l)
                self.merge_phase(l)
                self.resid_phase(l, self.w_out[l], self.MT, cfg.D // 128, xsrc, TS=min(2048, T))
                xsrc = self.XT
                self.norm_phase(xsrc, cfg.c_gffn, self.HT, final=False)
                self.ffnin_phase(l)
                self.resid_phase(l, self.w_fo[l], self.AT, cfg.DFF // 128, xsrc, TS=min(1024, T))
            self.norm_phase(xsrc, None, self.yT_out, final=True)
            P.finish()
            P.flush()
        return nc

    def phase_begin(self):
        nc = self.nc
        self.P.barrier()
        self.ph = ExitStack()
        self.phase_no += 1
        sbp = self.ph.enter_context(nc.sbuf_tensor("sb%d" % self.phase_no, [128, self.PH_BYTES], U8))
        psp = self.ph.enter_context(nc.psum_tensor("ps%d" % self.phase_no, [128, 8 * 512], F32))
        self.banks = [psp[:, b * 512:(b + 1) * 512] for b in range(8)]
        self.A = Alloc(sbp, 0, self.PH_BYTES)

    def phase_end(self):
        self.P.barrier()
        self.P.flush()
        self.ph.close()
        self.ph = None

    def layer_consts(self, l):
        cfg, P = self.cfg, self.P
        HD = cfg.HD
        P.barrier()
        t0 = P.dma('sp', lambda e: e.dma_start(out=self.cl, in_=self.cl_in[l]), P.ds("c0"))
        lq = self.cl[:, cfg.c_lq:cfg.c_lq + 2 * HD]
        lk = self.cl[:, cfg.c_lk:cfg.c_lk + 2 * HD]
        t1 = P.op('dve', lambda e: e.tensor_tensor(out=self.lamtmp, in0=lq, in1=lk, op=ALU.mult), [t0])
        t2 = P.op('dve', lambda e: e.tensor_reduce(
            out=self.lamred[:, 0:2], in_=self.lamtmp.rearrange("p (a b) -> p a b", b=HD),
            axis=mybir.AxisListType.X, op=ALU.add), [t1])
        t3 = P.op('act', lambda e: e.activation(out=self.lamred[:, 2:4], in_=self.lamred[:, 0:2], func=AF.Exp), [t2])
        li = lambda_init(l)
        t4 = P.op('dve', lambda e: e.tensor_tensor(out=self.neglam[:, 0:1], in0=self.lamred[:, 3:4],
                                                   in1=self.lamred[:, 2:3], op=ALU.subtract), [t3])
        t5 = P.op('dve', lambda e: e.tensor_scalar(out=self.neglam[:, 1:2], in0=self.neglam[:, 0:1],
                                                   scalar1=-li, scalar2=None, op0=ALU.add), [t4])
        t6 = P.op('dve', lambda e: e.tensor_scalar(out=self.gsub, in0=self.cl[:, cfg.c_gsub:cfg.c_gsub + cfg.VD],
                                                   scalar1=(1.0 - li), scalar2=None, op0=ALU.mult), [t5])
        P.barrier()

    def norm_phase(self, src, gcol, dst, final):
        cfg, P, A = self.cfg, self.P, self.A
        DC, D = cfg.DC, cfg.D
        self.phase_begin()
        A = self.A
        odt = F32 if final else BF16
        xin = [A.get([128, DC, 512], F32) for _ in range(2)]
        sq = [A.get([128, DC, 512], BF16) for _ in range(2)]
        hout = [A.get([128, DC, 512], odt) for _ in range(2)]
        rstd = [A.get([128, 512], F32) for _ in range(2)]
        gsrc = self.cg if final else self.cl
        gc = cfg.g_gfinal if final else gcol
        dsi = [P.ds("in0"), P.ds("in1")]
        dso = [P.ds("st0"), P.ds("st1")]
        hdone = [None, None]
        mmdone = [None, None]
        stdone = [None, None]
        r2done = [None, None]
        ldt = {}

        def issue_ld(i):
            b = i % 2
            ts = slice(i * 512, (i + 1) * 512)
            ldt[i] = P.dma('sp', lambda e: e.dma_start(out=xin[b], in_=fm(src)[:, :, ts]), dsi[b],
                           (hdone[b] or []))

        issue_ld(0)
        for i in range(cfg.NQ):
            b = i % 2
            ts = slice(i * 512, (i + 1) * 512)
            if i + 1 < cfg.NQ:
                issue_ld(i + 1)
            ld = ldt[i]
            sqt = P.op('act', lambda e, b=b: e.activation(out=sq[b], in_=xin[b], func=AF.Square), [ld, mmdone[b]])
            mm = None
            for c in range(DC):
                mm = P.op('pe', lambda e, b=b, c=c: e.matmul(self.banks[b], self.ones_bf, sq[b][:, c, :],
                                                             start=(c == 0), stop=(c == DC - 1)),
                          [sqt, r2done[b]] if c == 0 else [])
            mmdone[b] = mm
            r1 = P.op('dve', lambda e, b=b: e.tensor_scalar(out=rstd[b], in0=self.banks[b], scalar1=1.0 / D,
                                                            scalar2=NORM_EPS, op0=ALU.mult, op1=ALU.add),
                      [mm] + (hdone[b] or []))
            r2a = P.op('act', lambda e, b=b: e.activation(out=rstd[b], in_=rstd[b], func=AF.Sqrt), [r1])
            r2 = P.op('dve', lambda e, b=b: e.reciprocal(out=rstd[b], in_=rstd[b]), [r2a])
            r2done[b] = r1
            lastd = lastp = None
            for c in range(DC):
                eng = 'dve'
                tk = P.op(eng, lambda e, b=b, c=c: e.scalar_tensor_tensor(
                    out=hout[b][:, c, :], in0=xin[b][:, c, :], scalar=gsrc[:, gc + c:gc + c + 1],
                    in1=rstd[b], op0=ALU.mult, op1=ALU.mult), [r2, stdone[b]])
                if eng == 'dve':
                    lastd = tk
                else:
                    lastp = tk
            hdone[b] = [lastd, lastp]
            stdone[b] = P.dma('sp', lambda e, b=b, ts=ts: e.dma_start(out=fm(dst)[:, :, ts], in_=hout[b]), dso[b],
                              [lastd, lastp])
        self.phase_end()

    def linear(self, ins, TS, blocks):
        cfg, P, A = self.cfg, self.P, self.A
        T = cfg.T
        NS = T // TS
        NHALF = TS // 1024
        KCmax = max(kc for _, kc in ins)
        FBmax = max(sum(p[2] for p in b['pieces']) for b in blocks)
        in_sb = [A.get([128, kc, TS], BF16) for _, kc in ins]
        wbuf = [A.get([128, KCmax, FBmax], BF16) for _ in range(2)]
        dsw = [P.ds("w0"), P.ds("w1")]
        dsin = [P.ds("in0"), P.ds("in1")]
        wfree = [None, None]
        bankfree = [[None] * 4, [None] * 4]
        seq = [(s, bi) for s in range(NS) for bi in range(len(blocks))]
        wtok = {}

        def issue_w(n):
            s, bi = seq[n]
            wb = n % 2
            c0 = 0
            tk = None
            for (w2d, col0, ncols, in_idx) in blocks[bi]['pieces']:
                kc = ins[in_idx][1]
                tk = P.dma('pool', lambda e, wb=wb, c0=c0, w2d=w2d, col0=col0, ncols=ncols, kc=kc: e.dma_start(
                    out=wbuf[wb][:, 0:kc, c0:c0 + ncols],
                    in_=w2d.rearrange("(c p) f -> p c f", p=128)[:, :, col0:col0 + ncols]), dsw[wb], [wfree[wb]])
                c0 += ncols
            wtok[n] = tk

        issue_w(0)
        ucount = 0
        intok = [None] * len(ins)
        lastpe_all = None
        for n, (s, bi) in enumerate(seq):
            blk = blocks[bi]
            if bi == 0:
                for ii, (src, kc) in enumerate(ins):
                    tk = None
                    nsp = max(1, kc // 8)
                    for q in range(nsp):
                        cs = slice(q * kc // nsp, (q + 1) * kc // nsp)
                        tk = P.dma('sp', lambda e, ii=ii, cs=cs, s=s, src=src: e.dma_start(
                            out=in_sb[ii][:, cs, :], in_=fm(src)[:, cs, s * TS:(s + 1) * TS]), dsin[ii], [lastpe_all])
                    intok[ii] = tk
            if n + 1 < len(seq):
                issue_w(n + 1)
            wb = n % 2
            FB = sum(p[2] for p in blk['pieces'])
            kind = blk['kind']
            lastpe = None
            if kind == 'vtok':
                kc = ins[0][1]
                for g in range(TS // 512):
                    bs = ucount % 2
                    ucount += 1
                    bk = self.banks[bs * 4:bs * 4 + 4]
                    tok0 = s * TS + g * 512
                    for k in range(kc):
                        for j in range(4):
                            w_ = [wtok[n], intok[0], bankfree[bs][j]] if k == 0 else []
                            lastpe = P.op('pe', lambda e, k=k, j=j, g=g, wb=wb, bk=bk, FB=FB: e.matmul(
                                bk[j][:, 0:FB], in_sb[0][:, k, g * 512 + j * 128:g * 512 + (j + 1) * 128],
                                wbuf[wb][:, k, 0:FB], start=(k == 0), stop=(k == kc - 1)), w_)
                    rel = blk['epi'](0, tok0, bk, lastpe, None)
                    bankfree[bs] = rel
            else:
                nunits = FB // 256
                for u in range(nunits):
                    if kind == 'pair':
                        cols = [u * 128, FB // 2 + u * 128]
                        iidx = [blk['pieces'][0][3], blk['pieces'][1][3]]
                    else:
                        cols = [2 * u * 128, (2 * u + 1) * 128]
                        iidx = [blk['pieces'][0][3]] * 2
                    for h in range(NHALF):
                        bs = ucount % 2
                        ucount += 1
                        bk = self.banks[bs * 4:bs * 4 + 4]
                        tok0 = s * TS + h * 1024
                        pretoks = blk['pre'](u, tok0) if blk.get('pre') else None
                        kcs = [ins[iidx[0]][1], ins[iidx[1]][1]]
                        kc = max(kcs)
                        for k in range(kc):
                            for ci in range(2):
                                if k >= kcs[ci]:
                                    continue
                                for t in range(2):
                                    w_ = [wtok[n], intok[iidx[ci]], bankfree[bs][ci * 2 + t]] if k == 0 else []
                                    lastpe = P.op('pe', lambda e, k=k, ci=ci, t=t, h=h, wb=wb, bk=bk, cols=cols, iidx=iidx, kcs=kcs: e.matmul(
                                        bk[ci * 2 + t], wbuf[wb][:, k, cols[ci]:cols[ci] + 128],
                                        in_sb[iidx[ci]][:, k, h * 1024 + t * 512:h * 1024 + (t + 1) * 512],
                                        start=(k == 0), stop=(k == kcs[ci] - 1)), w_)
                        rel = blk['epi'](u, tok0, bk, lastpe, pretoks)
                        bankfree[bs] = rel
            wfree[wb] = lastpe
            lastpe_all = lastpe

    def stager(self, name, shape, dt, n=2):
        bufs = [self.A.get(shape, dt) for _ in range(n)]
        return {'bufs': bufs, 'tok': [None] * n, 'i': 0, 'ds': [self.P.ds("%s%d" % (name, i)) for i in range(n)]}

    def inproj_phase(self, l):
        cfg, P, A = self.cfg, self.P, self.A
        D, QK, VW, CC = cfg.D, cfg.QK, cfg.VW, cfg.CC
        self.phase_begin()
        A = self.A
        w = self.w_in[l]
        TS = min(2048, cfg.T)
        stg = self.stager("st", [128, 2, 1024], BF16)
        sig = [A.get([128, 1024], F32) for _ in range(2)]
        vst = self.stager("sv", [128, 4, 512], BF16)
        sigtok = [None, None]
        cnt = [0]
        bgc = cfg.c_bgate
        banks = self.banks

        def store2(s, row0s, dst, tok0, evs):
            b = s['i']
            buf = s['bufs'][b]
            t = None
            for ci, r0 in enumerate(row0s):
                t = P.dma('sp', lambda e, ci=ci, r0=r0, buf=buf: e.dma_start(
                    out=dst[r0:r0 + 128, tok0:tok0 + 1024], in_=buf[:, ci, :]), s['ds'][b], evs)
            s['tok'][b] = t
            s['i'] = (b + 1) % len(s['bufs'])

        def epi_copy(row_of_unit, dst):
            def f(u, tok0, bk, lastpe, pre):
                b = stg['i']
                buf = stg['bufs'][b]
                evs = []
                for ci in range(2):
                    for t in range(2):
                        eng = 'act' if (ci + t) % 2 == 0 else 'dve'
                        if eng == 'act':
                            tk = P.op('act', lambda e, ci=ci, t=t, buf=buf, bk=bk: e.activation(
                                out=buf[:, ci, t * 512:(t + 1) * 512], in_=bk[ci * 2 + t], func=AF.Copy),
                                [lastpe, stg['tok'][b]])
                        else:
                            tk = P.op('dve', lambda e, ci=ci, t=t, buf=buf, bk=bk: e.tensor_copy(
                                out=buf[:, ci, t * 512:(t + 1) * 512], in_=bk[ci * 2 + t]),
                                [lastpe, stg['tok'][b]])
                        evs.append(tk)
                r0 = row_of_unit(u)
                store2(stg, [r0, r0 + 128], dst, tok0, evs)
                return evs
            return f

        def epi_gates(row_of_unit):
            def f(u, tok0, bk, lastpe, pre):
                b = stg['i']
                buf = stg['bufs'][b]
                evs = []
                r0 = row_of_unit(u)
                for ci in range(2):
                    col = bgc + (r0 + ci * 128) // 128
                    for t in range(2):
                        tk = P.op('act', lambda e, ci=ci, t=t, buf=buf, bk=bk, col=col: e.activation(
                            out=buf[:, ci, t * 512:(t + 1) * 512], in_=bk[ci * 2 + t], func=AF.Sigmoid,
                            bias=self.cl[:, col:col + 1]), [lastpe, stg['tok'][b]])
                        evs.append(tk)
                store2(stg, [r0, r0 + 128], self.GT, tok0, evs)
                return evs
            return f

        def epi_glu(row_of_unit):
            def f(u, tok0, bk, lastpe, pre):
                b = stg['i']
                buf = stg['bufs'][b]
                sb_ = cnt[0] % 2
                cnt[0] += 1
                rel = [None] * 4
                evs = []
                for t in range(2):
                    ta = P.op('act', lambda e, t=t, bk=bk, sb_=sb_: e.activation(
                        out=sig[sb_][:, t * 512:(t + 1) * 512], in_=bk[2 + t], func=AF.Sigmoid),
                        [lastpe, sigtok[sb_]])
                    td = P.op('dve', lambda e, t=t, bk=bk, buf=buf, sb_=sb_: e.tensor_tensor(
                        out=buf[:, 0, t * 512:(t + 1) * 512], in0=bk[t], in1=sig[sb_][:, t * 512:(t + 1) * 512],
                        op=ALU.mult), [ta, stg['tok'][b]])
                    rel[2 + t] = ta
                    rel[t] = td
                    evs.append(td)
                sigtok[sb_] = evs[-1]
                r0 = row_of_unit(u)
                store2(stg, [r0], self.YT, tok0, evs)
                return rel
            return f

        def epi_v(col0):
            def f(u, tok0, bk, lastpe, pre):
                b = vst['i']
                buf = vst['bufs'][b]
                FB = min(512, VW)
                evs = []
                for j in range(4):
                    if j % 2 == 0:
                        tk = P.op('act', lambda e, j=j, buf=buf, bk=bk: e.activation(
                            out=buf[:, j, 0:FB], in_=bk[j][:, 0:FB], func=AF.Copy), [lastpe, vst['tok'][b]])
                    else:
                        tk = P.op('dve', lambda e, j=j, buf=buf, bk=bk: e.tensor_copy(
                            out=buf[:, j, 0:FB], in_=bk[j][:, 0:FB]), [lastpe, vst['tok'][b]])
                    evs.append(tk)
                vst['tok'][b] = P.dma('sp', lambda e, buf=buf: e.dma_start(
                    out=self.V[tok0:tok0 + 512, col0:col0 + FB].rearrange("(j p) f -> p j f", p=128),
                    in_=buf[:, :, 0:FB]), vst['ds'][b], evs)
                vst['i'] = (b + 1) % 2
                return evs
            return f

        blocks = []
        for c0 in range(0, 2 * QK, 512):
            blocks.append(dict(pieces=[(w, c0, 512, 0)], kind='single',
                               epi=epi_copy(lambda u, c0=c0: c0 + u * 256, self.QKT)))
        FBv = min(512, VW)
        for c0 in range(0, VW, FBv):
            blocks.append(dict(pieces=[(w, 2 * QK + c0, FBv, 0)], kind='vtok', epi=epi_v(c0)))
        ub = 2 * QK + VW
        for c0 in range(0, CC, 256):
            blocks.append(dict(pieces=[(w, ub + c0, 256, 0), (w, ub + CC + c0, 256, 0)], kind='pair',
                               epi=epi_glu(lambda u, c0=c0: c0 + u * 128)))
        gb = ub + 2 * CC
        for c0 in range(0, 2 * D, 512):
            blocks.append(dict(pieces=[(w, gb + c0, 512, 0)], kind='single',
                               epi=epi_gates(lambda u, c0=c0: c0 + u * 256)))
        self.linear([(self.HT, cfg.DC)], TS, blocks)
        self.phase_end()

    def merge_phase(self, l):
        cfg, P, A = self.cfg, self.P, self.A
        D = cfg.D
        self.phase_begin()
        A = self.A
        TS = min(1024, cfg.T)
        stg = self.stager("st", [128, 1, 1024], BF16)
        gbuf = [[A.get([128, 1024], BF16) for _ in range(2)] for _ in range(2)]
        gds = [P.ds("g0"), P.ds("g1")]
        gfree = [None, None]
        t1 = [A.get([128, 1024], F32) for _ in range(2)]
        t2 = [A.get([128, 1024], F32) for _ in range(2)]
        tfree = [None, None]
        cnt = [0]
        slot_of = {}

        def pre(row0):
            def f(u, tok0):
                s_ = cnt[0] % 2
                cnt[0] += 1
                r = row0 + u * 128
                tk = None
                for gi in range(2):
                    tk = P.dma('sp', lambda e, gi=gi, r=r, s_=s_: e.dma_start(
                        out=gbuf[s_][gi], in_=self.GT[gi * D + r:gi * D + r + 128, tok0:tok0 + 1024]),
                        gds[s_], [gfree[s_]])
                return (s_, tk)
            return f

        def epi(row0):
            def f(u, tok0, bk, lastpe, pretoks):
                s_, gtok = pretoks
                b = stg['i']
                buf = stg['bufs'][b]
                r = row0 + u * 128
                col = cfg.c_bcproj + r // 128
                rel = [None] * 4
                evs = []
                for t in range(2):
                    sl = slice(t * 512, (t + 1) * 512)
                    ta = P.op('dve', lambda e, t=t, sl=sl, bk=bk, s_=s_: e.tensor_tensor(
                        out=t1[s_][:, sl], in0=bk[t], in1=gbuf[s_][0][:, sl], op=ALU.mult),
                        [lastpe, gtok, tfree[s_]])
                    tb = P.op('dve', lambda e, t=t, sl=sl, bk=bk, s_=s_, col=col: e.scalar_tensor_tensor(
                        out=t2[s_][:, sl], in0=bk[2 + t], scalar=self.cl[:, col:col + 1], in1=gbuf[s_][1][:, sl],
                        op0=ALU.add, op1=ALU.mult), [lastpe, gtok, tfree[s_]])
                    tc = P.op('pool', lambda e, sl=sl, buf=buf, s_=s_: e.tensor_tensor(
                        out=buf[:, 0, sl], in0=t1[s_][:, sl], in1=t2[s_][:, sl], op=ALU.add),
                        [ta, tb, stg['tok'][b]])
                    rel[t] = ta
                    rel[2 + t] = tb
                    evs.append(tc)
                tfree[s_] = evs[-1]
                gfree[s_] = rel[3]
                stg['tok'][b] = P.dma('sp', lambda e, buf=buf, r=r: e.dma_start(
                    out=self.MT[r:r + 128, tok0:tok0 + 1024], in_=buf[:, 0, :]), stg['ds'][b], evs)
                stg['i'] = (b + 1) % 2
                return rel
            return f

        blocks = []
        for c0 in range(0, D, 256):
            blocks.append(dict(pieces=[(self.w_ap[l], c0, 256, 0), (self.w_cp[l], c0, 256, 1)], kind='pair',
                               pre=pre(c0), epi=epi(c0)))
        self.linear([(self.ONT, cfg.VW // 128), (self.CT, cfg.CCc)], TS, blocks)
        self.phase_end()

    def resid_phase(self, l, w2d, src, KC, xsrc, TS):
        cfg, P, A = self.cfg, self.P, self.A
        D = cfg.D
        self.phase_begin()
        A = self.A
        xold = [[A.get([128, 1024], F32) for _ in range(2)] for _ in range(2)]
        xds = [P.ds("g0"), P.ds("g1")]
        xfree = [None, None]
        xnew = self.stager("st", [128, 2, 1024], F32)
        tmp = [A.get([128, 1024], F32) for _ in range(2)]
        tmpfree = [None, None]
        cnt = [0]

        def pre(row0):
            def f(u, tok0):
                s_ = cnt[0] % 2
                cnt[0] += 1
                tk = None
                for ci in range(2):
                    r = row0 + u * 256 + ci * 128
                    tk = P.dma('sp', lambda e, ci=ci, r=r, s_=s_: e.dma_start(
                        out=xold[s_][ci], in_=xsrc[r:r + 128, tok0:tok0 + 1024]), xds[s_], [xfree[s_]])
                return (s_, tk)
            return f

        def epi(row0):
            def f(u, tok0, bk, lastpe, pretoks):
                s_, xtok = pretoks
                b = xnew['i']
                buf = xnew['bufs'][b]
                rel = [None] * 4
                evs = []
                for t in range(2):
                    sl = slice(t * 512, (t + 1) * 512)
                    ta = P.op('dve', lambda e, t=t, sl=sl, bk=bk, buf=buf, s_=s_: e.tensor_tensor(
                        out=buf[:, 0, sl], in0=bk[t], in1=xold[s_][0][:, sl], op=ALU.add),
                        [lastpe, xtok, xnew['tok'][b]])
                    tb = P.op('act', lambda e, t=t, sl=sl, bk=bk, s_=s_: e.activation(
                        out=tmp[s_][:, sl], in_=bk[2 + t], func=AF.Copy), [lastpe, tmpfree[s_]])
                    tc = P.op('pool', lambda e, sl=sl, buf=buf, s_=s_: e.tensor_tensor(
                        out=buf[:, 1, sl], in0=tmp[s_][:, sl], in1=xold[s_][1][:, sl], op=ALU.add),
                        [tb, xtok, xnew['tok'][b]])
                    rel[t] = ta
                    rel[2 + t] = tb
                    evs += [ta, tc]
                tmpfree[s_] = evs[-1]
                xfree[s_] = [evs[-1], evs[-2]]
                t_ = None
                for ci in range(2):
                    r = row0 + u * 256 + ci * 128
                    t_ = P.dma('sp', lambda e, ci=ci, r=r, buf=buf: e.dma_start(
                        out=self.XT[r:r + 128, tok0:tok0 + 1024], in_=buf[:, ci, :]), xnew['ds'][b], evs)
                xnew['tok'][b] = t_
                xnew['i'] = (b + 1) % 2
                return rel
            return f

        blocks = []
        FB = 256 if KC > 16 else min(512, D)
        for c0 in range(0, D, FB):
            blocks.append(dict(pieces=[(w2d, c0, FB, 0)], kind='single', pre=pre(c0), epi=epi(c0)))
        self._flatten_fix = True
        self.linear([(src, KC)], TS, blocks)
        self.phase_end()

    def ffnin_phase(self, l):
        cfg, P, A = self.cfg, self.P, self.A
        DFF = cfg.DFF
        self.phase_begin()
        A = self.A
        TS = min(2048, cfg.T)
        stg = self.stager("st", [128, 1, 1024], BF16)
        sil = [A.get([128, 1024], F32) for _ in range(2)]
        silfree = [None, None]
        cnt = [0]
        w = self.w_fi[l]

        def epi(row0):
            def f(u, tok0, bk, lastpe, pre):
                b = stg['i']
                buf = stg['bufs'][b]
                s_ = cnt[0] % 2
                cnt[0] += 1
                rel = [None] * 4
                evs = []
                for t in range(2):
                    sl = slice(t * 512, (t + 1) * 512)
                    ta = P.op('act', lambda e, t=t, sl=sl, bk=bk, s_=s_: e.activation(
                        out=sil[s_][:, sl], in_=bk[t], func=AF.Silu), [lastpe, silfree[s_]])
                    td = P.op('dve', lambda e, t=t, sl=sl, bk=bk, buf=buf, s_=s_: e.tensor_tensor(
                        out=buf[:, 0, sl], in0=bk[2 + t], in1=sil[s_][:, sl], op=ALU.mult), [ta, stg['tok'][b]])
                    rel[t] = ta
                    rel[2 + t] = td
                    evs.append(td)
                silfree[s_] = evs[-1]
                r = row0 + u * 128
                stg['tok'][b] = P.dma('sp', lambda e, buf=buf, r=r: e.dma_start(
                    out=self.AT[r:r + 128, tok0:tok0 + 1024], in_=buf[:, 0, :]), stg['ds'][b], evs)
                stg['i'] = (b + 1) % 2
                return rel
            return f

        blocks = []
        for c0 in range(0, DFF, 256):
            blocks.append(dict(pieces=[(w, c0, 256, 0), (w, DFF + c0, 256, 0)], kind='pair', epi=epi(c0)))
        self.linear([(self.HT, cfg.DC)], TS, blocks)
        self.phase_end()

    def attn_phase(self, l):
        cfg, P, A = self.cfg, self.P, self.A
        T, H, HD, VD, NK, NQ = cfg.T, cfg.H, cfg.HD, cfg.VD, cfg.NK, cfg.NQ
        self.phase_begin()
        A = self.A
        banks = self.banks
        scale = HD ** -0.5
        atab = A.get([128, cfg.AW], F32)
        qb = [A.get([128, 2, T], BF16) for _ in range(2)]
        kb = [A.get([128, 2, T], BF16) for _ in range(2)]
        vb = [A.get([128, NK, VD + 1], BF16) for _ in range(2)]
        NSB = 3
        NTMP = 6
        NPT = 8
        LA = 6
        sbanks = [4, 5, 7]
        tmp = [A.get([128, 512], F32) for _ in range(NTMP)]
        pt = [A.get([128, 512], BF16) for _ in range(NPT)]
        Om = [A.get([128, 4, VD], F32) for _ in range(2)]
        o = A.get([128, 4, VD], F32)
        junk = A.get([128, VD], F32)
        on = A.get([128, 4, VD], BF16)
        ost = [A.get([128, VD // 128, 512], BF16) for _ in range(2)]
        rec = A.get([128, 8], F32)
        ssq = A.get([128, 4], F32)
        rs2 = A.get([128, 4], F32)
        tb = banks[6].bitcast(BF16)
        ta = P.dma('sp', lambda e: e.dma_start(out=atab, in_=self.atab_in), P.ds("c0"))
        ones_tok = []
        for i in range(2):
            ones_tok.append(P.op('pool', lambda e, i=i: e.memset(vb[i][:, :, VD:VD + 1], 1.0)))
        dsl = [P.ds("in0"), P.ds("in1")]
        dso = [P.ds("st0"), P.ds("st1")]
        loadtok = [None, None]
        headdone = [None, None]

        def load_head(h):
            b = h % 2
            w_ = [headdone[b], ones_tok[b]]
            P.dma('sp', lambda e: e.dma_start(out=qb[b], in_=fm(self.QKT)[:, 2 * h:2 * h + 2, :]), dsl[b], w_)
            P.dma('sp', lambda e: e.dma_start(out=kb[b], in_=fm(self.QKT)[:, cfg.QK // 128 + 2 * h:cfg.QK // 128 + 2 * h + 2, :]),
                  dsl[b], w_)
            loadtok[b] = P.dma('sp', lambda e: e.dma_start(
                out=vb[b][:, :, 0:VD], in_=self.V.rearrange("(k p) f -> p k f", p=128)[:, :, h * VD:(h + 1) * VD]),
                dsl[b], w_)

        load_head(0)
        st = {'n': 0, 'sdve': [None] * NSB, 'tact': [None] * NTMP, 'ptpe': [None] * NPT,
              'accfree': [None] * 4, 'omfree': [None, None], 'ofree': None, 'onfree': None,
              'ostfree': [None, None], 'tbfree': None, 'osti': 0, 'ssqfree': None}
        deferred = []

        def run_head(h):
            hb = h % 2
            if h + 1 < H:
                load_head(h + 1)
            slope = 2.0 ** (-8.0 * (h + 1) / H)
            dmin = 80.0 / slope
            kts_of = {}
            for qc in range(NQ):
                keep = []
                for kt in range(NK):
                    dist = max(0, kt * 128 - (qc * 512 + 511), qc * 512 - (kt * 128 + 127))
                    if dist < dmin:
                        keep.append(kt)
                kts_of[qc] = keep
            steps = [(qc, m, kt) for qc in range(NQ) for m in range(2) for kt in kts_of[qc]]
            S_tok = {}
            P_tok = {}

            def emit_S(i):
                qc, m, kt = steps[i]
                n = st['n'] + i
                sbk = sbanks[n % NSB]
                s_tok = P.op('pe', lambda e: e.matmul(
                    banks[sbk], kb[hb][:, m, kt * 128:(kt + 1) * 128], qb[hb][:, m, qc * 512:(qc + 1) * 512],
                    start=True, stop=True), [loadtok[hb], st['sdve'][n % NSB]])
                u0 = 512 * qc - 128 * kt + cfg.AOFF
                t_tok = P.op('dve', lambda e: e.scalar_tensor_tensor(
                    out=tmp[n % NTMP], in0=atab[:, u0:u0 + 512], scalar=slope / scale, in1=banks[sbk],
                    op0=ALU.mult, op1=ALU.add), [s_tok, st['tact'][n % NTMP], ta])
                st['sdve'][n % NSB] = t_tok
                kcol = cfg.g_kmask + kt
                p_tok = P.op('act', lambda e: e.activation(
                    out=pt[n % NPT], in_=tmp[n % NTMP], func=AF.Exp, bias=self.cg[:, kcol:kcol + 1], scale=scale),
                    [t_tok, st['ptpe'][n % NPT]])
                st['tact'][n % NTMP] = p_tok
                P_tok[i] = p_tok

            def emit_PV(i):
                qc, m, kt = steps[i]
                n = st['n'] + i
                last = None
                for qs in range(4):
                    w_ = [P_tok[i]]
                    first = (kt == kts_of[qc][0])
                    lastk = (kt == kts_of[qc][-1])
                    if first:
                        w_.append(st['accfree'][qs])
                    last = P.op('pe', lambda e, qs=qs, first=first, lastk=lastk: e.matmul(
                        banks[qs][:, 0:VD + 1], pt[n % NPT][:, qs * 128:(qs + 1) * 128], vb[hb][:, kt, :],
                        start=first, stop=lastk), w_)
                st['ptpe'][n % NPT] = last
                if kt == kts_of[qc][-1]:
                    finish_group(qc, m, last)
                return last

            def finish_group(qc, m, lastpe):
                r_tok = []
                for qs in range(4):
                    c_ = P.op('dve', lambda e, qs=qs: e.tensor_scalar(
                        out=rec[:, m * 4 + qs:m * 4 + qs + 1], in0=banks[qs][:, VD:VD + 1], scalar1=1e-30,
                        scalar2=None, op0=ALU.max), [lastpe, st['omfree'][m]])
                    r_tok.append(P.op('dve', lambda e, qs=qs: e.reciprocal(
                        out=rec[:, m * 4 + qs:m * 4 + qs + 1], in_=rec[:, m * 4 + qs:m * 4 + qs + 1]), [c_]))
                ev = []
                for qs in range(4):
                    if qs % 2 == 0:
                        tk = P.op('dve', lambda e, qs=qs: e.tensor_scalar(
                            out=Om[m][:, qs, :], in0=banks[qs][:, 0:VD], scalar1=rec[:, m * 4 + qs:m * 4 + qs + 1],
                            scalar2=None, op0=ALU.mult), [r_tok[qs], st['omfree'][m]])
                    else:
                        tk = P.op('act', lambda e, qs=qs: e.activation(
                            out=Om[m][:, qs, :], in_=banks[qs][:, 0:VD], func=AF.Identity,
                            scale=rec[:, m * 4 + qs:m * 4 + qs + 1]), [r_tok[qs], st['omfree'][m]])
                    st['accfree'][qs] = tk
                    ev.append(tk)
                if m == 0:
                    st['om0'] = list(ev)
                    return
                c_tok = P.op('dve', lambda e: e.scalar_tensor_tensor(
                    out=o, in0=Om[1], scalar=self.neglam[:, 1:2], in1=Om[0], op0=ALU.mult, op1=ALU.add),
                    [list(ev), st['om0'], st['ofree']])
                st['omfree'] = [c_tok, c_tok]
                z_tok = P.op('pool', lambda e: e.memset(ssq, 0.0), [st['ssqfree']])
                sq_tok = None
                for qs in range(4):
                    sq_tok = P.op('act', lambda e, qs=qs: e.activation(
                        out=junk, in_=o[:, qs, :], func=AF.Square, accum_out=ssq[:, qs:qs + 1]),
                        [c_tok, z_tok, sq_tok])
                r1 = P.op('dve', lambda e: e.tensor_scalar(out=rs2, in0=ssq, scalar1=1.0 / VD, scalar2=NORM_EPS,
                                                           op0=ALU.mult, op1=ALU.add), [sq_tok, st['onfree']])
                r2a = P.op('act', lambda e: e.activation(out=rs2, in_=rs2, func=AF.Ln), [r1])
                r2 = P.op('act', lambda e: e.activation(out=rs2, in_=rs2, func=AF.Exp, scale=-0.5), [r2a])
                st['ssqfree'] = r1
                n_tok = None
                for qs in range(4):
                    n_tok = P.op('dve', lambda e, qs=qs: e.scalar_tensor_tensor(
                        out=on[:, qs, :], in0=o[:, qs, :], scalar=rs2[:, qs:qs + 1], in1=self.gsub,
                        op0=ALU.mult, op1=ALU.mult), [r2, st['onfree']])
                st['ofree'] = n_tok

                def do_transposes(n_tok=n_tok, qc=qc, h=h):
                    tl = None
                    for vc in range(VD // 128):
                        for qs in range(4):
                            i_ = vc * 4 + qs
                            tl = P.op('pe', lambda e, vc=vc, qs=qs, i_=i_: e.transpose(
                                out=tb[:, i_ * 128:(i_ + 1) * 128], in_=on[:, qs, vc * 128:(vc + 1) * 128],
                                identity=self.ident_bf), [n_tok, st['tbfree']])
                    ob = st['osti']
                    st['osti'] = 1 - ob
                    c_ = P.op('act', lambda e: e.activation(
                        out=ost[ob].rearrange("p a b -> p (a b)"), in_=tb, func=AF.Copy), [tl, st['ostfree'][ob]])
                    st['tbfree'] = c_
                    st['onfree'] = tl
                    st['ostfree'][ob] = P.dma('sp', lambda e: e.dma_start(
                        out=fm(self.ONT)[:, h * (VD // 128):(h + 1) * (VD // 128), qc * 512:(qc + 1) * 512],
                        in_=ost[ob]), dso[ob], [c_])
                deferred.append([6, do_transposes])

            ns = len(steps)
            for i0 in range(min(LA, ns)):
                emit_S(i0)
            lastpv = None
            for i in range(ns):
                lastpv = emit_PV(i)
                if i + LA < ns:
                    emit_S(i + LA)
                for d in deferred:
                    d[0] -= 1
                for d in [d for d in deferred if d[0] <= 0]:
                    d[1]()
                    deferred.remove(d)
            st['n'] += ns
            headdone[hb] = lastpv

        for h in range(H):
            run_head(h)
        for d in deferred:
            d[1]()
        self.phase_end()

    def conv1_phase(self, l):
        cfg, P, A = self.cfg, self.P, self.A
        T, CW, CCc = cfg.T, cfg.CW, cfg.CCc
        PAD = (CW - 1) // 2
        self.phase_begin()
        A = self.A
        banks = self.banks
        ypad = [A.get([128, T + 2 * PAD + 2], BF16) for _ in range(2)]
        diag = [A.get([128, CW, 128], BF16) for _ in range(2)]
        zst = [A.get([128, T], F32) for _ in range(2)]
        dsl = [P.ds("in0"), P.ds("in1")]
        dso = [P.ds("st0"), P.ds("st1")]
        halo = []
        for i in range(2):
            halo.append(P.op('pool', lambda e, i=i: e.memset(ypad[i][:, 0:PAD], 0.0)))
            halo.append(P.op('pool', lambda e, i=i: e.memset(ypad[i][:, PAD + T:PAD + T + PAD], 0.0)))
        pedone = [None, None]
        stdone = [None, None]
        bankfree = [None] * 8
        bn = 0
        ldtok = {}
        tm = A.get([128, T], BF16)
        tmtok = P.dma("pool", lambda e: e.dma_start(out=tm, in_=self.tmask_in), P.ds("pm"))

        def load(c):
            b = c % 2
            t_ = P.dma('sp', lambda e: e.dma_start(out=ypad[b][:, PAD:PAD + T], in_=self.YT[c * 128:(c + 1) * 128, :]),
                       dsl[b], [pedone[b]] + halo)
            ldtok[c] = P.op('pool' if c % 2 else 'dve', lambda e: e.tensor_tensor(
                out=ypad[b][:, PAD:PAD + T], in0=ypad[b][:, PAD:PAD + T], in1=tm, op=ALU.mult), [t_, tmtok])

        dltok = {}

        def build_diag(c):
            b = c % 2
            dl = [None, None]
            for j in range(CW):
                eng = 'dve' if j % 3 != 2 else 'pool'
                col = cfg.c_wdw + c * CW + j
                dl[0 if eng == 'dve' else 1] = P.op(eng, lambda e, j=j, col=col, b=b: e.tensor_scalar(
                    out=diag[b][:, j, :], in0=self.ident_bf, scalar1=self.cl[:, col:col + 1], scalar2=None,
                    op0=ALU.mult), [pedone[b]])
            dltok[c] = dl

        load(0)
        build_diag(0)
        for c in range(CCc):
            b = c % 2
            if c + 1 < CCc:
                load(c + 1)
                build_diag(c + 1)
            dl = dltok[c]
            evs = []
            last = None
            for tt in range(T // 512):
                bk = bn % 8
                bn += 1
                for j in range(CW):
                    w_ = [ldtok[c], dl[0], dl[1], bankfree[bk]] if j == 0 else []
                    last = P.op('pe', lambda e, j=j, tt=tt, bk=bk, b=b: e.matmul(
                        banks[bk], diag[b][:, j, :], ypad[b][:, tt * 512 + j:tt * 512 + j + 512],
                        start=(j == 0), stop=(j == CW - 1)), w_)
                bcol = cfg.c_bdw + c
                if tt % 2 == 0:
                    tk = P.op('act', lambda e, tt=tt, bk=bk, bcol=bcol, b=b: e.activation(
                        out=zst[b][:, tt * 512:(tt + 1) * 512], in_=banks[bk], func=AF.Identity,
                        bias=self.cl[:, bcol:bcol + 1]), [last, stdone[b]])
                else:
                    tk = P.op('dve', lambda e, tt=tt, bk=bk, bcol=bcol, b=b: e.tensor_scalar(
                        out=zst[b][:, tt * 512:(tt + 1) * 512], in0=banks[bk], scalar1=self.cl[:, bcol:bcol + 1],
                        scalar2=None, op0=ALU.add), [last, stdone[b]])
                bankfree[bk] = tk
                evs.append(tk)
            pedone[b] = last
            stdone[b] = P.dma('sp', lambda e, c=c, b=b: e.dma_start(out=self.ZT[c * 128:(c + 1) * 128, :], in_=zst[b]),
                              dso[b], evs)
        self.phase_end()

    def conv2_phase(self, l):
        cfg, P, A = self.cfg, self.P, self.A
        T, CCc, CC = cfg.T, cfg.CCc, cfg.CC
        self.phase_begin()
        A = self.A
        banks = self.banks
        z = [A.get([128, CCc, 512], F32) for _ in range(2)]
        zb = A.get([128, CCc, 512], BF16)
        zs = A.get([128, CCc, 512], BF16)
        mean = A.get([128, 512], F32)
        msq = A.get([128, 512], F32)
        rstd = A.get([128, 512], F32)
        t1 = [A.get([128, 512], F32) for _ in range(4)]
        outb = [A.get([128, CCc, 512], BF16) for _ in range(2)]
        dsl = [P.ds("in0"), P.ds("in1")]
        dso = [P.ds("st0"), P.ds("st1")]
        zfree = [None, None]
        stdone = [None, None]
        pedone = None
        statfree = None
        t1free = [None] * 4
        ld = {}

        def load(i):
            b = i % 2
            ld[i] = P.dma('sp', lambda e: e.dma_start(out=z[b], in_=fm(self.ZT)[:, :, i * 512:(i + 1) * 512]),
                          dsl[b], zfree[b] or [])

        load(0)
        nt = 0
        for i in range(T // 512):
            b = i % 2
            if i + 1 < T // 512:
                load(i + 1)
            a1 = P.op('act', lambda e, b=b: e.activation(out=zb, in_=z[b], func=AF.Copy), [ld[i], pedone])
            a2 = P.op('act', lambda e, b=b: e.activation(out=zs, in_=z[b], func=AF.Square), [ld[i], pedone])
            mm = None
            for c in range(CCc):
                mm = P.op('pe', lambda e, c=c: e.matmul(banks[0], self.ones_bf, zb[:, c, :], start=(c == 0),
                                                        stop=(c == CCc - 1)), [a1, statfree] if c == 0 else [])
            for c in range(CCc):
                mm = P.op('pe', lambda e, c=c: e.matmul(banks[1], self.ones_bf, zs[:, c, :], start=(c == 0),
                                                        stop=(c == CCc - 1)), [a2] if c == 0 else [])
            pedone = mm
            s1 = P.op('dve', lambda e: e.tensor_scalar(out=mean, in0=banks[0], scalar1=1.0 / CC, scalar2=None,
                                                       op0=ALU.mult), [mm] + (zfree[1 - b] or []))
            s2 = P.op('dve', lambda e: e.tensor_tensor(out=msq, in0=mean, in1=mean, op=ALU.mult), [s1])
            s3 = P.op('dve', lambda e: e.scalar_tensor_tensor(out=rstd, in0=banks[1], scalar=1.0 / CC, in1=msq,
                                                              op0=ALU.mult, op1=ALU.subtract), [s2])
            s4a = P.op('dve', lambda e: e.tensor_scalar(out=rstd, in0=rstd, scalar1=LN_EPS, scalar2=None,
                                                        op0=ALU.add), [s3])
            s4b = P.op('act', lambda e: e.activation(out=rstd, in_=rstd, func=AF.Sqrt), [s4a])
            s4 = P.op('dve', lambda e: e.reciprocal(out=rstd, in_=rstd), [s4b])
            statfree = s3
            lasts = []
            for c in range(CCc):
                eng = 'dve' if c % 3 != 2 else 'pool'
                tb_ = nt % 4
                nt += 1
                u1 = P.op(eng, lambda e, c=c, tb_=tb_, b=b: e.tensor_tensor(out=t1[tb_], in0=z[b][:, c, :], in1=mean,
                                                                       op=ALU.subtract), [s4, t1free[tb_]])
                u2 = P.op(eng, lambda e, c=c, tb_=tb_: e.tensor_tensor(out=t1[tb_], in0=t1[tb_], in1=rstd,
                                                                       op=ALU.mult), [u1])
                gcol = cfg.c_gcln + c
                bcol = cfg.c_bcln + c
                u3 = P.op('act', lambda e, c=c, tb_=tb_, gcol=gcol, bcol=bcol, b=b: e.activation(
                    out=outb[b][:, c, :], in_=t1[tb_], func=AF.Silu, bias=self.cl[:, bcol:bcol + 1],
                    scale=self.cl[:, gcol:gcol + 1]), [u2, stdone[b]])
                t1free[tb_] = u3
                lasts.append(u3)
            zfree[b] = [lasts[-1]]
            stdone[b] = P.dma('sp', lambda e, i=i, b=b: e.dma_start(out=fm(self.CT)[:, :, i * 512:(i + 1) * 512], in_=outb[b]),
                              dso[b], [lasts[-1]])
        self.phase_end()


def _consts(cfg, inp):
    DEPTH = cfg.DEPTH
    cl = np.zeros((DEPTH, 128, cfg.NCL), np.float32)

    def pc(v):
        return np.ascontiguousarray(v.reshape(-1, 128).T)

    for l in range(DEPTH):
        c = cl[l]
        c[:, cfg.c_gmix:cfg.c_gmix + cfg.DC] = pc(inp["g_mix"][l])
        c[:, cfg.c_bgate:cfg.c_bgate + 2 * cfg.DC] = pc(inp["b_gate"][l])
        c[:, cfg.c_bdw:cfg.c_bdw + cfg.CCc] = pc(inp["b_dw"][l])
        c[:, cfg.c_gcln:cfg.c_gcln + cfg.CCc] = pc(inp["g_conv_ln"][l])
        c[:, cfg.c_bcln:cfg.c_bcln + cfg.CCc] = pc(inp["b_conv_ln"][l])
        c[:, cfg.c_bcproj:cfg.c_bcproj + cfg.DC] = pc(inp["b_conv_proj"][l])
        c[:, cfg.c_gffn:cfg.c_gffn + cfg.DC] = pc(inp["g_ffn"][l])
        wd = inp["w_dw"][l][:, 0, :]
        wd = wd.T.reshape(cfg.CCc, 128, cfg.CW).transpose(1, 0, 2).reshape(128, cfg.CCc * cfg.CW)
        c[:, cfg.c_wdw:cfg.c_wdw + cfg.CCc * cfg.CW] = wd
        c[:, cfg.c_gsub:cfg.c_gsub + cfg.VD] = inp["g_subln"][l][None, :]
        c[:, cfg.c_lq:cfg.c_lq + 2 * cfg.HD] = inp["lambda_q"][l].reshape(1, -1)
        c[:, cfg.c_lk:cfg.c_lk + 2 * cfg.HD] = inp["lambda_k"][l].reshape(1, -1)
    return cl


def _globals(cfg, g_final, nvalid):
    cg = np.zeros((128, cfg.NCG), np.float32)
    cg[:, cfg.g_gfinal:cfg.g_gfinal + cfg.DC] = g_final.reshape(-1, 128).T
    cg[:, cfg.g_ident:cfg.g_ident + 128] = np.eye(128, dtype=np.float32)
    kpos = np.arange(cfg.T).reshape(cfg.NK, 128).T
    cg[:, cfg.g_kmask:cfg.g_kmask + cfg.NK] = np.where(kpos < nvalid, 0.0, MASK_NEG)
    return cg


def _atab(cfg):
    p = np.arange(128)[:, None]
    u = np.arange(cfg.AW)[None, :]
    return (-np.abs(u - cfg.AOFF - p)).astype(np.float32)


_NC_CACHE = {}


def run_trunk(cfg, seqs, inp, n_cores=8):
    key = (cfg.T, cfg.D, cfg.H, cfg.DFF, cfg.DEPTH)
    if key not in _NC_CACHE:
        _NC_CACHE[key] = Builder(cfg).build()
    nc = _NC_CACHE[key]
    cl = _consts(cfg, inp)
    atab = _atab(cfg)
    wmap = {k: np.ascontiguousarray(inp[k], dtype=np.float32) for k in
            ("w_in", "w_attn_proj", "w_conv_proj", "w_out", "w_ffn_in", "w_ffn_out")}
    in_maps = []
    if n_cores == 8 and len(seqs) == 6:
        core_of_seq = [0, 4, 1, 2, 5, 6]
    else:
        core_of_seq = list(range(len(seqs)))
    seq_of_core = {c: i for i, c in enumerate(core_of_seq)}
    for c in range(n_cores):
        xT = np.zeros((cfg.D, cfg.T), np.float32)
        if c in seq_of_core:
            s = seqs[seq_of_core[c]]
            S = s.shape[0]
            xT[:, :S] = s.T
        else:
            S = cfg.T
        tmask = np.zeros((128, cfg.T), np.float32)
        tmask[:, :S] = 1.0
        m = {"xT": xT, "cl": cl, "cg": _globals(cfg, inp["g_final"], S), "atab": atab, "tmask": tmask}
        m.update(wmap)
        in_maps.append(m)
    res = run_bass_kernel_spmd(nc, in_maps, core_ids=list(range(n_cores)))
    outs = []
    for i in range(len(seqs)):
        S = seqs[i].shape[0]
        outs.append(np.ascontiguousarray(res.results[core_of_seq[i]]["yT"][:, :S].T))
    return outs


def kernel(x_prompt, x_sample, g_mix, w_in, b_gate, lambda_q, lambda_k, g_subln,
           w_attn_proj, w_dw, b_dw, g_conv_ln, b_conv_ln, w_conv_proj, b_conv_proj,
           w_out, g_ffn, w_ffn_in, w_ffn_out, g_final):
    inp = dict(g_mix=g_mix, w_in=w_in, b_gate=b_gate, lambda_q=lambda_q, lambda_k=lambda_k, g_subln=g_subln,
               w_attn_proj=w_attn_proj, w_dw=w_dw, b_dw=b_dw, g_conv_ln=g_conv_ln, b_conv_ln=b_conv_ln,
               w_conv_proj=w_conv_proj, b_conv_proj=b_conv_proj, w_out=w_out, g_ffn=g_ffn,
               w_ffn_in=w_ffn_in, w_ffn_out=w_ffn_out, g_final=g_final)
    inp = {k: np.asarray(v, dtype=np.float32) for k, v in inp.items()}
    x_prompt = np.asarray(x_prompt, dtype=np.float32)
    x_sample = np.asarray(x_sample, dtype=np.float32)
    cfg = Cfg()
    seqs = [x_prompt[b] for b in range(x_prompt.shape[0])] + [x_sample[b] for b in range(x_sample.shape[0])]
    outs = run_trunk(cfg, seqs, inp)
    nb = x_prompt.shape[0]
    y_prompt = np.stack(outs[:nb]).astype(np.float32)
    y_sample = np.stack(outs[nb:]).astype(np.float32)
    return (y_prompt, y_sample)
```
